# Optimizing a Trainium2 kernel written in Bass

```python
import math
import jax, jax.numpy as jnp
from jax import lax
import numpy as np

D_MODEL = 1024
BATCH = 1
SEQ = 16384
DEPTH = 4

CHUNK = 64
MIX_A = D_MODEL // 2
MIX_B = D_MODEL // 2
MIX_C = D_MODEL // 2
MIX_D = D_MODEL // 2
S5_GROUP = 16
S5_GROUPS = MIX_A // S5_GROUP
S5_STATE = 64
CONV_W = 3
ATT_HEADS = 8
HEAD_DIM = MIX_C // ATT_HEADS
LEFT_CHUNKS = 8
BAND = (LEFT_CHUNKS + 1) * CHUNK
MAX_REL = 128
POOL_WINDOWS = (2, 4, 8, 16)
POOL_GROUP = MIX_D // len(POOL_WINDOWS)
D_FF = ((math.ceil(8 * D_MODEL / 3) + 255) // 256) * 256
D_PLE = 256
EV_IN = MIX_A + 3 * MIX_B
OD_IN = 3 * MIX_C + MIX_D
N_EVEN = (DEPTH + 1) // 2
N_ODD = DEPTH // 2
ALPHA = (2 * DEPTH) ** 0.25
BETA = (8 * DEPTH) ** -0.25
LN_EPS = 1e-5
NEG_INF = -1e30

kernel_name = "hybrid_s5_conv_chunkattn_pool_deepnorm"


def layer_norm(x, g, b):
    xf = x.astype(jnp.float32)
    mu = jnp.mean(xf, axis=-1, keepdims=True)
    var = jnp.mean(jnp.square(xf - mu), axis=-1, keepdims=True)
    y = (xf - mu) * lax.rsqrt(var + LN_EPS)
    return (y * g.astype(jnp.float32) + b.astype(jnp.float32)).astype(x.dtype)


def _complex_affine_combine(e1, e2):
    a1r, a1i, b1r, b1i = e1
    a2r, a2i, b2r, b2i = e2
    ar = a2r * a1r - a2i * a1i
    ai = a2r * a1i + a2i * a1r
    br = a2r * b1r - a2i * b1i + b2r
    bi = a2r * b1i + a2i * b1r + b2i
    return (ar, ai, br, bi)


def s5_mixer(u, lam_re, lam_im, log_dt, b_re, b_im, c_re, c_im, d_skip, w_glu, b_glu):
    f32 = jnp.float32
    bsz, L, _ = u.shape
    lre = lam_re.astype(f32)
    lim = lam_im.astype(f32)
    dt = jnp.exp(log_dt.astype(f32))[:, None]
    mag = jnp.exp(lre * dt)
    ang = lim * dt
    lb_re = mag * jnp.cos(ang)
    lb_im = mag * jnp.sin(ang)
    den = lre * lre + lim * lim
    nr = lb_re - 1.0
    ni = lb_im
    r_re = (nr * lre + ni * lim) / den
    r_im = (ni * lre - nr * lim) / den
    br = b_re.astype(f32)
    bi = b_im.astype(f32)
    bb_re = r_re[..., None] * br - r_im[..., None] * bi
    bb_im = r_re[..., None] * bi + r_im[..., None] * br
    uf = u.astype(f32).reshape(bsz, L, S5_GROUPS, S5_GROUP)
    bu_re = jnp.einsum('blgh,gph->blgp', uf, bb_re)
    bu_im = jnp.einsum('blgh,gph->blgp', uf, bb_im)
    a_re = jnp.broadcast_to(lb_re, bu_re.shape)
    a_im = jnp.broadcast_to(lb_im, bu_im.shape)
    _, _, xr, xi = lax.associative_scan(_complex_affine_combine, (a_re, a_im, bu_re, bu_im), axis=1)
    y = (jnp.einsum('ghp,blgp->blgh', c_re.astype(f32), xr)
         - jnp.einsum('ghp,blgp->blgh', c_im.astype(f32), xi)
         + d_skip.astype(f32).reshape(S5_GROUPS, S5_GROUP) * uf)
    y = y.reshape(bsz, L, MIX_A)
    g = jax.nn.gelu(y)
    out = g * jax.nn.sigmoid(g @ w_glu.astype(f32) + b_glu.astype(f32))
    return out.astype(u.dtype)


def short_conv_mixer(b_gate, c_gate, x_in, conv_w):
    L = x_in.shape[1]
    z = c_gate * x_in
    zp = jnp.pad(z, ((0, 0), (CONV_W - 1, 0), (0, 0)))
    y = conv_w[0] * zp[:, 0:L]
    for k in range(1, CONV_W):
        y = y + conv_w[k] * zp[:, k:k + L]
    return b_gate * y


def chunk_attention(q, k, v, rel_bias):
    bsz, L, _ = q.shape
    nc = L // CHUNK
    q = q.reshape(bsz, nc, CHUNK, ATT_HEADS, HEAD_DIM) * (HEAD_DIM ** -0.5)
    k = k.reshape(bsz, nc, CHUNK, ATT_HEADS, HEAD_DIM)
    v = v.reshape(bsz, nc, CHUNK, ATT_HEADS, HEAD_DIM)
    pad = ((0, 0), (LEFT_CHUNKS, 0), (0, 0), (0, 0), (0, 0))
    kp = jnp.pad(k, pad)
    vp = jnp.pad(v, pad)
    kb = jnp.concatenate([kp[:, j:j + nc] for j in range(LEFT_CHUNKS + 1)], axis=2)
    vb = jnp.concatenate([vp[:, j:j + nc] for j in range(LEFT_CHUNKS + 1)], axis=2)
    s = jnp.einsum('bcqhd,bckhd->bchqk', q, kb).astype(jnp.float32)
    qi = jnp.arange(CHUNK)[:, None]
    kj = jnp.arange(BAND)[None, :]
    rel = jnp.clip(qi + LEFT_CHUNKS * CHUNK - kj, -MAX_REL, MAX_REL) + MAX_REL
    bias = rel_bias.astype(jnp.float32)[:, rel]
    key_chunk = (jnp.arange(BAND) // CHUNK)[None, :] - LEFT_CHUNKS
    valid = (jnp.arange(nc)[:, None] + key_chunk) >= 0
    s = jnp.where(valid[None, :, None, None, :], s + bias[None, None], NEG_INF)
    pr = jax.nn.softmax(s, axis=-1).astype(vb.dtype)
    o = jnp.einsum('bchqk,bckhd->bcqhd', pr, vb)
    return o.reshape(bsz, L, MIX_C)


def pool_mixer(z, pool_w, pool_scale):
    bsz, L, _ = z.shape
    zf = z.astype(jnp.float32)
    cs = jnp.cumsum(zf, axis=1)
    t = jnp.arange(L)
    outs = []
    for gi, w in enumerate(POOL_WINDOWS):
        lo, hi = gi * POOL_GROUP, (gi + 1) * POOL_GROUP
        csg = cs[..., lo:hi]
        lagged = jnp.pad(csg[:, :L - w], ((0, 0), (w, 0), (0, 0)))
        count = jnp.minimum(t + 1, w).astype(jnp.float32)[None, :, None]
        outs.append((csg - lagged) / count - zf[..., lo:hi])
    pooled = jnp.stack(outs, axis=2)
    mixed = jnp.einsum('blgc,gcd->blgd', pooled, pool_w.astype(jnp.float32)).reshape(bsz, L, MIX_D)
    return (mixed * pool_scale.astype(jnp.float32)).astype(z.dtype)


def swiglu(x, w_up, w_down):
    h = x @ w_up
    g, u = jnp.split(h, 2, axis=-1)
    return (jax.nn.silu(g) * u) @ w_down


def setup_inputs(seed: int = 0) -> dict:
    key = jax.random.key(seed)
    ks = jax.random.split(key, 32)
    f32 = jnp.float32

    def nrm(k, shape, scale):
        return scale * jax.random.normal(k, shape, f32)

    x = nrm(ks[0], (BATCH, SEQ, D_MODEL), 1.0)
    p = nrm(ks[1], (DEPTH, BATCH, SEQ, D_PLE), 1.0)
    ev_w_in = nrm(ks[2], (N_EVEN, D_MODEL, EV_IN), D_MODEL ** -0.5)
    n_idx = jnp.arange(S5_STATE, dtype=f32)
    ev_lambda_re = -0.5 + nrm(ks[3], (N_EVEN, S5_GROUPS, S5_STATE), 0.01)
    ev_lambda_im = math.pi * n_idx + nrm(ks[4], (N_EVEN, S5_GROUPS, S5_STATE), 0.01)
    ev_log_dt = jax.random.uniform(ks[5], (N_EVEN, S5_GROUPS), f32, math.log(1e-3), math.log(1e-1))
    ev_b_re = nrm(ks[6], (N_EVEN, S5_GROUPS, S5_STATE, S5_GROUP), (2 * S5_GROUP) ** -0.5)
    ev_b_im = nrm(ks[7], (N_EVEN, S5_GROUPS, S5_STATE, S5_GROUP), (2 * S5_GROUP) ** -0.5)
    ev_c_re = nrm(ks[8], (N_EVEN, S5_GROUPS, S5_GROUP, S5_STATE), S5_STATE ** -0.5)
    ev_c_im = nrm(ks[9], (N_EVEN, S5_GROUPS, S5_GROUP, S5_STATE), S5_STATE ** -0.5)
    ev_d = nrm(ks[10], (N_EVEN, MIX_A), 1.0)
    ev_w_glu = nrm(ks[11], (N_EVEN, MIX_A, MIX_A), MIX_A ** -0.5)
    ev_b_glu = nrm(ks[12], (N_EVEN, MIX_A), 0.02)
    ev_conv_w = nrm(ks[13], (N_EVEN, CONV_W, MIX_B), CONV_W ** -0.5)
    ev_w_out = nrm(ks[14], (N_EVEN, D_MODEL, D_MODEL), BETA * D_MODEL ** -0.5)
    od_w_in = nrm(ks[15], (N_ODD, D_MODEL, OD_IN), D_MODEL ** -0.5)
    od_rel_bias = nrm(ks[16], (N_ODD, ATT_HEADS, 2 * MAX_REL + 1), 0.1)
    od_pool_w = nrm(ks[17], (N_ODD, len(POOL_WINDOWS), POOL_GROUP, POOL_GROUP), POOL_GROUP ** -0.5)
    od_pool_scale = 1.0 + nrm(ks[18], (N_ODD, MIX_D), 0.1)
    od_w_out = nrm(ks[19], (N_ODD, D_MODEL, D_MODEL), BETA * D_MODEL ** -0.5)
    ln_mix_g = 1.0 + nrm(ks[20], (DEPTH, D_MODEL), 0.01)
    ln_mix_b = nrm(ks[21], (DEPTH, D_MODEL), 0.01)
    ln_ffn_g = 1.0 + nrm(ks[22], (DEPTH, D_MODEL), 0.01)
    ln_ffn_b = nrm(ks[23], (DEPTH, D_MODEL), 0.01)
    ffn_w_up = nrm(ks[24], (DEPTH, D_MODEL, 2 * D_FF), D_MODEL ** -0.5)
    ffn_w_down = nrm(ks[25], (DEPTH, D_FF, D_MODEL), BETA * D_FF ** -0.5)
    ple_w_proj = nrm(ks[26], (DEPTH, D_PLE, D_MODEL), D_PLE ** -0.5)
    ple_w_gate = nrm(ks[27], (DEPTH, D_MODEL, D_MODEL), D_MODEL ** -0.5)
    ple_b_gate = nrm(ks[28], (DEPTH, D_MODEL), 0.01)
    return {
        "x": x, "p": p,
        "ev_w_in": ev_w_in, "ev_lambda_re": ev_lambda_re, "ev_lambda_im": ev_lambda_im,
        "ev_log_dt": ev_log_dt, "ev_b_re": ev_b_re, "ev_b_im": ev_b_im,
        "ev_c_re": ev_c_re, "ev_c_im": ev_c_im, "ev_d": ev_d,
        "ev_w_glu": ev_w_glu, "ev_b_glu": ev_b_glu, "ev_conv_w": ev_conv_w, "ev_w_out": ev_w_out,
        "od_w_in": od_w_in, "od_rel_bias": od_rel_bias, "od_pool_w": od_pool_w,
        "od_pool_scale": od_pool_scale, "od_w_out": od_w_out,
        "ln_mix_g": ln_mix_g, "ln_mix_b": ln_mix_b, "ln_ffn_g": ln_ffn_g, "ln_ffn_b": ln_ffn_b,
        "ffn_w_up": ffn_w_up, "ffn_w_down": ffn_w_down,
        "ple_w_proj": ple_w_proj, "ple_w_gate": ple_w_gate, "ple_b_gate": ple_b_gate,
    }


def reference(x, p, ev_w_in, ev_lambda_re, ev_lambda_im, ev_log_dt, ev_b_re, ev_b_im,
              ev_c_re, ev_c_im, ev_d, ev_w_glu, ev_b_glu, ev_conv_w, ev_w_out,
              od_w_in, od_rel_bias, od_pool_w, od_pool_scale, od_w_out,
              ln_mix_g, ln_mix_b, ln_ffn_g, ln_ffn_b, ffn_w_up, ffn_w_down,
              ple_w_proj, ple_w_gate, ple_b_gate):
    for i in range(DEPTH):
        if i % 2 == 0:
            e = i // 2
            h = x @ ev_w_in[e]
            u_a, b_g, c_g, x_b = jnp.split(h, [MIX_A, MIX_A + MIX_B, MIX_A + 2 * MIX_B], axis=-1)
            y_a = s5_mixer(u_a, ev_lambda_re[e], ev_lambda_im[e], ev_log_dt[e], ev_b_re[e], ev_b_im[e],
                           ev_c_re[e], ev_c_im[e], ev_d[e], ev_w_glu[e], ev_b_glu[e])
            y_b = short_conv_mixer(b_g, c_g, x_b, ev_conv_w[e])
            mix = jnp.concatenate([y_a, y_b], axis=-1) @ ev_w_out[e]
        else:
            o = i // 2
            h = x @ od_w_in[o]
            q, k, v, z = jnp.split(h, [MIX_C, 2 * MIX_C, 3 * MIX_C], axis=-1)
            y_c = chunk_attention(q, k, v, od_rel_bias[o])
            y_d = pool_mixer(z, od_pool_w[o], od_pool_scale[o])
            mix = jnp.concatenate([y_c, y_d], axis=-1) @ od_w_out[o]
        x = layer_norm(ALPHA * x + mix, ln_mix_g[i], ln_mix_b[i])
        x = layer_norm(ALPHA * x + swiglu(x, ffn_w_up[i], ffn_w_down[i]), ln_ffn_g[i], ln_ffn_b[i])
        x = x + jax.nn.sigmoid(x @ ple_w_gate[i] + ple_b_gate[i]) * (p[i] @ ple_w_proj[i])
    return x
```

```python
import math
import numpy as np
import concourse.bass as bass
import concourse.mybir as mybir
from concourse.bass_utils import run_bass_kernel_spmd

F32 = mybir.dt.float32
BF16 = mybir.dt.bfloat16
AF = mybir.ActivationFunctionType
ALU = mybir.AluOpType
AX = mybir.AxisListType
NCORE = 8
MAGIC = 12582912.0
TWO_PI = 2.0 * math.pi

D_MODEL = 1024
SEQ = 16384
DEPTH = 4
D_FF = 2816
ALPHA = (2 * DEPTH) ** 0.25
LN_EPS = 1e-5


class Prog:
    ENGS = ("sync", "gpsimd", "act", "dve", "pe")
    NDS = 8

    def __init__(self):
        self.nc = bass.Bass("TRN2", target_bir_lowering=False)
        self.ops = {e: [] for e in self.ENGS}
        self.cnt = {}
        self.lastw = {}
        self.reads = {}
        self.seen = {e: {} for e in self.ENGS}
        self.ndma = {e: 0 for e in self.ENGS}
        self.ctx = []
        self.out_waits = []

    def enter(self, cm):
        v = cm.__enter__()
        self.ctx.append(cm)
        return v

    def sb(self, name, shape, dt):
        return self.enter(self.nc.sbuf_tensor(name, list(shape), dt))

    def ps(self, name, shape, dt=F32):
        return self.enter(self.nc.psum_tensor(name, list(shape), dt))

    def dram_in(self, name, shape, dt=F32):
        return self.nc.dram_tensor(name, list(shape), dt, kind="ExternalInput").ap()

    def dram_out(self, name, shape, dt=F32):
        return self.nc.dram_tensor(name, list(shape), dt, kind="ExternalOutput").ap()

    def _op(self, eng, fn, r, w, dma=False, is_out=False):
        waits = {}

        def need(sv):
            s, v = sv
            waits[s] = max(waits.get(s, 0), v)

        for k in r:
            if k in self.lastw:
                need(self.lastw[k])
        for k in w:
            if k in self.lastw:
                need(self.lastw[k])
            for sv in self.reads.get(k, ()):
                need(sv)
        if dma:
            i = self.ndma[eng]
            self.ndma[eng] += 1
            sem = "%s_d%d" % (eng, i % self.NDS)
            inc = 16
        else:
            sem = eng
            inc = 1
        prev = self.cnt.get(sem, 0)
        self.cnt[sem] = prev + inc
        me = (sem, self.cnt[sem])
        wl = []
        if dma and prev > 0:
            waits[sem] = max(waits.get(sem, 0), prev)
        for s, v in waits.items():
            if eng == "pe" and s == "pe":
                continue
            if self.seen[eng].get(s, 0) >= v:
                continue
            self.seen[eng][s] = v
            wl.append((s, v))
        self.ops[eng].append((wl, fn, sem, inc))
        for k in w:
            self.lastw[k] = me
            self.reads[k] = []
        for k in r:
            self.reads.setdefault(k, []).append(me)
        if is_out:
            self.out_waits.append(me)

    def dma(self, out, in_, r=(), w=(), cast=False, is_out=False, q=None):
        eng = "gpsimd" if cast else (q or "sync")
        self._op(eng, lambda e: e.dma_start(out=out, in_=in_), r, w, dma=True, is_out=is_out)

    def act(self, fn, r=(), w=()):
        self._op("act", fn, r, w)

    def dve(self, fn, r=(), w=()):
        self._op("dve", fn, r, w)

    def pe(self, fn, r=(), w=()):
        self._op("pe", fn, r, w)

    def finish(self):
        nc = self.nc
        fin = {}
        for s, v in self.out_waits:
            fin[s] = max(fin.get(s, 0), v)
        names = sorted(self.cnt.keys())
        sems = {}
        for n in names:
            sems[n] = self.enter(nc.semaphore(n))
        block = self.enter(nc.Block())
        engmap = {"sync": block.sync, "gpsimd": block.gpsimd, "act": block.scalar,
                  "dve": block.vector, "pe": block.tensor}

        def make(ename):
            def body(e):
                for wl, fn, sem, inc in self.ops[ename]:
                    for s, v in wl:
                        e.wait_ge(sems[s], v)
                    fn(e).then_inc(sems[sem], inc)
                if ename == "sync":
                    for s, v in fin.items():
                        e.wait_ge(sems[s], v)
            return body

        for ename in self.ENGS:
            if self.ops[ename] or ename == "sync":
                engmap[ename](make(ename))
        for cm in reversed(self.ctx):
            cm.__exit__(None, None, None)
        self.ctx = []
        return nc


def run(nc, in_maps):
    res = run_bass_kernel_spmd(nc, in_maps, core_ids=list(range(NCORE)))
    return res.results


_CACHE = {}


def build_lin(K, M, N, mode="plain", func=None, has_bias=False, has_scale=False):
    P = Prog()
    nc = P.nc
    KT = K // 128
    MO = M // 2 if mode == "swiglu" else M
    MT = MO // 128
    NT = N // 512
    inT = P.dram_in("inT", [K, N])
    W = P.dram_in("W", [K, M])
    outT = P.dram_out("outT", [MO, N])
    bias = P.dram_in("bias", [128, MT]) if has_bias else None
    scale = P.dram_in("scale", [128, MT]) if has_scale else None
    A = P.dram_in("A", [MO, N]) if mode == "fma" else None
    B = P.dram_in("B", [MO, N]) if mode == "fma" else None
    inb = P.sb("inb", [128, KT, N], BF16)
    NW = 3
    wb = [P.sb("wb%d" % i, [128, KT, 128], BF16) for i in range(NW * (2 if mode == "swiglu" else 1))]
    ot = [P.sb("ot%d" % i, [128, N], F32) for i in range(2)]
    pss = [P.ps("ps%d" % i, [128, 512]) for i in range(4)]
    if has_bias:
        bt = P.sb("bt", [128, MT], F32)
        P.dma(bt[:], bias, w=["bt"])
    if has_scale:
        st = P.sb("st", [128, MT], F32)
        P.dma(st[:], scale, w=["st"])
    if mode == "mulin":
        gf = P.sb("gf", [128, KT, N], F32)
        xs = [P.sb("xs%d" % i, [128, N], F32) for i in range(2)]
        tq = P.sb("tq", [128, N], F32)
        for kt in range(KT):
            x = xs[kt % 2]
            xk = "xs%d" % (kt % 2)
            P.dma(x[:], inT[kt * 128:(kt + 1) * 128, :], w=[xk])
            P.dve(lambda e, x=x: e.tensor_tensor(out=tq[:], in0=x[:], in1=x[:], op=ALU.mult), r=[xk], w=["tq"])
            P.dve(lambda e: e.tensor_scalar(out=tq[:], in0=tq[:], scalar1=0.044715, scalar2=1.0, op0=ALU.mult, op1=ALU.add), r=["tq"], w=["tq"])
            P.dve(lambda e, x=x: e.tensor_tensor(out=tq[:], in0=tq[:], in1=x[:], op=ALU.mult), r=["tq", xk], w=["tq"])
            P.act(lambda e: e.activation(out=tq[:], in_=tq[:], func=AF.Sigmoid, scale=1.5957691216057308), r=["tq"], w=["tq"])
            P.dve(lambda e, x=x, kt=kt: e.tensor_tensor(out=gf[:, kt, :], in0=tq[:], in1=x[:], op=ALU.mult), r=["tq", xk], w=["gf%d" % kt])
            P.act(lambda e, kt=kt: e.activation(out=inb[:, kt, :], in_=gf[:, kt, :], func=AF.Copy), r=["gf%d" % kt], w=["inb%d" % kt])
    else:
        for kt in range(KT):
            P.dma(inb[:, kt, :], inT[kt * 128:(kt + 1) * 128, :], w=["inb%d" % kt], cast=True)
    if mode == "swiglu":
        tmp = [P.sb("tmp%d" % i, [128, 512], F32) for i in range(2)]
    if mode == "fma":
        At = [P.sb("At%d" % i, [128, N], F32) for i in range(2)]
        Bt = [P.sb("Bt%d" % i, [128, N], F32) for i in range(2)]
    Wv = W.rearrange("(kt p) m -> p kt m", p=128)
    pi = 0
    for m in range(MT):
        s = m % NW
        o = ot[m % 2]
        ok = "ot%d" % (m % 2)
        P.dma(wb[s][:], Wv[:, :, m * 128:(m + 1) * 128], w=["wb%d" % s], cast=True)
        if mode == "swiglu":
            P.dma(wb[NW + s][:], Wv[:, :, MO + m * 128:MO + (m + 1) * 128], w=["wb%d" % (NW + s)], cast=True)
        if mode == "fma":
            P.dma(At[m % 2][:], A[m * 128:(m + 1) * 128, :], w=["At%d" % (m % 2)])
            P.dma(Bt[m % 2][:], B[m * 128:(m + 1) * 128, :], w=["Bt%d" % (m % 2)])
        for nt in range(NT):
            sl = slice(nt * 512, (nt + 1) * 512)
            ps = pss[pi % 4]
            pk = "ps%d" % (pi % 4)
            pi += 1
            for kt in range(KT):
                P.pe(lambda e, ps=ps, s=s, kt=kt, sl=sl: e.matmul(ps[:], lhsT=wb[s][:, kt, :], rhs=inb[:, kt, sl], start=(kt == 0), stop=(kt == KT - 1)),
                     r=["wb%d" % s, "inb%d" % kt], w=[pk])
            if mode == "swiglu":
                ps2 = pss[pi % 4]
                pk2 = "ps%d" % (pi % 4)
                pi += 1
                for kt in range(KT):
                    P.pe(lambda e, ps2=ps2, s=s, kt=kt, sl=sl: e.matmul(ps2[:], lhsT=wb[NW + s][:, kt, :], rhs=inb[:, kt, sl], start=(kt == 0), stop=(kt == KT - 1)),
                         r=["wb%d" % (NW + s), "inb%d" % kt], w=[pk2])
                t = tmp[nt % 2]
                tk = "tmp%d" % (nt % 2)
                P.act(lambda e, t=t, ps=ps: e.activation(out=t[:], in_=ps[:], func=AF.Silu), r=[pk], w=[tk])
                P.dve(lambda e, o=o, t=t, ps2=ps2, sl=sl: e.tensor_tensor(out=o[:, sl], in0=t[:], in1=ps2[:], op=ALU.mult), r=[tk, pk2], w=[ok])
            elif mode == "fma":
                P.dve(lambda e, o=o, ps=ps, sl=sl, m=m: e.tensor_tensor(out=o[:, sl], in0=Bt[m % 2][:, sl], in1=ps[:], op=ALU.mult), r=[pk, "Bt%d" % (m % 2)], w=[ok])
                P.dve(lambda e, o=o, sl=sl, m=m: e.tensor_tensor(out=o[:, sl], in0=o[:, sl], in1=At[m % 2][:, sl], op=ALU.add), r=[ok, "At%d" % (m % 2)], w=[ok])
            elif mode == "mulin":
                P.act(lambda e, o=o, ps=ps, sl=sl, m=m: e.activation(out=o[:, sl], in_=ps[:], func=AF.Sigmoid, bias=bt[:, m:m + 1]), r=[pk, "bt"], w=[ok])
                P.dve(lambda e, o=o, sl=sl, m=m: e.tensor_tensor(out=o[:, sl], in0=o[:, sl], in1=gf[:, m, sl], op=ALU.mult), r=[ok, "gf%d" % m], w=[ok])
            else:
                kw = {}
                rr = [pk]
                if has_bias:
                    kw["bias"] = bt[:, m:m + 1]
                    rr.append("bt")
                if has_scale:
                    kw["scale"] = st[:, m:m + 1]
                    rr.append("st")
                f = func if func is not None else AF.Identity
                P.act(lambda e, o=o, ps=ps, sl=sl, kw=kw, f=f: e.activation(out=o[:, sl], in_=ps[:], func=f, **kw), r=rr, w=[ok])
        P.dma(outT[m * 128:(m + 1) * 128, :], o[:], r=[ok], w=["out"], is_out=True)
    return P.finish()


def build_ple(N):
    P = Prog()
    KT, K2, MT, NT = 8, 2, 8, N // 512
    x2T = P.dram_in("x2T", [1024, N]); pT = P.dram_in("pT", [256, N])
    Wg = P.dram_in("Wg", [1024, 1024]); Wp = P.dram_in("Wp", [256, 1024]); bias = P.dram_in("bias", [128, MT])
    outT = P.dram_out("outT", [1024, N])
    inb = P.sb("inb", [128, KT, N], BF16); pb = P.sb("pb", [128, K2, N], BF16)
    bt = P.sb("bt", [128, MT], F32)
    P.dma(bt[:], bias, w=["bt"])
    for kt in range(KT):
        P.dma(inb[:, kt, :], x2T[kt * 128:(kt + 1) * 128, :], w=["inb%d" % kt], cast=True)
    for kt in range(K2):
        P.dma(pb[:, kt, :], pT[kt * 128:(kt + 1) * 128, :], w=["pb"], cast=True)
    NW = 3
    wg = [P.sb("wg%d" % i, [128, KT, 128], BF16) for i in range(NW)]
    wp = [P.sb("wp%d" % i, [128, K2, 128], BF16) for i in range(NW)]
    At = [P.sb("At%d" % i, [128, N], F32) for i in range(2)]
    ot = [P.sb("ot%d" % i, [128, N], F32) for i in range(2)]
    tmp = [P.sb("tmp%d" % i, [128, 512], F32) for i in range(2)]
    pss = [P.ps("ps%d" % i, [128, 512]) for i in range(4)]
    Wgv = Wg.rearrange("(kt p) m -> p kt m", p=128); Wpv = Wp.rearrange("(kt p) m -> p kt m", p=128)
    pi = 0
    for m in range(MT):
        s_ = m % NW
        o = ot[m % 2]; ok = "ot%d" % (m % 2); A = At[m % 2]; ak = "At%d" % (m % 2)
        ms = slice(m * 128, (m + 1) * 128)
        P.dma(wg[s_][:], Wgv[:, :, ms], w=["wg%d" % s_], cast=True)
        P.dma(wp[s_][:], Wpv[:, :, ms], w=["wp%d" % s_], cast=True)
        P.dma(A[:], x2T[ms, :], w=[ak], q="act")
        for nt in range(NT):
            sl = slice(nt * 512, (nt + 1) * 512)
            ps = pss[pi % 4]; pk = "ps%d" % (pi % 4); pi += 1
            ps2 = pss[pi % 4]; pk2 = "ps%d" % (pi % 4); pi += 1
            for kt in range(KT):
                P.pe(lambda e, ps=ps, s_=s_, kt=kt, sl=sl: e.matmul(ps[:], lhsT=wg[s_][:, kt, :], rhs=inb[:, kt, sl], start=(kt == 0), stop=(kt == KT - 1)), r=["wg%d" % s_, "inb%d" % kt], w=[pk])
            for kt in range(K2):
                P.pe(lambda e, ps2=ps2, s_=s_, kt=kt, sl=sl: e.matmul(ps2[:], lhsT=wp[s_][:, kt, :], rhs=pb[:, kt, sl], start=(kt == 0), stop=(kt == K2 - 1)), r=["wp%d" % s_, "pb"], w=[pk2])
            t = tmp[nt % 2]; tk = "tmp%d" % (nt % 2)
            P.act(lambda e, t=t, ps=ps, m=m: e.activation(out=t[:], in_=ps[:], func=AF.Sigmoid, bias=bt[:, m:m + 1]), r=[pk, "bt"], w=[tk])
            P.dve(lambda e, o=o, t=t, ps2=ps2, sl=sl: e.tensor_tensor(out=o[:, sl], in0=t[:], in1=ps2[:], op=ALU.mult), r=[tk, pk2], w=[ok])
            P.dve(lambda e, o=o, A=A, sl=sl: e.tensor_tensor(out=o[:, sl], in0=o[:, sl], in1=A[:, sl], op=ALU.add), r=[ok, ak], w=[ok])
        P.dma(outT[ms, :], o[:], r=[ok], w=["out"], is_out=True)
    return P.finish()


def ple_dev(x2T, pT, Wg, bg, Wp):
    T = x2T.shape[1]
    N = T // NCORE
    key = ("ple", N)
    if key not in _CACHE:
        _CACHE[key] = build_ple(N)
    b = np.ascontiguousarray(bg.reshape(8, 128).T)
    maps = [{"x2T": np.ascontiguousarray(x2T[:, c * N:(c + 1) * N]), "pT": np.ascontiguousarray(pT[:, c * N:(c + 1) * N]),
             "Wg": np.ascontiguousarray(Wg), "Wp": np.ascontiguousarray(Wp), "bias": b} for c in range(NCORE)]
    res = run(_CACHE[key], maps)
    return np.concatenate([r["outT"] for r in res], axis=1)


def build_ffn(N):
    P = Prog()
    KT, JT, MT, NT = 8, D_FF // 128, 8, N // 512
    inT = P.dram_in("inT", [1024, N]); Wu = P.dram_in("Wu", [1024, 2 * D_FF]); Wd = P.dram_in("Wd", [D_FF, 1024])
    outT = P.dram_out("outT", [1024, N])
    inb = P.sb("inb", [128, KT, N], BF16)
    actb = P.sb("actb", [128, JT, N], BF16)
    for kt in range(KT):
        P.dma(inb[:, kt, :], inT[kt * 128:(kt + 1) * 128, :], w=["inb%d" % kt], cast=True)
    NW = 3
    wb = [P.sb("wb%d" % i, [128, KT, 128], BF16) for i in range(2 * NW)]
    wd = [P.sb("wd%d" % i, [128, JT, 128], BF16) for i in range(2)]
    ot = [P.sb("ot%d" % i, [128, N], F32) for i in range(2)]
    tmp = [P.sb("tmp%d" % i, [128, 512], F32) for i in range(2)]
    pss = [P.ps("ps%d" % i, [128, 512]) for i in range(6)]
    Wuv = Wu.rearrange("(kt p) m -> p kt m", p=128); Wdv = Wd.rearrange("(jt p) m -> p jt m", p=128)
    pi = 0
    for j in range(JT):
        s_ = j % NW
        P.dma(wb[s_][:], Wuv[:, :, j * 128:(j + 1) * 128], w=["wb%d" % s_], cast=True)
        P.dma(wb[NW + s_][:], Wuv[:, :, D_FF + j * 128:D_FF + (j + 1) * 128], w=["wb%d" % (NW + s_)], cast=True)
        for nt in range(NT):
            sl = slice(nt * 512, (nt + 1) * 512)
            ps = pss[pi % 6]; pk = "ps%d" % (pi % 6); pi += 1
            ps2 = pss[pi % 6]; pk2 = "ps%d" % (pi % 6); pi += 1
            for kt in range(KT):
                P.pe(lambda e, ps=ps, s_=s_, kt=kt, sl=sl: e.matmul(ps[:], lhsT=wb[s_][:, kt, :], rhs=inb[:, kt, sl], start=(kt == 0), stop=(kt == KT - 1)), r=["wb%d" % s_, "inb%d" % kt], w=[pk])
            for kt in range(KT):
                P.pe(lambda e, ps2=ps2, s_=s_, kt=kt, sl=sl: e.matmul(ps2[:], lhsT=wb[NW + s_][:, kt, :], rhs=inb[:, kt, sl], start=(kt == 0), stop=(kt == KT - 1)), r=["wb%d" % (NW + s_), "inb%d" % kt], w=[pk2])
            t = tmp[nt % 2]; tk = "tmp%d" % (nt % 2)
            P.act(lambda e, t=t, ps=ps: e.activation(out=t[:], in_=ps[:], func=AF.Silu), r=[pk], w=[tk])
            P.dve(lambda e, t=t, ps2=ps2, j=j, sl=sl: e.tensor_tensor(out=actb[:, j, sl], in0=t[:], in1=ps2[:], op=ALU.mult), r=[tk, pk2], w=["actb%d" % j])
    allact = ["actb%d" % j for j in range(JT)]
    for m in range(MT):
        s_ = m % 2
        o = ot[m % 2]; ok = "ot%d" % (m % 2)
        P.dma(wd[s_][:], Wdv[:, :, m * 128:(m + 1) * 128], w=["wd%d" % s_], cast=True)
        for nt in range(NT):
            sl = slice(nt * 512, (nt + 1) * 512)
            ps = pss[pi % 6]; pk = "ps%d" % (pi % 6); pi += 1
            for j in range(JT):
                P.pe(lambda e, ps=ps, s_=s_, j=j, sl=sl: e.matmul(ps[:], lhsT=wd[s_][:, j, :], rhs=actb[:, j, sl], start=(j == 0), stop=(j == JT - 1)), r=["wd%d" % s_] + allact, w=[pk])
            P.act(lambda e, o=o, ps=ps, sl=sl: e.activation(out=o[:, sl], in_=ps[:], func=AF.Identity), r=[pk], w=[ok])
        P.dma(outT[m * 128:(m + 1) * 128, :], o[:], r=[ok], w=["out"], is_out=True)
    return P.finish()


def ffn_dev(x1T, Wu, Wd):
    T = x1T.shape[1]
    N = T // NCORE
    key = ("ffn", N)
    if key not in _CACHE:
        _CACHE[key] = build_ffn(N)
    maps = [{"inT": np.ascontiguousarray(x1T[:, c * N:(c + 1) * N]), "Wu": np.ascontiguousarray(Wu), "Wd": np.ascontiguousarray(Wd)} for c in range(NCORE)]
    res = run(_CACHE[key], maps)
    return np.concatenate([r["outT"] for r in res], axis=1)


def lin(inT_full, W, mode="plain", func=None, bias=None, scale=None, A=None, B=None):
    K, T = inT_full.shape
    M = W.shape[1]
    N = T // NCORE
    key = ("lin", K, M, N, mode, str(func), bias is not None, scale is not None)
    if key not in _CACHE:
        _CACHE[key] = build_lin(K, M, N, mode, func, bias is not None, scale is not None)
    nc = _CACHE[key]
    MO = M // 2 if mode == "swiglu" else M
    maps = []
    for c in range(NCORE):
        sl = slice(c * N, (c + 1) * N)
        d = {"inT": np.ascontiguousarray(inT_full[:, sl]), "W": np.ascontiguousarray(W)}
        if bias is not None:
            d["bias"] = np.ascontiguousarray(bias.reshape(MO // 128, 128).T)
        if scale is not None:
            d["scale"] = np.ascontiguousarray(scale.reshape(MO // 128, 128).T)
        if A is not None:
            d["A"] = np.ascontiguousarray(A[:, sl])
            d["B"] = np.ascontiguousarray(B[:, sl])
        maps.append(d)
    res = run(nc, maps)
    return np.concatenate([r["outT"] for r in res], axis=1)


def build_s5(NB):
    P = Prog()
    G, GC = 4, 2
    ub = P.dram_in("ub", [G, 128, NB])
    yb = P.dram_out("yb", [G, 128, NB])
    prm = {n: P.dram_in(n, [128, GC]) for n in ("lre", "lim", "ldt")}
    bin_ = {n: P.dram_in(n, [128, GC, 16]) for n in ("bre", "bim", "cre", "cim")}
    dcol_d = P.dram_in("dcol", [128, G])
    cmask_d = P.dram_in("cmask", [128, 128])
    ident_d = P.dram_in("ident", [128, 128])
    NL = int(math.log2(NB))
    t = {}
    for n in ("lre", "lim", "ldt", "dt", "a", "ang", "nr", "den", "rre", "rim", "nrim", "t1", "t2", "m2", "ire", "iim", "niim"):
        t[n] = P.sb("t_" + n, [128, GC], F32)
    for n in ("ak", "arg", "tr", "mag", "cs", "sn"):
        t[n] = P.sb("t_" + n, [128, 8, GC], F32)
    for n in ("bre", "bim", "cre", "cim", "bbre", "bbim", "t16a", "t16b"):
        t[n] = P.sb("t_" + n, [128, GC, 16], F32)
    PRE = P.sb("PRE", [128, 9, GC], F32); PIM = P.sb("PIM", [128, 9, GC], F32); NPIM = P.sb("NPIM", [128, 9, GC], F32)
    PWRE = P.sb("PWRE", [128, NL, GC], F32); PWIM = P.sb("PWIM", [128, NL, GC], F32); NPWIM = P.sb("NPWIM", [128, NL, GC], F32)
    BTre = P.sb("BTre", [128, GC, 128], F32); BTim = P.sb("BTim", [128, GC, 128], F32)
    CTre = P.sb("CTre", [128, GC, 128], F32); CTim = P.sb("CTim", [128, GC, 128], F32)
    BPre = P.sb("BPre", [128, GC, 128], F32); BPimn = P.sb("BPimn", [128, GC, 128], F32)
    X1 = P.sb("X1", [128, GC, 128], F32); X2 = P.sb("X2", [128, GC, 128], F32)
    X3 = P.sb("X3", [128, GC, 128], F32); X4 = P.sb("X4", [128, GC, 128], F32)
    dcol = P.sb("dcolt", [128, G], F32); cmask = P.sb("cmaskt", [128, 128], F32); ident = P.sb("identt", [128, 128], F32)
    dtmp = P.sb("dtmp", [128, 128], F32)
    Bre = P.sb("Bre", [128, G, 128], BF16); Bim = P.sb("Bim", [128, G, 128], BF16)
    Cre = P.sb("Cre", [128, GC, 128], BF16); Cimn = P.sb("Cimn", [128, GC, 128], BF16)
    Dm = P.sb("Dm", [128, G, 128], BF16)
    ubb = P.sb("ubb", [128, G, NB], BF16)
    Hre = P.sb("Hre", [128, GC, NB], F32); Him = P.sb("Him", [128, GC, NB], F32)
    Hbre = P.sb("Hbre", [128, GC, NB], BF16); Hbim = P.sb("Hbim", [128, GC, NB], BF16)
    Tres = [P.sb("Tre%d" % g, [128, NB // 2], F32) for g in range(GC)]
    Tims = [P.sb("Tim%d" % g, [128, NB // 2], F32) for g in range(GC)]
    yo = [P.sb("yo%d" % i, [128, NB], F32) for i in range(2)]
    pss = [P.ps("ps%d" % i, [128, 512]) for i in range(4)]
    K = ["prep"]

    for n in ("lre", "lim", "ldt"):
        P.dma(t[n][:], prm[n], w=K)
    for n in ("bre", "bim", "cre", "cim"):
        P.dma(t[n][:], bin_[n], w=K)
    P.dma(dcol[:], dcol_d, w=K); P.dma(cmask[:], cmask_d, w=K); P.dma(ident[:], ident_d, w=K)
    for g in range(G):
        P.dma(ubb[:, g, :], ub[g], w=["ubb%d" % g], cast=True)
    P.dve(lambda e: e.memset(Bre[:], 0.0), w=["Bpad"])
    P.dve(lambda e: e.memset(Bim[:], 0.0), w=["Bpad"])

    def tt(o, a, b, op, r=K, w=K):
        P.dve(lambda e: e.tensor_tensor(out=o, in0=a, in1=b, op=op), r=r, w=w)

    def ts(o, a, s1, op0, s2=None, op1=None, r=K, w=K):
        if op1 is None:
            P.dve(lambda e: e.tensor_scalar(out=o, in0=a, scalar1=s1, scalar2=None, op0=op0), r=r, w=w)
        else:
            P.dve(lambda e: e.tensor_scalar(out=o, in0=a, scalar1=s1, scalar2=s2, op0=op0, op1=op1), r=r, w=w)

    def stt(o, a, s_, b, op0, op1, r=K, w=K):
        P.dve(lambda e: e.scalar_tensor_tensor(out=o, in0=a, scalar=s_, in1=b, op0=op0, op1=op1), r=r, w=w)

    def act(o, a, f, **kw):
        P.act(lambda e: e.activation(out=o, in_=a, func=f, **kw), r=K, w=K)

    def sin_of(o, arg):
        ts(t["tr"][:], arg, 1.0 / TWO_PI, ALU.mult, MAGIC, ALU.add)
        ts(t["tr"][:], t["tr"][:], MAGIC, ALU.subtract, -TWO_PI, ALU.mult)
        tt(t["tr"][:], t["tr"][:], arg, ALU.add)
        ts(t["tr"][:], t["tr"][:], math.pi, ALU.min, -math.pi, ALU.max)
        act(o, t["tr"][:], AF.Sin)

    act(t["dt"][:], t["ldt"][:], AF.Exp)
    tt(t["a"][:], t["lre"][:], t["dt"][:], ALU.mult)
    tt(t["ang"][:], t["lim"][:], t["dt"][:], ALU.mult)
    P.dve(lambda e: e.memset(PRE[:, 0, :], 1.0), r=K, w=K)
    P.dve(lambda e: e.memset(PIM[:, 0, :], 0.0), r=K, w=K)
    for k in range(1, 9):
        ts(t["ak"][:, k - 1, :], t["a"][:], float(k), ALU.mult)
        ts(t["arg"][:, k - 1, :], t["ang"][:], float(k), ALU.mult)
    act(t["mag"][:], t["ak"][:], AF.Exp)
    sin_of(t["sn"][:], t["arg"][:])
    ts(t["arg"][:], t["arg"][:], math.pi / 2, ALU.add)
    sin_of(t["cs"][:], t["arg"][:])
    tt(PRE[:, 1:9, :], t["mag"][:], t["cs"][:], ALU.mult)
    tt(PIM[:, 1:9, :], t["mag"][:], t["sn"][:], ALU.mult)
    ts(NPIM[:], PIM[:], -1.0, ALU.mult)
    ts(t["nr"][:], PRE[:, 1, :], -1.0, ALU.add)
    tt(t["t1"][:], t["lre"][:], t["lre"][:], ALU.mult)
    tt(t["t2"][:], t["lim"][:], t["lim"][:], ALU.mult)
    tt(t["den"][:], t["t1"][:], t["t2"][:], ALU.add)
    P.dve(lambda e: e.reciprocal(out=t["den"][:], in_=t["den"][:]), r=K, w=K)
    tt(t["t1"][:], t["nr"][:], t["lre"][:], ALU.mult)
    tt(t["t2"][:], PIM[:, 1, :], t["lim"][:], ALU.mult)
    tt(t["t1"][:], t["t1"][:], t["t2"][:], ALU.add)
    tt(t["rre"][:], t["t1"][:], t["den"][:], ALU.mult)
    tt(t["t1"][:], PIM[:, 1, :], t["lre"][:], ALU.mult)
    tt(t["t2"][:], t["nr"][:], t["lim"][:], ALU.mult)
    tt(t["t1"][:], t["t1"][:], t["t2"][:], ALU.subtract)
    tt(t["rim"][:], t["t1"][:], t["den"][:], ALU.mult)
    ts(t["nrim"][:], t["rim"][:], -1.0, ALU.mult)
    tt(t["t1"][:], PRE[:, 8, :], PRE[:, 8, :], ALU.mult)
    tt(t["t2"][:], PIM[:, 8, :], PIM[:, 8, :], ALU.mult)
    tt(t["m2"][:], t["t1"][:], t["t2"][:], ALU.add)
    P.dve(lambda e: e.reciprocal(out=t["m2"][:], in_=t["m2"][:]), r=K, w=K)
    tt(t["ire"][:], PRE[:, 8, :], t["m2"][:], ALU.mult)
    tt(t["niim"][:], PIM[:, 8, :], t["m2"][:], ALU.mult)
    ts(t["iim"][:], t["niim"][:], -1.0, ALU.mult)
    for gc in range(GC):
        gs = slice(gc, gc + 1)
        ts(t["t16a"][:, gc, :], t["bre"][:, gc, :], t["rre"][:, gs], ALU.mult)
        ts(t["t16b"][:, gc, :], t["bim"][:, gc, :], t["rre"][:, gs], ALU.mult)
    for gc in range(GC):
        gs = slice(gc, gc + 1)
        stt(t["bbre"][:, gc, :], t["bim"][:, gc, :], t["nrim"][:, gs], t["t16a"][:, gc, :], ALU.mult, ALU.add)
        stt(t["bbim"][:, gc, :], t["bre"][:, gc, :], t["rim"][:, gs], t["t16b"][:, gc, :], ALU.mult, ALU.add)
    KB = ["prepB"]
    for gc in range(GC):
        gs = slice(gc, gc + 1)
        for i in range(8):
            isl = slice(i * 16, (i + 1) * 16)
            ts(X1[:, gc, isl], t["bbre"][:, gc, :], PRE[:, 7 - i, gs], ALU.mult, r=K, w=KB)
            ts(X2[:, gc, isl], t["bbim"][:, gc, :], PRE[:, 7 - i, gs], ALU.mult, r=K, w=KB)
            ts(X3[:, gc, isl], t["cre"][:, gc, :], PRE[:, i + 1, gs], ALU.mult, r=K, w=KB)
            ts(X4[:, gc, isl], t["cim"][:, gc, :], PRE[:, i + 1, gs], ALU.mult, r=K, w=KB)
    KC = ["prepC"]
    for gc in range(GC):
        gs = slice(gc, gc + 1)
        for i in range(8):
            isl = slice(i * 16, (i + 1) * 16)
            stt(BTre[:, gc, isl], t["bbim"][:, gc, :], NPIM[:, 7 - i, gs], X1[:, gc, isl], ALU.mult, ALU.add, r=K + KB, w=KC)
            stt(BTim[:, gc, isl], t["bbre"][:, gc, :], PIM[:, 7 - i, gs], X2[:, gc, isl], ALU.mult, ALU.add, r=K + KB, w=KC)
            stt(CTre[:, gc, isl], t["cim"][:, gc, :], NPIM[:, i + 1, gs], X3[:, gc, isl], ALU.mult, ALU.add, r=K + KB, w=KC)
            stt(CTim[:, gc, isl], t["cre"][:, gc, :], PIM[:, i + 1, gs], X4[:, gc, isl], ALU.mult, ALU.add, r=K + KB, w=KC)
    KD = ["prepD"]
    for gc in range(GC):
        gs = slice(gc, gc + 1)
        ts(X1[:, gc, :], BTre[:, gc, :], t["ire"][:, gs], ALU.mult, r=K + KC, w=KD)
        ts(X2[:, gc, :], BTim[:, gc, :], t["ire"][:, gs], ALU.mult, r=K + KC, w=KD)
    KE = ["prepE"]
    for gc in range(GC):
        gs = slice(gc, gc + 1)
        stt(BPre[:, gc, :], BTim[:, gc, :], t["niim"][:, gs], X1[:, gc, :], ALU.mult, ALU.add, r=K + KC + KD, w=KE)
        stt(X3[:, gc, :], BTre[:, gc, :], t["iim"][:, gs], X2[:, gc, :], ALU.mult, ALU.add, r=K + KC + KD, w=KE)
    ts(BPimn[:], X3[:], -1.0, ALU.mult, r=KE, w=KE)
    P.act(lambda e: e.activation(out=Cre[:], in_=CTre[:], func=AF.Copy), r=KC, w=["Cw"])
    P.act(lambda e: e.activation(out=Cimn[:], in_=CTim[:], func=AF.Copy, scale=-1.0), r=KC, w=["Cw"])
    for g in range(G):
        hf, gc = g // 2, g % 2
        hs = slice(hf * 64, (hf + 1) * 64)
        for n_, (src, dst) in enumerate(((BTre, Bre), (BTim, Bim))):
            ps = pss[n_]; pk = "ps%d" % n_
            P.pe(lambda e, src=src, gc=gc, hs=hs, ps=ps: e.matmul(ps[:, 0:64], lhsT=src[hs, gc, :], rhs=ident[hs, hs], start=True, stop=True), r=KC + K, w=[pk])
            P.act(lambda e, dst=dst, g=g, hs=hs, ps=ps: e.activation(out=dst[:, g, hs], in_=ps[:, 0:64], func=AF.Copy), r=[pk, "Bpad"], w=["Bpad"])
        ps = pss[2]
        P.pe(lambda e, gc=gc, hs=hs, ps=ps: e.matmul(ps[:, 0:128], lhsT=BPre[hs, gc, :], rhs=CTre[hs, gc, :], start=True, stop=False), r=KE + KC, w=["ps2"])
        P.pe(lambda e, gc=gc, hs=hs, ps=ps: e.matmul(ps[:, 0:128], lhsT=BPimn[hs, gc, :], rhs=CTim[hs, gc, :], start=False, stop=True), r=KE + KC, w=["ps2"])
        P.dve(lambda e, ps=ps: e.tensor_tensor(out=dtmp[:], in0=ps[:, 0:128], in1=cmask[:], op=ALU.mult), r=K + ["ps2"], w=["dtmp"])
        stt(Dm[:, g, :], ident[:], dcol[:, g:g + 1], dtmp[:], ALU.mult, ALU.add, r=K + ["dtmp"], w=["Dm"])
    tt(PWRE[:, 0, :], PRE[:, 8, :], PRE[:, 8, :], ALU.max)
    tt(PWIM[:, 0, :], PIM[:, 8, :], PIM[:, 8, :], ALU.max)
    for k in range(1, NL):
        tt(t["t1"][:], PWRE[:, k - 1, :], PWRE[:, k - 1, :], ALU.mult)
        tt(t["t2"][:], PWIM[:, k - 1, :], PWIM[:, k - 1, :], ALU.mult)
        tt(PWRE[:, k, :], t["t1"][:], t["t2"][:], ALU.subtract)
        tt(t["t1"][:], PWRE[:, k - 1, :], PWIM[:, k - 1, :], ALU.mult)
        ts(PWIM[:, k, :], t["t1"][:], 2.0, ALU.mult)
    ts(NPWIM[:], PWIM[:], -1.0, ALU.mult)

    pi = 0
    NT = NB // 512
    for gc in range(GC):
        for nt in range(NT):
            sl = slice(nt * 512, (nt + 1) * 512)
            for wsrc, H in ((Bre, Hre), (Bim, Him)):
                ps = pss[pi % 4]; pk = "ps%d" % (pi % 4); pi += 1
                P.pe(lambda e, ps=ps, wsrc=wsrc, gc=gc, sl=sl: e.matmul(ps[:], lhsT=wsrc[:, gc, :], rhs=ubb[:, gc, sl], start=True, stop=False), r=["Bpad", "ubb%d" % gc], w=[pk])
                P.pe(lambda e, ps=ps, wsrc=wsrc, gc=gc, sl=sl: e.matmul(ps[:], lhsT=wsrc[:, 2 + gc, :], rhs=ubb[:, 2 + gc, sl], start=False, stop=True), r=["Bpad", "ubb%d" % (2 + gc)], w=[pk])
                P.act(lambda e, ps=ps, H=H, gc=gc, sl=sl: e.activation(out=H[:, gc, sl], in_=ps[:], func=AF.Copy), r=[pk], w=["H%d" % gc])
    for k in range(NL):
        s = 1 << k
        for g in range(GC):
            hk = ["H%d" % g]
            hr = Hre[:, g, :]; hi = Him[:, g, :]
            vr = hr.rearrange("p (m t) -> p m t", t=2 * s); vi = hi.rearrange("p (m t) -> p m t", t=2 * s)
            tr_, sr_ = vr[:, :, 2 * s - 1], vr[:, :, s - 1]
            ti_, si_ = vi[:, :, 2 * s - 1], vi[:, :, s - 1]
            a_r, a_i, na_i = PWRE[:, k, g:g + 1], PWIM[:, k, g:g + 1], NPWIM[:, k, g:g + 1]
            stt(tr_, sr_, a_r, tr_, ALU.mult, ALU.add, r=K + hk, w=hk)
            stt(tr_, si_, na_i, tr_, ALU.mult, ALU.add, r=K + hk, w=hk)
            stt(ti_, si_, a_r, ti_, ALU.mult, ALU.add, r=K + hk, w=hk)
            stt(ti_, sr_, a_i, ti_, ALU.mult, ALU.add, r=K + hk, w=hk)
    for g in range(GC):
        hk = ["H%d" % g]
        P.dve(lambda e, g=g: e.memset(Hre[:, g, NB - 1:NB], 0.0), r=hk, w=hk)
        P.dve(lambda e, g=g: e.memset(Him[:, g, NB - 1:NB], 0.0), r=hk, w=hk)
    for k in range(NL - 1, -1, -1):
        s = 1 << k
        m = NB // (2 * s)
        for g in range(GC):
            hk = ["H%d" % g]
            tk = ["T%d" % g]
            TR, TI = Tres[g], Tims[g]
            hr = Hre[:, g, :]; hi = Him[:, g, :]
            vr = hr.rearrange("p (m t) -> p m t", t=2 * s); vi = hi.rearrange("p (m t) -> p m t", t=2 * s)
            Rr, Lr = vr[:, :, 2 * s - 1], vr[:, :, s - 1]
            Ri, Li = vi[:, :, 2 * s - 1], vi[:, :, s - 1]
            a_r, a_i, na_i = PWRE[:, k, g:g + 1], PWIM[:, k, g:g + 1], NPWIM[:, k, g:g + 1]
            kk = K + hk + tk
            stt(TR[:, 0:m], Rr, a_r, Lr, ALU.mult, ALU.add, r=kk, w=tk)
            stt(TR[:, 0:m], Ri, na_i, TR[:, 0:m], ALU.mult, ALU.add, r=kk, w=tk)
            stt(TI[:, 0:m], Ri, a_r, Li, ALU.mult, ALU.add, r=kk, w=tk)
            stt(TI[:, 0:m], Rr, a_i, TI[:, 0:m], ALU.mult, ALU.add, r=kk, w=tk)
            P.act(lambda e, Lr=Lr, Rr=Rr: e.activation(out=Lr, in_=Rr, func=AF.Copy), r=kk, w=hk)
            P.act(lambda e, Li=Li, Ri=Ri: e.activation(out=Li, in_=Ri, func=AF.Copy), r=kk, w=hk)
            P.dve(lambda e, Rr=Rr, m=m, TR=TR: e.tensor_copy(out=Rr, in_=TR[:, 0:m]), r=kk, w=hk)
            P.dve(lambda e, Ri=Ri, m=m, TI=TI: e.tensor_copy(out=Ri, in_=TI[:, 0:m]), r=kk, w=hk)
    for g in range(GC):
        hk = ["H%d" % g]
        P.act(lambda e, g=g: e.activation(out=Hbre[:, g, :], in_=Hre[:, g, :], func=AF.Copy), r=hk, w=["Hb%d" % g])
        P.act(lambda e, g=g: e.activation(out=Hbim[:, g, :], in_=Him[:, g, :], func=AF.Copy), r=hk, w=["Hb%d" % g])
    for g in range(G):
        hf, gc = g // 2, g % 2
        hs = slice(hf * 64, (hf + 1) * 64)
        o = yo[g % 2]; ok = "yo%d" % (g % 2)
        for nt in range(NT):
            sl = slice(nt * 512, (nt + 1) * 512)
            ps = pss[pi % 4]; pk = "ps%d" % (pi % 4); pi += 1
            rr = ["Cw", "Dm", "ubb%d" % g, "Hb%d" % gc]
            P.pe(lambda e, ps=ps, gc=gc, hs=hs, sl=sl: e.matmul(ps[:], lhsT=Cre[hs, gc, :], rhs=Hbre[hs, gc, sl], start=True, stop=False), r=rr, w=[pk])
            P.pe(lambda e, ps=ps, gc=gc, hs=hs, sl=sl: e.matmul(ps[:], lhsT=Cimn[hs, gc, :], rhs=Hbim[hs, gc, sl], start=False, stop=False), r=rr, w=[pk])
            P.pe(lambda e, ps=ps, g=g, sl=sl: e.matmul(ps[:], lhsT=Dm[:, g, :], rhs=ubb[:, g, sl], start=False, stop=True), r=rr, w=[pk])
            P.act(lambda e, ps=ps, o=o, sl=sl: e.activation(out=o[:, sl], in_=ps[:], func=AF.Copy), r=[pk], w=[ok])
        P.dma(yb[g], o[:], r=[ok], w=["out"], is_out=True)
    return P.finish()


def s5_mixer_dev(uT, lam_re, lam_im, log_dt, b_re, b_im, c_re, c_im, d_skip):
    T = uT.shape[1]
    NB = T // 8
    key = ("s5", NB)
    if key not in _CACHE:
        _CACHE[key] = build_s5(NB)
    ub = uT.reshape(32, 16, NB, 8).transpose(0, 3, 1, 2).reshape(32, 128, NB)
    ii, jj = np.arange(128) // 16, np.arange(128) // 16
    cmask = (jj[None, :] >= ii[:, None]).astype(np.float32)
    ident = np.eye(128, dtype=np.float32)
    maps = []

    def pl(a):
        sh = a.shape[2:]
        a = a.reshape((2, 2, 64) + sh)
        a = np.moveaxis(a, 1, 2)
        return np.ascontiguousarray(a.reshape((128, 2) + sh))

    for c in range(NCORE):
        gs = slice(4 * c, 4 * c + 4)
        maps.append({
            "ub": np.ascontiguousarray(ub[gs]),
            "lre": pl(lam_re[gs]), "lim": pl(lam_im[gs]),
            "ldt": pl(np.ascontiguousarray(np.broadcast_to(log_dt[gs][:, None], (4, 64)))),
            "bre": pl(b_re[gs]), "bim": pl(b_im[gs]),
            "cre": pl(np.ascontiguousarray(c_re[gs].transpose(0, 2, 1))), "cim": pl(np.ascontiguousarray(c_im[gs].transpose(0, 2, 1))),
            "dcol": np.ascontiguousarray(np.tile(d_skip.reshape(32, 16)[gs].T, (8, 1))),
            "cmask": cmask, "ident": ident,
        })
    res = run(_CACHE[key], maps)
    yb = np.concatenate([r["yb"] for r in res], axis=0)
    return np.ascontiguousarray(yb.reshape(32, 8, 16, NB).transpose(0, 2, 3, 1).reshape(512, T))


def build_resln(N, D):
    P = Prog()
    x = P.dram_in("x", [N, D]); m = P.dram_in("m", [N, D]); gb = P.dram_in("gb", [128, 2, D])
    y = P.dram_out("y", [N, D])
    gbt = P.sb("gbt", [128, 2, D], F32)
    P.dma(gbt[:], gb, w=["gb"], q="gpsimd")
    NB_ = 4
    xt = [P.sb("xt%d" % i, [128, D], F32) for i in range(NB_)]
    mt = [P.sb("mt%d" % i, [128, D], F32) for i in range(NB_)]
    yt = [P.sb("yt%d" % i, [128, D], F32) for i in range(NB_)]
    jk = [P.sb("jk%d" % i, [128, D], F32) for i in range(2)]
    st = [P.sb("st%d" % i, [128, 8], F32) for i in range(NB_)]
    NTL = N // 128

    def loads(i):
        b = i % NB_
        rs = slice(i * 128, (i + 1) * 128)
        P.dma(xt[b][:], x[rs, :], w=["xt%d" % b], q="sync")
        P.dma(mt[b][:], m[rs, :], w=["mt%d" % b], q="act")

    for i in range(min(3, NTL)):
        loads(i)
    for i in range(NTL):
        b = i % NB_
        xk, mk, yk, sk = "xt%d" % b, "mt%d" % b, "yt%d" % b, "st%d" % b
        X, M, Y, S = xt[b], mt[b], yt[b], st[b]
        J0, J1 = jk[0], jk[1]
        rs = slice(i * 128, (i + 1) * 128)
        P.dve(lambda e, S=S: e.memset(S[:], 0.0), w=[sk])
        P.dve(lambda e, X=X, M=M: e.scalar_tensor_tensor(out=X[:], in0=X[:], scalar=float(ALPHA), in1=M[:], op0=ALU.mult, op1=ALU.add), r=[xk, mk], w=[xk])
        P.act(lambda e, X=X, S=S, J0=J0: e.activation(out=J0[:], in_=X[:], func=AF.Copy, accum_out=S[:, 0:1]), r=[xk, sk], w=["jk0", sk])
        P.act(lambda e, X=X, S=S, J1=J1: e.activation(out=J1[:], in_=X[:], func=AF.Square, accum_out=S[:, 1:2]), r=[xk, sk], w=["jk1", sk])
        if i + 3 < NTL:
            loads(i + 3)
        P.dve(lambda e, S=S: e.tensor_scalar(out=S[:, 2:4], in0=S[:, 0:2], scalar1=1.0 / D, scalar2=None, op0=ALU.mult), r=[sk], w=[sk])
        P.dve(lambda e, S=S: e.tensor_tensor(out=S[:, 4:5], in0=S[:, 2:3], in1=S[:, 2:3], op=ALU.mult), r=[sk], w=[sk])
        P.dve(lambda e, S=S: e.tensor_tensor(out=S[:, 4:5], in0=S[:, 3:4], in1=S[:, 4:5], op=ALU.subtract), r=[sk], w=[sk])
        P.dve(lambda e, S=S: e.tensor_scalar(out=S[:, 4:5], in0=S[:, 4:5], scalar1=LN_EPS, scalar2=None, op0=ALU.add), r=[sk], w=[sk])
        P.act(lambda e, S=S: e.activation(out=S[:, 4:5], in_=S[:, 4:5], func=AF.Sqrt), r=[sk], w=[sk])
        P.dve(lambda e, S=S: e.reciprocal(out=S[:, 5:6], in_=S[:, 4:5]), r=[sk], w=[sk])
        P.dve(lambda e, S=S: e.scalar_tensor_tensor(out=S[:, 6:7], in0=S[:, 2:3], scalar=-1.0, in1=S[:, 5:6], op0=ALU.mult, op1=ALU.mult), r=[sk], w=[sk])
        P.act(lambda e, X=X, Y=Y, S=S: e.activation(out=Y[:], in_=X[:], func=AF.Identity, scale=S[:, 5:6], bias=S[:, 6:7]), r=[xk, sk], w=[yk])
        P.dve(lambda e, Y=Y: e.tensor_tensor(out=Y[:], in0=Y[:], in1=gbt[:, 0, :], op=ALU.mult), r=[yk, "gb"], w=[yk])
        P.dve(lambda e, Y=Y: e.tensor_tensor(out=Y[:], in0=Y[:], in1=gbt[:, 1, :], op=ALU.add), r=[yk, "gb"], w=[yk])
        P.dma(y[rs, :], Y[:], r=[yk], w=["out"], is_out=True, q="gpsimd")
    return P.finish()


def resln(x_tm, m_tm, g, b):
    T, D = x_tm.shape
    N = T // NCORE
    key = ("resln", N, D)
    if key not in _CACHE:
        _CACHE[key] = build_resln(N, D)
    gb = np.ascontiguousarray(np.broadcast_to(np.stack([g, b])[None], (128, 2, D))).astype(np.float32)
    maps = [{"x": np.ascontiguousarray(x_tm[c * N:(c + 1) * N]), "m": np.ascontiguousarray(m_tm[c * N:(c + 1) * N]), "gb": gb} for c in range(NCORE)]
    res = run(_CACHE[key], maps)
    return np.concatenate([r["y"] for r in res], axis=0)


def build_conv(N):
    P = Prog()
    bT = P.dram_in("bT", [512, N]); cT = P.dram_in("cT", [512, N + 2]); xT = P.dram_in("xT", [512, N + 2])
    w = P.dram_in("w", [128, 4, 3])
    yT = P.dram_out("yT", [512, N])
    wt = P.sb("wt", [128, 4, 3], F32)
    P.dma(wt[:], w, w=["w"])
    for a in range(4):
        bt = P.sb("bt%d" % a, [128, N], F32); ct = P.sb("ct%d" % a, [128, N + 2], F32); xt = P.sb("xt%d" % a, [128, N + 2], F32)
        acc = P.sb("acc%d" % a, [128, N], F32)
        rs = slice(a * 128, (a + 1) * 128)
        k = "c%d" % a
        P.dma(bt[:], bT[rs, :], w=[k + "b"]); P.dma(ct[:], cT[rs, :], w=[k]); P.dma(xt[:], xT[rs, :], w=[k + "x"])
        P.dve(lambda e, ct=ct, xt=xt: e.tensor_tensor(out=ct[:], in0=ct[:], in1=xt[:], op=ALU.mult), r=[k, k + "x"], w=[k])
        P.dve(lambda e, ct=ct, acc=acc, a=a: e.tensor_scalar(out=acc[:], in0=ct[:, 0:N], scalar1=wt[:, a, 0:1], scalar2=None, op0=ALU.mult), r=[k, "w"], w=[k + "a"])
        for j in (1, 2):
            P.dve(lambda e, ct=ct, acc=acc, a=a, j=j: e.scalar_tensor_tensor(out=acc[:], in0=ct[:, j:j + N], scalar=wt[:, a, j:j + 1], in1=acc[:], op0=ALU.mult, op1=ALU.add), r=[k, "w", k + "a"], w=[k + "a"])
        P.dve(lambda e, acc=acc, bt=bt: e.tensor_tensor(out=acc[:], in0=acc[:], in1=bt[:], op=ALU.mult), r=[k + "a", k + "b"], w=[k + "a"])
        P.dma(yT[rs, :], acc[:], r=[k + "a"], w=["out"], is_out=True)
    return P.finish()


def conv_dev(bT, cT, xT, cw):
    T = bT.shape[1]
    N = T // NCORE
    key = ("conv", N)
    if key not in _CACHE:
        _CACHE[key] = build_conv(N)
    cp = np.concatenate([np.zeros((512, 2), np.float32), cT], axis=1)
    xp = np.concatenate([np.zeros((512, 2), np.float32), xT], axis=1)
    w = np.ascontiguousarray(cw.reshape(3, 4, 128).transpose(2, 1, 0))
    maps = [{"bT": np.ascontiguousarray(bT[:, c * N:(c + 1) * N]), "cT": np.ascontiguousarray(cp[:, c * N:(c + 1) * N + 2]),
             "xT": np.ascontiguousarray(xp[:, c * N:(c + 1) * N + 2]), "w": w} for c in range(NCORE)]
    res = run(_CACHE[key], maps)
    return np.concatenate([r["yT"] for r in res], axis=1)


def build_pool(N):
    P = Prog()
    zT = P.dram_in("zT", [512, N + 16]); invc = P.dram_in("invc", [128, 4, N])
    oT = P.dram_out("oT", [512, N])
    for gi in range(4):
        z = P.sb("z%d" % gi, [128, N + 16], F32)
        sa = P.sb("sa%d" % gi, [128, N + 16], F32); sb_ = P.sb("sb%d" % gi, [128, N + 16], F32)
        ic = P.sb("ic%d" % gi, [128, N], F32)
        rs = slice(gi * 128, (gi + 1) * 128)
        k = "p%d" % gi
        P.dma(z[:], zT[rs, :], w=[k + "z"]); P.dma(ic[:], invc[:, gi, :], w=[k + "i"])
        cur, curk = z, k + "z"
        bufs = [(sa, k + "a"), (sb_, k + "b")]
        for step in range(gi + 1):
            sh = 1 << step
            nxt, nk = bufs[step % 2]
            P.dve(lambda e, cur=cur, nxt=nxt, sh=sh: e.tensor_tensor(out=nxt[:, sh:], in0=cur[:, sh:], in1=cur[:, 0:N + 16 - sh], op=ALU.add), r=[curk], w=[nk])
            cur, curk = nxt, nk
        o, okey = bufs[(gi + 1) % 2]
        P.dve(lambda e, cur=cur, o=o, ic=ic: e.tensor_tensor(out=o[:, 16:], in0=cur[:, 16:], in1=ic[:], op=ALU.mult), r=[curk, k + "i"], w=[okey])
        P.dve(lambda e, o=o, z=z: e.tensor_tensor(out=o[:, 16:], in0=o[:, 16:], in1=z[:, 16:], op=ALU.subtract), r=[okey, k + "z"], w=[okey])
        P.dma(oT[rs, :], o[:, 16:], r=[okey], w=["out"], is_out=True)
    return P.finish()


def pool_dev(zT):
    T = zT.shape[1]
    N = T // NCORE
    key = ("pool", N)
    if key not in _CACHE:
        _CACHE[key] = build_pool(N)
    zp = np.concatenate([np.zeros((512, 16), np.float32), zT], axis=1)
    t = np.arange(T)
    inv = np.stack([1.0 / np.minimum(t + 1, w) for w in (2, 4, 8, 16)]).astype(np.float32)
    maps = []
    for c in range(NCORE):
        ic = np.ascontiguousarray(np.broadcast_to(inv[None, :, c * N:(c + 1) * N], (128, 4, N)))
        maps.append({"zT": np.ascontiguousarray(zp[:, c * N:(c + 1) * N + 16]), "invc": ic})
    res = run(_CACHE[key], maps)
    return np.concatenate([r["oT"] for r in res], axis=1)


def build_attn(T):
    P = Prog()
    NBK = T // 128
    qT = P.dram_in("qT", [64, T]); kT = P.dram_in("kT", [64, T]); v = P.dram_in("v", [T, 64])
    bias = P.dram_in("bias", [128, 5, 128])
    o_tm = P.dram_out("o", [T, 64])
    qb = P.sb("qb", [64, T], BF16); kb = P.sb("kb", [64, T], BF16); vb = P.sb("vb", [128, NBK, 65], BF16)
    bf = P.sb("bf", [128, 5, 128], F32); eb = P.sb("eb", [128, 5, 128], F32)
    P.dve(lambda e: e.memset(vb[:], 1.0), w=["vb"])
    P.dma(bf[:], bias, w=["bf"])
    P.dma(qb[:], qT, w=["qb"], cast=True); P.dma(kb[:], kT, w=["kb"], cast=True)
    vv = v.rearrange("(n p) d -> p n d", p=128)
    for j0 in range(0, NBK, 16):
        j1 = min(NBK, j0 + 16)
        P.dma(vb[:, j0:j1, 0:64], vv[:, j0:j1, :], w=["vb"], cast=True)
    P.act(lambda e: e.activation(out=eb[:], in_=bf[:], func=AF.Exp), r=["bf"], w=["eb"])
    P.act(lambda e: e.activation(out=qb[:], in_=qb[:], func=AF.Copy, scale=0.125), r=["qb"], w=["qb"])
    NBUF = 3
    psS = [P.ps("psS%d" % i, [128, 5, 128]) for i in range(2)]
    psO = [P.ps("psO%d" % i, [128, 512]) for i in range(2)]
    pf = [P.sb("pf%d" % i, [128, 5, 128], F32) for i in range(NBUF)]
    pt = [P.sb("pt%d" % i, [128, 5, 128], BF16) for i in range(NBUF)]
    rec = [P.sb("rec%d" % i, [128, 1], F32) for i in range(NBUF)]
    ob = [P.sb("ob%d" % i, [128, 16, 64], F32) for i in range(2)]
    o_v = o_tm.rearrange("(n p) d -> p n d", p=128)
    for m in range(NBK):
        b = m % 2
        S, O = psS[b], psO[b]
        PF, PT, RC = pf[m % NBUF], pt[m % NBUF], rec[m % NBUF]
        sk, okk, fk, pk, rk = "S%d" % b, "O%d" % b, "pf%d" % (m % NBUF), "pt%d" % (m % NBUF), "rc%d" % (m % NBUF)
        OB = ob[(m // 16) % 2]; obk = "ob%d" % ((m // 16) % 2)
        qs = slice(m * 128, (m + 1) * 128)
        i0_ = max(0, 4 - m)
        val = list(range(i0_, 5))
        for i in val:
            kt = m - 4 + i
            P.pe(lambda e, S=S, i=i, kt=kt, qs=qs: e.matmul(S[:, i, :], lhsT=kb[:, kt * 128:(kt + 1) * 128], rhs=qb[:, qs], start=True, stop=True), r=["kb", "qb"], w=[sk])
        if i0_ < 4:
            P.act(lambda e, S=S, PF=PF, i0_=i0_: e.activation(out=PF[:, i0_:4, :], in_=S[:, i0_:4, :], func=AF.Exp), r=[sk], w=[fk])
        P.act(lambda e, S=S, PF=PF: e.activation(out=PF[:, 4, :], in_=S[:, 4, :], func=AF.Exp), r=[sk], w=[fk])
        P.dve(lambda e, PF=PF, PT=PT, i0_=i0_: e.tensor_tensor(out=PT[:, i0_:5, :], in0=PF[:, i0_:5, :], in1=eb[:, i0_:5, :], op=ALU.mult), r=[fk, "eb"], w=[pk])
        for n, i in enumerate(val):
            kt = m - 4 + i
            P.pe(lambda e, O=O, PT=PT, i=i, kt=kt, n=n: e.matmul(O[:, 0:65], lhsT=PT[:, i, :], rhs=vb[:, kt, :], start=(n == 0), stop=(n == len(val) - 1)), r=["vb", pk], w=[okk])
        P.dve(lambda e, O=O, RC=RC: e.reciprocal(out=RC[:], in_=O[:, 64:65]), r=[okk], w=[rk])
        c = m % 16
        P.dve(lambda e, O=O, RC=RC, OB=OB, c=c: e.tensor_scalar(out=OB[:, c, :], in0=O[:, 0:64], scalar1=RC[:, 0:1], scalar2=None, op0=ALU.mult), r=[okk, rk], w=[obk])
        if m % 16 == 15:
            g0 = (m // 16) * 16
            P.dma(o_v[:, g0:g0 + 16, :], OB[:], r=[obk], w=["out"], is_out=True)
    return P.finish()


def attn_dev(qT, kT, vT, rel_bias):
    T = qT.shape[1]
    key = ("attn", T)
    if key not in _CACHE:
        _CACHE[key] = build_attn(T)
    kk = np.arange(640)[:, None]; qq = np.arange(128)[None, :]
    qc = qq // 64; kc = kk // 64
    dist = (qq - (kk - 512))
    rel = np.clip(dist, -128, 128) + 128
    band = kc - qc
    valid = (band >= 0) & (band <= 8)
    maps = []
    for h in range(NCORE):
        b2 = np.where(valid, rel_bias[h][rel], np.float32(-30000.0)).astype(np.float32)
        b2 = np.ascontiguousarray(b2.reshape(5, 128, 128).transpose(1, 0, 2))
        hs = slice(h * 64, (h + 1) * 64)
        maps.append({"qT": np.ascontiguousarray(qT[hs]), "kT": np.ascontiguousarray(kT[hs]),
                     "v": np.ascontiguousarray(vT[hs].T), "bias": b2})
    res = run(_CACHE[key], maps)
    return np.ascontiguousarray(np.concatenate([r["o"].T for r in res], axis=0))


def _outproj(P, mixin, Wout, outT, N, pss, pi, mixkeys):
    NT = N // 512
    Wv = Wout.rearrange("(kt p) m -> p kt m", p=128)
    wo = [P.sb("wo%d" % i, [128, 8, 128], BF16) for i in range(3)]
    ot = [P.sb("oto%d" % i, [128, N], F32) for i in range(2)]
    for m in range(8):
        s_ = m % 3
        o = ot[m % 2]; ok = "oto%d" % (m % 2)
        P.dma(wo[s_][:], Wv[:, :, m * 128:(m + 1) * 128], w=["wo%d" % s_], cast=True)
        for nt in range(NT):
            sl = slice(nt * 512, (nt + 1) * 512)
            ps = pss[pi % len(pss)]; pk = "ps%d" % (pi % len(pss)); pi += 1
            for kt in range(8):
                P.pe(lambda e, ps=ps, s_=s_, kt=kt, sl=sl: e.matmul(ps[:], lhsT=wo[s_][:, kt, :], rhs=mixin[:, kt, sl], start=(kt == 0), stop=(kt == 7)), r=["wo%d" % s_, mixkeys[kt]], w=[pk])
            P.act(lambda e, o=o, ps=ps, sl=sl: e.activation(out=o[:, sl], in_=ps[:], func=AF.Identity), r=[pk], w=[ok])
        P.dma(outT[m * 128:(m + 1) * 128, :], o[:], r=[ok], w=["out"], is_out=True)
    return pi


def build_even_tail(N):
    P = Prog()
    NT = N // 512
    yS = P.dram_in("yS", [512, N]); bT = P.dram_in("bT", [512, N]); cT = P.dram_in("cT", [512, N + 2]); xT = P.dram_in("xT", [512, N + 2])
    cw = P.dram_in("cw", [128, 4, 3]); Wg = P.dram_in("Wg", [512, 512]); bg = P.dram_in("bg", [128, 4]); Wout = P.dram_in("Wout", [1024, 1024])
    outT = P.dram_out("outT", [1024, N])
    mixin = P.sb("mixin", [128, 8, N], BF16)
    mixkeys = ["mix%d" % k for k in range(8)]
    gf = P.sb("gf", [128, 4, N], F32); inb = P.sb("inb", [128, 4, N], BF16)
    xs = [P.sb("xs%d" % i, [128, N], F32) for i in range(2)]
    tq = P.sb("tq", [128, N], F32)
    bt_ = P.sb("bgt", [128, 4], F32); wt = P.sb("cwt", [128, 4, 3], F32)
    P.dma(bt_[:], bg, w=["bg"]); P.dma(wt[:], cw, w=["cw"])
    pss = [P.ps("ps%d" % i, [128, 512]) for i in range(4)]
    for kt in range(4):
        x = xs[kt % 2]; xk = "xs%d" % (kt % 2)
        P.dma(x[:], yS[kt * 128:(kt + 1) * 128, :], w=[xk])
        P.dve(lambda e, x=x: e.tensor_tensor(out=tq[:], in0=x[:], in1=x[:], op=ALU.mult), r=[xk], w=["tq"])
        P.dve(lambda e: e.tensor_scalar(out=tq[:], in0=tq[:], scalar1=0.044715, scalar2=1.0, op0=ALU.mult, op1=ALU.add), r=["tq"], w=["tq"])
        P.dve(lambda e, x=x: e.tensor_tensor(out=tq[:], in0=tq[:], in1=x[:], op=ALU.mult), r=["tq", xk], w=["tq"])
        P.act(lambda e: e.activation(out=tq[:], in_=tq[:], func=AF.Sigmoid, scale=1.5957691216057308), r=["tq"], w=["tq"])
        P.dve(lambda e, x=x, kt=kt: e.tensor_tensor(out=gf[:, kt, :], in0=tq[:], in1=x[:], op=ALU.mult), r=["tq", xk], w=["gf%d" % kt])
        P.act(lambda e, kt=kt: e.activation(out=inb[:, kt, :], in_=gf[:, kt, :], func=AF.Copy), r=["gf%d" % kt], w=["inb%d" % kt])
    cb_ = [P.sb("cvb%d" % i, [128, N], F32) for i in range(2)]
    cc_ = [P.sb("cvc%d" % i, [128, N + 2], F32) for i in range(2)]
    cx_ = [P.sb("cvx%d" % i, [128, N + 2], F32) for i in range(2)]
    ca_ = [P.sb("cva%d" % i, [128, N], F32) for i in range(2)]
    for a in range(4):
        b = a % 2
        bt, ct, xt, acc = cb_[b], cc_[b], cx_[b], ca_[b]
        rs = slice(a * 128, (a + 1) * 128)
        k = "cv%d" % b
        P.dma(bt[:], bT[rs, :], w=[k + "b"], q="act"); P.dma(ct[:], cT[rs, :], w=[k], q="sync"); P.dma(xt[:], xT[rs, :], w=[k + "x"], q="act")
        P.dve(lambda e, ct=ct, xt=xt: e.tensor_tensor(out=ct[:], in0=ct[:], in1=xt[:], op=ALU.mult), r=[k, k + "x"], w=[k])
        P.dve(lambda e, ct=ct, acc=acc, a=a: e.tensor_scalar(out=acc[:], in0=ct[:, 0:N], scalar1=wt[:, a, 0:1], scalar2=None, op0=ALU.mult), r=[k, "cw"], w=[k + "a"])
        for j in (1, 2):
            P.dve(lambda e, ct=ct, acc=acc, a=a, j=j: e.scalar_tensor_tensor(out=acc[:], in0=ct[:, j:j + N], scalar=wt[:, a, j:j + 1], in1=acc[:], op0=ALU.mult, op1=ALU.add), r=[k, "cw", k + "a"], w=[k + "a"])
        P.dve(lambda e, acc=acc, bt=bt, a=a: e.tensor_tensor(out=mixin[:, 4 + a, :], in0=acc[:], in1=bt[:], op=ALU.mult), r=[k + "a", k + "b"], w=[mixkeys[4 + a]])
    Wgv = Wg.rearrange("(kt p) m -> p kt m", p=128)
    wg = [P.sb("wg%d" % i, [128, 4, 128], BF16) for i in range(2)]
    sg = [P.sb("sg%d" % i, [128, 512], F32) for i in range(2)]
    pi = 0
    for m in range(4):
        s_ = m % 2
        P.dma(wg[s_][:], Wgv[:, :, m * 128:(m + 1) * 128], w=["wg%d" % s_], cast=True)
        for nt in range(NT):
            sl = slice(nt * 512, (nt + 1) * 512)
            ps = pss[pi % 4]; pk = "ps%d" % (pi % 4); pi += 1
            for kt in range(4):
                P.pe(lambda e, ps=ps, s_=s_, kt=kt, sl=sl: e.matmul(ps[:], lhsT=wg[s_][:, kt, :], rhs=inb[:, kt, sl], start=(kt == 0), stop=(kt == 3)), r=["wg%d" % s_, "inb%d" % kt], w=[pk])
            t = sg[nt % 2]; tk = "sg%d" % (nt % 2)
            P.act(lambda e, t=t, ps=ps, m=m: e.activation(out=t[:], in_=ps[:], func=AF.Sigmoid, bias=bt_[:, m:m + 1]), r=[pk, "bg"], w=[tk])
            P.dve(lambda e, t=t, m=m, sl=sl: e.tensor_tensor(out=mixin[:, m, sl], in0=t[:], in1=gf[:, m, sl], op=ALU.mult), r=[tk, "gf%d" % m], w=[mixkeys[m]])
    _outproj(P, mixin, Wout, outT, N, pss, pi, mixkeys)
    return P.finish()


def even_tail_dev(yS, hT, cw, Wg, bg, Wout):
    T = yS.shape[1]
    N = T // NCORE
    key = ("even_tail", N)
    if key not in _CACHE:
        _CACHE[key] = build_even_tail(N)
    bT, cT, xT = hT[512:1024], hT[1024:1536], hT[1536:2048]
    cp = np.concatenate([np.zeros((512, 2), np.float32), cT], axis=1)
    xp = np.concatenate([np.zeros((512, 2), np.float32), xT], axis=1)
    w = np.ascontiguousarray(cw.reshape(3, 4, 128).transpose(2, 1, 0))
    b = np.ascontiguousarray(bg.reshape(4, 128).T)
    maps = [{"yS": np.ascontiguousarray(yS[:, c * N:(c + 1) * N]), "bT": np.ascontiguousarray(bT[:, c * N:(c + 1) * N]),
             "cT": np.ascontiguousarray(cp[:, c * N:(c + 1) * N + 2]), "xT": np.ascontiguousarray(xp[:, c * N:(c + 1) * N + 2]),
             "cw": w, "Wg": np.ascontiguousarray(Wg), "bg": b, "Wout": np.ascontiguousarray(Wout)} for c in range(NCORE)]
    res = run(_CACHE[key], maps)
    return np.concatenate([r["outT"] for r in res], axis=1)


def build_odd_tail(N):
    P = Prog()
    NT = N // 512
    yc = P.dram_in("yc", [512, N]); zT = P.dram_in("zT", [512, N + 16]); invc = P.dram_in("invc", [128, 4, N])
    pw = P.dram_in("pw", [128, 4, 128]); psc = P.dram_in("psc", [128, 4]); Wout = P.dram_in("Wout", [1024, 1024])
    outT = P.dram_out("outT", [1024, N])
    mixin = P.sb("mixin", [128, 8, N], BF16)
    mixkeys = ["mix%d" % k for k in range(8)]
    for kt in range(4):
        P.dma(mixin[:, kt, :], yc[kt * 128:(kt + 1) * 128, :], w=[mixkeys[kt]], cast=True)
    pwb = P.sb("pwb", [128, 4, 128], BF16); sct = P.sb("sct", [128, 4], F32)
    P.dma(pwb[:], pw, w=["pw"], cast=True); P.dma(sct[:], psc, w=["psc"])
    pooled = P.sb("pooled", [128, 4, N], BF16)
    pss = [P.ps("ps%d" % i, [128, 512]) for i in range(4)]
    zb = [P.sb("pz%d" % i, [128, N + 16], F32) for i in range(2)]
    sab = [P.sb("psa%d" % i, [128, N + 16], F32) for i in range(2)]
    sbb = [P.sb("psb%d" % i, [128, N + 16], F32) for i in range(2)]
    icb = [P.sb("pic%d" % i, [128, N], F32) for i in range(2)]
    pi = 0
    for gi in range(4):
        b = gi % 2
        z, sa, sb_, ic = zb[b], sab[b], sbb[b], icb[b]
        rs = slice(gi * 128, (gi + 1) * 128)
        k = "pl%d" % b
        P.dma(z[:], zT[rs, :], w=[k + "z"], q="sync"); P.dma(ic[:], invc[:, gi, :], w=[k + "i"], q="act")
        cur, curk = z, k + "z"
        bufs = [(sa, k + "a"), (sb_, k + "b")]
        for step in range(gi + 1):
            sh = 1 << step
            nxt, nk = bufs[step % 2]
            P.dve(lambda e, cur=cur, nxt=nxt, sh=sh: e.tensor_tensor(out=nxt[:, sh:], in0=cur[:, sh:], in1=cur[:, 0:N + 16 - sh], op=ALU.add), r=[curk], w=[nk])
            cur, curk = nxt, nk
        o, okey = bufs[(gi + 1) % 2]
        P.dve(lambda e, cur=cur, o=o, ic=ic: e.tensor_tensor(out=o[:, 16:], in0=cur[:, 16:], in1=ic[:], op=ALU.mult), r=[curk, k + "i"], w=[okey])
        P.dve(lambda e, o=o, z=z, gi=gi: e.tensor_tensor(out=pooled[:, gi, :], in0=o[:, 16:], in1=z[:, 16:], op=ALU.subtract), r=[okey, k + "z"], w=["pooled%d" % gi])
        for nt in range(NT):
            sl = slice(nt * 512, (nt + 1) * 512)
            ps = pss[pi % 4]; pk = "ps%d" % (pi % 4); pi += 1
            P.pe(lambda e, ps=ps, gi=gi, sl=sl: e.matmul(ps[:], lhsT=pwb[:, gi, :], rhs=pooled[:, gi, sl], start=True, stop=True), r=["pw", "pooled%d" % gi], w=[pk])
            P.act(lambda e, ps=ps, gi=gi, sl=sl: e.activation(out=mixin[:, 4 + gi, sl], in_=ps[:], func=AF.Identity, scale=sct[:, gi:gi + 1]), r=[pk, "psc"], w=[mixkeys[4 + gi]])
    _outproj(P, mixin, Wout, outT, N, pss, pi, mixkeys)
    return P.finish()


def odd_tail_dev(ycT, zT, pool_w, pool_scale, Wout):
    T = zT.shape[1]
    N = T // NCORE
    key = ("odd_tail", N)
    if key not in _CACHE:
        _CACHE[key] = build_odd_tail(N)
    zp = np.concatenate([np.zeros((512, 16), np.float32), zT], axis=1)
    t = np.arange(T)
    inv = np.stack([1.0 / np.minimum(t + 1, w) for w in (2, 4, 8, 16)]).astype(np.float32)
    pw = np.ascontiguousarray(pool_w.transpose(1, 0, 2))
    psc = np.ascontiguousarray(pool_scale.reshape(4, 128).T)
    maps = []
    for c in range(NCORE):
        ic = np.ascontiguousarray(np.broadcast_to(inv[None, :, c * N:(c + 1) * N], (128, 4, N)))
        maps.append({"yc": np.ascontiguousarray(ycT[:, c * N:(c + 1) * N]), "zT": np.ascontiguousarray(zp[:, c * N:(c + 1) * N + 16]),
                     "invc": ic, "pw": pw, "psc": psc, "Wout": np.ascontiguousarray(Wout)})
    res = run(_CACHE[key], maps)
    return np.concatenate([r["outT"] for r in res], axis=1)


def kernel(x, p, ev_w_in, ev_lambda_re, ev_lambda_im, ev_log_dt, ev_b_re, ev_b_im,
           ev_c_re, ev_c_im, ev_d, ev_w_glu, ev_b_glu, ev_conv_w, ev_w_out,
           od_w_in, od_rel_bias, od_pool_w, od_pool_scale, od_w_out,
           ln_mix_g, ln_mix_b, ln_ffn_g, ln_ffn_b, ffn_w_up, ffn_w_down,
           ple_w_proj, ple_w_gate, ple_b_gate):
    f = lambda a: np.asarray(a, dtype=np.float32)
    x_tm = f(x)[0]
    xT = np.ascontiguousarray(x_tm.T)
    for i in range(DEPTH):
        if i % 2 == 0:
            e = i // 2
            hT = lin(xT, f(ev_w_in[e]))
            yS = s5_mixer_dev(np.ascontiguousarray(hT[0:512]), f(ev_lambda_re[e]), f(ev_lambda_im[e]), f(ev_log_dt[e]),
                              f(ev_b_re[e]), f(ev_b_im[e]), f(ev_c_re[e]), f(ev_c_im[e]), f(ev_d[e]))
            mixT = even_tail_dev(yS, hT, f(ev_conv_w[e]), f(ev_w_glu[e]), f(ev_b_glu[e]), f(ev_w_out[e]))
        else:
            o = i // 2
            hT = lin(xT, f(od_w_in[o]))
            ycT = attn_dev(hT[0:512], hT[512:1024], hT[1024:1536], f(od_rel_bias[o]))
            mixT = odd_tail_dev(ycT, hT[1536:2048], f(od_pool_w[o]), f(od_pool_scale[o]), f(od_w_out[o]))
        x1 = resln(x_tm, np.ascontiguousarray(mixT.T), f(ln_mix_g[i]), f(ln_mix_b[i]))
        x1T = np.ascontiguousarray(x1.T)
        ffnT = ffn_dev(x1T, f(ffn_w_up[i]), f(ffn_w_down[i]))
        x2 = resln(x1, np.ascontiguousarray(ffnT.T), f(ln_ffn_g[i]), f(ln_ffn_b[i]))
        x2T = np.ascontiguousarray(x2.T)
        pT = np.ascontiguousarray(f(p[i])[0].T)
        xT = ple_dev(x2T, pT, f(ple_w_gate[i]), f(ple_b_gate[i]), f(ple_w_proj[i]))
        x_tm = np.ascontiguousarray(xT.T)
    return x_tm[None].astype(np.float32)
```

```python
import math
import numpy as np
import concourse.bass as bass
import concourse.mybir as mybir
from concourse.bass_utils import run_bass_kernel_spmd

F32 = mybir.dt.float32
BF16 = mybir.dt.bfloat16
AF = mybir.ActivationFunctionType
ALU = mybir.AluOpType
AX = mybir.AxisListType
NCORE = 8
MAGIC = 12582912.0
TWO_PI = 2.0 * math.pi

D_MODEL = 1024
SEQ = 16384
DEPTH = 4
D_FF = 2816
ALPHA = (2 * DEPTH) ** 0.25
LN_EPS = 1e-5


class Prog:
    ENGS = ("sync", "gpsimd", "act", "dve", "pe")
    NDS = 8

    def __init__(self):
        self.nc = bass.Bass("TRN2", target_bir_lowering=False)
        self.ops = {e: [] for e in self.ENGS}
        self.cnt = {}
        self.lastw = {}
        self.reads = {}
        self.seen = {e: {} for e in self.ENGS}
        self.ndma = {e: 0 for e in self.ENGS}
        self.ctx = []
        self.out_waits = []

    def enter(self, cm):
        v = cm.__enter__()
        self.ctx.append(cm)
        return v

    def sb(self, name, shape, dt):
        return self.enter(self.nc.sbuf_tensor(name, list(shape), dt))

    def ps(self, name, shape, dt=F32):
        return self.enter(self.nc.psum_tensor(name, list(shape), dt))

    def dram_in(self, name, shape, dt=F32):
        return self.nc.dram_tensor(name, list(shape), dt, kind="ExternalInput").ap()

    def dram_out(self, name, shape, dt=F32):
        return self.nc.dram_tensor(name, list(shape), dt, kind="ExternalOutput").ap()

    def _op(self, eng, fn, r, w, dma=False, is_out=False):
        waits = {}

        def need(sv):
            s, v = sv
            waits[s] = max(waits.get(s, 0), v)

        for k in r:
            if k in self.lastw:
                need(self.lastw[k])
        for k in w:
            if k in self.lastw:
                need(self.lastw[k])
            for sv in self.reads.get(k, ()):
                need(sv)
        if dma:
            i = self.ndma[eng]
            self.ndma[eng] += 1
            sem = "%s_d%d" % (eng, i % self.NDS)
            inc = 16
        else:
            sem = eng
            inc = 1
        prev = self.cnt.get(sem, 0)
        self.cnt[sem] = prev + inc
        me = (sem, self.cnt[sem])
        wl = []
        if dma and prev > 0:
            waits[sem] = max(waits.get(sem, 0), prev)
        for s, v in waits.items():
            if eng == "pe" and s == "pe":
                continue
            if self.seen[eng].get(s, 0) >= v:
                continue
            self.seen[eng][s] = v
            wl.append((s, v))
        self.ops[eng].append((wl, fn, sem, inc))
        for k in w:
            self.lastw[k] = me
            self.reads[k] = []
        for k in r:
            self.reads.setdefault(k, []).append(me)
        if is_out:
            self.out_waits.append(me)

    def dma(self, out, in_, r=(), w=(), cast=False, is_out=False, q=None):
        eng = "gpsimd" if cast else (q or "sync")
        self._op(eng, lambda e: e.dma_start(out=out, in_=in_), r, w, dma=True, is_out=is_out)

    def act(self, fn, r=(), w=()):
        self._op("act", fn, r, w)

    def dve(self, fn, r=(), w=()):
        self._op("dve", fn, r, w)

    def pe(self, fn, r=(), w=()):
        self._op("pe", fn, r, w)

    def finish(self):
        nc = self.nc
        fin = {}
        for s, v in self.out_waits:
            fin[s] = max(fin.get(s, 0), v)
        names = sorted(self.cnt.keys())
        sems = {}
        for n in names:
            sems[n] = self.enter(nc.semaphore(n))
        block = self.enter(nc.Block())
        engmap = {"sync": block.sync, "gpsimd": block.gpsimd, "act": block.scalar,
                  "dve": block.vector, "pe": block.tensor}

        def make(ename):
            def body(e):
                for wl, fn, sem, inc in self.ops[ename]:
                    for s, v in wl:
                        e.wait_ge(sems[s], v)
                    fn(e).then_inc(sems[sem], inc)
                if ename == "sync":
                    for s, v in fin.items():
                        e.wait_ge(sems[s], v)
            return body

        for ename in self.ENGS:
            if self.ops[ename] or ename == "sync":
                engmap[ename](make(ename))
        for cm in reversed(self.ctx):
            cm.__exit__(None, None, None)
        self.ctx = []
        return nc


def run(nc, in_maps):
    res = run_bass_kernel_spmd(nc, in_maps, core_ids=list(range(NCORE)))
    return res.results


_CACHE = {}


def build_lin(K, M, N, mode="plain", func=None, has_bias=False, has_scale=False):
    P = Prog()
    nc = P.nc
    KT = K // 128
    MO = M // 2 if mode == "swiglu" else M
    MT = MO // 128
    NT = N // 512
    inT = P.dram_in("inT", [K, N])
    W = P.dram_in("W", [K, M])
    outT = P.dram_out("outT", [MO, N])
    bias = P.dram_in("bias", [128, MT]) if has_bias else None
    scale = P.dram_in("scale", [128, MT]) if has_scale else None
    A = P.dram_in("A", [MO, N]) if mode == "fma" else None
    B = P.dram_in("B", [MO, N]) if mode == "fma" else None
    inb = P.sb("inb", [128, KT, N], BF16)
    NW = 3
    wb = [P.sb("wb%d" % i, [128, KT, 128], BF16) for i in range(NW * (2 if mode == "swiglu" else 1))]
    ot = [P.sb("ot%d" % i, [128, N], F32) for i in range(2)]
    pss = [P.ps("ps%d" % i, [128, 512]) for i in range(4)]
    if has_bias:
        bt = P.sb("bt", [128, MT], F32)
        P.dma(bt[:], bias, w=["bt"])
    if has_scale:
        st = P.sb("st", [128, MT], F32)
        P.dma(st[:], scale, w=["st"])
    if mode == "mulin":
        gf = P.sb("gf", [128, KT, N], F32)
        xs = [P.sb("xs%d" % i, [128, N], F32) for i in range(2)]
        tq = P.sb("tq", [128, N], F32)
        for kt in range(KT):
            x = xs[kt % 2]
            xk = "xs%d" % (kt % 2)
            P.dma(x[:], inT[kt * 128:(kt + 1) * 128, :], w=[xk])
            P.dve(lambda e, x=x: e.tensor_tensor(out=tq[:], in0=x[:], in1=x[:], op=ALU.mult), r=[xk], w=["tq"])
            P.dve(lambda e: e.tensor_scalar(out=tq[:], in0=tq[:], scalar1=0.044715, scalar2=1.0, op0=ALU.mult, op1=ALU.add), r=["tq"], w=["tq"])
            P.dve(lambda e, x=x: e.tensor_tensor(out=tq[:], in0=tq[:], in1=x[:], op=ALU.mult), r=["tq", xk], w=["tq"])
            P.act(lambda e: e.activation(out=tq[:], in_=tq[:], func=AF.Sigmoid, scale=1.5957691216057308), r=["tq"], w=["tq"])
            P.dve(lambda e, x=x, kt=kt: e.tensor_tensor(out=gf[:, kt, :], in0=tq[:], in1=x[:], op=ALU.mult), r=["tq", xk], w=["gf%d" % kt])
            P.act(lambda e, kt=kt: e.activation(out=inb[:, kt, :], in_=gf[:, kt, :], func=AF.Copy), r=["gf%d" % kt], w=["inb%d" % kt])
    else:
        for kt in range(KT):
            P.dma(inb[:, kt, :], inT[kt * 128:(kt + 1) * 128, :], w=["inb%d" % kt], cast=True)
    if mode == "swiglu":
        tmp = [P.sb("tmp%d" % i, [128, 512], F32) for i in range(2)]
    if mode == "fma":
        At = [P.sb("At%d" % i, [128, N], F32) for i in range(2)]
        Bt = [P.sb("Bt%d" % i, [128, N], F32) for i in range(2)]
    Wv = W.rearrange("(kt p) m -> p kt m", p=128)
    pi = 0
    for m in range(MT):
        s = m % NW
        o = ot[m % 2]
        ok = "ot%d" % (m % 2)
        P.dma(wb[s][:], Wv[:, :, m * 128:(m + 1) * 128], w=["wb%d" % s], cast=True)
        if mode == "swiglu":
            P.dma(wb[NW + s][:], Wv[:, :, MO + m * 128:MO + (m + 1) * 128], w=["wb%d" % (NW + s)], cast=True)
        if mode == "fma":
            P.dma(At[m % 2][:], A[m * 128:(m + 1) * 128, :], w=["At%d" % (m % 2)])
            P.dma(Bt[m % 2][:], B[m * 128:(m + 1) * 128, :], w=["Bt%d" % (m % 2)])
        for nt in range(NT):
            sl = slice(nt * 512, (nt + 1) * 512)
            ps = pss[pi % 4]
            pk = "ps%d" % (pi % 4)
            pi += 1
            for kt in range(KT):
                P.pe(lambda e, ps=ps, s=s, kt=kt, sl=sl: e.matmul(ps[:], lhsT=wb[s][:, kt, :], rhs=inb[:, kt, sl], start=(kt == 0), stop=(kt == KT - 1)),
                     r=["wb%d" % s, "inb%d" % kt], w=[pk])
            if mode == "swiglu":
                ps2 = pss[pi % 4]
                pk2 = "ps%d" % (pi % 4)
                pi += 1
                for kt in range(KT):
                    P.pe(lambda e, ps2=ps2, s=s, kt=kt, sl=sl: e.matmul(ps2[:], lhsT=wb[NW + s][:, kt, :], rhs=inb[:, kt, sl], start=(kt == 0), stop=(kt == KT - 1)),
                         r=["wb%d" % (NW + s), "inb%d" % kt], w=[pk2])
                t = tmp[nt % 2]
                tk = "tmp%d" % (nt % 2)
                P.act(lambda e, t=t, ps=ps: e.activation(out=t[:], in_=ps[:], func=AF.Silu), r=[pk], w=[tk])
                P.dve(lambda e, o=o, t=t, ps2=ps2, sl=sl: e.tensor_tensor(out=o[:, sl], in0=t[:], in1=ps2[:], op=ALU.mult), r=[tk, pk2], w=[ok])
            elif mode == "fma":
                P.dve(lambda e, o=o, ps=ps, sl=sl, m=m: e.tensor_tensor(out=o[:, sl], in0=Bt[m % 2][:, sl], in1=ps[:], op=ALU.mult), r=[pk, "Bt%d" % (m % 2)], w=[ok])
                P.dve(lambda e, o=o, sl=sl, m=m: e.tensor_tensor(out=o[:, sl], in0=o[:, sl], in1=At[m % 2][:, sl], op=ALU.add), r=[ok, "At%d" % (m % 2)], w=[ok])
            elif mode == "mulin":
                P.act(lambda e, o=o, ps=ps, sl=sl, m=m: e.activation(out=o[:, sl], in_=ps[:], func=AF.Sigmoid, bias=bt[:, m:m + 1]), r=[pk, "bt"], w=[ok])
                P.dve(lambda e, o=o, sl=sl, m=m: e.tensor_tensor(out=o[:, sl], in0=o[:, sl], in1=gf[:, m, sl], op=ALU.mult), r=[ok, "gf%d" % m], w=[ok])
            else:
                kw = {}
                rr = [pk]
                if has_bias:
                    kw["bias"] = bt[:, m:m + 1]
                    rr.append("bt")
                if has_scale:
                    kw["scale"] = st[:, m:m + 1]
                    rr.append("st")
                f = func if func is not None else AF.Identity
                P.act(lambda e, o=o, ps=ps, sl=sl, kw=kw, f=f: e.activation(out=o[:, sl], in_=ps[:], func=f, **kw), r=rr, w=[ok])
        P.dma(outT[m * 128:(m + 1) * 128, :], o[:], r=[ok], w=["out"], is_out=True)
    return P.finish()


def build_ple(N):
    P = Prog()
    KT, K2, MT, NT = 8, 2, 8, N // 512
    x2T = P.dram_in("x2T", [1024, N]); pT = P.dram_in("pT", [256, N])
    Wg = P.dram_in("Wg", [1024, 1024]); Wp = P.dram_in("Wp", [256, 1024]); bias = P.dram_in("bias", [128, MT])
    outT = P.dram_out("outT", [1024, N])
    inb = P.sb("inb", [128, KT, N], BF16); pb = P.sb("pb", [128, K2, N], BF16)
    bt = P.sb("bt", [128, MT], F32)
    P.dma(bt[:], bias, w=["bt"])
    for kt in range(KT):
        P.dma(inb[:, kt, :], x2T[kt * 128:(kt + 1) * 128, :], w=["inb%d" % kt], cast=True)
    for kt in range(K2):
        P.dma(pb[:, kt, :], pT[kt * 128:(kt + 1) * 128, :], w=["pb"], cast=True)
    NW = 3
    wg = [P.sb("wg%d" % i, [128, KT, 128], BF16) for i in range(NW)]
    wp = [P.sb("wp%d" % i, [128, K2, 128], BF16) for i in range(NW)]
    At = [P.sb("At%d" % i, [128, N], F32) for i in range(2)]
    ot = [P.sb("ot%d" % i, [128, N], F32) for i in range(2)]
    tmp = [P.sb("tmp%d" % i, [128, 512], F32) for i in range(2)]
    pss = [P.ps("ps%d" % i, [128, 512]) for i in range(4)]
    Wgv = Wg.rearrange("(kt p) m -> p kt m", p=128); Wpv = Wp.rearrange("(kt p) m -> p kt m", p=128)
    pi = 0
    for m in range(MT):
        s_ = m % NW
        o = ot[m % 2]; ok = "ot%d" % (m % 2); A = At[m % 2]; ak = "At%d" % (m % 2)
        ms = slice(m * 128, (m + 1) * 128)
        P.dma(wg[s_][:], Wgv[:, :, ms], w=["wg%d" % s_], cast=True)
        P.dma(wp[s_][:], Wpv[:, :, ms], w=["wp%d" % s_], cast=True)
        P.dma(A[:], x2T[ms, :], w=[ak], q="act")
        for nt in range(NT):
            sl = slice(nt * 512, (nt + 1) * 512)
            ps = pss[pi % 4]; pk = "ps%d" % (pi % 4); pi += 1
            ps2 = pss[pi % 4]; pk2 = "ps%d" % (pi % 4); pi += 1
            for kt in range(KT):
                P.pe(lambda e, ps=ps, s_=s_, kt=kt, sl=sl: e.matmul(ps[:], lhsT=wg[s_][:, kt, :], rhs=inb[:, kt, sl], start=(kt == 0), stop=(kt == KT - 1)), r=["wg%d" % s_, "inb%d" % kt], w=[pk])
            for kt in range(K2):
                P.pe(lambda e, ps2=ps2, s_=s_, kt=kt, sl=sl: e.matmul(ps2[:], lhsT=wp[s_][:, kt, :], rhs=pb[:, kt, sl], start=(kt == 0), stop=(kt == K2 - 1)), r=["wp%d" % s_, "pb"], w=[pk2])
            t = tmp[nt % 2]; tk = "tmp%d" % (nt % 2)
            P.act(lambda e, t=t, ps=ps, m=m: e.activation(out=t[:], in_=ps[:], func=AF.Sigmoid, bias=bt[:, m:m + 1]), r=[pk, "bt"], w=[tk])
            P.dve(lambda e, o=o, t=t, ps2=ps2, sl=sl: e.tensor_tensor(out=o[:, sl], in0=t[:], in1=ps2[:], op=ALU.mult), r=[tk, pk2], w=[ok])
            P.dve(lambda e, o=o, A=A, sl=sl: e.tensor_tensor(out=o[:, sl], in0=o[:, sl], in1=A[:, sl], op=ALU.add), r=[ok, ak], w=[ok])
        P.dma(outT[ms, :], o[:], r=[ok], w=["out"], is_out=True)
    return P.finish()


def ple_dev(x2T, pT, Wg, bg, Wp):
    T = x2T.shape[1]
    N = T // NCORE
    key = ("ple", N)
    if key not in _CACHE:
        _CACHE[key] = build_ple(N)
    b = np.ascontiguousarray(bg.reshape(8, 128).T)
    maps = [{"x2T": np.ascontiguousarray(x2T[:, c * N:(c + 1) * N]), "pT": np.ascontiguousarray(pT[:, c * N:(c + 1) * N]),
             "Wg": np.ascontiguousarray(Wg), "Wp": np.ascontiguousarray(Wp), "bias": b} for c in range(NCORE)]
    res = run(_CACHE[key], maps)
    return np.concatenate([r["outT"] for r in res], axis=1)


def build_ffn(N):
    P = Prog()
    KT, JT, MT, NT = 8, D_FF // 128, 8, N // 512
    inT = P.dram_in("inT", [1024, N]); Wu = P.dram_in("Wu", [1024, 2 * D_FF]); Wd = P.dram_in("Wd", [D_FF, 1024])
    outT = P.dram_out("outT", [1024, N])
    inb = P.sb("inb", [128, KT, N], BF16)
    actb = P.sb("actb", [128, JT, N], BF16)
    for kt in range(KT):
        P.dma(inb[:, kt, :], inT[kt * 128:(kt + 1) * 128, :], w=["inb%d" % kt], cast=True)
    NW = 3
    wb = [P.sb("wb%d" % i, [128, KT, 128], BF16) for i in range(2 * NW)]
    wd = [P.sb("wd%d" % i, [128, JT, 128], BF16) for i in range(2)]
    ot = [P.sb("ot%d" % i, [128, N], F32) for i in range(2)]
    tmp = [P.sb("tmp%d" % i, [128, 512], F32) for i in range(2)]
    pss = [P.ps("ps%d" % i, [128, 512]) for i in range(6)]
    Wuv = Wu.rearrange("(kt p) m -> p kt m", p=128); Wdv = Wd.rearrange("(jt p) m -> p jt m", p=128)
    pi = 0
    for j in range(JT):
        s_ = j % NW
        P.dma(wb[s_][:], Wuv[:, :, j * 128:(j + 1) * 128], w=["wb%d" % s_], cast=True)
        P.dma(wb[NW + s_][:], Wuv[:, :, D_FF + j * 128:D_FF + (j + 1) * 128], w=["wb%d" % (NW + s_)], cast=True)
        for nt in range(NT):
            sl = slice(nt * 512, (nt + 1) * 512)
            ps = pss[pi % 6]; pk = "ps%d" % (pi % 6); pi += 1
            ps2 = pss[pi % 6]; pk2 = "ps%d" % (pi % 6); pi += 1
            for kt in range(KT):
                P.pe(lambda e, ps=ps, s_=s_, kt=kt, sl=sl: e.matmul(ps[:], lhsT=wb[s_][:, kt, :], rhs=inb[:, kt, sl], start=(kt == 0), stop=(kt == KT - 1)), r=["wb%d" % s_, "inb%d" % kt], w=[pk])
            for kt in range(KT):
                P.pe(lambda e, ps2=ps2, s_=s_, kt=kt, sl=sl: e.matmul(ps2[:], lhsT=wb[NW + s_][:, kt, :], rhs=inb[:, kt, sl], start=(kt == 0), stop=(kt == KT - 1)), r=["wb%d" % (NW + s_), "inb%d" % kt], w=[pk2])
            t = tmp[nt % 2]; tk = "tmp%d" % (nt % 2)
            P.act(lambda e, t=t, ps=ps: e.activation(out=t[:], in_=ps[:], func=AF.Silu), r=[pk], w=[tk])
            P.dve(lambda e, t=t, ps2=ps2, j=j, sl=sl: e.tensor_tensor(out=actb[:, j, sl], in0=t[:], in1=ps2[:], op=ALU.mult), r=[tk, pk2], w=["actb%d" % j])
    allact = ["actb%d" % j for j in range(JT)]
    for m in range(MT):
        s_ = m % 2
        o = ot[m % 2]; ok = "ot%d" % (m % 2)
        P.dma(wd[s_][:], Wdv[:, :, m * 128:(m + 1) * 128], w=["wd%d" % s_], cast=True)
        for nt in range(NT):
            sl = slice(nt * 512, (nt + 1) * 512)
            ps = pss[pi % 6]; pk = "ps%d" % (pi % 6); pi += 1
            for j in range(JT):
                P.pe(lambda e, ps=ps, s_=s_, j=j, sl=sl: e.matmul(ps[:], lhsT=wd[s_][:, j, :], rhs=actb[:, j, sl], start=(j == 0), stop=(j == JT - 1)), r=["wd%d" % s_] + allact, w=[pk])
            P.act(lambda e, o=o, ps=ps, sl=sl: e.activation(out=o[:, sl], in_=ps[:], func=AF.Identity), r=[pk], w=[ok])
        P.dma(outT[m * 128:(m + 1) * 128, :], o[:], r=[ok], w=["out"], is_out=True)
    return P.finish()


def ffn_dev(x1T, Wu, Wd):
    T = x1T.shape[1]
    N = T // NCORE
    key = ("ffn", N)
    if key not in _CACHE:
        _CACHE[key] = build_ffn(N)
    maps = [{"inT": np.ascontiguousarray(x1T[:, c * N:(c + 1) * N]), "Wu": np.ascontiguousarray(Wu), "Wd": np.ascontiguousarray(Wd)} for c in range(NCORE)]
    res = run(_CACHE[key], maps)
    return np.concatenate([r["outT"] for r in res], axis=1)


def lin(inT_full, W, mode="plain", func=None, bias=None, scale=None, A=None, B=None):
    K, T = inT_full.shape
    M = W.shape[1]
    N = T // NCORE
    key = ("lin", K, M, N, mode, str(func), bias is not None, scale is not None)
    if key not in _CACHE:
        _CACHE[key] = build_lin(K, M, N, mode, func, bias is not None, scale is not None)
    nc = _CACHE[key]
    MO = M // 2 if mode == "swiglu" else M
    maps = []
    for c in range(NCORE):
        sl = slice(c * N, (c + 1) * N)
        d = {"inT": np.ascontiguousarray(inT_full[:, sl]), "W": np.ascontiguousarray(W)}
        if bias is not None:
            d["bias"] = np.ascontiguousarray(bias.reshape(MO // 128, 128).T)
        if scale is not None:
            d["scale"] = np.ascontiguousarray(scale.reshape(MO // 128, 128).T)
        if A is not None:
            d["A"] = np.ascontiguousarray(A[:, sl])
            d["B"] = np.ascontiguousarray(B[:, sl])
        maps.append(d)
    res = run(nc, maps)
    return np.concatenate([r["outT"] for r in res], axis=1)


def build_s5(NB):
    P = Prog()
    G, GC = 4, 2
    ub = P.dram_in("ub", [G, 128, NB])
    yb = P.dram_out("yb", [G, 128, NB])
    prm = {n: P.dram_in(n, [128, GC]) for n in ("lre", "lim", "ldt")}
    bin_ = {n: P.dram_in(n, [128, GC, 16]) for n in ("bre", "bim", "cre", "cim")}
    dcol_d = P.dram_in("dcol", [128, G])
    cmask_d = P.dram_in("cmask", [128, 128])
    ident_d = P.dram_in("ident", [128, 128])
    NL = int(math.log2(NB))
    t = {}
    for n in ("lre", "lim", "ldt", "dt", "a", "ang", "nr", "den", "rre", "rim", "nrim", "t1", "t2", "m2", "ire", "iim", "niim"):
        t[n] = P.sb("t_" + n, [128, GC], F32)
    for n in ("ak", "arg", "tr", "mag", "cs", "sn"):
        t[n] = P.sb("t_" + n, [128, 8, GC], F32)
    for n in ("bre", "bim", "cre", "cim", "bbre", "bbim", "t16a", "t16b"):
        t[n] = P.sb("t_" + n, [128, GC, 16], F32)
    PRE = P.sb("PRE", [128, 9, GC], F32); PIM = P.sb("PIM", [128, 9, GC], F32); NPIM = P.sb("NPIM", [128, 9, GC], F32)
    PWRE = P.sb("PWRE", [128, NL, GC], F32); PWIM = P.sb("PWIM", [128, NL, GC], F32); NPWIM = P.sb("NPWIM", [128, NL, GC], F32)
    BTre = P.sb("BTre", [128, GC, 128], F32); BTim = P.sb("BTim", [128, GC, 128], F32)
    CTre = P.sb("CTre", [128, GC, 128], F32); CTim = P.sb("CTim", [128, GC, 128], F32)
    BPre = P.sb("BPre", [128, GC, 128], F32); BPimn = P.sb("BPimn", [128, GC, 128], F32)
    X1 = P.sb("X1", [128, GC, 128], F32); X2 = P.sb("X2", [128, GC, 128], F32)
    X3 = P.sb("X3", [128, GC, 128], F32); X4 = P.sb("X4", [128, GC, 128], F32)
    dcol = P.sb("dcolt", [128, G], F32); cmask = P.sb("cmaskt", [128, 128], F32); ident = P.sb("identt", [128, 128], F32)
    dtmp = P.sb("dtmp", [128, 128], F32)
    Bre = P.sb("Bre", [128, G, 128], BF16); Bim = P.sb("Bim", [128, G, 128], BF16)
    Cre = P.sb("Cre", [128, GC, 128], BF16); Cimn = P.sb("Cimn", [128, GC, 128], BF16)
    Dm = P.sb("Dm", [128, G, 128], BF16)
    ubb = P.sb("ubb", [128, G, NB], BF16)
    Hre = P.sb("Hre", [128, GC, NB], F32); Him = P.sb("Him", [128, GC, NB], F32)
    Hbre = P.sb("Hbre", [128, GC, NB], BF16); Hbim = P.sb("Hbim", [128, GC, NB], BF16)
    Tres = [P.sb("Tre%d" % g, [128, NB // 2], F32) for g in range(GC)]
    Tims = [P.sb("Tim%d" % g, [128, NB // 2], F32) for g in range(GC)]
    yo = [P.sb("yo%d" % i, [128, NB], F32) for i in range(2)]
    pss = [P.ps("ps%d" % i, [128, 512]) for i in range(4)]
    K = ["prep"]

    for n in ("lre", "lim", "ldt"):
        P.dma(t[n][:], prm[n], w=K)
    for n in ("bre", "bim", "cre", "cim"):
        P.dma(t[n][:], bin_[n], w=K)
    P.dma(dcol[:], dcol_d, w=K); P.dma(cmask[:], cmask_d, w=K); P.dma(ident[:], ident_d, w=K)
    for g in range(G):
        P.dma(ubb[:, g, :], ub[g], w=["ubb%d" % g], cast=True)
    P.dve(lambda e: e.memset(Bre[:], 0.0), w=["Bpad"])
    P.dve(lambda e: e.memset(Bim[:], 0.0), w=["Bpad"])

    def tt(o, a, b, op, r=K, w=K):
        P.dve(lambda e: e.tensor_tensor(out=o, in0=a, in1=b, op=op), r=r, w=w)

    def ts(o, a, s1, op0, s2=None, op1=None, r=K, w=K):
        if op1 is None:
            P.dve(lambda e: e.tensor_scalar(out=o, in0=a, scalar1=s1, scalar2=None, op0=op0), r=r, w=w)
        else:
            P.dve(lambda e: e.tensor_scalar(out=o, in0=a, scalar1=s1, scalar2=s2, op0=op0, op1=op1), r=r, w=w)

    def stt(o, a, s_, b, op0, op1, r=K, w=K):
        P.dve(lambda e: e.scalar_tensor_tensor(out=o, in0=a, scalar=s_, in1=b, op0=op0, op1=op1), r=r, w=w)

    def act(o, a, f, **kw):
        P.act(lambda e: e.activation(out=o, in_=a, func=f, **kw), r=K, w=K)

    def sin_of(o, arg):
        ts(t["tr"][:], arg, 1.0 / TWO_PI, ALU.mult, MAGIC, ALU.add)
        ts(t["tr"][:], t["tr"][:], MAGIC, ALU.subtract, -TWO_PI, ALU.mult)
        tt(t["tr"][:], t["tr"][:], arg, ALU.add)
        ts(t["tr"][:], t["tr"][:], math.pi, ALU.min, -math.pi, ALU.max)
        act(o, t["tr"][:], AF.Sin)

    act(t["dt"][:], t["ldt"][:], AF.Exp)
    tt(t["a"][:], t["lre"][:], t["dt"][:], ALU.mult)
    tt(t["ang"][:], t["lim"][:], t["dt"][:], ALU.mult)
    P.dve(lambda e: e.memset(PRE[:, 0, :], 1.0), r=K, w=K)
    P.dve(lambda e: e.memset(PIM[:, 0, :], 0.0), r=K, w=K)
    for k in range(1, 9):
        ts(t["ak"][:, k - 1, :], t["a"][:], float(k), ALU.mult)
        ts(t["arg"][:, k - 1, :], t["ang"][:], float(k), ALU.mult)
    act(t["mag"][:], t["ak"][:], AF.Exp)
    sin_of(t["sn"][:], t["arg"][:])
    ts(t["arg"][:], t["arg"][:], math.pi / 2, ALU.add)
    sin_of(t["cs"][:], t["arg"][:])
    tt(PRE[:, 1:9, :], t["mag"][:], t["cs"][:], ALU.mult)
    tt(PIM[:, 1:9, :], t["mag"][:], t["sn"][:], ALU.mult)
    ts(NPIM[:], PIM[:], -1.0, ALU.mult)
    ts(t["nr"][:], PRE[:, 1, :], -1.0, ALU.add)
    tt(t["t1"][:], t["lre"][:], t["lre"][:], ALU.mult)
    tt(t["t2"][:], t["lim"][:], t["lim"][:], ALU.mult)
    tt(t["den"][:], t["t1"][:], t["t2"][:], ALU.add)
    P.dve(lambda e: e.reciprocal(out=t["den"][:], in_=t["den"][:]), r=K, w=K)
    tt(t["t1"][:], t["nr"][:], t["lre"][:], ALU.mult)
    tt(t["t2"][:], PIM[:, 1, :], t["lim"][:], ALU.mult)
    tt(t["t1"][:], t["t1"][:], t["t2"][:], ALU.add)
    tt(t["rre"][:], t["t1"][:], t["den"][:], ALU.mult)
    tt(t["t1"][:], PIM[:, 1, :], t["lre"][:], ALU.mult)
    tt(t["t2"][:], t["nr"][:], t["lim"][:], ALU.mult)
    tt(t["t1"][:], t["t1"][:], t["t2"][:], ALU.subtract)
    tt(t["rim"][:], t["t1"][:], t["den"][:], ALU.mult)
    ts(t["nrim"][:], t["rim"][:], -1.0, ALU.mult)
    tt(t["t1"][:], PRE[:, 8, :], PRE[:, 8, :], ALU.mult)
    tt(t["t2"][:], PIM[:, 8, :], PIM[:, 8, :], ALU.mult)
    tt(t["m2"][:], t["t1"][:], t["t2"][:], ALU.add)
    P.dve(lambda e: e.reciprocal(out=t["m2"][:], in_=t["m2"][:]), r=K, w=K)
    tt(t["ire"][:], PRE[:, 8, :], t["m2"][:], ALU.mult)
    tt(t["niim"][:], PIM[:, 8, :], t["m2"][:], ALU.mult)
    ts(t["iim"][:], t["niim"][:], -1.0, ALU.mult)
    for gc in range(GC):
        gs = slice(gc, gc + 1)
        ts(t["t16a"][:, gc, :], t["bre"][:, gc, :], t["rre"][:, gs], ALU.mult)
        ts(t["t16b"][:, gc, :], t["bim"][:, gc, :], t["rre"][:, gs], ALU.mult)
    for gc in range(GC):
        gs = slice(gc, gc + 1)
        stt(t["bbre"][:, gc, :], t["bim"][:, gc, :], t["nrim"][:, gs], t["t16a"][:, gc, :], ALU.mult, ALU.add)
        stt(t["bbim"][:, gc, :], t["bre"][:, gc, :], t["rim"][:, gs], t["t16b"][:, gc, :], ALU.mult, ALU.add)
    KB = ["prepB"]
    for gc in range(GC):
        gs = slice(gc, gc + 1)
        for i in range(8):
            isl = slice(i * 16, (i + 1) * 16)
            ts(X1[:, gc, isl], t["bbre"][:, gc, :], PRE[:, 7 - i, gs], ALU.mult, r=K, w=KB)
            ts(X2[:, gc, isl], t["bbim"][:, gc, :], PRE[:, 7 - i, gs], ALU.mult, r=K, w=KB)
            ts(X3[:, gc, isl], t["cre"][:, gc, :], PRE[:, i + 1, gs], ALU.mult, r=K, w=KB)
            ts(X4[:, gc, isl], t["cim"][:, gc, :], PRE[:, i + 1, gs], ALU.mult, r=K, w=KB)
    KC = ["prepC"]
    for gc in range(GC):
        gs = slice(gc, gc + 1)
        for i in range(8):
            isl = slice(i * 16, (i + 1) * 16)
            stt(BTre[:, gc, isl], t["bbim"][:, gc, :], NPIM[:, 7 - i, gs], X1[:, gc, isl], ALU.mult, ALU.add, r=K + KB, w=KC)
            stt(BTim[:, gc, isl], t["bbre"][:, gc, :], PIM[:, 7 - i, gs], X2[:, gc, isl], ALU.mult, ALU.add, r=K + KB, w=KC)
            stt(CTre[:, gc, isl], t["cim"][:, gc, :], NPIM[:, i + 1, gs], X3[:, gc, isl], ALU.mult, ALU.add, r=K + KB, w=KC)
            stt(CTim[:, gc, isl], t["cre"][:, gc, :], PIM[:, i + 1, gs], X4[:, gc, isl], ALU.mult, ALU.add, r=K + KB, w=KC)
    KD = ["prepD"]
    for gc in range(GC):
        gs = slice(gc, gc + 1)
        ts(X1[:, gc, :], BTre[:, gc, :], t["ire"][:, gs], ALU.mult, r=K + KC, w=KD)
        ts(X2[:, gc, :], BTim[:, gc, :], t["ire"][:, gs], ALU.mult, r=K + KC, w=KD)
    KE = ["prepE"]
    for gc in range(GC):
        gs = slice(gc, gc + 1)
        stt(BPre[:, gc, :], BTim[:, gc, :], t["niim"][:, gs], X1[:, gc, :], ALU.mult, ALU.add, r=K + KC + KD, w=KE)
        stt(X3[:, gc, :], BTre[:, gc, :], t["iim"][:, gs], X2[:, gc, :], ALU.mult, ALU.add, r=K + KC + KD, w=KE)
    ts(BPimn[:], X3[:], -1.0, ALU.mult, r=KE, w=KE)
    P.act(lambda e: e.activation(out=Cre[:], in_=CTre[:], func=AF.Copy), r=KC, w=["Cw"])
    P.act(lambda e: e.activation(out=Cimn[:], in_=CTim[:], func=AF.Copy, scale=-1.0), r=KC, w=["Cw"])
    for g in range(G):
        hf, gc = g // 2, g % 2
        hs = slice(hf * 64, (hf + 1) * 64)
        for n_, (src, dst) in enumerate(((BTre, Bre), (BTim, Bim))):
            ps = pss[n_]; pk = "ps%d" % n_
            P.pe(lambda e, src=src, gc=gc, hs=hs, ps=ps: e.matmul(ps[:, 0:64], lhsT=src[hs, gc, :], rhs=ident[hs, hs], start=True, stop=True), r=KC + K, w=[pk])
            P.act(lambda e, dst=dst, g=g, hs=hs, ps=ps: e.activation(out=dst[:, g, hs], in_=ps[:, 0:64], func=AF.Copy), r=[pk, "Bpad"], w=["Bpad"])
        ps = pss[2]
        P.pe(lambda e, gc=gc, hs=hs, ps=ps: e.matmul(ps[:, 0:128], lhsT=BPre[hs, gc, :], rhs=CTre[hs, gc, :], start=True, stop=False), r=KE + KC, w=["ps2"])
        P.pe(lambda e, gc=gc, hs=hs, ps=ps: e.matmul(ps[:, 0:128], lhsT=BPimn[hs, gc, :], rhs=CTim[hs, gc, :], start=False, stop=True), r=KE + KC, w=["ps2"])
        P.dve(lambda e, ps=ps: e.tensor_tensor(out=dtmp[:], in0=ps[:, 0:128], in1=cmask[:], op=ALU.mult), r=K + ["ps2"], w=["dtmp"])
        stt(Dm[:, g, :], ident[:], dcol[:, g:g + 1], dtmp[:], ALU.mult, ALU.add, r=K + ["dtmp"], w=["Dm"])
    tt(PWRE[:, 0, :], PRE[:, 8, :], PRE[:, 8, :], ALU.max)
    tt(PWIM[:, 0, :], PIM[:, 8, :], PIM[:, 8, :], ALU.max)
    for k in range(1, NL):
        tt(t["t1"][:], PWRE[:, k - 1, :], PWRE[:, k - 1, :], ALU.mult)
        tt(t["t2"][:], PWIM[:, k - 1, :], PWIM[:, k - 1, :], ALU.mult)
        tt(PWRE[:, k, :], t["t1"][:], t["t2"][:], ALU.subtract)
        tt(t["t1"][:], PWRE[:, k - 1, :], PWIM[:, k - 1, :], ALU.mult)
        ts(PWIM[:, k, :], t["t1"][:], 2.0, ALU.mult)
    ts(NPWIM[:], PWIM[:], -1.0, ALU.mult)

    pi = 0
    NT = NB // 512
    for gc in range(GC):
        for nt in range(NT):
            sl = slice(nt * 512, (nt + 1) * 512)
            for wsrc, H in ((Bre, Hre), (Bim, Him)):
                ps = pss[pi % 4]; pk = "ps%d" % (pi % 4); pi += 1
                P.pe(lambda e, ps=ps, wsrc=wsrc, gc=gc, sl=sl: e.matmul(ps[:], lhsT=wsrc[:, gc, :], rhs=ubb[:, gc, sl], start=True, stop=False), r=["Bpad", "ubb%d" % gc], w=[pk])
                P.pe(lambda e, ps=ps, wsrc=wsrc, gc=gc, sl=sl: e.matmul(ps[:], lhsT=wsrc[:, 2 + gc, :], rhs=ubb[:, 2 + gc, sl], start=False, stop=True), r=["Bpad", "ubb%d" % (2 + gc)], w=[pk])
                P.act(lambda e, ps=ps, H=H, gc=gc, sl=sl: e.activation(out=H[:, gc, sl], in_=ps[:], func=AF.Copy), r=[pk], w=["H%d" % gc])
    for k in range(NL):
        s = 1 << k
        for g in range(GC):
            hk = ["H%d" % g]
            hr = Hre[:, g, :]; hi = Him[:, g, :]
            vr = hr.rearrange("p (m t) -> p m t", t=2 * s); vi = hi.rearrange("p (m t) -> p m t", t=2 * s)
            tr_, sr_ = vr[:, :, 2 * s - 1], vr[:, :, s - 1]
            ti_, si_ = vi[:, :, 2 * s - 1], vi[:, :, s - 1]
            a_r, a_i, na_i = PWRE[:, k, g:g + 1], PWIM[:, k, g:g + 1], NPWIM[:, k, g:g + 1]
            stt(tr_, sr_, a_r, tr_, ALU.mult, ALU.add, r=K + hk, w=hk)
            stt(tr_, si_, na_i, tr_, ALU.mult, ALU.add, r=K + hk, w=hk)
            stt(ti_, si_, a_r, ti_, ALU.mult, ALU.add, r=K + hk, w=hk)
            stt(ti_, sr_, a_i, ti_, ALU.mult, ALU.add, r=K + hk, w=hk)
    for g in range(GC):
        hk = ["H%d" % g]
        P.dve(lambda e, g=g: e.memset(Hre[:, g, NB - 1:NB], 0.0), r=hk, w=hk)
        P.dve(lambda e, g=g: e.memset(Him[:, g, NB - 1:NB], 0.0), r=hk, w=hk)
    for k in range(NL - 1, -1, -1):
        s = 1 << k
        m = NB // (2 * s)
        for g in range(GC):
            hk = ["H%d" % g]
            tk = ["T%d" % g]
            TR, TI = Tres[g], Tims[g]
            hr = Hre[:, g, :]; hi = Him[:, g, :]
            vr = hr.rearrange("p (m t) -> p m t", t=2 * s); vi = hi.rearrange("p (m t) -> p m t", t=2 * s)
            Rr, Lr = vr[:, :, 2 * s - 1], vr[:, :, s - 1]
            Ri, Li = vi[:, :, 2 * s - 1], vi[:, :, s - 1]
            a_r, a_i, na_i = PWRE[:, k, g:g + 1], PWIM[:, k, g:g + 1], NPWIM[:, k, g:g + 1]
            kk = K + hk + tk
            stt(TR[:, 0:m], Rr, a_r, Lr, ALU.mult, ALU.add, r=kk, w=tk)
            stt(TR[:, 0:m], Ri, na_i, TR[:, 0:m], ALU.mult, ALU.add, r=kk, w=tk)
            stt(TI[:, 0:m], Ri, a_r, Li, ALU.mult, ALU.add, r=kk, w=tk)
            stt(TI[:, 0:m], Rr, a_i, TI[:, 0:m], ALU.mult, ALU.add, r=kk, w=tk)
            P.act(lambda e, Lr=Lr, Rr=Rr: e.activation(out=Lr, in_=Rr, func=AF.Copy), r=kk, w=hk)
            P.act(lambda e, Li=Li, Ri=Ri: e.activation(out=Li, in_=Ri, func=AF.Copy), r=kk, w=hk)
            P.dve(lambda e, Rr=Rr, m=m, TR=TR: e.tensor_copy(out=Rr, in_=TR[:, 0:m]), r=kk, w=hk)
            P.dve(lambda e, Ri=Ri, m=m, TI=TI: e.tensor_copy(out=Ri, in_=TI[:, 0:m]), r=kk, w=hk)
    for g in range(GC):
        hk = ["H%d" % g]
        P.act(lambda e, g=g: e.activation(out=Hbre[:, g, :], in_=Hre[:, g, :], func=AF.Copy), r=hk, w=["Hb%d" % g])
        P.act(lambda e, g=g: e.activation(out=Hbim[:, g, :], in_=Him[:, g, :], func=AF.Copy), r=hk, w=["Hb%d" % g])
    for g in range(G):
        hf, gc = g // 2, g % 2
        hs = slice(hf * 64, (hf + 1) * 64)
        o = yo[g % 2]; ok = "yo%d" % (g % 2)
        for nt in range(NT):
            sl = slice(nt * 512, (nt + 1) * 512)
            ps = pss[pi % 4]; pk = "ps%d" % (pi % 4); pi += 1
            rr = ["Cw", "Dm", "ubb%d" % g, "Hb%d" % gc]
            P.pe(lambda e, ps=ps, gc=gc, hs=hs, sl=sl: e.matmul(ps[:], lhsT=Cre[hs, gc, :], rhs=Hbre[hs, gc, sl], start=True, stop=False), r=rr, w=[pk])
            P.pe(lambda e, ps=ps, gc=gc, hs=hs, sl=sl: e.matmul(ps[:], lhsT=Cimn[hs, gc, :], rhs=Hbim[hs, gc, sl], start=False, stop=False), r=rr, w=[pk])
            P.pe(lambda e, ps=ps, g=g, sl=sl: e.matmul(ps[:], lhsT=Dm[:, g, :], rhs=ubb[:, g, sl], start=False, stop=True), r=rr, w=[pk])
            P.act(lambda e, ps=ps, o=o, sl=sl: e.activation(out=o[:, sl], in_=ps[:], func=AF.Copy), r=[pk], w=[ok])
        P.dma(yb[g], o[:], r=[ok], w=["out"], is_out=True)
    return P.finish()


def s5_mixer_dev(uT, lam_re, lam_im, log_dt, b_re, b_im, c_re, c_im, d_skip):
    T = uT.shape[1]
    NB = T // 8
    key = ("s5", NB)
    if key not in _CACHE:
        _CACHE[key] = build_s5(NB)
    ub = uT.reshape(32, 16, NB, 8).transpose(0, 3, 1, 2).reshape(32, 128, NB)
    ii, jj = np.arange(128) // 16, np.arange(128) // 16
    cmask = (jj[None, :] >= ii[:, None]).astype(np.float32)
    ident = np.eye(128, dtype=np.float32)
    maps = []

    def pl(a):
        sh = a.shape[2:]
        a = a.reshape((2, 2, 64) + sh)
        a = np.moveaxis(a, 1, 2)
        return np.ascontiguousarray(a.reshape((128, 2) + sh))

    for c in range(NCORE):
        gs = slice(4 * c, 4 * c + 4)
        maps.append({
            "ub": np.ascontiguousarray(ub[gs]),
            "lre": pl(lam_re[gs]), "lim": pl(lam_im[gs]),
            "ldt": pl(np.ascontiguousarray(np.broadcast_to(log_dt[gs][:, None], (4, 64)))),
            "bre": pl(b_re[gs]), "bim": pl(b_im[gs]),
            "cre": pl(np.ascontiguousarray(c_re[gs].transpose(0, 2, 1))), "cim": pl(np.ascontiguousarray(c_im[gs].transpose(0, 2, 1))),
            "dcol": np.ascontiguousarray(np.tile(d_skip.reshape(32, 16)[gs].T, (8, 1))),
            "cmask": cmask, "ident": ident,
        })
    res = run(_CACHE[key], maps)
    yb = np.concatenate([r["yb"] for r in res], axis=0)
    return np.ascontiguousarray(yb.reshape(32, 8, 16, NB).transpose(0, 2, 3, 1).reshape(512, T))


def build_resln(N, D):
    P = Prog()
    x = P.dram_in("x", [N, D]); m = P.dram_in("m", [N, D]); gb = P.dram_in("gb", [128, 2, D])
    y = P.dram_out("y", [N, D])
    gbt = P.sb("gbt", [128, 2, D], F32)
    P.dma(gbt[:], gb, w=["gb"], q="gpsimd")
    NB_ = 6
    xt = [P.sb("xt%d" % i, [128, D], F32) for i in range(NB_)]
    mt = [P.sb("mt%d" % i, [128, D], F32) for i in range(NB_)]
    yt = [P.sb("yt%d" % i, [128, D], F32) for i in range(NB_)]
    jk = [P.sb("jk%d" % i, [128, D], F32) for i in range(2)]
    st = [P.sb("st%d" % i, [128, 8], F32) for i in range(NB_)]
    NTL = N // 128
    PF = 4

    def names(i):
        b = i % NB_
        return xt[b], mt[b], yt[b], st[b], "xt%d" % b, "mt%d" % b, "yt%d" % b, "st%d" % b

    def loads(i):
        X, M, Y, S, xk, mk, yk, sk = names(i)
        rs = slice(i * 128, (i + 1) * 128)
        P.dma(X[:], x[rs, :], w=[xk], q="sync")
        P.dma(M[:], m[rs, :], w=[mk], q="act")

    def stage_a(i):
        X, M, Y, S, xk, mk, yk, sk = names(i)
        P.dve(lambda e: e.memset(S[:], 0.0), w=[sk])
        P.dve(lambda e: e.scalar_tensor_tensor(out=X[:], in0=X[:], scalar=float(ALPHA), in1=M[:], op0=ALU.mult, op1=ALU.add), r=[xk, mk], w=[xk])
        P.act(lambda e: e.activation(out=jk[0][:], in_=X[:], func=AF.Copy, accum_out=S[:, 0:1]), r=[xk, sk], w=["jk0", sk])
        P.act(lambda e: e.activation(out=jk[1][:], in_=X[:], func=AF.Square, accum_out=S[:, 1:2]), r=[xk, sk], w=["jk1", sk])

    def stage_b(i):
        X, M, Y, S, xk, mk, yk, sk = names(i)
        P.dve(lambda e: e.tensor_scalar(out=S[:, 2:4], in0=S[:, 0:2], scalar1=1.0 / D, scalar2=None, op0=ALU.mult), r=[sk], w=[sk])
        P.dve(lambda e: e.tensor_tensor(out=S[:, 4:5], in0=S[:, 2:3], in1=S[:, 2:3], op=ALU.mult), r=[sk], w=[sk])
        P.dve(lambda e: e.tensor_tensor(out=S[:, 4:5], in0=S[:, 3:4], in1=S[:, 4:5], op=ALU.subtract), r=[sk], w=[sk])
        P.dve(lambda e: e.tensor_scalar(out=S[:, 4:5], in0=S[:, 4:5], scalar1=LN_EPS, scalar2=None, op0=ALU.add), r=[sk], w=[sk])
        P.act(lambda e: e.activation(out=S[:, 4:5], in_=S[:, 4:5], func=AF.Sqrt), r=[sk], w=[sk])

    def stage_c(i):
        X, M, Y, S, xk, mk, yk, sk = names(i)
        P.dve(lambda e: e.reciprocal(out=S[:, 5:6], in_=S[:, 4:5]), r=[sk], w=[sk])
        P.dve(lambda e: e.scalar_tensor_tensor(out=S[:, 6:7], in0=S[:, 2:3], scalar=-1.0, in1=S[:, 5:6], op0=ALU.mult, op1=ALU.mult), r=[sk], w=[sk])
        P.act(lambda e: e.activation(out=Y[:], in_=X[:], func=AF.Identity, scale=S[:, 5:6], bias=S[:, 6:7]), r=[xk, sk], w=[yk])

    def stage_d(i):
        X, M, Y, S, xk, mk, yk, sk = names(i)
        rs = slice(i * 128, (i + 1) * 128)
        P.dve(lambda e: e.tensor_tensor(out=Y[:], in0=Y[:], in1=gbt[:, 0, :], op=ALU.mult), r=[yk, "gb"], w=[yk])
        P.dve(lambda e: e.tensor_tensor(out=Y[:], in0=Y[:], in1=gbt[:, 1, :], op=ALU.add), r=[yk, "gb"], w=[yk])
        P.dma(y[rs, :], Y[:], r=[yk], w=["out"], is_out=True, q="gpsimd")

    for i in range(min(PF, NTL)):
        loads(i)
    for s_ in range(NTL + 3):
        if 0 <= s_ - 3 < NTL:
            stage_d(s_ - 3)
        if 0 <= s_ - 2 < NTL:
            stage_c(s_ - 2)
        if 0 <= s_ - 1 < NTL:
            stage_b(s_ - 1)
        if s_ < NTL:
            stage_a(s_)
            if s_ + PF < NTL:
                loads(s_ + PF)
    return P.finish()


def resln(x_tm, m_tm, g, b):
    T, D = x_tm.shape
    N = T // NCORE
    key = ("resln", N, D)
    if key not in _CACHE:
        _CACHE[key] = build_resln(N, D)
    gb = np.ascontiguousarray(np.broadcast_to(np.stack([g, b])[None], (128, 2, D))).astype(np.float32)
    maps = [{"x": np.ascontiguousarray(x_tm[c * N:(c + 1) * N]), "m": np.ascontiguousarray(m_tm[c * N:(c + 1) * N]), "gb": gb} for c in range(NCORE)]
    res = run(_CACHE[key], maps)
    return np.concatenate([r["y"] for r in res], axis=0)


def build_conv(N):
    P = Prog()
    bT = P.dram_in("bT", [512, N]); cT = P.dram_in("cT", [512, N + 2]); xT = P.dram_in("xT", [512, N + 2])
    w = P.dram_in("w", [128, 4, 3])
    yT = P.dram_out("yT", [512, N])
    wt = P.sb("wt", [128, 4, 3], F32)
    P.dma(wt[:], w, w=["w"])
    for a in range(4):
        bt = P.sb("bt%d" % a, [128, N], F32); ct = P.sb("ct%d" % a, [128, N + 2], F32); xt = P.sb("xt%d" % a, [128, N + 2], F32)
        acc = P.sb("acc%d" % a, [128, N], F32)
        rs = slice(a * 128, (a + 1) * 128)
        k = "c%d" % a
        P.dma(bt[:], bT[rs, :], w=[k + "b"]); P.dma(ct[:], cT[rs, :], w=[k]); P.dma(xt[:], xT[rs, :], w=[k + "x"])
        P.dve(lambda e, ct=ct, xt=xt: e.tensor_tensor(out=ct[:], in0=ct[:], in1=xt[:], op=ALU.mult), r=[k, k + "x"], w=[k])
        P.dve(lambda e, ct=ct, acc=acc, a=a: e.tensor_scalar(out=acc[:], in0=ct[:, 0:N], scalar1=wt[:, a, 0:1], scalar2=None, op0=ALU.mult), r=[k, "w"], w=[k + "a"])
        for j in (1, 2):
            P.dve(lambda e, ct=ct, acc=acc, a=a, j=j: e.scalar_tensor_tensor(out=acc[:], in0=ct[:, j:j + N], scalar=wt[:, a, j:j + 1], in1=acc[:], op0=ALU.mult, op1=ALU.add), r=[k, "w", k + "a"], w=[k + "a"])
        P.dve(lambda e, acc=acc, bt=bt: e.tensor_tensor(out=acc[:], in0=acc[:], in1=bt[:], op=ALU.mult), r=[k + "a", k + "b"], w=[k + "a"])
        P.dma(yT[rs, :], acc[:], r=[k + "a"], w=["out"], is_out=True)
    return P.finish()


def conv_dev(bT, cT, xT, cw):
    T = bT.shape[1]
    N = T // NCORE
    key = ("conv", N)
    if key not in _CACHE:
        _CACHE[key] = build_conv(N)
    cp = np.concatenate([np.zeros((512, 2), np.float32), cT], axis=1)
    xp = np.concatenate([np.zeros((512, 2), np.float32), xT], axis=1)
    w = np.ascontiguousarray(cw.reshape(3, 4, 128).transpose(2, 1, 0))
    maps = [{"bT": np.ascontiguousarray(bT[:, c * N:(c + 1) * N]), "cT": np.ascontiguousarray(cp[:, c * N:(c + 1) * N + 2]),
             "xT": np.ascontiguousarray(xp[:, c * N:(c + 1) * N + 2]), "w": w} for c in range(NCORE)]
    res = run(_CACHE[key], maps)
    return np.concatenate([r["yT"] for r in res], axis=1)


def build_pool(N):
    P = Prog()
    zT = P.dram_in("zT", [512, N + 16]); invc = P.dram_in("invc", [128, 4, N])
    oT = P.dram_out("oT", [512, N])
    for gi in range(4):
        z = P.sb("z%d" % gi, [128, N + 16], F32)
        sa = P.sb("sa%d" % gi, [128, N + 16], F32); sb_ = P.sb("sb%d" % gi, [128, N + 16], F32)
        ic = P.sb("ic%d" % gi, [128, N], F32)
        rs = slice(gi * 128, (gi + 1) * 128)
        k = "p%d" % gi
        P.dma(z[:], zT[rs, :], w=[k + "z"]); P.dma(ic[:], invc[:, gi, :], w=[k + "i"])
        cur, curk = z, k + "z"
        bufs = [(sa, k + "a"), (sb_, k + "b")]
        for step in range(gi + 1):
            sh = 1 << step
            nxt, nk = bufs[step % 2]
            P.dve(lambda e, cur=cur, nxt=nxt, sh=sh: e.tensor_tensor(out=nxt[:, sh:], in0=cur[:, sh:], in1=cur[:, 0:N + 16 - sh], op=ALU.add), r=[curk], w=[nk])
            cur, curk = nxt, nk
        o, okey = bufs[(gi + 1) % 2]
        P.dve(lambda e, cur=cur, o=o, ic=ic: e.tensor_tensor(out=o[:, 16:], in0=cur[:, 16:], in1=ic[:], op=ALU.mult), r=[curk, k + "i"], w=[okey])
        P.dve(lambda e, o=o, z=z: e.tensor_tensor(out=o[:, 16:], in0=o[:, 16:], in1=z[:, 16:], op=ALU.subtract), r=[okey, k + "z"], w=[okey])
        P.dma(oT[rs, :], o[:, 16:], r=[okey], w=["out"], is_out=True)
    return P.finish()


def pool_dev(zT):
    T = zT.shape[1]
    N = T // NCORE
    key = ("pool", N)
    if key not in _CACHE:
        _CACHE[key] = build_pool(N)
    zp = np.concatenate([np.zeros((512, 16), np.float32), zT], axis=1)
    t = np.arange(T)
    inv = np.stack([1.0 / np.minimum(t + 1, w) for w in (2, 4, 8, 16)]).astype(np.float32)
    maps = []
    for c in range(NCORE):
        ic = np.ascontiguousarray(np.broadcast_to(inv[None, :, c * N:(c + 1) * N], (128, 4, N)))
        maps.append({"zT": np.ascontiguousarray(zp[:, c * N:(c + 1) * N + 16]), "invc": ic})
    res = run(_CACHE[key], maps)
    return np.concatenate([r["oT"] for r in res], axis=1)


def build_attn(T):
    P = Prog()
    NBK = T // 128
    qT = P.dram_in("qT", [64, T]); kT = P.dram_in("kT", [64, T]); v = P.dram_in("v", [T, 64])
    bias = P.dram_in("bias", [128, 5, 128])
    o_tm = P.dram_out("o", [T, 64])
    qb = P.sb("qb", [64, T], BF16); kb = P.sb("kb", [64, T], BF16); vb = P.sb("vb", [128, NBK, 65], BF16)
    bf = P.sb("bf", [128, 5, 128], F32); eb = P.sb("eb", [128, 5, 128], F32)
    P.dve(lambda e: e.memset(vb[:], 1.0), w=["vb"])
    P.dma(bf[:], bias, w=["bf"])
    P.dma(qb[:], qT, w=["qb"], cast=True); P.dma(kb[:], kT, w=["kb"], cast=True)
    vv = v.rearrange("(n p) d -> p n d", p=128)
    for j0 in range(0, NBK, 16):
        j1 = min(NBK, j0 + 16)
        P.dma(vb[:, j0:j1, 0:64], vv[:, j0:j1, :], w=["vb"], cast=True)
    P.act(lambda e: e.activation(out=eb[:], in_=bf[:], func=AF.Exp), r=["bf"], w=["eb"])
    P.act(lambda e: e.activation(out=qb[:], in_=qb[:], func=AF.Copy, scale=0.125), r=["qb"], w=["qb"])
    NBUF = 3
    psS = [P.ps("psS%d" % i, [128, 5, 128]) for i in range(2)]
    psO = [P.ps("psO%d" % i, [128, 512]) for i in range(2)]
    pf = [P.sb("pf%d" % i, [128, 5, 128], F32) for i in range(NBUF)]
    pt = [P.sb("pt%d" % i, [128, 5, 128], BF16) for i in range(NBUF)]
    rec = [P.sb("rec%d" % i, [128, 1], F32) for i in range(NBUF)]
    ob = [P.sb("ob%d" % i, [128, 16, 64], F32) for i in range(2)]
    o_v = o_tm.rearrange("(n p) d -> p n d", p=128)
    def names(m):
        b = m % 2
        return (psS[b], psO[b], pf[m % NBUF], pt[m % NBUF], rec[m % NBUF],
                "S%d" % b, "O%d" % b, "pf%d" % (m % NBUF), "pt%d" % (m % NBUF), "rc%d" % (m % NBUF))

    def front(m):
        S, O, PF, PT, RC, sk, okk, fk, pk, rk = names(m)
        qs = slice(m * 128, (m + 1) * 128)
        i0_ = max(0, 4 - m)
        for i in range(i0_, 5):
            kt = m - 4 + i
            P.pe(lambda e, i=i, kt=kt: e.matmul(S[:, i, :], lhsT=kb[:, kt * 128:(kt + 1) * 128], rhs=qb[:, qs], start=True, stop=True), r=["kb", "qb"], w=[sk])
        if i0_ < 4:
            P.act(lambda e: e.activation(out=PF[:, i0_:4, :], in_=S[:, i0_:4, :], func=AF.Exp), r=[sk], w=[fk])
        P.act(lambda e: e.activation(out=PF[:, 4, :], in_=S[:, 4, :], func=AF.Exp), r=[sk], w=[fk])
        P.dve(lambda e: e.tensor_tensor(out=PT[:, i0_:5, :], in0=PF[:, i0_:5, :], in1=eb[:, i0_:5, :], op=ALU.mult), r=[fk, "eb"], w=[pk])

    def back(m):
        S, O, PF, PT, RC, sk, okk, fk, pk, rk = names(m)
        OB = ob[(m // 16) % 2]; obk = "ob%d" % ((m // 16) % 2)
        val = list(range(max(0, 4 - m), 5))
        for n, i in enumerate(val):
            kt = m - 4 + i
            P.pe(lambda e, i=i, kt=kt, n=n: e.matmul(O[:, 0:65], lhsT=PT[:, i, :], rhs=vb[:, kt, :], start=(n == 0), stop=(n == len(val) - 1)), r=["vb", pk], w=[okk])
        P.dve(lambda e: e.reciprocal(out=RC[:], in_=O[:, 64:65]), r=[okk], w=[rk])
        c = m % 16
        P.dve(lambda e: e.tensor_scalar(out=OB[:, c, :], in0=O[:, 0:64], scalar1=RC[:, 0:1], scalar2=None, op0=ALU.mult), r=[okk, rk], w=[obk])
        if m % 16 == 15:
            g0 = (m // 16) * 16
            P.dma(o_v[:, g0:g0 + 16, :], OB[:], r=[obk], w=["out"], is_out=True)

    for s_ in range(NBK + 1):
        if s_ < NBK:
            front(s_)
        if s_ >= 1:
            back(s_ - 1)
    return P.finish()


def attn_dev(qT, kT, vT, rel_bias):
    T = qT.shape[1]
    key = ("attn", T)
    if key not in _CACHE:
        _CACHE[key] = build_attn(T)
    kk = np.arange(640)[:, None]; qq = np.arange(128)[None, :]
    qc = qq // 64; kc = kk // 64
    dist = (qq - (kk - 512))
    rel = np.clip(dist, -128, 128) + 128
    band = kc - qc
    valid = (band >= 0) & (band <= 8)
    maps = []
    for h in range(NCORE):
        b2 = np.where(valid, rel_bias[h][rel], np.float32(-30000.0)).astype(np.float32)
        b2 = np.ascontiguousarray(b2.reshape(5, 128, 128).transpose(1, 0, 2))
        hs = slice(h * 64, (h + 1) * 64)
        maps.append({"qT": np.ascontiguousarray(qT[hs]), "kT": np.ascontiguousarray(kT[hs]),
                     "v": np.ascontiguousarray(vT[hs].T), "bias": b2})
    res = run(_CACHE[key], maps)
    return np.ascontiguousarray(np.concatenate([r["o"].T for r in res], axis=0))


def _outproj(P, mixin, Wout, outT, N, pss, pi, mixkeys):
    NT = N // 512
    Wv = Wout.rearrange("(kt p) m -> p kt m", p=128)
    wo = [P.sb("wo%d" % i, [128, 8, 128], BF16) for i in range(3)]
    ot = [P.sb("oto%d" % i, [128, N], F32) for i in range(2)]
    for m in range(8):
        s_ = m % 3
        o = ot[m % 2]; ok = "oto%d" % (m % 2)
        P.dma(wo[s_][:], Wv[:, :, m * 128:(m + 1) * 128], w=["wo%d" % s_], cast=True)
        for nt in range(NT):
            sl = slice(nt * 512, (nt + 1) * 512)
            ps = pss[pi % len(pss)]; pk = "ps%d" % (pi % len(pss)); pi += 1
            for kt in range(8):
                P.pe(lambda e, ps=ps, s_=s_, kt=kt, sl=sl: e.matmul(ps[:], lhsT=wo[s_][:, kt, :], rhs=mixin[:, kt, sl], start=(kt == 0), stop=(kt == 7)), r=["wo%d" % s_, mixkeys[kt]], w=[pk])
            P.act(lambda e, o=o, ps=ps, sl=sl: e.activation(out=o[:, sl], in_=ps[:], func=AF.Identity), r=[pk], w=[ok])
        P.dma(outT[m * 128:(m + 1) * 128, :], o[:], r=[ok], w=["out"], is_out=True)
    return pi


def build_even_tail(N):
    P = Prog()
    NT = N // 512
    yS = P.dram_in("yS", [512, N]); bT = P.dram_in("bT", [512, N]); cT = P.dram_in("cT", [512, N + 2]); xT = P.dram_in("xT", [512, N + 2])
    cw = P.dram_in("cw", [128, 4, 3]); Wg = P.dram_in("Wg", [512, 512]); bg = P.dram_in("bg", [128, 4]); Wout = P.dram_in("Wout", [1024, 1024])
    outT = P.dram_out("outT", [1024, N])
    mixin = P.sb("mixin", [128, 8, N], BF16)
    mixkeys = ["mix%d" % k for k in range(8)]
    gf = P.sb("gf", [128, 4, N], F32); inb = P.sb("inb", [128, 4, N], BF16)
    xs = [P.sb("xs%d" % i, [128, N], F32) for i in range(2)]
    tq = P.sb("tq", [128, N], F32)
    bt_ = P.sb("bgt", [128, 4], F32); wt = P.sb("cwt", [128, 4, 3], F32)
    P.dma(bt_[:], bg, w=["bg"]); P.dma(wt[:], cw, w=["cw"])
    pss = [P.ps("ps%d" % i, [128, 512]) for i in range(4)]
    for kt in range(4):
        x = xs[kt % 2]; xk = "xs%d" % (kt % 2)
        P.dma(x[:], yS[kt * 128:(kt + 1) * 128, :], w=[xk])
        P.dve(lambda e, x=x: e.tensor_tensor(out=tq[:], in0=x[:], in1=x[:], op=ALU.mult), r=[xk], w=["tq"])
        P.dve(lambda e: e.tensor_scalar(out=tq[:], in0=tq[:], scalar1=0.044715, scalar2=1.0, op0=ALU.mult, op1=ALU.add), r=["tq"], w=["tq"])
        P.dve(lambda e, x=x: e.tensor_tensor(out=tq[:], in0=tq[:], in1=x[:], op=ALU.mult), r=["tq", xk], w=["tq"])
        P.act(lambda e: e.activation(out=tq[:], in_=tq[:], func=AF.Sigmoid, scale=1.5957691216057308), r=["tq"], w=["tq"])
        P.dve(lambda e, x=x, kt=kt: e.tensor_tensor(out=gf[:, kt, :], in0=tq[:], in1=x[:], op=ALU.mult), r=["tq", xk], w=["gf%d" % kt])
        P.act(lambda e, kt=kt: e.activation(out=inb[:, kt, :], in_=gf[:, kt, :], func=AF.Copy), r=["gf%d" % kt], w=["inb%d" % kt])
    cb_ = [P.sb("cvb%d" % i, [128, N], F32) for i in range(2)]
    cc_ = [P.sb("cvc%d" % i, [128, N + 2], F32) for i in range(2)]
    cx_ = [P.sb("cvx%d" % i, [128, N + 2], F32) for i in range(2)]
    ca_ = [P.sb("cva%d" % i, [128, N], F32) for i in range(2)]
    for a in range(4):
        b = a % 2
        bt, ct, xt, acc = cb_[b], cc_[b], cx_[b], ca_[b]
        rs = slice(a * 128, (a + 1) * 128)
        k = "cv%d" % b
        P.dma(bt[:], bT[rs, :], w=[k + "b"], q="act"); P.dma(ct[:], cT[rs, :], w=[k], q="sync"); P.dma(xt[:], xT[rs, :], w=[k + "x"], q="act")
        P.dve(lambda e, ct=ct, xt=xt: e.tensor_tensor(out=ct[:], in0=ct[:], in1=xt[:], op=ALU.mult), r=[k, k + "x"], w=[k])
        P.dve(lambda e, ct=ct, acc=acc, a=a: e.tensor_scalar(out=acc[:], in0=ct[:, 0:N], scalar1=wt[:, a, 0:1], scalar2=None, op0=ALU.mult), r=[k, "cw"], w=[k + "a"])
        for j in (1, 2):
            P.dve(lambda e, ct=ct, acc=acc, a=a, j=j: e.scalar_tensor_tensor(out=acc[:], in0=ct[:, j:j + N], scalar=wt[:, a, j:j + 1], in1=acc[:], op0=ALU.mult, op1=ALU.add), r=[k, "cw", k + "a"], w=[k + "a"])
        P.dve(lambda e, acc=acc, bt=bt, a=a: e.tensor_tensor(out=mixin[:, 4 + a, :], in0=acc[:], in1=bt[:], op=ALU.mult), r=[k + "a", k + "b"], w=[mixkeys[4 + a]])
    Wgv = Wg.rearrange("(kt p) m -> p kt m", p=128)
    wg = [P.sb("wg%d" % i, [128, 4, 128], BF16) for i in range(2)]
    sg = [P.sb("sg%d" % i, [128, 512], F32) for i in range(2)]
    pi = 0
    for m in range(4):
        s_ = m % 2
        P.dma(wg[s_][:], Wgv[:, :, m * 128:(m + 1) * 128], w=["wg%d" % s_], cast=True)
        for nt in range(NT):
            sl = slice(nt * 512, (nt + 1) * 512)
            ps = pss[pi % 4]; pk = "ps%d" % (pi % 4); pi += 1
            for kt in range(4):
                P.pe(lambda e, ps=ps, s_=s_, kt=kt, sl=sl: e.matmul(ps[:], lhsT=wg[s_][:, kt, :], rhs=inb[:, kt, sl], start=(kt == 0), stop=(kt == 3)), r=["wg%d" % s_, "inb%d" % kt], w=[pk])
            t = sg[nt % 2]; tk = "sg%d" % (nt % 2)
            P.act(lambda e, t=t, ps=ps, m=m: e.activation(out=t[:], in_=ps[:], func=AF.Sigmoid, bias=bt_[:, m:m + 1]), r=[pk, "bg"], w=[tk])
            P.dve(lambda e, t=t, m=m, sl=sl: e.tensor_tensor(out=mixin[:, m, sl], in0=t[:], in1=gf[:, m, sl], op=ALU.mult), r=[tk, "gf%d" % m], w=[mixkeys[m]])
    _outproj(P, mixin, Wout, outT, N, pss, pi, mixkeys)
    return P.finish()


def even_tail_dev(yS, hT, cw, Wg, bg, Wout):
    T = yS.shape[1]
    N = T // NCORE
    key = ("even_tail", N)
    if key not in _CACHE:
        _CACHE[key] = build_even_tail(N)
    bT, cT, xT = hT[512:1024], hT[1024:1536], hT[1536:2048]
    cp = np.concatenate([np.zeros((512, 2), np.float32), cT], axis=1)
    xp = np.concatenate([np.zeros((512, 2), np.float32), xT], axis=1)
    w = np.ascontiguousarray(cw.reshape(3, 4, 128).transpose(2, 1, 0))
    b = np.ascontiguousarray(bg.reshape(4, 128).T)
    maps = [{"yS": np.ascontiguousarray(yS[:, c * N:(c + 1) * N]), "bT": np.ascontiguousarray(bT[:, c * N:(c + 1) * N]),
             "cT": np.ascontiguousarray(cp[:, c * N:(c + 1) * N + 2]), "xT": np.ascontiguousarray(xp[:, c * N:(c + 1) * N + 2]),
             "cw": w, "Wg": np.ascontiguousarray(Wg), "bg": b, "Wout": np.ascontiguousarray(Wout)} for c in range(NCORE)]
    res = run(_CACHE[key], maps)
    return np.concatenate([r["outT"] for r in res], axis=1)


def build_odd_tail(N):
    P = Prog()
    NT = N // 512
    yc = P.dram_in("yc", [512, N]); zT = P.dram_in("zT", [512, N + 16]); invc = P.dram_in("invc", [128, 4, N])
    pw = P.dram_in("pw", [128, 4, 128]); psc = P.dram_in("psc", [128, 4]); Wout = P.dram_in("Wout", [1024, 1024])
    outT = P.dram_out("outT", [1024, N])
    mixin = P.sb("mixin", [128, 8, N], BF16)
    mixkeys = ["mix%d" % k for k in range(8)]
    for kt in range(4):
        P.dma(mixin[:, kt, :], yc[kt * 128:(kt + 1) * 128, :], w=[mixkeys[kt]], cast=True)
    pwb = P.sb("pwb", [128, 4, 128], BF16); sct = P.sb("sct", [128, 4], F32)
    P.dma(pwb[:], pw, w=["pw"], cast=True); P.dma(sct[:], psc, w=["psc"])
    pooled = P.sb("pooled", [128, 4, N], BF16)
    pss = [P.ps("ps%d" % i, [128, 512]) for i in range(4)]
    zb = [P.sb("pz%d" % i, [128, N + 16], F32) for i in range(2)]
    sab = [P.sb("psa%d" % i, [128, N + 16], F32) for i in range(2)]
    sbb = [P.sb("psb%d" % i, [128, N + 16], F32) for i in range(2)]
    icb = [P.sb("pic%d" % i, [128, N], F32) for i in range(2)]
    pi = 0
    for gi in range(4):
        b = gi % 2
        z, sa, sb_, ic = zb[b], sab[b], sbb[b], icb[b]
        rs = slice(gi * 128, (gi + 1) * 128)
        k = "pl%d" % b
        P.dma(z[:], zT[rs, :], w=[k + "z"], q="sync"); P.dma(ic[:], invc[:, gi, :], w=[k + "i"], q="act")
        cur, curk = z, k + "z"
        bufs = [(sa, k + "a"), (sb_, k + "b")]
        for step in range(gi + 1):
            sh = 1 << step
            nxt, nk = bufs[step % 2]
            P.dve(lambda e, cur=cur, nxt=nxt, sh=sh: e.tensor_tensor(out=nxt[:, sh:], in0=cur[:, sh:], in1=cur[:, 0:N + 16 - sh], op=ALU.add), r=[curk], w=[nk])
            cur, curk = nxt, nk
        o, okey = bufs[(gi + 1) % 2]
        P.dve(lambda e, cur=cur, o=o, ic=ic: e.tensor_tensor(out=o[:, 16:], in0=cur[:, 16:], in1=ic[:], op=ALU.mult), r=[curk, k + "i"], w=[okey])
        P.dve(lambda e, o=o, z=z, gi=gi: e.tensor_tensor(out=pooled[:, gi, :], in0=o[:, 16:], in1=z[:, 16:], op=ALU.subtract), r=[okey, k + "z"], w=["pooled%d" % gi])
        for nt in range(NT):
            sl = slice(nt * 512, (nt + 1) * 512)
            ps = pss[pi % 4]; pk = "ps%d" % (pi % 4); pi += 1
            P.pe(lambda e, ps=ps, gi=gi, sl=sl: e.matmul(ps[:], lhsT=pwb[:, gi, :], rhs=pooled[:, gi, sl], start=True, stop=True), r=["pw", "pooled%d" % gi], w=[pk])
            P.act(lambda e, ps=ps, gi=gi, sl=sl: e.activation(out=mixin[:, 4 + gi, sl], in_=ps[:], func=AF.Identity, scale=sct[:, gi:gi + 1]), r=[pk, "psc"], w=[mixkeys[4 + gi]])
    _outproj(P, mixin, Wout, outT, N, pss, pi, mixkeys)
    return P.finish()


def odd_tail_dev(ycT, zT, pool_w, pool_scale, Wout):
    T = zT.shape[1]
    N = T // NCORE
    key = ("odd_tail", N)
    if key not in _CACHE:
        _CACHE[key] = build_odd_tail(N)
    zp = np.concatenate([np.zeros((512, 16), np.float32), zT], axis=1)
    t = np.arange(T)
    inv = np.stack([1.0 / np.minimum(t + 1, w) for w in (2, 4, 8, 16)]).astype(np.float32)
    pw = np.ascontiguousarray(pool_w.transpose(1, 0, 2))
    psc = np.ascontiguousarray(pool_scale.reshape(4, 128).T)
    maps = []
    for c in range(NCORE):
        ic = np.ascontiguousarray(np.broadcast_to(inv[None, :, c * N:(c + 1) * N], (128, 4, N)))
        maps.append({"yc": np.ascontiguousarray(ycT[:, c * N:(c + 1) * N]), "zT": np.ascontiguousarray(zp[:, c * N:(c + 1) * N + 16]),
                     "invc": ic, "pw": pw, "psc": psc, "Wout": np.ascontiguousarray(Wout)})
    res = run(_CACHE[key], maps)
    return np.concatenate([r["outT"] for r in res], axis=1)


def kernel(x, p, ev_w_in, ev_lambda_re, ev_lambda_im, ev_log_dt, ev_b_re, ev_b_im,
           ev_c_re, ev_c_im, ev_d, ev_w_glu, ev_b_glu, ev_conv_w, ev_w_out,
           od_w_in, od_rel_bias, od_pool_w, od_pool_scale, od_w_out,
           ln_mix_g, ln_mix_b, ln_ffn_g, ln_ffn_b, ffn_w_up, ffn_w_down,
           ple_w_proj, ple_w_gate, ple_b_gate):
    f = lambda a: np.asarray(a, dtype=np.float32)
    x_tm = f(x)[0]
    xT = np.ascontiguousarray(x_tm.T)
    for i in range(DEPTH):
        if i % 2 == 0:
            e = i // 2
            hT = lin(xT, f(ev_w_in[e]))
            yS = s5_mixer_dev(np.ascontiguousarray(hT[0:512]), f(ev_lambda_re[e]), f(ev_lambda_im[e]), f(ev_log_dt[e]),
                              f(ev_b_re[e]), f(ev_b_im[e]), f(ev_c_re[e]), f(ev_c_im[e]), f(ev_d[e]))
            mixT = even_tail_dev(yS, hT, f(ev_conv_w[e]), f(ev_w_glu[e]), f(ev_b_glu[e]), f(ev_w_out[e]))
        else:
            o = i // 2
            hT = lin(xT, f(od_w_in[o]))
            ycT = attn_dev(hT[0:512], hT[512:1024], hT[1024:1536], f(od_rel_bias[o]))
            mixT = odd_tail_dev(ycT, hT[1536:2048], f(od_pool_w[o]), f(od_pool_scale[o]), f(od_w_out[o]))
        x1 = resln(x_tm, np.ascontiguousarray(mixT.T), f(ln_mix_g[i]), f(ln_mix_b[i]))
        x1T = np.ascontiguousarray(x1.T)
        ffnT = ffn_dev(x1T, f(ffn_w_up[i]), f(ffn_w_down[i]))
        x2 = resln(x1, np.ascontiguousarray(ffnT.T), f(ln_ffn_g[i]), f(ln_ffn_b[i]))
        x2T = np.ascontiguousarray(x2.T)
        pT = np.ascontiguousarray(f(p[i])[0].T)
        xT = ple_dev(x2T, pT, f(ple_w_gate[i]), f(ple_b_gate[i]), f(ple_w_proj[i]))
        x_tm = np.ascontiguousarray(xT.T)
    return x_tm[None].astype(np.float32)
```

```python
import math
import numpy as np
import concourse.bass as bass
import concourse.mybir as mybir
from concourse.bass_utils import run_bass_kernel_spmd

F32 = mybir.dt.float32
BF16 = mybir.dt.bfloat16
AF = mybir.ActivationFunctionType
ALU = mybir.AluOpType
AX = mybir.AxisListType
NCORE = 8
MAGIC = 12582912.0
TWO_PI = 2.0 * math.pi

D_MODEL = 1024
SEQ = 16384
DEPTH = 4
D_FF = 2816
ALPHA = (2 * DEPTH) ** 0.25
LN_EPS = 1e-5


class Prog:
    ENGS = ("sync", "gpsimd", "act", "dve", "pe")
    NDS = 8

    def __init__(self):
        self.nc = bass.Bass("TRN2", target_bir_lowering=False)
        self.ops = {e: [] for e in self.ENGS}
        self.cnt = {}
        self.lastw = {}
        self.reads = {}
        self.seen = {e: {} for e in self.ENGS}
        self.ndma = {e: 0 for e in self.ENGS}
        self.ctx = []
        self.out_waits = []

    def enter(self, cm):
        v = cm.__enter__()
        self.ctx.append(cm)
        return v

    def sb(self, name, shape, dt):
        return self.enter(self.nc.sbuf_tensor(name, list(shape), dt))

    def ps(self, name, shape, dt=F32):
        return self.enter(self.nc.psum_tensor(name, list(shape), dt))

    def dram_in(self, name, shape, dt=F32):
        return self.nc.dram_tensor(name, list(shape), dt, kind="ExternalInput").ap()

    def dram_out(self, name, shape, dt=F32):
        return self.nc.dram_tensor(name, list(shape), dt, kind="ExternalOutput").ap()

    def _op(self, eng, fn, r, w, dma=False, is_out=False):
        waits = {}

        def need(sv):
            s, v = sv
            waits[s] = max(waits.get(s, 0), v)

        for k in r:
            if k in self.lastw:
                need(self.lastw[k])
        for k in w:
            if k in self.lastw:
                need(self.lastw[k])
            for sv in self.reads.get(k, ()):
                need(sv)
        if dma:
            i = self.ndma[eng]
            self.ndma[eng] += 1
            sem = "%s_d%d" % (eng, i % self.NDS)
            inc = 16
        else:
            sem = eng
            inc = 1
        prev = self.cnt.get(sem, 0)
        self.cnt[sem] = prev + inc
        me = (sem, self.cnt[sem])
        wl = []
        if dma and prev > 0:
            waits[sem] = max(waits.get(sem, 0), prev)
        for s, v in waits.items():
            if eng == "pe" and s == "pe":
                continue
            if self.seen[eng].get(s, 0) >= v:
                continue
            self.seen[eng][s] = v
            wl.append((s, v))
        self.ops[eng].append((wl, fn, sem, inc))
        for k in w:
            self.lastw[k] = me
            self.reads[k] = []
        for k in r:
            self.reads.setdefault(k, []).append(me)
        if is_out:
            self.out_waits.append(me)

    def dma(self, out, in_, r=(), w=(), cast=False, is_out=False, q=None):
        eng = "gpsimd" if cast else (q or "sync")
        self._op(eng, lambda e: e.dma_start(out=out, in_=in_), r, w, dma=True, is_out=is_out)

    def act(self, fn, r=(), w=()):
        self._op("act", fn, r, w)

    def dve(self, fn, r=(), w=()):
        self._op("dve", fn, r, w)

    def pe(self, fn, r=(), w=()):
        self._op("pe", fn, r, w)

    def finish(self):
        nc = self.nc
        fin = {}
        for s, v in self.out_waits:
            fin[s] = max(fin.get(s, 0), v)
        names = sorted(self.cnt.keys())
        sems = {}
        for n in names:
            sems[n] = self.enter(nc.semaphore(n))
        block = self.enter(nc.Block())
        engmap = {"sync": block.sync, "gpsimd": block.gpsimd, "act": block.scalar,
                  "dve": block.vector, "pe": block.tensor}

        def make(ename):
            def body(e):
                for wl, fn, sem, inc in self.ops[ename]:
                    for s, v in wl:
                        e.wait_ge(sems[s], v)
                    fn(e).then_inc(sems[sem], inc)
                if ename == "sync":
                    for s, v in fin.items():
                        e.wait_ge(sems[s], v)
            return body

        for ename in self.ENGS:
            if self.ops[ename] or ename == "sync":
                engmap[ename](make(ename))
        for cm in reversed(self.ctx):
            cm.__exit__(None, None, None)
        self.ctx = []
        return nc


def run(nc, in_maps):
    res = run_bass_kernel_spmd(nc, in_maps, core_ids=list(range(NCORE)))
    return res.results


_CACHE = {}


def build_lin(K, M, N, mode="plain", func=None, has_bias=False, has_scale=False):
    P = Prog()
    nc = P.nc
    KT = K // 128
    MO = M // 2 if mode == "swiglu" else M
    MT = MO // 128
    NT = N // 512
    inT = P.dram_in("inT", [K, N])
    W = P.dram_in("W", [K, M])
    outT = P.dram_out("outT", [MO, N])
    bias = P.dram_in("bias", [128, MT]) if has_bias else None
    scale = P.dram_in("scale", [128, MT]) if has_scale else None
    A = P.dram_in("A", [MO, N]) if mode == "fma" else None
    B = P.dram_in("B", [MO, N]) if mode == "fma" else None
    inb = P.sb("inb", [128, KT, N], BF16)
    NW = 3
    wb = [P.sb("wb%d" % i, [128, KT, 128], BF16) for i in range(NW * (2 if mode == "swiglu" else 1))]
    ot = [P.sb("ot%d" % i, [128, N], F32) for i in range(2)]
    pss = [P.ps("ps%d" % i, [128, 512]) for i in range(4)]
    if has_bias:
        bt = P.sb("bt", [128, MT], F32)
        P.dma(bt[:], bias, w=["bt"])
    if has_scale:
        st = P.sb("st", [128, MT], F32)
        P.dma(st[:], scale, w=["st"])
    if mode == "mulin":
        gf = P.sb("gf", [128, KT, N], F32)
        xs = [P.sb("xs%d" % i, [128, N], F32) for i in range(2)]
        tq = P.sb("tq", [128, N], F32)
        for kt in range(KT):
            x = xs[kt % 2]
            xk = "xs%d" % (kt % 2)
            P.dma(x[:], inT[kt * 128:(kt + 1) * 128, :], w=[xk])
            P.dve(lambda e, x=x: e.tensor_tensor(out=tq[:], in0=x[:], in1=x[:], op=ALU.mult), r=[xk], w=["tq"])
            P.dve(lambda e: e.tensor_scalar(out=tq[:], in0=tq[:], scalar1=0.044715, scalar2=1.0, op0=ALU.mult, op1=ALU.add), r=["tq"], w=["tq"])
            P.dve(lambda e, x=x: e.tensor_tensor(out=tq[:], in0=tq[:], in1=x[:], op=ALU.mult), r=["tq", xk], w=["tq"])
            P.act(lambda e: e.activation(out=tq[:], in_=tq[:], func=AF.Sigmoid, scale=1.5957691216057308), r=["tq"], w=["tq"])
            P.dve(lambda e, x=x, kt=kt: e.tensor_tensor(out=gf[:, kt, :], in0=tq[:], in1=x[:], op=ALU.mult), r=["tq", xk], w=["gf%d" % kt])
            P.act(lambda e, kt=kt: e.activation(out=inb[:, kt, :], in_=gf[:, kt, :], func=AF.Copy), r=["gf%d" % kt], w=["inb%d" % kt])
    else:
        for kt in range(KT):
            P.dma(inb[:, kt, :], inT[kt * 128:(kt + 1) * 128, :], w=["inb%d" % kt], cast=True)
    if mode == "swiglu":
        tmp = [P.sb("tmp%d" % i, [128, 512], F32) for i in range(2)]
    if mode == "fma":
        At = [P.sb("At%d" % i, [128, N], F32) for i in range(2)]
        Bt = [P.sb("Bt%d" % i, [128, N], F32) for i in range(2)]
    Wv = W.rearrange("(kt p) m -> p kt m", p=128)
    pi = 0
    for m in range(MT):
        s = m % NW
        o = ot[m % 2]
        ok = "ot%d" % (m % 2)
        P.dma(wb[s][:], Wv[:, :, m * 128:(m + 1) * 128], w=["wb%d" % s], cast=True)
        if mode == "swiglu":
            P.dma(wb[NW + s][:], Wv[:, :, MO + m * 128:MO + (m + 1) * 128], w=["wb%d" % (NW + s)], cast=True)
        if mode == "fma":
            P.dma(At[m % 2][:], A[m * 128:(m + 1) * 128, :], w=["At%d" % (m % 2)])
            P.dma(Bt[m % 2][:], B[m * 128:(m + 1) * 128, :], w=["Bt%d" % (m % 2)])
        for nt in range(NT):
            sl = slice(nt * 512, (nt + 1) * 512)
            ps = pss[pi % 4]
            pk = "ps%d" % (pi % 4)
            pi += 1
            for kt in range(KT):
                P.pe(lambda e, ps=ps, s=s, kt=kt, sl=sl: e.matmul(ps[:], lhsT=wb[s][:, kt, :], rhs=inb[:, kt, sl], start=(kt == 0), stop=(kt == KT - 1)),
                     r=["wb%d" % s, "inb%d" % kt], w=[pk])
            if mode == "swiglu":
                ps2 = pss[pi % 4]
                pk2 = "ps%d" % (pi % 4)
                pi += 1
                for kt in range(KT):
                    P.pe(lambda e, ps2=ps2, s=s, kt=kt, sl=sl: e.matmul(ps2[:], lhsT=wb[NW + s][:, kt, :], rhs=inb[:, kt, sl], start=(kt == 0), stop=(kt == KT - 1)),
                         r=["wb%d" % (NW + s), "inb%d" % kt], w=[pk2])
                t = tmp[nt % 2]
                tk = "tmp%d" % (nt % 2)
                P.act(lambda e, t=t, ps=ps: e.activation(out=t[:], in_=ps[:], func=AF.Silu), r=[pk], w=[tk])
                P.dve(lambda e, o=o, t=t, ps2=ps2, sl=sl: e.tensor_tensor(out=o[:, sl], in0=t[:], in1=ps2[:], op=ALU.mult), r=[tk, pk2], w=[ok])
            elif mode == "fma":
                P.dve(lambda e, o=o, ps=ps, sl=sl, m=m: e.tensor_tensor(out=o[:, sl], in0=Bt[m % 2][:, sl], in1=ps[:], op=ALU.mult), r=[pk, "Bt%d" % (m % 2)], w=[ok])
                P.dve(lambda e, o=o, sl=sl, m=m: e.tensor_tensor(out=o[:, sl], in0=o[:, sl], in1=At[m % 2][:, sl], op=ALU.add), r=[ok, "At%d" % (m % 2)], w=[ok])
            elif mode == "mulin":
                P.act(lambda e, o=o, ps=ps, sl=sl, m=m: e.activation(out=o[:, sl], in_=ps[:], func=AF.Sigmoid, bias=bt[:, m:m + 1]), r=[pk, "bt"], w=[ok])
                P.dve(lambda e, o=o, sl=sl, m=m: e.tensor_tensor(out=o[:, sl], in0=o[:, sl], in1=gf[:, m, sl], op=ALU.mult), r=[ok, "gf%d" % m], w=[ok])
            else:
                kw = {}
                rr = [pk]
                if has_bias:
                    kw["bias"] = bt[:, m:m + 1]
                    rr.append("bt")
                if has_scale:
                    kw["scale"] = st[:, m:m + 1]
                    rr.append("st")
                f = func if func is not None else AF.Identity
                P.act(lambda e, o=o, ps=ps, sl=sl, kw=kw, f=f: e.activation(out=o[:, sl], in_=ps[:], func=f, **kw), r=rr, w=[ok])
        P.dma(outT[m * 128:(m + 1) * 128, :], o[:], r=[ok], w=["out"], is_out=True)
    return P.finish()


def build_ple(N, with_next=False):
    P = Prog()
    KT, K2, MT, NT = 8, 2, 8, N // 512
    x2T = P.dram_in("x2T", [1024, N]); pT = P.dram_in("pT", [256, N])
    Wg = P.dram_in("Wg", [1024, 1024]); Wp = P.dram_in("Wp", [256, 1024]); bias = P.dram_in("bias", [128, MT])
    outT = P.dram_out("outT", [1024, N])
    if with_next:
        Win = P.dram_in("Win", [1024, 2048]); hT = P.dram_out("hT", [2048, N])
        x3b = P.sb("x3b", [128, 8, N], BF16)
    inb = P.sb("inb", [128, KT, N], BF16); pb = P.sb("pb", [128, K2, N], BF16)
    bt = P.sb("bt", [128, MT], F32)
    P.dma(bt[:], bias, w=["bt"])
    for kt in range(KT):
        P.dma(inb[:, kt, :], x2T[kt * 128:(kt + 1) * 128, :], w=["inb%d" % kt], cast=True)
    for kt in range(K2):
        P.dma(pb[:, kt, :], pT[kt * 128:(kt + 1) * 128, :], w=["pb"], cast=True)
    NW = 3
    wg = [P.sb("wg%d" % i, [128, KT, 128], BF16) for i in range(NW)]
    wp = [P.sb("wp%d" % i, [128, K2, 128], BF16) for i in range(NW)]
    At = [P.sb("At%d" % i, [128, N], F32) for i in range(2)]
    ot = [P.sb("ot%d" % i, [128, N], F32) for i in range(2)]
    tmp = [P.sb("tmp%d" % i, [128, 512], F32) for i in range(2)]
    pss = [P.ps("ps%d" % i, [128, 512]) for i in range(4)]
    Wgv = Wg.rearrange("(kt p) m -> p kt m", p=128); Wpv = Wp.rearrange("(kt p) m -> p kt m", p=128)
    pi = 0
    for m in range(MT):
        s_ = m % NW
        o = ot[m % 2]; ok = "ot%d" % (m % 2); A = At[m % 2]; ak = "At%d" % (m % 2)
        ms = slice(m * 128, (m + 1) * 128)
        P.dma(wg[s_][:], Wgv[:, :, ms], w=["wg%d" % s_], cast=True)
        P.dma(wp[s_][:], Wpv[:, :, ms], w=["wp%d" % s_], cast=True)
        P.dma(A[:], x2T[ms, :], w=[ak], q="act")
        for nt in range(NT):
            sl = slice(nt * 512, (nt + 1) * 512)
            ps = pss[pi % 4]; pk = "ps%d" % (pi % 4); pi += 1
            ps2 = pss[pi % 4]; pk2 = "ps%d" % (pi % 4); pi += 1
            for kt in range(KT):
                P.pe(lambda e, ps=ps, s_=s_, kt=kt, sl=sl: e.matmul(ps[:], lhsT=wg[s_][:, kt, :], rhs=inb[:, kt, sl], start=(kt == 0), stop=(kt == KT - 1)), r=["wg%d" % s_, "inb%d" % kt], w=[pk])
            for kt in range(K2):
                P.pe(lambda e, ps2=ps2, s_=s_, kt=kt, sl=sl: e.matmul(ps2[:], lhsT=wp[s_][:, kt, :], rhs=pb[:, kt, sl], start=(kt == 0), stop=(kt == K2 - 1)), r=["wp%d" % s_, "pb"], w=[pk2])
            t = tmp[nt % 2]; tk = "tmp%d" % (nt % 2)
            P.act(lambda e, t=t, ps=ps, m=m: e.activation(out=t[:], in_=ps[:], func=AF.Sigmoid, bias=bt[:, m:m + 1]), r=[pk, "bt"], w=[tk])
            P.dve(lambda e, o=o, t=t, ps2=ps2, sl=sl: e.tensor_tensor(out=o[:, sl], in0=t[:], in1=ps2[:], op=ALU.mult), r=[tk, pk2], w=[ok])
            P.dve(lambda e, o=o, A=A, sl=sl: e.tensor_tensor(out=o[:, sl], in0=o[:, sl], in1=A[:, sl], op=ALU.add), r=[ok, ak], w=[ok])
            if with_next:
                P.dve(lambda e, o=o, sl=sl, m=m: e.tensor_copy(out=x3b[:, m, sl], in_=o[:, sl]), r=[ok], w=["x3b%d" % m])
        P.dma(outT[ms, :], o[:], r=[ok], w=["out"], is_out=True)
    if with_next:
        Wiv = Win.rearrange("(kt p) m -> p kt m", p=128)
        wi = [P.sb("wi%d" % i, [128, 8, 128], BF16) for i in range(3)]
        ht = [P.sb("ht%d" % i, [128, N], F32) for i in range(2)]
        for m in range(16):
            s_ = m % 3
            o = ht[m % 2]; ok = "ht%d" % (m % 2)
            P.dma(wi[s_][:], Wiv[:, :, m * 128:(m + 1) * 128], w=["wi%d" % s_], cast=True)
            for nt in range(NT):
                sl = slice(nt * 512, (nt + 1) * 512)
                ps = pss[pi % 4]; pk = "ps%d" % (pi % 4); pi += 1
                for kt in range(8):
                    P.pe(lambda e, ps=ps, s_=s_, kt=kt, sl=sl: e.matmul(ps[:], lhsT=wi[s_][:, kt, :], rhs=x3b[:, kt, sl], start=(kt == 0), stop=(kt == 7)), r=["wi%d" % s_, "x3b%d" % kt], w=[pk])
                P.act(lambda e, o=o, ps=ps, sl=sl: e.activation(out=o[:, sl], in_=ps[:], func=AF.Identity), r=[pk], w=[ok])
            P.dma(hT[m * 128:(m + 1) * 128, :], o[:], r=[ok], w=["out"], is_out=True)
    return P.finish()


def ple_dev(x2T, pT, Wg, bg, Wp, Win=None):
    T = x2T.shape[1]
    N = T // NCORE
    key = ("ple", N, Win is not None)
    if key not in _CACHE:
        _CACHE[key] = build_ple(N, Win is not None)
    b = np.ascontiguousarray(bg.reshape(8, 128).T)
    maps = [{"x2T": np.ascontiguousarray(x2T[:, c * N:(c + 1) * N]), "pT": np.ascontiguousarray(pT[:, c * N:(c + 1) * N]),
             "Wg": np.ascontiguousarray(Wg), "Wp": np.ascontiguousarray(Wp), "bias": b} for c in range(NCORE)]
    if Win is not None:
        for d in maps:
            d["Win"] = np.ascontiguousarray(Win)
    res = run(_CACHE[key], maps)
    xT = np.concatenate([r["outT"] for r in res], axis=1)
    if Win is None:
        return xT, None
    return xT, np.concatenate([r["hT"] for r in res], axis=1)


def build_ffn(N):
    P = Prog()
    KT, JT, MT, NT = 8, D_FF // 128, 8, N // 512
    inT = P.dram_in("inT", [1024, N]); Wu = P.dram_in("Wu", [1024, 2 * D_FF]); Wd = P.dram_in("Wd", [D_FF, 1024])
    outT = P.dram_out("outT", [1024, N])
    inb = P.sb("inb", [128, KT, N], BF16)
    actb = P.sb("actb", [128, JT, N], BF16)
    for kt in range(KT):
        P.dma(inb[:, kt, :], inT[kt * 128:(kt + 1) * 128, :], w=["inb%d" % kt], cast=True)
    NW = 3
    wb = [P.sb("wb%d" % i, [128, KT, 128], BF16) for i in range(2 * NW)]
    wd = [P.sb("wd%d" % i, [128, JT, 128], BF16) for i in range(2)]
    ot = [P.sb("ot%d" % i, [128, N], F32) for i in range(2)]
    tmp = [P.sb("tmp%d" % i, [128, 512], F32) for i in range(2)]
    pss = [P.ps("ps%d" % i, [128, 512]) for i in range(6)]
    Wuv = Wu.rearrange("(kt p) m -> p kt m", p=128); Wdv = Wd.rearrange("(jt p) m -> p jt m", p=128)
    pi = 0
    for j in range(JT):
        s_ = j % NW
        P.dma(wb[s_][:], Wuv[:, :, j * 128:(j + 1) * 128], w=["wb%d" % s_], cast=True)
        P.dma(wb[NW + s_][:], Wuv[:, :, D_FF + j * 128:D_FF + (j + 1) * 128], w=["wb%d" % (NW + s_)], cast=True)
        for nt in range(NT):
            sl = slice(nt * 512, (nt + 1) * 512)
            ps = pss[pi % 6]; pk = "ps%d" % (pi % 6); pi += 1
            ps2 = pss[pi % 6]; pk2 = "ps%d" % (pi % 6); pi += 1
            for kt in range(KT):
                P.pe(lambda e, ps=ps, s_=s_, kt=kt, sl=sl: e.matmul(ps[:], lhsT=wb[s_][:, kt, :], rhs=inb[:, kt, sl], start=(kt == 0), stop=(kt == KT - 1)), r=["wb%d" % s_, "inb%d" % kt], w=[pk])
            for kt in range(KT):
                P.pe(lambda e, ps2=ps2, s_=s_, kt=kt, sl=sl: e.matmul(ps2[:], lhsT=wb[NW + s_][:, kt, :], rhs=inb[:, kt, sl], start=(kt == 0), stop=(kt == KT - 1)), r=["wb%d" % (NW + s_), "inb%d" % kt], w=[pk2])
            t = tmp[nt % 2]; tk = "tmp%d" % (nt % 2)
            P.act(lambda e, t=t, ps=ps: e.activation(out=t[:], in_=ps[:], func=AF.Silu), r=[pk], w=[tk])
            P.dve(lambda e, t=t, ps2=ps2, j=j, sl=sl: e.tensor_tensor(out=actb[:, j, sl], in0=t[:], in1=ps2[:], op=ALU.mult), r=[tk, pk2], w=["actb%d" % j])
    allact = ["actb%d" % j for j in range(JT)]
    for m in range(MT):
        s_ = m % 2
        o = ot[m % 2]; ok = "ot%d" % (m % 2)
        P.dma(wd[s_][:], Wdv[:, :, m * 128:(m + 1) * 128], w=["wd%d" % s_], cast=True)
        for nt in range(NT):
            sl = slice(nt * 512, (nt + 1) * 512)
            ps = pss[pi % 6]; pk = "ps%d" % (pi % 6); pi += 1
            for j in range(JT):
                P.pe(lambda e, ps=ps, s_=s_, j=j, sl=sl: e.matmul(ps[:], lhsT=wd[s_][:, j, :], rhs=actb[:, j, sl], start=(j == 0), stop=(j == JT - 1)), r=["wd%d" % s_] + allact, w=[pk])
            P.act(lambda e, o=o, ps=ps, sl=sl: e.activation(out=o[:, sl], in_=ps[:], func=AF.Identity), r=[pk], w=[ok])
        P.dma(outT[m * 128:(m + 1) * 128, :], o[:], r=[ok], w=["out"], is_out=True)
    return P.finish()


def ffn_dev(x1T, Wu, Wd):
    T = x1T.shape[1]
    N = T // NCORE
    key = ("ffn", N)
    if key not in _CACHE:
        _CACHE[key] = build_ffn(N)
    maps = [{"inT": np.ascontiguousarray(x1T[:, c * N:(c + 1) * N]), "Wu": np.ascontiguousarray(Wu), "Wd": np.ascontiguousarray(Wd)} for c in range(NCORE)]
    res = run(_CACHE[key], maps)
    return np.concatenate([r["outT"] for r in res], axis=1)


def lin(inT_full, W, mode="plain", func=None, bias=None, scale=None, A=None, B=None):
    K, T = inT_full.shape
    M = W.shape[1]
    N = T // NCORE
    key = ("lin", K, M, N, mode, str(func), bias is not None, scale is not None)
    if key not in _CACHE:
        _CACHE[key] = build_lin(K, M, N, mode, func, bias is not None, scale is not None)
    nc = _CACHE[key]
    MO = M // 2 if mode == "swiglu" else M
    maps = []
    for c in range(NCORE):
        sl = slice(c * N, (c + 1) * N)
        d = {"inT": np.ascontiguousarray(inT_full[:, sl]), "W": np.ascontiguousarray(W)}
        if bias is not None:
            d["bias"] = np.ascontiguousarray(bias.reshape(MO // 128, 128).T)
        if scale is not None:
            d["scale"] = np.ascontiguousarray(scale.reshape(MO // 128, 128).T)
        if A is not None:
            d["A"] = np.ascontiguousarray(A[:, sl])
            d["B"] = np.ascontiguousarray(B[:, sl])
        maps.append(d)
    res = run(nc, maps)
    return np.concatenate([r["outT"] for r in res], axis=1)


def build_s5(NB):
    P = Prog()
    G, GC = 4, 2
    ub = P.dram_in("ub", [G, 128, NB])
    yb = P.dram_out("yb", [G, 128, NB])
    prm = {n: P.dram_in(n, [128, GC]) for n in ("lre", "lim", "ldt")}
    bin_ = {n: P.dram_in(n, [128, GC, 16]) for n in ("bre", "bim", "cre", "cim")}
    dcol_d = P.dram_in("dcol", [128, G])
    cmask_d = P.dram_in("cmask", [128, 128])
    ident_d = P.dram_in("ident", [128, 128])
    NL = int(math.log2(NB))
    t = {}
    for n in ("lre", "lim", "ldt", "dt", "a", "ang", "nr", "den", "rre", "rim", "nrim", "t1", "t2", "m2", "ire", "iim", "niim"):
        t[n] = P.sb("t_" + n, [128, GC], F32)
    for n in ("ak", "arg", "tr", "mag", "cs", "sn"):
        t[n] = P.sb("t_" + n, [128, 8, GC], F32)
    for n in ("bre", "bim", "cre", "cim", "bbre", "bbim", "t16a", "t16b"):
        t[n] = P.sb("t_" + n, [128, GC, 16], F32)
    PRE = P.sb("PRE", [128, 9, GC], F32); PIM = P.sb("PIM", [128, 9, GC], F32); NPIM = P.sb("NPIM", [128, 9, GC], F32)
    PWRE = P.sb("PWRE", [128, NL, GC], F32); PWIM = P.sb("PWIM", [128, NL, GC], F32); NPWIM = P.sb("NPWIM", [128, NL, GC], F32)
    BTre = P.sb("BTre", [128, GC, 128], F32); BTim = P.sb("BTim", [128, GC, 128], F32)
    CTre = P.sb("CTre", [128, GC, 128], F32); CTim = P.sb("CTim", [128, GC, 128], F32)
    BPre = P.sb("BPre", [128, GC, 128], F32); BPimn = P.sb("BPimn", [128, GC, 128], F32)
    X1 = P.sb("X1", [128, GC, 128], F32); X2 = P.sb("X2", [128, GC, 128], F32)
    X3 = P.sb("X3", [128, GC, 128], F32); X4 = P.sb("X4", [128, GC, 128], F32)
    dcol = P.sb("dcolt", [128, G], F32); cmask = P.sb("cmaskt", [128, 128], F32); ident = P.sb("identt", [128, 128], F32)
    dtmp = P.sb("dtmp", [128, 128], F32)
    Bre = P.sb("Bre", [128, G, 128], BF16); Bim = P.sb("Bim", [128, G, 128], BF16)
    Cre = P.sb("Cre", [128, GC, 128], BF16); Cimn = P.sb("Cimn", [128, GC, 128], BF16)
    Dm = P.sb("Dm", [128, G, 128], BF16)
    ubb = P.sb("ubb", [128, G, NB], BF16)
    Hre = P.sb("Hre", [128, GC, NB], F32); Him = P.sb("Him", [128, GC, NB], F32)
    Hbre = P.sb("Hbre", [128, GC, NB], BF16); Hbim = P.sb("Hbim", [128, GC, NB], BF16)
    Tres = [P.sb("Tre%d" % g, [128, NB // 2], F32) for g in range(GC)]
    Tims = [P.sb("Tim%d" % g, [128, NB // 2], F32) for g in range(GC)]
    yo = [P.sb("yo%d" % i, [128, NB], F32) for i in range(2)]
    pss = [P.ps("ps%d" % i, [128, 512]) for i in range(4)]
    K = ["prep"]

    for n in ("lre", "lim", "ldt"):
        P.dma(t[n][:], prm[n], w=K)
    for n in ("bre", "bim", "cre", "cim"):
        P.dma(t[n][:], bin_[n], w=K)
    P.dma(dcol[:], dcol_d, w=K); P.dma(cmask[:], cmask_d, w=K); P.dma(ident[:], ident_d, w=K)
    for g in range(G):
        P.dma(ubb[:, g, :], ub[g], w=["ubb%d" % g], cast=True)
    P.dve(lambda e: e.memset(Bre[:], 0.0), w=["Bpad"])
    P.dve(lambda e: e.memset(Bim[:], 0.0), w=["Bpad"])

    def tt(o, a, b, op, r=K, w=K):
        P.dve(lambda e: e.tensor_tensor(out=o, in0=a, in1=b, op=op), r=r, w=w)

    def ts(o, a, s1, op0, s2=None, op1=None, r=K, w=K):
        if op1 is None:
            P.dve(lambda e: e.tensor_scalar(out=o, in0=a, scalar1=s1, scalar2=None, op0=op0), r=r, w=w)
        else:
            P.dve(lambda e: e.tensor_scalar(out=o, in0=a, scalar1=s1, scalar2=s2, op0=op0, op1=op1), r=r, w=w)

    def stt(o, a, s_, b, op0, op1, r=K, w=K):
        P.dve(lambda e: e.scalar_tensor_tensor(out=o, in0=a, scalar=s_, in1=b, op0=op0, op1=op1), r=r, w=w)

    def act(o, a, f, **kw):
        P.act(lambda e: e.activation(out=o, in_=a, func=f, **kw), r=K, w=K)

    def sin_of(o, arg):
        ts(t["tr"][:], arg, 1.0 / TWO_PI, ALU.mult, MAGIC, ALU.add)
        ts(t["tr"][:], t["tr"][:], MAGIC, ALU.subtract, -TWO_PI, ALU.mult)
        tt(t["tr"][:], t["tr"][:], arg, ALU.add)
        ts(t["tr"][:], t["tr"][:], math.pi, ALU.min, -math.pi, ALU.max)
        act(o, t["tr"][:], AF.Sin)

    act(t["dt"][:], t["ldt"][:], AF.Exp)
    tt(t["a"][:], t["lre"][:], t["dt"][:], ALU.mult)
    tt(t["ang"][:], t["lim"][:], t["dt"][:], ALU.mult)
    P.dve(lambda e: e.memset(PRE[:, 0, :], 1.0), r=K, w=K)
    P.dve(lambda e: e.memset(PIM[:, 0, :], 0.0), r=K, w=K)
    for k in range(1, 9):
        ts(t["ak"][:, k - 1, :], t["a"][:], float(k), ALU.mult)
        ts(t["arg"][:, k - 1, :], t["ang"][:], float(k), ALU.mult)
    act(t["mag"][:], t["ak"][:], AF.Exp)
    sin_of(t["sn"][:], t["arg"][:])
    ts(t["arg"][:], t["arg"][:], math.pi / 2, ALU.add)
    sin_of(t["cs"][:], t["arg"][:])
    tt(PRE[:, 1:9, :], t["mag"][:], t["cs"][:], ALU.mult)
    tt(PIM[:, 1:9, :], t["mag"][:], t["sn"][:], ALU.mult)
    ts(NPIM[:], PIM[:], -1.0, ALU.mult)
    ts(t["nr"][:], PRE[:, 1, :], -1.0, ALU.add)
    tt(t["t1"][:], t["lre"][:], t["lre"][:], ALU.mult)
    tt(t["t2"][:], t["lim"][:], t["lim"][:], ALU.mult)
    tt(t["den"][:], t["t1"][:], t["t2"][:], ALU.add)
    P.dve(lambda e: e.reciprocal(out=t["den"][:], in_=t["den"][:]), r=K, w=K)
    tt(t["t1"][:], t["nr"][:], t["lre"][:], ALU.mult)
    tt(t["t2"][:], PIM[:, 1, :], t["lim"][:], ALU.mult)
    tt(t["t1"][:], t["t1"][:], t["t2"][:], ALU.add)
    tt(t["rre"][:], t["t1"][:], t["den"][:], ALU.mult)
    tt(t["t1"][:], PIM[:, 1, :], t["lre"][:], ALU.mult)
    tt(t["t2"][:], t["nr"][:], t["lim"][:], ALU.mult)
    tt(t["t1"][:], t["t1"][:], t["t2"][:], ALU.subtract)
    tt(t["rim"][:], t["t1"][:], t["den"][:], ALU.mult)
    ts(t["nrim"][:], t["rim"][:], -1.0, ALU.mult)
    tt(t["t1"][:], PRE[:, 8, :], PRE[:, 8, :], ALU.mult)
    tt(t["t2"][:], PIM[:, 8, :], PIM[:, 8, :], ALU.mult)
    tt(t["m2"][:], t["t1"][:], t["t2"][:], ALU.add)
    P.dve(lambda e: e.reciprocal(out=t["m2"][:], in_=t["m2"][:]), r=K, w=K)
    tt(t["ire"][:], PRE[:, 8, :], t["m2"][:], ALU.mult)
    tt(t["niim"][:], PIM[:, 8, :], t["m2"][:], ALU.mult)
    ts(t["iim"][:], t["niim"][:], -1.0, ALU.mult)
    for gc in range(GC):
        gs = slice(gc, gc + 1)
        ts(t["t16a"][:, gc, :], t["bre"][:, gc, :], t["rre"][:, gs], ALU.mult)
        ts(t["t16b"][:, gc, :], t["bim"][:, gc, :], t["rre"][:, gs], ALU.mult)
    for gc in range(GC):
        gs = slice(gc, gc + 1)
        stt(t["bbre"][:, gc, :], t["bim"][:, gc, :], t["nrim"][:, gs], t["t16a"][:, gc, :], ALU.mult, ALU.add)
        stt(t["bbim"][:, gc, :], t["bre"][:, gc, :], t["rim"][:, gs], t["t16b"][:, gc, :], ALU.mult, ALU.add)
    KB = ["prepB"]
    for gc in range(GC):
        gs = slice(gc, gc + 1)
        for i in range(8):
            isl = slice(i * 16, (i + 1) * 16)
            ts(X1[:, gc, isl], t["bbre"][:, gc, :], PRE[:, 7 - i, gs], ALU.mult, r=K, w=KB)
            ts(X2[:, gc, isl], t["bbim"][:, gc, :], PRE[:, 7 - i, gs], ALU.mult, r=K, w=KB)
            ts(X3[:, gc, isl], t["cre"][:, gc, :], PRE[:, i + 1, gs], ALU.mult, r=K, w=KB)
            ts(X4[:, gc, isl], t["cim"][:, gc, :], PRE[:, i + 1, gs], ALU.mult, r=K, w=KB)
    KC = ["prepC"]
    for gc in range(GC):
        gs = slice(gc, gc + 1)
        for i in range(8):
            isl = slice(i * 16, (i + 1) * 16)
            stt(BTre[:, gc, isl], t["bbim"][:, gc, :], NPIM[:, 7 - i, gs], X1[:, gc, isl], ALU.mult, ALU.add, r=K + KB, w=KC)
            stt(BTim[:, gc, isl], t["bbre"][:, gc, :], PIM[:, 7 - i, gs], X2[:, gc, isl], ALU.mult, ALU.add, r=K + KB, w=KC)
            stt(CTre[:, gc, isl], t["cim"][:, gc, :], NPIM[:, i + 1, gs], X3[:, gc, isl], ALU.mult, ALU.add, r=K + KB, w=KC)
            stt(CTim[:, gc, isl], t["cre"][:, gc, :], PIM[:, i + 1, gs], X4[:, gc, isl], ALU.mult, ALU.add, r=K + KB, w=KC)
    KD = ["prepD"]
    for gc in range(GC):
        gs = slice(gc, gc + 1)
        ts(X1[:, gc, :], BTre[:, gc, :], t["ire"][:, gs], ALU.mult, r=K + KC, w=KD)
        ts(X2[:, gc, :], BTim[:, gc, :], t["ire"][:, gs], ALU.mult, r=K + KC, w=KD)
    KE = ["prepE"]
    for gc in range(GC):
        gs = slice(gc, gc + 1)
        stt(BPre[:, gc, :], BTim[:, gc, :], t["niim"][:, gs], X1[:, gc, :], ALU.mult, ALU.add, r=K + KC + KD, w=KE)
        stt(X3[:, gc, :], BTre[:, gc, :], t["iim"][:, gs], X2[:, gc, :], ALU.mult, ALU.add, r=K + KC + KD, w=KE)
    ts(BPimn[:], X3[:], -1.0, ALU.mult, r=KE, w=KE)
    P.act(lambda e: e.activation(out=Cre[:], in_=CTre[:], func=AF.Copy), r=KC, w=["Cw"])
    P.act(lambda e: e.activation(out=Cimn[:], in_=CTim[:], func=AF.Copy, scale=-1.0), r=KC, w=["Cw"])
    for g in range(G):
        hf, gc = g // 2, g % 2
        hs = slice(hf * 64, (hf + 1) * 64)
        for n_, (src, dst) in enumerate(((BTre, Bre), (BTim, Bim))):
            ps = pss[n_]; pk = "ps%d" % n_
            P.pe(lambda e, src=src, gc=gc, hs=hs, ps=ps: e.matmul(ps[:, 0:64], lhsT=src[hs, gc, :], rhs=ident[hs, hs], start=True, stop=True), r=KC + K, w=[pk])
            P.act(lambda e, dst=dst, g=g, hs=hs, ps=ps: e.activation(out=dst[:, g, hs], in_=ps[:, 0:64], func=AF.Copy), r=[pk, "Bpad"], w=["Bpad"])
        ps = pss[2]
        P.pe(lambda e, gc=gc, hs=hs, ps=ps: e.matmul(ps[:, 0:128], lhsT=BPre[hs, gc, :], rhs=CTre[hs, gc, :], start=True, stop=False), r=KE + KC, w=["ps2"])
        P.pe(lambda e, gc=gc, hs=hs, ps=ps: e.matmul(ps[:, 0:128], lhsT=BPimn[hs, gc, :], rhs=CTim[hs, gc, :], start=False, stop=True), r=KE + KC, w=["ps2"])
        P.dve(lambda e, ps=ps: e.tensor_tensor(out=dtmp[:], in0=ps[:, 0:128], in1=cmask[:], op=ALU.mult), r=K + ["ps2"], w=["dtmp"])
        stt(Dm[:, g, :], ident[:], dcol[:, g:g + 1], dtmp[:], ALU.mult, ALU.add, r=K + ["dtmp"], w=["Dm"])
    tt(PWRE[:, 0, :], PRE[:, 8, :], PRE[:, 8, :], ALU.max)
    tt(PWIM[:, 0, :], PIM[:, 8, :], PIM[:, 8, :], ALU.max)
    for k in range(1, NL):
        tt(t["t1"][:], PWRE[:, k - 1, :], PWRE[:, k - 1, :], ALU.mult)
        tt(t["t2"][:], PWIM[:, k - 1, :], PWIM[:, k - 1, :], ALU.mult)
        tt(PWRE[:, k, :], t["t1"][:], t["t2"][:], ALU.subtract)
        tt(t["t1"][:], PWRE[:, k - 1, :], PWIM[:, k - 1, :], ALU.mult)
        ts(PWIM[:, k, :], t["t1"][:], 2.0, ALU.mult)
    ts(NPWIM[:], PWIM[:], -1.0, ALU.mult)

    pi = 0
    NT = NB // 512
    for gc in range(GC):
        for nt in range(NT):
            sl = slice(nt * 512, (nt + 1) * 512)
            for wsrc, H in ((Bre, Hre), (Bim, Him)):
                ps = pss[pi % 4]; pk = "ps%d" % (pi % 4); pi += 1
                P.pe(lambda e, ps=ps, wsrc=wsrc, gc=gc, sl=sl: e.matmul(ps[:], lhsT=wsrc[:, gc, :], rhs=ubb[:, gc, sl], start=True, stop=False), r=["Bpad", "ubb%d" % gc], w=[pk])
                P.pe(lambda e, ps=ps, wsrc=wsrc, gc=gc, sl=sl: e.matmul(ps[:], lhsT=wsrc[:, 2 + gc, :], rhs=ubb[:, 2 + gc, sl], start=False, stop=True), r=["Bpad", "ubb%d" % (2 + gc)], w=[pk])
                P.act(lambda e, ps=ps, H=H, gc=gc, sl=sl: e.activation(out=H[:, gc, sl], in_=ps[:], func=AF.Copy), r=[pk], w=["H%d" % gc])
    for k in range(NL):
        s = 1 << k
        for g in range(GC):
            hk = ["H%d" % g]
            hr = Hre[:, g, :]; hi = Him[:, g, :]
            vr = hr.rearrange("p (m t) -> p m t", t=2 * s); vi = hi.rearrange("p (m t) -> p m t", t=2 * s)
            tr_, sr_ = vr[:, :, 2 * s - 1], vr[:, :, s - 1]
            ti_, si_ = vi[:, :, 2 * s - 1], vi[:, :, s - 1]
            a_r, a_i, na_i = PWRE[:, k, g:g + 1], PWIM[:, k, g:g + 1], NPWIM[:, k, g:g + 1]
            stt(tr_, sr_, a_r, tr_, ALU.mult, ALU.add, r=K + hk, w=hk)
            stt(tr_, si_, na_i, tr_, ALU.mult, ALU.add, r=K + hk, w=hk)
            stt(ti_, si_, a_r, ti_, ALU.mult, ALU.add, r=K + hk, w=hk)
            stt(ti_, sr_, a_i, ti_, ALU.mult, ALU.add, r=K + hk, w=hk)
    for g in range(GC):
        hk = ["H%d" % g]
        P.dve(lambda e, g=g: e.memset(Hre[:, g, NB - 1:NB], 0.0), r=hk, w=hk)
        P.dve(lambda e, g=g: e.memset(Him[:, g, NB - 1:NB], 0.0), r=hk, w=hk)
    for k in range(NL - 1, -1, -1):
        s = 1 << k
        m = NB // (2 * s)
        for g in range(GC):
            hk = ["H%d" % g]
            tk = ["T%d" % g]
            TR, TI = Tres[g], Tims[g]
            hr = Hre[:, g, :]; hi = Him[:, g, :]
            vr = hr.rearrange("p (m t) -> p m t", t=2 * s); vi = hi.rearrange("p (m t) -> p m t", t=2 * s)
            Rr, Lr = vr[:, :, 2 * s - 1], vr[:, :, s - 1]
            Ri, Li = vi[:, :, 2 * s - 1], vi[:, :, s - 1]
            a_r, a_i, na_i = PWRE[:, k, g:g + 1], PWIM[:, k, g:g + 1], NPWIM[:, k, g:g + 1]
            kk = K + hk + tk
            stt(TR[:, 0:m], Rr, a_r, Lr, ALU.mult, ALU.add, r=kk, w=tk)
            stt(TR[:, 0:m], Ri, na_i, TR[:, 0:m], ALU.mult, ALU.add, r=kk, w=tk)
            stt(TI[:, 0:m], Ri, a_r, Li, ALU.mult, ALU.add, r=kk, w=tk)
            stt(TI[:, 0:m], Rr, a_i, TI[:, 0:m], ALU.mult, ALU.add, r=kk, w=tk)
            P.act(lambda e, Lr=Lr, Rr=Rr: e.activation(out=Lr, in_=Rr, func=AF.Copy), r=kk, w=hk)
            P.act(lambda e, Li=Li, Ri=Ri: e.activation(out=Li, in_=Ri, func=AF.Copy), r=kk, w=hk)
            P.dve(lambda e, Rr=Rr, m=m, TR=TR: e.tensor_copy(out=Rr, in_=TR[:, 0:m]), r=kk, w=hk)
            P.dve(lambda e, Ri=Ri, m=m, TI=TI: e.tensor_copy(out=Ri, in_=TI[:, 0:m]), r=kk, w=hk)
    for g in range(GC):
        hk = ["H%d" % g]
        P.act(lambda e, g=g: e.activation(out=Hbre[:, g, :], in_=Hre[:, g, :], func=AF.Copy), r=hk, w=["Hb%d" % g])
        P.act(lambda e, g=g: e.activation(out=Hbim[:, g, :], in_=Him[:, g, :], func=AF.Copy), r=hk, w=["Hb%d" % g])
    for g in range(G):
        hf, gc = g // 2, g % 2
        hs = slice(hf * 64, (hf + 1) * 64)
        o = yo[g % 2]; ok = "yo%d" % (g % 2)
        for nt in range(NT):
            sl = slice(nt * 512, (nt + 1) * 512)
            ps = pss[pi % 4]; pk = "ps%d" % (pi % 4); pi += 1
            rr = ["Cw", "Dm", "ubb%d" % g, "Hb%d" % gc]
            P.pe(lambda e, ps=ps, gc=gc, hs=hs, sl=sl: e.matmul(ps[:], lhsT=Cre[hs, gc, :], rhs=Hbre[hs, gc, sl], start=True, stop=False), r=rr, w=[pk])
            P.pe(lambda e, ps=ps, gc=gc, hs=hs, sl=sl: e.matmul(ps[:], lhsT=Cimn[hs, gc, :], rhs=Hbim[hs, gc, sl], start=False, stop=False), r=rr, w=[pk])
            P.pe(lambda e, ps=ps, g=g, sl=sl: e.matmul(ps[:], lhsT=Dm[:, g, :], rhs=ubb[:, g, sl], start=False, stop=True), r=rr, w=[pk])
            P.act(lambda e, ps=ps, o=o, sl=sl: e.activation(out=o[:, sl], in_=ps[:], func=AF.Copy), r=[pk], w=[ok])
        P.dma(yb[g], o[:], r=[ok], w=["out"], is_out=True)
    return P.finish()


def s5_mixer_dev(uT, lam_re, lam_im, log_dt, b_re, b_im, c_re, c_im, d_skip):
    T = uT.shape[1]
    NB = T // 8
    key = ("s5", NB)
    if key not in _CACHE:
        _CACHE[key] = build_s5(NB)
    ub = uT.reshape(32, 16, NB, 8).transpose(0, 3, 1, 2).reshape(32, 128, NB)
    ii, jj = np.arange(128) // 16, np.arange(128) // 16
    cmask = (jj[None, :] >= ii[:, None]).astype(np.float32)
    ident = np.eye(128, dtype=np.float32)
    maps = []

    def pl(a):
        sh = a.shape[2:]
        a = a.reshape((2, 2, 64) + sh)
        a = np.moveaxis(a, 1, 2)
        return np.ascontiguousarray(a.reshape((128, 2) + sh))

    for c in range(NCORE):
        gs = slice(4 * c, 4 * c + 4)
        maps.append({
            "ub": np.ascontiguousarray(ub[gs]),
            "lre": pl(lam_re[gs]), "lim": pl(lam_im[gs]),
            "ldt": pl(np.ascontiguousarray(np.broadcast_to(log_dt[gs][:, None], (4, 64)))),
            "bre": pl(b_re[gs]), "bim": pl(b_im[gs]),
            "cre": pl(np.ascontiguousarray(c_re[gs].transpose(0, 2, 1))), "cim": pl(np.ascontiguousarray(c_im[gs].transpose(0, 2, 1))),
            "dcol": np.ascontiguousarray(np.tile(d_skip.reshape(32, 16)[gs].T, (8, 1))),
            "cmask": cmask, "ident": ident,
        })
    res = run(_CACHE[key], maps)
    yb = np.concatenate([r["yb"] for r in res], axis=0)
    return np.ascontiguousarray(yb.reshape(32, 8, 16, NB).transpose(0, 2, 3, 1).reshape(512, T))


def build_resln(N, D):
    P = Prog()
    x = P.dram_in("x", [N, D]); m = P.dram_in("m", [N, D]); gb = P.dram_in("gb", [128, 2, D])
    y = P.dram_out("y", [N, D])
    gbt = P.sb("gbt", [128, 2, D], F32)
    P.dma(gbt[:], gb, w=["gb"], q="gpsimd")
    NB_ = 6
    xt = [P.sb("xt%d" % i, [128, D], F32) for i in range(NB_)]
    mt = [P.sb("mt%d" % i, [128, D], F32) for i in range(NB_)]
    yt = [P.sb("yt%d" % i, [128, D], F32) for i in range(NB_)]
    jk = [P.sb("jk%d" % i, [128, D], F32) for i in range(2)]
    st = [P.sb("st%d" % i, [128, 8], F32) for i in range(NB_)]
    NTL = N // 128
    PF = 4

    def names(i):
        b = i % NB_
        return xt[b], mt[b], yt[b], st[b], "xt%d" % b, "mt%d" % b, "yt%d" % b, "st%d" % b

    def loads(i):
        X, M, Y, S, xk, mk, yk, sk = names(i)
        rs = slice(i * 128, (i + 1) * 128)
        P.dma(X[:], x[rs, :], w=[xk], q="sync")
        P.dma(M[:], m[rs, :], w=[mk], q="act")

    def stage_a(i):
        X, M, Y, S, xk, mk, yk, sk = names(i)
        P.dve(lambda e: e.memset(S[:], 0.0), w=[sk])
        P.dve(lambda e: e.scalar_tensor_tensor(out=X[:], in0=X[:], scalar=float(ALPHA), in1=M[:], op0=ALU.mult, op1=ALU.add), r=[xk, mk], w=[xk])
        P.act(lambda e: e.activation(out=jk[0][:], in_=X[:], func=AF.Copy, accum_out=S[:, 0:1]), r=[xk, sk], w=["jk0", sk])
        P.act(lambda e: e.activation(out=jk[1][:], in_=X[:], func=AF.Square, accum_out=S[:, 1:2]), r=[xk, sk], w=["jk1", sk])

    def stage_b(i):
        X, M, Y, S, xk, mk, yk, sk = names(i)
        P.dve(lambda e: e.tensor_scalar(out=S[:, 2:4], in0=S[:, 0:2], scalar1=1.0 / D, scalar2=None, op0=ALU.mult), r=[sk], w=[sk])
        P.dve(lambda e: e.tensor_tensor(out=S[:, 4:5], in0=S[:, 2:3], in1=S[:, 2:3], op=ALU.mult), r=[sk], w=[sk])
        P.dve(lambda e: e.tensor_tensor(out=S[:, 4:5], in0=S[:, 3:4], in1=S[:, 4:5], op=ALU.subtract), r=[sk], w=[sk])
        P.dve(lambda e: e.tensor_scalar(out=S[:, 4:5], in0=S[:, 4:5], scalar1=LN_EPS, scalar2=None, op0=ALU.add), r=[sk], w=[sk])
        P.act(lambda e: e.activation(out=S[:, 4:5], in_=S[:, 4:5], func=AF.Sqrt), r=[sk], w=[sk])

    def stage_c(i):
        X, M, Y, S, xk, mk, yk, sk = names(i)
        P.dve(lambda e: e.reciprocal(out=S[:, 5:6], in_=S[:, 4:5]), r=[sk], w=[sk])
        P.dve(lambda e: e.scalar_tensor_tensor(out=S[:, 6:7], in0=S[:, 2:3], scalar=-1.0, in1=S[:, 5:6], op0=ALU.mult, op1=ALU.mult), r=[sk], w=[sk])
        P.act(lambda e: e.activation(out=Y[:], in_=X[:], func=AF.Identity, scale=S[:, 5:6], bias=S[:, 6:7]), r=[xk, sk], w=[yk])

    def stage_d(i):
        X, M, Y, S, xk, mk, yk, sk = names(i)
        rs = slice(i * 128, (i + 1) * 128)
        P.dve(lambda e: e.tensor_tensor(out=Y[:], in0=Y[:], in1=gbt[:, 0, :], op=ALU.mult), r=[yk, "gb"], w=[yk])
        P.dve(lambda e: e.tensor_tensor(out=Y[:], in0=Y[:], in1=gbt[:, 1, :], op=ALU.add), r=[yk, "gb"], w=[yk])
        P.dma(y[rs, :], Y[:], r=[yk], w=["out"], is_out=True, q="gpsimd")

    for i in range(min(PF, NTL)):
        loads(i)
    for s_ in range(NTL + 3):
        if 0 <= s_ - 3 < NTL:
            stage_d(s_ - 3)
        if 0 <= s_ - 2 < NTL:
            stage_c(s_ - 2)
        if 0 <= s_ - 1 < NTL:
            stage_b(s_ - 1)
        if s_ < NTL:
            stage_a(s_)
            if s_ + PF < NTL:
                loads(s_ + PF)
    return P.finish()


def resln(x_tm, m_tm, g, b):
    T, D = x_tm.shape
    N = T // NCORE
    key = ("resln", N, D)
    if key not in _CACHE:
        _CACHE[key] = build_resln(N, D)
    gb = np.ascontiguousarray(np.broadcast_to(np.stack([g, b])[None], (128, 2, D))).astype(np.float32)
    maps = [{"x": np.ascontiguousarray(x_tm[c * N:(c + 1) * N]), "m": np.ascontiguousarray(m_tm[c * N:(c + 1) * N]), "gb": gb} for c in range(NCORE)]
    res = run(_CACHE[key], maps)
    return np.concatenate([r["y"] for r in res], axis=0)


def build_conv(N):
    P = Prog()
    bT = P.dram_in("bT", [512, N]); cT = P.dram_in("cT", [512, N + 2]); xT = P.dram_in("xT", [512, N + 2])
    w = P.dram_in("w", [128, 4, 3])
    yT = P.dram_out("yT", [512, N])
    wt = P.sb("wt", [128, 4, 3], F32)
    P.dma(wt[:], w, w=["w"])
    for a in range(4):
        bt = P.sb("bt%d" % a, [128, N], F32); ct = P.sb("ct%d" % a, [128, N + 2], F32); xt = P.sb("xt%d" % a, [128, N + 2], F32)
        acc = P.sb("acc%d" % a, [128, N], F32)
        rs = slice(a * 128, (a + 1) * 128)
        k = "c%d" % a
        P.dma(bt[:], bT[rs, :], w=[k + "b"]); P.dma(ct[:], cT[rs, :], w=[k]); P.dma(xt[:], xT[rs, :], w=[k + "x"])
        P.dve(lambda e, ct=ct, xt=xt: e.tensor_tensor(out=ct[:], in0=ct[:], in1=xt[:], op=ALU.mult), r=[k, k + "x"], w=[k])
        P.dve(lambda e, ct=ct, acc=acc, a=a: e.tensor_scalar(out=acc[:], in0=ct[:, 0:N], scalar1=wt[:, a, 0:1], scalar2=None, op0=ALU.mult), r=[k, "w"], w=[k + "a"])
        for j in (1, 2):
            P.dve(lambda e, ct=ct, acc=acc, a=a, j=j: e.scalar_tensor_tensor(out=acc[:], in0=ct[:, j:j + N], scalar=wt[:, a, j:j + 1], in1=acc[:], op0=ALU.mult, op1=ALU.add), r=[k, "w", k + "a"], w=[k + "a"])
        P.dve(lambda e, acc=acc, bt=bt: e.tensor_tensor(out=acc[:], in0=acc[:], in1=bt[:], op=ALU.mult), r=[k + "a", k + "b"], w=[k + "a"])
        P.dma(yT[rs, :], acc[:], r=[k + "a"], w=["out"], is_out=True)
    return P.finish()


def conv_dev(bT, cT, xT, cw):
    T = bT.shape[1]
    N = T // NCORE
    key = ("conv", N)
    if key not in _CACHE:
        _CACHE[key] = build_conv(N)
    cp = np.concatenate([np.zeros((512, 2), np.float32), cT], axis=1)
    xp = np.concatenate([np.zeros((512, 2), np.float32), xT], axis=1)
    w = np.ascontiguousarray(cw.reshape(3, 4, 128).transpose(2, 1, 0))
    maps = [{"bT": np.ascontiguousarray(bT[:, c * N:(c + 1) * N]), "cT": np.ascontiguousarray(cp[:, c * N:(c + 1) * N + 2]),
             "xT": np.ascontiguousarray(xp[:, c * N:(c + 1) * N + 2]), "w": w} for c in range(NCORE)]
    res = run(_CACHE[key], maps)
    return np.concatenate([r["yT"] for r in res], axis=1)


def build_pool(N):
    P = Prog()
    zT = P.dram_in("zT", [512, N + 16]); invc = P.dram_in("invc", [128, 4, N])
    oT = P.dram_out("oT", [512, N])
    for gi in range(4):
        z = P.sb("z%d" % gi, [128, N + 16], F32)
        sa = P.sb("sa%d" % gi, [128, N + 16], F32); sb_ = P.sb("sb%d" % gi, [128, N + 16], F32)
        ic = P.sb("ic%d" % gi, [128, N], F32)
        rs = slice(gi * 128, (gi + 1) * 128)
        k = "p%d" % gi
        P.dma(z[:], zT[rs, :], w=[k + "z"]); P.dma(ic[:], invc[:, gi, :], w=[k + "i"])
        cur, curk = z, k + "z"
        bufs = [(sa, k + "a"), (sb_, k + "b")]
        for step in range(gi + 1):
            sh = 1 << step
            nxt, nk = bufs[step % 2]
            P.dve(lambda e, cur=cur, nxt=nxt, sh=sh: e.tensor_tensor(out=nxt[:, sh:], in0=cur[:, sh:], in1=cur[:, 0:N + 16 - sh], op=ALU.add), r=[curk], w=[nk])
            cur, curk = nxt, nk
        o, okey = bufs[(gi + 1) % 2]
        P.dve(lambda e, cur=cur, o=o, ic=ic: e.tensor_tensor(out=o[:, 16:], in0=cur[:, 16:], in1=ic[:], op=ALU.mult), r=[curk, k + "i"], w=[okey])
        P.dve(lambda e, o=o, z=z: e.tensor_tensor(out=o[:, 16:], in0=o[:, 16:], in1=z[:, 16:], op=ALU.subtract), r=[okey, k + "z"], w=[okey])
        P.dma(oT[rs, :], o[:, 16:], r=[okey], w=["out"], is_out=True)
    return P.finish()


def pool_dev(zT):
    T = zT.shape[1]
    N = T // NCORE
    key = ("pool", N)
    if key not in _CACHE:
        _CACHE[key] = build_pool(N)
    zp = np.concatenate([np.zeros((512, 16), np.float32), zT], axis=1)
    t = np.arange(T)
    inv = np.stack([1.0 / np.minimum(t + 1, w) for w in (2, 4, 8, 16)]).astype(np.float32)
    maps = []
    for c in range(NCORE):
        ic = np.ascontiguousarray(np.broadcast_to(inv[None, :, c * N:(c + 1) * N], (128, 4, N)))
        maps.append({"zT": np.ascontiguousarray(zp[:, c * N:(c + 1) * N + 16]), "invc": ic})
    res = run(_CACHE[key], maps)
    return np.concatenate([r["oT"] for r in res], axis=1)


def build_attn(T):
    P = Prog()
    NBK = T // 128
    qT = P.dram_in("qT", [64, T]); kT = P.dram_in("kT", [64, T]); v = P.dram_in("v", [T, 64])
    bias = P.dram_in("bias", [128, 5, 128])
    o_tm = P.dram_out("o", [T, 64])
    qb = P.sb("qb", [64, T], BF16); kb = P.sb("kb", [64, T], BF16); vb = P.sb("vb", [128, NBK, 65], BF16)
    bf = P.sb("bf", [128, 5, 128], F32); eb = P.sb("eb", [128, 5, 128], F32)
    P.dve(lambda e: e.memset(vb[:], 1.0), w=["vb"])
    P.dma(bf[:], bias, w=["bf"])
    P.dma(qb[:], qT, w=["qb"], cast=True); P.dma(kb[:], kT, w=["kb"], cast=True)
    vv = v.rearrange("(n p) d -> p n d", p=128)
    for j0 in range(0, NBK, 16):
        j1 = min(NBK, j0 + 16)
        P.dma(vb[:, j0:j1, 0:64], vv[:, j0:j1, :], w=["vb"], cast=True)
    P.act(lambda e: e.activation(out=eb[:], in_=bf[:], func=AF.Exp), r=["bf"], w=["eb"])
    P.act(lambda e: e.activation(out=qb[:], in_=qb[:], func=AF.Copy, scale=0.125), r=["qb"], w=["qb"])
    NBUF = 3
    psS = [P.ps("psS%d" % i, [128, 5, 128]) for i in range(2)]
    psO = [P.ps("psO%d" % i, [128, 512]) for i in range(2)]
    pf = [P.sb("pf%d" % i, [128, 5, 128], F32) for i in range(NBUF)]
    pt = [P.sb("pt%d" % i, [128, 5, 128], BF16) for i in range(NBUF)]
    rec = [P.sb("rec%d" % i, [128, 1], F32) for i in range(NBUF)]
    ob = [P.sb("ob%d" % i, [128, 16, 64], F32) for i in range(2)]
    o_v = o_tm.rearrange("(n p) d -> p n d", p=128)
    def names(m):
        b = m % 2
        return (psS[b], psO[b], pf[m % NBUF], pt[m % NBUF], rec[m % NBUF],
                "S%d" % b, "O%d" % b, "pf%d" % (m % NBUF), "pt%d" % (m % NBUF), "rc%d" % (m % NBUF))

    def front(m):
        S, O, PF, PT, RC, sk, okk, fk, pk, rk = names(m)
        qs = slice(m * 128, (m + 1) * 128)
        i0_ = max(0, 4 - m)
        for i in range(i0_, 5):
            kt = m - 4 + i
            P.pe(lambda e, i=i, kt=kt: e.matmul(S[:, i, :], lhsT=kb[:, kt * 128:(kt + 1) * 128], rhs=qb[:, qs], start=True, stop=True), r=["kb", "qb"], w=[sk])
        if i0_ < 4:
            P.act(lambda e: e.activation(out=PF[:, i0_:4, :], in_=S[:, i0_:4, :], func=AF.Exp), r=[sk], w=[fk])
        P.act(lambda e: e.activation(out=PF[:, 4, :], in_=S[:, 4, :], func=AF.Exp), r=[sk], w=[fk])
        P.dve(lambda e: e.tensor_tensor(out=PT[:, i0_:5, :], in0=PF[:, i0_:5, :], in1=eb[:, i0_:5, :], op=ALU.mult), r=[fk, "eb"], w=[pk])

    def back(m):
        S, O, PF, PT, RC, sk, okk, fk, pk, rk = names(m)
        OB = ob[(m // 16) % 2]; obk = "ob%d" % ((m // 16) % 2)
        val = list(range(max(0, 4 - m), 5))
        for n, i in enumerate(val):
            kt = m - 4 + i
            P.pe(lambda e, i=i, kt=kt, n=n: e.matmul(O[:, 0:65], lhsT=PT[:, i, :], rhs=vb[:, kt, :], start=(n == 0), stop=(n == len(val) - 1)), r=["vb", pk], w=[okk])
        P.dve(lambda e: e.reciprocal(out=RC[:], in_=O[:, 64:65]), r=[okk], w=[rk])
        c = m % 16
        P.dve(lambda e: e.tensor_scalar(out=OB[:, c, :], in0=O[:, 0:64], scalar1=RC[:, 0:1], scalar2=None, op0=ALU.mult), r=[okk, rk], w=[obk])
        if m % 16 == 15:
            g0 = (m // 16) * 16
            P.dma(o_v[:, g0:g0 + 16, :], OB[:], r=[obk], w=["out"], is_out=True)

    for s_ in range(NBK + 1):
        if s_ < NBK:
            front(s_)
        if s_ >= 1:
            back(s_ - 1)
    return P.finish()


def attn_dev(qT, kT, vT, rel_bias):
    T = qT.shape[1]
    key = ("attn", T)
    if key not in _CACHE:
        _CACHE[key] = build_attn(T)
    kk = np.arange(640)[:, None]; qq = np.arange(128)[None, :]
    qc = qq // 64; kc = kk // 64
    dist = (qq - (kk - 512))
    rel = np.clip(dist, -128, 128) + 128
    band = kc - qc
    valid = (band >= 0) & (band <= 8)
    maps = []
    for h in range(NCORE):
        b2 = np.where(valid, rel_bias[h][rel], np.float32(-30000.0)).astype(np.float32)
        b2 = np.ascontiguousarray(b2.reshape(5, 128, 128).transpose(1, 0, 2))
        hs = slice(h * 64, (h + 1) * 64)
        maps.append({"qT": np.ascontiguousarray(qT[hs]), "kT": np.ascontiguousarray(kT[hs]),
                     "v": np.ascontiguousarray(vT[hs].T), "bias": b2})
    res = run(_CACHE[key], maps)
    return np.ascontiguousarray(np.concatenate([r["o"].T for r in res], axis=0))


def _outproj(P, mixin, Wout, outT, N, pss, pi, mixkeys):
    NT = N // 512
    Wv = Wout.rearrange("(kt p) m -> p kt m", p=128)
    wo = [P.sb("wo%d" % i, [128, 8, 128], BF16) for i in range(3)]
    ot = [P.sb("oto%d" % i, [128, N], F32) for i in range(2)]
    for m in range(8):
        s_ = m % 3
        o = ot[m % 2]; ok = "oto%d" % (m % 2)
        P.dma(wo[s_][:], Wv[:, :, m * 128:(m + 1) * 128], w=["wo%d" % s_], cast=True)
        for nt in range(NT):
            sl = slice(nt * 512, (nt + 1) * 512)
            ps = pss[pi % len(pss)]; pk = "ps%d" % (pi % len(pss)); pi += 1
            for kt in range(8):
                P.pe(lambda e, ps=ps, s_=s_, kt=kt, sl=sl: e.matmul(ps[:], lhsT=wo[s_][:, kt, :], rhs=mixin[:, kt, sl], start=(kt == 0), stop=(kt == 7)), r=["wo%d" % s_, mixkeys[kt]], w=[pk])
            P.act(lambda e, o=o, ps=ps, sl=sl: e.activation(out=o[:, sl], in_=ps[:], func=AF.Identity), r=[pk], w=[ok])
        P.dma(outT[m * 128:(m + 1) * 128, :], o[:], r=[ok], w=["out"], is_out=True)
    return pi


def build_even_tail(N):
    P = Prog()
    NT = N // 512
    yS = P.dram_in("yS", [512, N]); bT = P.dram_in("bT", [512, N]); cT = P.dram_in("cT", [512, N + 2]); xT = P.dram_in("xT", [512, N + 2])
    cw = P.dram_in("cw", [128, 4, 3]); Wg = P.dram_in("Wg", [512, 512]); bg = P.dram_in("bg", [128, 4]); Wout = P.dram_in("Wout", [1024, 1024])
    outT = P.dram_out("outT", [1024, N])
    mixin = P.sb("mixin", [128, 8, N], BF16)
    mixkeys = ["mix%d" % k for k in range(8)]
    gf = P.sb("gf", [128, 4, N], F32); inb = P.sb("inb", [128, 4, N], BF16)
    xs = [P.sb("xs%d" % i, [128, N], F32) for i in range(2)]
    tq = P.sb("tq", [128, N], F32)
    bt_ = P.sb("bgt", [128, 4], F32); wt = P.sb("cwt", [128, 4, 3], F32)
    P.dma(bt_[:], bg, w=["bg"]); P.dma(wt[:], cw, w=["cw"])
    pss = [P.ps("ps%d" % i, [128, 512]) for i in range(4)]
    for kt in range(4):
        x = xs[kt % 2]; xk = "xs%d" % (kt % 2)
        P.dma(x[:], yS[kt * 128:(kt + 1) * 128, :], w=[xk])
        P.dve(lambda e, x=x: e.tensor_tensor(out=tq[:], in0=x[:], in1=x[:], op=ALU.mult), r=[xk], w=["tq"])
        P.dve(lambda e: e.tensor_scalar(out=tq[:], in0=tq[:], scalar1=0.044715, scalar2=1.0, op0=ALU.mult, op1=ALU.add), r=["tq"], w=["tq"])
        P.dve(lambda e, x=x: e.tensor_tensor(out=tq[:], in0=tq[:], in1=x[:], op=ALU.mult), r=["tq", xk], w=["tq"])
        P.act(lambda e: e.activation(out=tq[:], in_=tq[:], func=AF.Sigmoid, scale=1.5957691216057308), r=["tq"], w=["tq"])
        P.dve(lambda e, x=x, kt=kt: e.tensor_tensor(out=gf[:, kt, :], in0=tq[:], in1=x[:], op=ALU.mult), r=["tq", xk], w=["gf%d" % kt])
        P.act(lambda e, kt=kt: e.activation(out=inb[:, kt, :], in_=gf[:, kt, :], func=AF.Copy), r=["gf%d" % kt], w=["inb%d" % kt])
    cb_ = [P.sb("cvb%d" % i, [128, N], F32) for i in range(2)]
    cc_ = [P.sb("cvc%d" % i, [128, N + 2], F32) for i in range(2)]
    cx_ = [P.sb("cvx%d" % i, [128, N + 2], F32) for i in range(2)]
    ca_ = [P.sb("cva%d" % i, [128, N], F32) for i in range(2)]
    for a in range(4):
        b = a % 2
        bt, ct, xt, acc = cb_[b], cc_[b], cx_[b], ca_[b]
        rs = slice(a * 128, (a + 1) * 128)
        k = "cv%d" % b
        P.dma(bt[:], bT[rs, :], w=[k + "b"], q="act"); P.dma(ct[:], cT[rs, :], w=[k], q="sync"); P.dma(xt[:], xT[rs, :], w=[k + "x"], q="act")
        P.dve(lambda e, ct=ct, xt=xt: e.tensor_tensor(out=ct[:], in0=ct[:], in1=xt[:], op=ALU.mult), r=[k, k + "x"], w=[k])
        P.dve(lambda e, ct=ct, acc=acc, a=a: e.tensor_scalar(out=acc[:], in0=ct[:, 0:N], scalar1=wt[:, a, 0:1], scalar2=None, op0=ALU.mult), r=[k, "cw"], w=[k + "a"])
        for j in (1, 2):
            P.dve(lambda e, ct=ct, acc=acc, a=a, j=j: e.scalar_tensor_tensor(out=acc[:], in0=ct[:, j:j + N], scalar=wt[:, a, j:j + 1], in1=acc[:], op0=ALU.mult, op1=ALU.add), r=[k, "cw", k + "a"], w=[k + "a"])
        P.dve(lambda e, acc=acc, bt=bt, a=a: e.tensor_tensor(out=mixin[:, 4 + a, :], in0=acc[:], in1=bt[:], op=ALU.mult), r=[k + "a", k + "b"], w=[mixkeys[4 + a]])
    Wgv = Wg.rearrange("(kt p) m -> p kt m", p=128)
    wg = [P.sb("wg%d" % i, [128, 4, 128], BF16) for i in range(2)]
    sg = [P.sb("sg%d" % i, [128, 512], F32) for i in range(2)]
    pi = 0
    for m in range(4):
        s_ = m % 2
        P.dma(wg[s_][:], Wgv[:, :, m * 128:(m + 1) * 128], w=["wg%d" % s_], cast=True)
        for nt in range(NT):
            sl = slice(nt * 512, (nt + 1) * 512)
            ps = pss[pi % 4]; pk = "ps%d" % (pi % 4); pi += 1
            for kt in range(4):
                P.pe(lambda e, ps=ps, s_=s_, kt=kt, sl=sl: e.matmul(ps[:], lhsT=wg[s_][:, kt, :], rhs=inb[:, kt, sl], start=(kt == 0), stop=(kt == 3)), r=["wg%d" % s_, "inb%d" % kt], w=[pk])
            t = sg[nt % 2]; tk = "sg%d" % (nt % 2)
            P.act(lambda e, t=t, ps=ps, m=m: e.activation(out=t[:], in_=ps[:], func=AF.Sigmoid, bias=bt_[:, m:m + 1]), r=[pk, "bg"], w=[tk])
            P.dve(lambda e, t=t, m=m, sl=sl: e.tensor_tensor(out=mixin[:, m, sl], in0=t[:], in1=gf[:, m, sl], op=ALU.mult), r=[tk, "gf%d" % m], w=[mixkeys[m]])
    _outproj(P, mixin, Wout, outT, N, pss, pi, mixkeys)
    return P.finish()


def even_tail_dev(yS, hT, cw, Wg, bg, Wout):
    T = yS.shape[1]
    N = T // NCORE
    key = ("even_tail", N)
    if key not in _CACHE:
        _CACHE[key] = build_even_tail(N)
    bT, cT, xT = hT[512:1024], hT[1024:1536], hT[1536:2048]
    cp = np.concatenate([np.zeros((512, 2), np.float32), cT], axis=1)
    xp = np.concatenate([np.zeros((512, 2), np.float32), xT], axis=1)
    w = np.ascontiguousarray(cw.reshape(3, 4, 128).transpose(2, 1, 0))
    b = np.ascontiguousarray(bg.reshape(4, 128).T)
    maps = [{"yS": np.ascontiguousarray(yS[:, c * N:(c + 1) * N]), "bT": np.ascontiguousarray(bT[:, c * N:(c + 1) * N]),
             "cT": np.ascontiguousarray(cp[:, c * N:(c + 1) * N + 2]), "xT": np.ascontiguousarray(xp[:, c * N:(c + 1) * N + 2]),
             "cw": w, "Wg": np.ascontiguousarray(Wg), "bg": b, "Wout": np.ascontiguousarray(Wout)} for c in range(NCORE)]
    res = run(_CACHE[key], maps)
    return np.concatenate([r["outT"] for r in res], axis=1)


def build_odd_tail(N):
    P = Prog()
    NT = N // 512
    yc = P.dram_in("yc", [512, N]); zT = P.dram_in("zT", [512, N + 16]); invc = P.dram_in("invc", [128, 4, N])
    pw = P.dram_in("pw", [128, 4, 128]); psc = P.dram_in("psc", [128, 4]); Wout = P.dram_in("Wout", [1024, 1024])
    outT = P.dram_out("outT", [1024, N])
    mixin = P.sb("mixin", [128, 8, N], BF16)
    mixkeys = ["mix%d" % k for k in range(8)]
    for kt in range(4):
        P.dma(mixin[:, kt, :], yc[kt * 128:(kt + 1) * 128, :], w=[mixkeys[kt]], cast=True)
    pwb = P.sb("pwb", [128, 4, 128], BF16); sct = P.sb("sct", [128, 4], F32)
    P.dma(pwb[:], pw, w=["pw"], cast=True); P.dma(sct[:], psc, w=["psc"])
    pooled = P.sb("pooled", [128, 4, N], BF16)
    pss = [P.ps("ps%d" % i, [128, 512]) for i in range(4)]
    zb = [P.sb("pz%d" % i, [128, N + 16], F32) for i in range(2)]
    sab = [P.sb("psa%d" % i, [128, N + 16], F32) for i in range(2)]
    sbb = [P.sb("psb%d" % i, [128, N + 16], F32) for i in range(2)]
    icb = [P.sb("pic%d" % i, [128, N], F32) for i in range(2)]
    pi = 0
    for gi in range(4):
        b = gi % 2
        z, sa, sb_, ic = zb[b], sab[b], sbb[b], icb[b]
        rs = slice(gi * 128, (gi + 1) * 128)
        k = "pl%d" % b
        P.dma(z[:], zT[rs, :], w=[k + "z"], q="sync"); P.dma(ic[:], invc[:, gi, :], w=[k + "i"], q="act")
        cur, curk = z, k + "z"
        bufs = [(sa, k + "a"), (sb_, k + "b")]
        for step in range(gi + 1):
            sh = 1 << step
            nxt, nk = bufs[step % 2]
            P.dve(lambda e, cur=cur, nxt=nxt, sh=sh: e.tensor_tensor(out=nxt[:, sh:], in0=cur[:, sh:], in1=cur[:, 0:N + 16 - sh], op=ALU.add), r=[curk], w=[nk])
            cur, curk = nxt, nk
        o, okey = bufs[(gi + 1) % 2]
        P.dve(lambda e, cur=cur, o=o, ic=ic: e.tensor_tensor(out=o[:, 16:], in0=cur[:, 16:], in1=ic[:], op=ALU.mult), r=[curk, k + "i"], w=[okey])
        P.dve(lambda e, o=o, z=z, gi=gi: e.tensor_tensor(out=pooled[:, gi, :], in0=o[:, 16:], in1=z[:, 16:], op=ALU.subtract), r=[okey, k + "z"], w=["pooled%d" % gi])
        for nt in range(NT):
            sl = slice(nt * 512, (nt + 1) * 512)
            ps = pss[pi % 4]; pk = "ps%d" % (pi % 4); pi += 1
            P.pe(lambda e, ps=ps, gi=gi, sl=sl: e.matmul(ps[:], lhsT=pwb[:, gi, :], rhs=pooled[:, gi, sl], start=True, stop=True), r=["pw", "pooled%d" % gi], w=[pk])
            P.act(lambda e, ps=ps, gi=gi, sl=sl: e.activation(out=mixin[:, 4 + gi, sl], in_=ps[:], func=AF.Identity, scale=sct[:, gi:gi + 1]), r=[pk, "psc"], w=[mixkeys[4 + gi]])
    _outproj(P, mixin, Wout, outT, N, pss, pi, mixkeys)
    return P.finish()


def odd_tail_dev(ycT, zT, pool_w, pool_scale, Wout):
    T = zT.shape[1]
    N = T // NCORE
    key = ("odd_tail", N)
    if key not in _CACHE:
        _CACHE[key] = build_odd_tail(N)
    zp = np.concatenate([np.zeros((512, 16), np.float32), zT], axis=1)
    t = np.arange(T)
    inv = np.stack([1.0 / np.minimum(t + 1, w) for w in (2, 4, 8, 16)]).astype(np.float32)
    pw = np.ascontiguousarray(pool_w.transpose(1, 0, 2))
    psc = np.ascontiguousarray(pool_scale.reshape(4, 128).T)
    maps = []
    for c in range(NCORE):
        ic = np.ascontiguousarray(np.broadcast_to(inv[None, :, c * N:(c + 1) * N], (128, 4, N)))
        maps.append({"yc": np.ascontiguousarray(ycT[:, c * N:(c + 1) * N]), "zT": np.ascontiguousarray(zp[:, c * N:(c + 1) * N + 16]),
                     "invc": ic, "pw": pw, "psc": psc, "Wout": np.ascontiguousarray(Wout)})
    res = run(_CACHE[key], maps)
    return np.concatenate([r["outT"] for r in res], axis=1)


def kernel(x, p, ev_w_in, ev_lambda_re, ev_lambda_im, ev_log_dt, ev_b_re, ev_b_im,
           ev_c_re, ev_c_im, ev_d, ev_w_glu, ev_b_glu, ev_conv_w, ev_w_out,
           od_w_in, od_rel_bias, od_pool_w, od_pool_scale, od_w_out,
           ln_mix_g, ln_mix_b, ln_ffn_g, ln_ffn_b, ffn_w_up, ffn_w_down,
           ple_w_proj, ple_w_gate, ple_b_gate):
    f = lambda a: np.asarray(a, dtype=np.float32)
    x_tm = f(x)[0]
    xT = np.ascontiguousarray(x_tm.T)
    hT = lin(xT, f(ev_w_in[0]))
    for i in range(DEPTH):
        if i % 2 == 0:
            e = i // 2
            yS = s5_mixer_dev(np.ascontiguousarray(hT[0:512]), f(ev_lambda_re[e]), f(ev_lambda_im[e]), f(ev_log_dt[e]),
                              f(ev_b_re[e]), f(ev_b_im[e]), f(ev_c_re[e]), f(ev_c_im[e]), f(ev_d[e]))
            mixT = even_tail_dev(yS, hT, f(ev_conv_w[e]), f(ev_w_glu[e]), f(ev_b_glu[e]), f(ev_w_out[e]))
        else:
            o = i // 2
            ycT = attn_dev(hT[0:512], hT[512:1024], hT[1024:1536], f(od_rel_bias[o]))
            mixT = odd_tail_dev(ycT, hT[1536:2048], f(od_pool_w[o]), f(od_pool_scale[o]), f(od_w_out[o]))
        x1 = resln(x_tm, np.ascontiguousarray(mixT.T), f(ln_mix_g[i]), f(ln_mix_b[i]))
        x1T = np.ascontiguousarray(x1.T)
        ffnT = ffn_dev(x1T, f(ffn_w_up[i]), f(ffn_w_down[i]))
        x2 = resln(x1, np.ascontiguousarray(ffnT.T), f(ln_ffn_g[i]), f(ln_ffn_b[i]))
        x2T = np.ascontiguousarray(x2.T)
        pT = np.ascontiguousarray(f(p[i])[0].T)
        if i + 1 < DEPTH:
            wn = f(od_w_in[(i + 1) // 2]) if (i + 1) % 2 == 1 else f(ev_w_in[(i + 1) // 2])
        else:
            wn = None
        xT, hT = ple_dev(x2T, pT, f(ple_w_gate[i]), f(ple_b_gate[i]), f(ple_w_proj[i]), wn)
        x_tm = np.ascontiguousarray(xT.T)
    return x_tm[None].astype(np.float32)
```

```python
import math
import numpy as np
import concourse.bass as bass
import concourse.mybir as mybir
from concourse.bass_utils import run_bass_kernel_spmd

F32 = mybir.dt.float32
BF16 = mybir.dt.bfloat16
AF = mybir.ActivationFunctionType
ALU = mybir.AluOpType
AX = mybir.AxisListType
NCORE = 8
MAGIC = 12582912.0
TWO_PI = 2.0 * math.pi

D_MODEL = 1024
SEQ = 16384
DEPTH = 4
D_FF = 2816
ALPHA = (2 * DEPTH) ** 0.25
LN_EPS = 1e-5


class Prog:
    ENGS = ("sync", "gpsimd", "act", "dve", "pe")
    NDS = 8

    def __init__(self):
        self.nc = bass.Bass("TRN2", target_bir_lowering=False)
        self.ops = {e: [] for e in self.ENGS}
        self.cnt = {}
        self.lastw = {}
        self.reads = {}
        self.seen = {e: {} for e in self.ENGS}
        self.ndma = {e: 0 for e in self.ENGS}
        self.ctx = []
        self.out_waits = []

    def enter(self, cm):
        v = cm.__enter__()
        self.ctx.append(cm)
        return v

    def sb(self, name, shape, dt):
        return self.enter(self.nc.sbuf_tensor(name, list(shape), dt))

    def ps(self, name, shape, dt=F32):
        return self.enter(self.nc.psum_tensor(name, list(shape), dt))

    def dram_in(self, name, shape, dt=F32):
        return self.nc.dram_tensor(name, list(shape), dt, kind="ExternalInput").ap()

    def dram_out(self, name, shape, dt=F32):
        return self.nc.dram_tensor(name, list(shape), dt, kind="ExternalOutput").ap()

    def _op(self, eng, fn, r, w, dma=False, is_out=False):
        waits = {}

        def need(sv):
            s, v = sv
            waits[s] = max(waits.get(s, 0), v)

        for k in r:
            if k in self.lastw:
                need(self.lastw[k])
        for k in w:
            if k in self.lastw:
                need(self.lastw[k])
            for sv in self.reads.get(k, ()):
                need(sv)
        if dma:
            i = self.ndma[eng]
            self.ndma[eng] += 1
            sem = "%s_d%d" % (eng, i % self.NDS)
            inc = 16
        else:
            sem = eng
            inc = 1
        prev = self.cnt.get(sem, 0)
        self.cnt[sem] = prev + inc
        me = (sem, self.cnt[sem])
        wl = []
        if dma and prev > 0:
            waits[sem] = max(waits.get(sem, 0), prev)
        for s, v in waits.items():
            if eng == "pe" and s == "pe":
                continue
            if self.seen[eng].get(s, 0) >= v:
                continue
            self.seen[eng][s] = v
            wl.append((s, v))
        self.ops[eng].append((wl, fn, sem, inc))
        for k in w:
            self.lastw[k] = me
            self.reads[k] = []
        for k in r:
            self.reads.setdefault(k, []).append(me)
        if is_out:
            self.out_waits.append(me)

    def dma(self, out, in_, r=(), w=(), cast=False, is_out=False, q=None):
        eng = "gpsimd" if cast else (q or "sync")
        self._op(eng, lambda e: e.dma_start(out=out, in_=in_), r, w, dma=True, is_out=is_out)

    def act(self, fn, r=(), w=()):
        self._op("act", fn, r, w)

    def dve(self, fn, r=(), w=()):
        self._op("dve", fn, r, w)

    def pe(self, fn, r=(), w=()):
        self._op("pe", fn, r, w)

    def finish(self):
        nc = self.nc
        fin = {}
        for s, v in self.out_waits:
            fin[s] = max(fin.get(s, 0), v)
        names = sorted(self.cnt.keys())
        sems = {}
        for n in names:
            sems[n] = self.enter(nc.semaphore(n))
        block = self.enter(nc.Block())
        engmap = {"sync": block.sync, "gpsimd": block.gpsimd, "act": block.scalar,
                  "dve": block.vector, "pe": block.tensor}

        def make(ename):
            def body(e):
                for wl, fn, sem, inc in self.ops[ename]:
                    for s, v in wl:
                        e.wait_ge(sems[s], v)
                    fn(e).then_inc(sems[sem], inc)
                if ename == "sync":
                    for s, v in fin.items():
                        e.wait_ge(sems[s], v)
            return body

        for ename in self.ENGS:
            if self.ops[ename] or ename == "sync":
                engmap[ename](make(ename))
        for cm in reversed(self.ctx):
            cm.__exit__(None, None, None)
        self.ctx = []
        return nc


def make_stage(P, ncol, n=3):
    P._stg = [P.sb("stg%d" % i, [128, ncol], F32) for i in range(n)]
    P._stgi = 0


def stage_cast(P, dst, src, npart, ncol, wkey, scale=None):
    idx = P._stgi
    P._stgi += 1
    b = idx % len(P._stg)
    st = P._stg[b][0:npart, 0:ncol]
    sk = "stg%d" % b
    P.dma(st, src, w=[sk], q=("sync" if idx % 2 == 0 else "act"))
    if idx % 2 == 0:
        if scale is None:
            P.act(lambda e: e.activation(out=dst, in_=st, func=AF.Copy), r=[sk], w=[wkey])
        else:
            P.act(lambda e: e.activation(out=dst, in_=st, func=AF.Copy, scale=float(scale)), r=[sk], w=[wkey])
    else:
        if scale is None:
            P.dve(lambda e: e.tensor_copy(out=dst, in_=st), r=[sk], w=[wkey])
        else:
            P.dve(lambda e: e.tensor_scalar(out=dst, in0=st, scalar1=float(scale), scalar2=None, op0=ALU.mult), r=[sk], w=[wkey])


def run(nc, in_maps):
    res = run_bass_kernel_spmd(nc, in_maps, core_ids=list(range(NCORE)))
    return res.results


_CACHE = {}


def build_lin(K, M, N, mode="plain", func=None, has_bias=False, has_scale=False):
    P = Prog()
    nc = P.nc
    KT = K // 128
    MO = M // 2 if mode == "swiglu" else M
    MT = MO // 128
    NT = N // 512
    inT = P.dram_in("inT", [K, N])
    W = P.dram_in("W", [K, M])
    outT = P.dram_out("outT", [MO, N])
    bias = P.dram_in("bias", [128, MT]) if has_bias else None
    scale = P.dram_in("scale", [128, MT]) if has_scale else None
    A = P.dram_in("A", [MO, N]) if mode == "fma" else None
    B = P.dram_in("B", [MO, N]) if mode == "fma" else None
    inb = P.sb("inb", [128, KT, N], BF16)
    NW = 3
    wb = [P.sb("wb%d" % i, [128, KT, 128], BF16) for i in range(NW * (2 if mode == "swiglu" else 1))]
    ot = [P.sb("ot%d" % i, [128, N], F32) for i in range(2)]
    pss = [P.ps("ps%d" % i, [128, 512]) for i in range(4)]
    if has_bias:
        bt = P.sb("bt", [128, MT], F32)
        P.dma(bt[:], bias, w=["bt"])
    if has_scale:
        st = P.sb("st", [128, MT], F32)
        P.dma(st[:], scale, w=["st"])
    if mode == "mulin":
        gf = P.sb("gf", [128, KT, N], F32)
        xs = [P.sb("xs%d" % i, [128, N], F32) for i in range(2)]
        tq = P.sb("tq", [128, N], F32)
        for kt in range(KT):
            x = xs[kt % 2]
            xk = "xs%d" % (kt % 2)
            P.dma(x[:], inT[kt * 128:(kt + 1) * 128, :], w=[xk])
            P.dve(lambda e, x=x: e.tensor_tensor(out=tq[:], in0=x[:], in1=x[:], op=ALU.mult), r=[xk], w=["tq"])
            P.dve(lambda e: e.tensor_scalar(out=tq[:], in0=tq[:], scalar1=0.044715, scalar2=1.0, op0=ALU.mult, op1=ALU.add), r=["tq"], w=["tq"])
            P.dve(lambda e, x=x: e.tensor_tensor(out=tq[:], in0=tq[:], in1=x[:], op=ALU.mult), r=["tq", xk], w=["tq"])
            P.act(lambda e: e.activation(out=tq[:], in_=tq[:], func=AF.Sigmoid, scale=1.5957691216057308), r=["tq"], w=["tq"])
            P.dve(lambda e, x=x, kt=kt: e.tensor_tensor(out=gf[:, kt, :], in0=tq[:], in1=x[:], op=ALU.mult), r=["tq", xk], w=["gf%d" % kt])
            P.act(lambda e, kt=kt: e.activation(out=inb[:, kt, :], in_=gf[:, kt, :], func=AF.Copy), r=["gf%d" % kt], w=["inb%d" % kt])
    else:
        make_stage(P, N)
        for kt in range(KT):
            stage_cast(P, inb[:, kt, :], inT[kt * 128:(kt + 1) * 128, :], 128, N, "inb%d" % kt)
    if mode == "swiglu":
        tmp = [P.sb("tmp%d" % i, [128, 512], F32) for i in range(2)]
    if mode == "fma":
        At = [P.sb("At%d" % i, [128, N], F32) for i in range(2)]
        Bt = [P.sb("Bt%d" % i, [128, N], F32) for i in range(2)]
    Wv = W.rearrange("(kt p) m -> p kt m", p=128)
    pi = 0
    for m in range(MT):
        s = m % NW
        o = ot[m % 2]
        ok = "ot%d" % (m % 2)
        P.dma(wb[s][:], Wv[:, :, m * 128:(m + 1) * 128], w=["wb%d" % s], cast=True)
        if mode == "swiglu":
            P.dma(wb[NW + s][:], Wv[:, :, MO + m * 128:MO + (m + 1) * 128], w=["wb%d" % (NW + s)], cast=True)
        if mode == "fma":
            P.dma(At[m % 2][:], A[m * 128:(m + 1) * 128, :], w=["At%d" % (m % 2)])
            P.dma(Bt[m % 2][:], B[m * 128:(m + 1) * 128, :], w=["Bt%d" % (m % 2)])
        for nt in range(NT):
            sl = slice(nt * 512, (nt + 1) * 512)
            ps = pss[pi % 4]
            pk = "ps%d" % (pi % 4)
            pi += 1
            for kt in range(KT):
                P.pe(lambda e, ps=ps, s=s, kt=kt, sl=sl: e.matmul(ps[:], lhsT=wb[s][:, kt, :], rhs=inb[:, kt, sl], start=(kt == 0), stop=(kt == KT - 1)),
                     r=["wb%d" % s, "inb%d" % kt], w=[pk])
            if mode == "swiglu":
                ps2 = pss[pi % 4]
                pk2 = "ps%d" % (pi % 4)
                pi += 1
                for kt in range(KT):
                    P.pe(lambda e, ps2=ps2, s=s, kt=kt, sl=sl: e.matmul(ps2[:], lhsT=wb[NW + s][:, kt, :], rhs=inb[:, kt, sl], start=(kt == 0), stop=(kt == KT - 1)),
                         r=["wb%d" % (NW + s), "inb%d" % kt], w=[pk2])
                t = tmp[nt % 2]
                tk = "tmp%d" % (nt % 2)
                P.act(lambda e, t=t, ps=ps: e.activation(out=t[:], in_=ps[:], func=AF.Silu), r=[pk], w=[tk])
                P.dve(lambda e, o=o, t=t, ps2=ps2, sl=sl: e.tensor_tensor(out=o[:, sl], in0=t[:], in1=ps2[:], op=ALU.mult), r=[tk, pk2], w=[ok])
            elif mode == "fma":
                P.dve(lambda e, o=o, ps=ps, sl=sl, m=m: e.tensor_tensor(out=o[:, sl], in0=Bt[m % 2][:, sl], in1=ps[:], op=ALU.mult), r=[pk, "Bt%d" % (m % 2)], w=[ok])
                P.dve(lambda e, o=o, sl=sl, m=m: e.tensor_tensor(out=o[:, sl], in0=o[:, sl], in1=At[m % 2][:, sl], op=ALU.add), r=[ok, "At%d" % (m % 2)], w=[ok])
            elif mode == "mulin":
                P.act(lambda e, o=o, ps=ps, sl=sl, m=m: e.activation(out=o[:, sl], in_=ps[:], func=AF.Sigmoid, bias=bt[:, m:m + 1]), r=[pk, "bt"], w=[ok])
                P.dve(lambda e, o=o, sl=sl, m=m: e.tensor_tensor(out=o[:, sl], in0=o[:, sl], in1=gf[:, m, sl], op=ALU.mult), r=[ok, "gf%d" % m], w=[ok])
            else:
                kw = {}
                rr = [pk]
                if has_bias:
                    kw["bias"] = bt[:, m:m + 1]
                    rr.append("bt")
                if has_scale:
                    kw["scale"] = st[:, m:m + 1]
                    rr.append("st")
                f = func if func is not None else AF.Identity
                P.act(lambda e, o=o, ps=ps, sl=sl, kw=kw, f=f: e.activation(out=o[:, sl], in_=ps[:], func=f, **kw), r=rr, w=[ok])
        P.dma(outT[m * 128:(m + 1) * 128, :], o[:], r=[ok], w=["out"], is_out=True)
    return P.finish()


def build_ple(N, with_next=False):
    P = Prog()
    KT, K2, MT, NT = 8, 2, 8, N // 512
    x2T = P.dram_in("x2T", [1024, N]); pT = P.dram_in("pT", [256, N])
    Wg = P.dram_in("Wg", [1024, 1024]); Wp = P.dram_in("Wp", [256, 1024]); bias = P.dram_in("bias", [128, MT])
    outT = P.dram_out("outT", [1024, N])
    if with_next:
        Win = P.dram_in("Win", [1024, 2048]); hT = P.dram_out("hT", [2048, N])
        x3b = P.sb("x3b", [128, 8, N], BF16)
    inb = P.sb("inb", [128, KT, N], BF16); pb = P.sb("pb", [128, K2, N], BF16)
    bt = P.sb("bt", [128, MT], F32)
    P.dma(bt[:], bias, w=["bt"])
    make_stage(P, N)
    for kt in range(KT):
        stage_cast(P, inb[:, kt, :], x2T[kt * 128:(kt + 1) * 128, :], 128, N, "inb%d" % kt)
    for kt in range(K2):
        stage_cast(P, pb[:, kt, :], pT[kt * 128:(kt + 1) * 128, :], 128, N, "pb")
    NW = 3
    wg = [P.sb("wg%d" % i, [128, KT, 128], BF16) for i in range(NW)]
    wp = [P.sb("wp%d" % i, [128, K2, 128], BF16) for i in range(NW)]
    At = [P.sb("At%d" % i, [128, N], F32) for i in range(2)]
    ot = [P.sb("ot%d" % i, [128, N], F32) for i in range(2)]
    tmp = [P.sb("tmp%d" % i, [128, 512], F32) for i in range(2)]
    pss = [P.ps("ps%d" % i, [128, 512]) for i in range(4)]
    Wgv = Wg.rearrange("(kt p) m -> p kt m", p=128); Wpv = Wp.rearrange("(kt p) m -> p kt m", p=128)
    pi = 0
    for m in range(MT):
        s_ = m % NW
        o = ot[m % 2]; ok = "ot%d" % (m % 2); A = At[m % 2]; ak = "At%d" % (m % 2)
        ms = slice(m * 128, (m + 1) * 128)
        P.dma(wg[s_][:], Wgv[:, :, ms], w=["wg%d" % s_], cast=True)
        P.dma(wp[s_][:], Wpv[:, :, ms], w=["wp%d" % s_], cast=True)
        P.dma(A[:], x2T[ms, :], w=[ak], q="act")
        for nt in range(NT):
            sl = slice(nt * 512, (nt + 1) * 512)
            ps = pss[pi % 4]; pk = "ps%d" % (pi % 4); pi += 1
            ps2 = pss[pi % 4]; pk2 = "ps%d" % (pi % 4); pi += 1
            for kt in range(KT):
                P.pe(lambda e, ps=ps, s_=s_, kt=kt, sl=sl: e.matmul(ps[:], lhsT=wg[s_][:, kt, :], rhs=inb[:, kt, sl], start=(kt == 0), stop=(kt == KT - 1)), r=["wg%d" % s_, "inb%d" % kt], w=[pk])
            for kt in range(K2):
                P.pe(lambda e, ps2=ps2, s_=s_, kt=kt, sl=sl: e.matmul(ps2[:], lhsT=wp[s_][:, kt, :], rhs=pb[:, kt, sl], start=(kt == 0), stop=(kt == K2 - 1)), r=["wp%d" % s_, "pb"], w=[pk2])
            t = tmp[nt % 2]; tk = "tmp%d" % (nt % 2)
            P.act(lambda e, t=t, ps=ps, m=m: e.activation(out=t[:], in_=ps[:], func=AF.Sigmoid, bias=bt[:, m:m + 1]), r=[pk, "bt"], w=[tk])
            P.dve(lambda e, o=o, t=t, ps2=ps2, sl=sl: e.tensor_tensor(out=o[:, sl], in0=t[:], in1=ps2[:], op=ALU.mult), r=[tk, pk2], w=[ok])
            P.dve(lambda e, o=o, A=A, sl=sl: e.tensor_tensor(out=o[:, sl], in0=o[:, sl], in1=A[:, sl], op=ALU.add), r=[ok, ak], w=[ok])
            if with_next:
                P.dve(lambda e, o=o, sl=sl, m=m: e.tensor_copy(out=x3b[:, m, sl], in_=o[:, sl]), r=[ok], w=["x3b%d" % m])
        P.dma(outT[ms, :], o[:], r=[ok], w=["out"], is_out=True)
    if with_next:
        Wiv = Win.rearrange("(kt p) m -> p kt m", p=128)
        wi = [P.sb("wi%d" % i, [128, 8, 128], BF16) for i in range(3)]
        ht = [P.sb("ht%d" % i, [128, N], F32) for i in range(2)]
        for m in range(16):
            s_ = m % 3
            o = ht[m % 2]; ok = "ht%d" % (m % 2)
            P.dma(wi[s_][:], Wiv[:, :, m * 128:(m + 1) * 128], w=["wi%d" % s_], cast=True)
            for nt in range(NT):
                sl = slice(nt * 512, (nt + 1) * 512)
                ps = pss[pi % 4]; pk = "ps%d" % (pi % 4); pi += 1
                for kt in range(8):
                    P.pe(lambda e, ps=ps, s_=s_, kt=kt, sl=sl: e.matmul(ps[:], lhsT=wi[s_][:, kt, :], rhs=x3b[:, kt, sl], start=(kt == 0), stop=(kt == 7)), r=["wi%d" % s_, "x3b%d" % kt], w=[pk])
                P.act(lambda e, o=o, ps=ps, sl=sl: e.activation(out=o[:, sl], in_=ps[:], func=AF.Identity), r=[pk], w=[ok])
            P.dma(hT[m * 128:(m + 1) * 128, :], o[:], r=[ok], w=["out"], is_out=True)
    return P.finish()


def ple_dev(x2T, pT, Wg, bg, Wp, Win=None):
    T = x2T.shape[1]
    N = T // NCORE
    key = ("ple", N, Win is not None)
    if key not in _CACHE:
        _CACHE[key] = build_ple(N, Win is not None)
    b = np.ascontiguousarray(bg.reshape(8, 128).T)
    maps = [{"x2T": np.ascontiguousarray(x2T[:, c * N:(c + 1) * N]), "pT": np.ascontiguousarray(pT[:, c * N:(c + 1) * N]),
             "Wg": np.ascontiguousarray(Wg), "Wp": np.ascontiguousarray(Wp), "bias": b} for c in range(NCORE)]
    if Win is not None:
        for d in maps:
            d["Win"] = np.ascontiguousarray(Win)
    res = run(_CACHE[key], maps)
    xT = np.concatenate([r["outT"] for r in res], axis=1)
    if Win is None:
        return xT, None
    return xT, np.concatenate([r["hT"] for r in res], axis=1)


def build_ffn(N):
    P = Prog()
    KT, JT, MT, NT = 8, D_FF // 128, 8, N // 512
    inT = P.dram_in("inT", [1024, N]); Wu = P.dram_in("Wu", [1024, 2 * D_FF]); Wd = P.dram_in("Wd", [D_FF, 1024])
    outT = P.dram_out("outT", [1024, N])
    inb = P.sb("inb", [128, KT, N], BF16)
    actb = P.sb("actb", [128, JT, N], BF16)
    make_stage(P, N)
    for kt in range(KT):
        stage_cast(P, inb[:, kt, :], inT[kt * 128:(kt + 1) * 128, :], 128, N, "inb%d" % kt)
    NW = 3
    wb = [P.sb("wb%d" % i, [128, KT, 128], BF16) for i in range(2 * NW)]
    wd = [P.sb("wd%d" % i, [128, JT, 128], BF16) for i in range(2)]
    ot = [P.sb("ot%d" % i, [128, N], F32) for i in range(2)]
    tmp = [P.sb("tmp%d" % i, [128, 512], F32) for i in range(2)]
    pss = [P.ps("ps%d" % i, [128, 512]) for i in range(6)]
    Wuv = Wu.rearrange("(kt p) m -> p kt m", p=128); Wdv = Wd.rearrange("(jt p) m -> p jt m", p=128)
    pi = 0
    for j in range(JT):
        s_ = j % NW
        P.dma(wb[s_][:], Wuv[:, :, j * 128:(j + 1) * 128], w=["wb%d" % s_], cast=True)
        P.dma(wb[NW + s_][:], Wuv[:, :, D_FF + j * 128:D_FF + (j + 1) * 128], w=["wb%d" % (NW + s_)], cast=True)
        for nt in range(NT):
            sl = slice(nt * 512, (nt + 1) * 512)
            ps = pss[pi % 6]; pk = "ps%d" % (pi % 6); pi += 1
            ps2 = pss[pi % 6]; pk2 = "ps%d" % (pi % 6); pi += 1
            for kt in range(KT):
                P.pe(lambda e, ps=ps, s_=s_, kt=kt, sl=sl: e.matmul(ps[:], lhsT=wb[s_][:, kt, :], rhs=inb[:, kt, sl], start=(kt == 0), stop=(kt == KT - 1)), r=["wb%d" % s_, "inb%d" % kt], w=[pk])
            for kt in range(KT):
                P.pe(lambda e, ps2=ps2, s_=s_, kt=kt, sl=sl: e.matmul(ps2[:], lhsT=wb[NW + s_][:, kt, :], rhs=inb[:, kt, sl], start=(kt == 0), stop=(kt == KT - 1)), r=["wb%d" % (NW + s_), "inb%d" % kt], w=[pk2])
            t = tmp[nt % 2]; tk = "tmp%d" % (nt % 2)
            P.act(lambda e, t=t, ps=ps: e.activation(out=t[:], in_=ps[:], func=AF.Silu), r=[pk], w=[tk])
            P.dve(lambda e, t=t, ps2=ps2, j=j, sl=sl: e.tensor_tensor(out=actb[:, j, sl], in0=t[:], in1=ps2[:], op=ALU.mult), r=[tk, pk2], w=["actb%d" % j])
    allact = ["actb%d" % j for j in range(JT)]
    for m in range(MT):
        s_ = m % 2
        o = ot[m % 2]; ok = "ot%d" % (m % 2)
        P.dma(wd[s_][:], Wdv[:, :, m * 128:(m + 1) * 128], w=["wd%d" % s_], cast=True)
        for nt in range(NT):
            sl = slice(nt * 512, (nt + 1) * 512)
            ps = pss[pi % 6]; pk = "ps%d" % (pi % 6); pi += 1
            for j in range(JT):
                P.pe(lambda e, ps=ps, s_=s_, j=j, sl=sl: e.matmul(ps[:], lhsT=wd[s_][:, j, :], rhs=actb[:, j, sl], start=(j == 0), stop=(j == JT - 1)), r=["wd%d" % s_] + allact, w=[pk])
            P.act(lambda e, o=o, ps=ps, sl=sl: e.activation(out=o[:, sl], in_=ps[:], func=AF.Identity), r=[pk], w=[ok])
        P.dma(outT[m * 128:(m + 1) * 128, :], o[:], r=[ok], w=["out"], is_out=True)
    return P.finish()


def ffn_dev(x1T, Wu, Wd):
    T = x1T.shape[1]
    N = T // NCORE
    key = ("ffn", N)
    if key not in _CACHE:
        _CACHE[key] = build_ffn(N)
    maps = [{"inT": np.ascontiguousarray(x1T[:, c * N:(c + 1) * N]), "Wu": np.ascontiguousarray(Wu), "Wd": np.ascontiguousarray(Wd)} for c in range(NCORE)]
    res = run(_CACHE[key], maps)
    return np.concatenate([r["outT"] for r in res], axis=1)


def lin(inT_full, W, mode="plain", func=None, bias=None, scale=None, A=None, B=None):
    K, T = inT_full.shape
    M = W.shape[1]
    N = T // NCORE
    key = ("lin", K, M, N, mode, str(func), bias is not None, scale is not None)
    if key not in _CACHE:
        _CACHE[key] = build_lin(K, M, N, mode, func, bias is not None, scale is not None)
    nc = _CACHE[key]
    MO = M // 2 if mode == "swiglu" else M
    maps = []
    for c in range(NCORE):
        sl = slice(c * N, (c + 1) * N)
        d = {"inT": np.ascontiguousarray(inT_full[:, sl]), "W": np.ascontiguousarray(W)}
        if bias is not None:
            d["bias"] = np.ascontiguousarray(bias.reshape(MO // 128, 128).T)
        if scale is not None:
            d["scale"] = np.ascontiguousarray(scale.reshape(MO // 128, 128).T)
        if A is not None:
            d["A"] = np.ascontiguousarray(A[:, sl])
            d["B"] = np.ascontiguousarray(B[:, sl])
        maps.append(d)
    res = run(nc, maps)
    return np.concatenate([r["outT"] for r in res], axis=1)


def build_s5(NB):
    P = Prog()
    G, GC = 4, 2
    ub = P.dram_in("ub", [G, 128, NB])
    yb = P.dram_out("yb", [G, 128, NB])
    prm = {n: P.dram_in(n, [128, GC]) for n in ("lre", "lim", "ldt")}
    bin_ = {n: P.dram_in(n, [128, GC, 16]) for n in ("bre", "bim", "cre", "cim")}
    dcol_d = P.dram_in("dcol", [128, G])
    cmask_d = P.dram_in("cmask", [128, 128])
    ident_d = P.dram_in("ident", [128, 128])
    NL = int(math.log2(NB))
    t = {}
    for n in ("lre", "lim", "ldt", "dt", "a", "ang", "nr", "den", "rre", "rim", "nrim", "t1", "t2", "m2", "ire", "iim", "niim"):
        t[n] = P.sb("t_" + n, [128, GC], F32)
    for n in ("ak", "arg", "tr", "mag", "cs", "sn"):
        t[n] = P.sb("t_" + n, [128, 8, GC], F32)
    for n in ("bre", "bim", "cre", "cim", "bbre", "bbim", "t16a", "t16b"):
        t[n] = P.sb("t_" + n, [128, GC, 16], F32)
    PRE = P.sb("PRE", [128, 9, GC], F32); PIM = P.sb("PIM", [128, 9, GC], F32); NPIM = P.sb("NPIM", [128, 9, GC], F32)
    PWRE = P.sb("PWRE", [128, NL, GC], F32); PWIM = P.sb("PWIM", [128, NL, GC], F32); NPWIM = P.sb("NPWIM", [128, NL, GC], F32)
    BTre = P.sb("BTre", [128, GC, 128], F32); BTim = P.sb("BTim", [128, GC, 128], F32)
    CTre = P.sb("CTre", [128, GC, 128], F32); CTim = P.sb("CTim", [128, GC, 128], F32)
    BPre = P.sb("BPre", [128, GC, 128], F32); BPimn = P.sb("BPimn", [128, GC, 128], F32)
    X1 = P.sb("X1", [128, GC, 128], F32); X2 = P.sb("X2", [128, GC, 128], F32)
    X3 = P.sb("X3", [128, GC, 128], F32); X4 = P.sb("X4", [128, GC, 128], F32)
    dcol = P.sb("dcolt", [128, G], F32); cmask = P.sb("cmaskt", [128, 128], F32); ident = P.sb("identt", [128, 128], F32)
    dtmp = P.sb("dtmp", [128, 128], F32)
    Bre = P.sb("Bre", [128, G, 128], BF16); Bim = P.sb("Bim", [128, G, 128], BF16)
    Cre = P.sb("Cre", [128, GC, 128], BF16); Cimn = P.sb("Cimn", [128, GC, 128], BF16)
    Dm = P.sb("Dm", [128, G, 128], BF16)
    ubb = P.sb("ubb", [128, G, NB], BF16)
    Hre = P.sb("Hre", [128, GC, NB], F32); Him = P.sb("Him", [128, GC, NB], F32)
    Hbre = P.sb("Hbre", [128, GC, NB], BF16); Hbim = P.sb("Hbim", [128, GC, NB], BF16)
    Tres = [P.sb("Tre%d" % g, [128, NB // 2], F32) for g in range(GC)]
    Tims = [P.sb("Tim%d" % g, [128, NB // 2], F32) for g in range(GC)]
    yo = [P.sb("yo%d" % i, [128, NB], F32) for i in range(2)]
    pss = [P.ps("ps%d" % i, [128, 512]) for i in range(4)]
    K = ["prep"]

    for n in ("lre", "lim", "ldt"):
        P.dma(t[n][:], prm[n], w=K)
    for n in ("bre", "bim", "cre", "cim"):
        P.dma(t[n][:], bin_[n], w=K)
    P.dma(dcol[:], dcol_d, w=K); P.dma(cmask[:], cmask_d, w=K); P.dma(ident[:], ident_d, w=K)
    make_stage(P, NB, n=2)
    for g in range(G):
        stage_cast(P, ubb[:, g, :], ub[g], 128, NB, "ubb%d" % g)
    P.dve(lambda e: e.memset(Bre[:], 0.0), w=["Bpad"])
    P.dve(lambda e: e.memset(Bim[:], 0.0), w=["Bpad"])

    def tt(o, a, b, op, r=K, w=K):
        P.dve(lambda e: e.tensor_tensor(out=o, in0=a, in1=b, op=op), r=r, w=w)

    def ts(o, a, s1, op0, s2=None, op1=None, r=K, w=K):
        if op1 is None:
            P.dve(lambda e: e.tensor_scalar(out=o, in0=a, scalar1=s1, scalar2=None, op0=op0), r=r, w=w)
        else:
            P.dve(lambda e: e.tensor_scalar(out=o, in0=a, scalar1=s1, scalar2=s2, op0=op0, op1=op1), r=r, w=w)

    def stt(o, a, s_, b, op0, op1, r=K, w=K):
        P.dve(lambda e: e.scalar_tensor_tensor(out=o, in0=a, scalar=s_, in1=b, op0=op0, op1=op1), r=r, w=w)

    def act(o, a, f, **kw):
        P.act(lambda e: e.activation(out=o, in_=a, func=f, **kw), r=K, w=K)

    def sin_of(o, arg):
        ts(t["tr"][:], arg, 1.0 / TWO_PI, ALU.mult, MAGIC, ALU.add)
        ts(t["tr"][:], t["tr"][:], MAGIC, ALU.subtract, -TWO_PI, ALU.mult)
        tt(t["tr"][:], t["tr"][:], arg, ALU.add)
        ts(t["tr"][:], t["tr"][:], math.pi, ALU.min, -math.pi, ALU.max)
        act(o, t["tr"][:], AF.Sin)

    act(t["dt"][:], t["ldt"][:], AF.Exp)
    tt(t["a"][:], t["lre"][:], t["dt"][:], ALU.mult)
    tt(t["ang"][:], t["lim"][:], t["dt"][:], ALU.mult)
    P.dve(lambda e: e.memset(PRE[:, 0, :], 1.0), r=K, w=K)
    P.dve(lambda e: e.memset(PIM[:, 0, :], 0.0), r=K, w=K)
    for k in range(1, 9):
        ts(t["ak"][:, k - 1, :], t["a"][:], float(k), ALU.mult)
        ts(t["arg"][:, k - 1, :], t["ang"][:], float(k), ALU.mult)
    act(t["mag"][:], t["ak"][:], AF.Exp)
    sin_of(t["sn"][:], t["arg"][:])
    ts(t["arg"][:], t["arg"][:], math.pi / 2, ALU.add)
    sin_of(t["cs"][:], t["arg"][:])
    tt(PRE[:, 1:9, :], t["mag"][:], t["cs"][:], ALU.mult)
    tt(PIM[:, 1:9, :], t["mag"][:], t["sn"][:], ALU.mult)
    ts(NPIM[:], PIM[:], -1.0, ALU.mult)
    ts(t["nr"][:], PRE[:, 1, :], -1.0, ALU.add)
    tt(t["t1"][:], t["lre"][:], t["lre"][:], ALU.mult)
    tt(t["t2"][:], t["lim"][:], t["lim"][:], ALU.mult)
    tt(t["den"][:], t["t1"][:], t["t2"][:], ALU.add)
    P.dve(lambda e: e.reciprocal(out=t["den"][:], in_=t["den"][:]), r=K, w=K)
    tt(t["t1"][:], t["nr"][:], t["lre"][:], ALU.mult)
    tt(t["t2"][:], PIM[:, 1, :], t["lim"][:], ALU.mult)
    tt(t["t1"][:], t["t1"][:], t["t2"][:], ALU.add)
    tt(t["rre"][:], t["t1"][:], t["den"][:], ALU.mult)
    tt(t["t1"][:], PIM[:, 1, :], t["lre"][:], ALU.mult)
    tt(t["t2"][:], t["nr"][:], t["lim"][:], ALU.mult)
    tt(t["t1"][:], t["t1"][:], t["t2"][:], ALU.subtract)
    tt(t["rim"][:], t["t1"][:], t["den"][:], ALU.mult)
    ts(t["nrim"][:], t["rim"][:], -1.0, ALU.mult)
    tt(t["t1"][:], PRE[:, 8, :], PRE[:, 8, :], ALU.mult)
    tt(t["t2"][:], PIM[:, 8, :], PIM[:, 8, :], ALU.mult)
    tt(t["m2"][:], t["t1"][:], t["t2"][:], ALU.add)
    P.dve(lambda e: e.reciprocal(out=t["m2"][:], in_=t["m2"][:]), r=K, w=K)
    tt(t["ire"][:], PRE[:, 8, :], t["m2"][:], ALU.mult)
    tt(t["niim"][:], PIM[:, 8, :], t["m2"][:], ALU.mult)
    ts(t["iim"][:], t["niim"][:], -1.0, ALU.mult)
    for gc in range(GC):
        gs = slice(gc, gc + 1)
        ts(t["t16a"][:, gc, :], t["bre"][:, gc, :], t["rre"][:, gs], ALU.mult)
        ts(t["t16b"][:, gc, :], t["bim"][:, gc, :], t["rre"][:, gs], ALU.mult)
    for gc in range(GC):
        gs = slice(gc, gc + 1)
        stt(t["bbre"][:, gc, :], t["bim"][:, gc, :], t["nrim"][:, gs], t["t16a"][:, gc, :], ALU.mult, ALU.add)
        stt(t["bbim"][:, gc, :], t["bre"][:, gc, :], t["rim"][:, gs], t["t16b"][:, gc, :], ALU.mult, ALU.add)
    KB = ["prepB"]
    for gc in range(GC):
        gs = slice(gc, gc + 1)
        for i in range(8):
            isl = slice(i * 16, (i + 1) * 16)
            ts(X1[:, gc, isl], t["bbre"][:, gc, :], PRE[:, 7 - i, gs], ALU.mult, r=K, w=KB)
            ts(X2[:, gc, isl], t["bbim"][:, gc, :], PRE[:, 7 - i, gs], ALU.mult, r=K, w=KB)
            ts(X3[:, gc, isl], t["cre"][:, gc, :], PRE[:, i + 1, gs], ALU.mult, r=K, w=KB)
            ts(X4[:, gc, isl], t["cim"][:, gc, :], PRE[:, i + 1, gs], ALU.mult, r=K, w=KB)
    KC = ["prepC"]
    for gc in range(GC):
        gs = slice(gc, gc + 1)
        for i in range(8):
            isl = slice(i * 16, (i + 1) * 16)
            stt(BTre[:, gc, isl], t["bbim"][:, gc, :], NPIM[:, 7 - i, gs], X1[:, gc, isl], ALU.mult, ALU.add, r=K + KB, w=KC)
            stt(BTim[:, gc, isl], t["bbre"][:, gc, :], PIM[:, 7 - i, gs], X2[:, gc, isl], ALU.mult, ALU.add, r=K + KB, w=KC)
            stt(CTre[:, gc, isl], t["cim"][:, gc, :], NPIM[:, i + 1, gs], X3[:, gc, isl], ALU.mult, ALU.add, r=K + KB, w=KC)
            stt(CTim[:, gc, isl], t["cre"][:, gc, :], PIM[:, i + 1, gs], X4[:, gc, isl], ALU.mult, ALU.add, r=K + KB, w=KC)
    KD = ["prepD"]
    for gc in range(GC):
        gs = slice(gc, gc + 1)
        ts(X1[:, gc, :], BTre[:, gc, :], t["ire"][:, gs], ALU.mult, r=K + KC, w=KD)
        ts(X2[:, gc, :], BTim[:, gc, :], t["ire"][:, gs], ALU.mult, r=K + KC, w=KD)
    KE = ["prepE"]
    for gc in range(GC):
        gs = slice(gc, gc + 1)
        stt(BPre[:, gc, :], BTim[:, gc, :], t["niim"][:, gs], X1[:, gc, :], ALU.mult, ALU.add, r=K + KC + KD, w=KE)
        stt(X3[:, gc, :], BTre[:, gc, :], t["iim"][:, gs], X2[:, gc, :], ALU.mult, ALU.add, r=K + KC + KD, w=KE)
    ts(BPimn[:], X3[:], -1.0, ALU.mult, r=KE, w=KE)
    P.act(lambda e: e.activation(out=Cre[:], in_=CTre[:], func=AF.Copy), r=KC, w=["Cw"])
    P.act(lambda e: e.activation(out=Cimn[:], in_=CTim[:], func=AF.Copy, scale=-1.0), r=KC, w=["Cw"])
    for g in range(G):
        hf, gc = g // 2, g % 2
        hs = slice(hf * 64, (hf + 1) * 64)
        for n_, (src, dst) in enumerate(((BTre, Bre), (BTim, Bim))):
            ps = pss[n_]; pk = "ps%d" % n_
            P.pe(lambda e, src=src, gc=gc, hs=hs, ps=ps: e.matmul(ps[:, 0:64], lhsT=src[hs, gc, :], rhs=ident[hs, hs], start=True, stop=True), r=KC + K, w=[pk])
            P.act(lambda e, dst=dst, g=g, hs=hs, ps=ps: e.activation(out=dst[:, g, hs], in_=ps[:, 0:64], func=AF.Copy), r=[pk, "Bpad"], w=["Bpad"])
        ps = pss[2]
        P.pe(lambda e, gc=gc, hs=hs, ps=ps: e.matmul(ps[:, 0:128], lhsT=BPre[hs, gc, :], rhs=CTre[hs, gc, :], start=True, stop=False), r=KE + KC, w=["ps2"])
        P.pe(lambda e, gc=gc, hs=hs, ps=ps: e.matmul(ps[:, 0:128], lhsT=BPimn[hs, gc, :], rhs=CTim[hs, gc, :], start=False, stop=True), r=KE + KC, w=["ps2"])
        P.dve(lambda e, ps=ps: e.tensor_tensor(out=dtmp[:], in0=ps[:, 0:128], in1=cmask[:], op=ALU.mult), r=K + ["ps2"], w=["dtmp"])
        stt(Dm[:, g, :], ident[:], dcol[:, g:g + 1], dtmp[:], ALU.mult, ALU.add, r=K + ["dtmp"], w=["Dm"])
    tt(PWRE[:, 0, :], PRE[:, 8, :], PRE[:, 8, :], ALU.max)
    tt(PWIM[:, 0, :], PIM[:, 8, :], PIM[:, 8, :], ALU.max)
    for k in range(1, NL):
        tt(t["t1"][:], PWRE[:, k - 1, :], PWRE[:, k - 1, :], ALU.mult)
        tt(t["t2"][:], PWIM[:, k - 1, :], PWIM[:, k - 1, :], ALU.mult)
        tt(PWRE[:, k, :], t["t1"][:], t["t2"][:], ALU.subtract)
        tt(t["t1"][:], PWRE[:, k - 1, :], PWIM[:, k - 1, :], ALU.mult)
        ts(PWIM[:, k, :], t["t1"][:], 2.0, ALU.mult)
    ts(NPWIM[:], PWIM[:], -1.0, ALU.mult)

    pi = 0
    NT = NB // 512
    for gc in range(GC):
        for nt in range(NT):
            sl = slice(nt * 512, (nt + 1) * 512)
            for wsrc, H in ((Bre, Hre), (Bim, Him)):
                ps = pss[pi % 4]; pk = "ps%d" % (pi % 4); pi += 1
                P.pe(lambda e, ps=ps, wsrc=wsrc, gc=gc, sl=sl: e.matmul(ps[:], lhsT=wsrc[:, gc, :], rhs=ubb[:, gc, sl], start=True, stop=False), r=["Bpad", "ubb%d" % gc], w=[pk])
                P.pe(lambda e, ps=ps, wsrc=wsrc, gc=gc, sl=sl: e.matmul(ps[:], lhsT=wsrc[:, 2 + gc, :], rhs=ubb[:, 2 + gc, sl], start=False, stop=True), r=["Bpad", "ubb%d" % (2 + gc)], w=[pk])
                P.act(lambda e, ps=ps, H=H, gc=gc, sl=sl: e.activation(out=H[:, gc, sl], in_=ps[:], func=AF.Copy), r=[pk], w=["H%d" % gc])
    def views(g, s):
        hr = Hre[:, g, :]; hi = Him[:, g, :]
        vr = hr.rearrange("p (m t) -> p m t", t=2 * s); vi = hi.rearrange("p (m t) -> p m t", t=2 * s)
        return vr[:, :, 2 * s - 1], vr[:, :, s - 1], vi[:, :, 2 * s - 1], vi[:, :, s - 1]

    for k in range(NL):
        s = 1 << k
        for ph in range(2):
            for g in range(GC):
                hk = ["H%d" % g]
                tr_, sr_, ti_, si_ = views(g, s)
                a_r, a_i, na_i = PWRE[:, k, g:g + 1], PWIM[:, k, g:g + 1], NPWIM[:, k, g:g + 1]
                if ph == 0:
                    stt(tr_, sr_, a_r, tr_, ALU.mult, ALU.add, r=K + hk, w=["Hr%d" % g])
                    stt(ti_, si_, a_r, ti_, ALU.mult, ALU.add, r=K + hk, w=["Hi%d" % g])
                else:
                    stt(tr_, si_, na_i, tr_, ALU.mult, ALU.add, r=K + ["Hr%d" % g, "Hi%d" % g], w=hk + ["Hr%d" % g])
                    stt(ti_, sr_, a_i, ti_, ALU.mult, ALU.add, r=K + ["Hr%d" % g, "Hi%d" % g], w=hk + ["Hi%d" % g])
    for g in range(GC):
        hk = ["H%d" % g, "Hr%d" % g, "Hi%d" % g]
        P.dve(lambda e, g=g: e.memset(Hre[:, g, NB - 1:NB], 0.0), r=hk, w=hk)
        P.dve(lambda e, g=g: e.memset(Him[:, g, NB - 1:NB], 0.0), r=hk, w=hk)
    for k in range(NL - 1, -1, -1):
        s = 1 << k
        m = NB // (2 * s)
        for ph in range(4):
            for g in range(GC):
                hk = ["H%d" % g, "Hr%d" % g, "Hi%d" % g]
                TR, TI = Tres[g], Tims[g]
                Rr, Lr, Ri, Li = views(g, s)
                a_r, a_i, na_i = PWRE[:, k, g:g + 1], PWIM[:, k, g:g + 1], NPWIM[:, k, g:g + 1]
                if ph == 0:
                    stt(TR[:, 0:m], Rr, a_r, Lr, ALU.mult, ALU.add, r=K + hk, w=["Tr%d" % g])
                    stt(TI[:, 0:m], Ri, a_r, Li, ALU.mult, ALU.add, r=K + hk, w=["Ti%d" % g])
                elif ph == 1:
                    stt(TR[:, 0:m], Ri, na_i, TR[:, 0:m], ALU.mult, ALU.add, r=K + hk + ["Tr%d" % g], w=["Tr%d" % g])
                    stt(TI[:, 0:m], Rr, a_i, TI[:, 0:m], ALU.mult, ALU.add, r=K + hk + ["Ti%d" % g], w=["Ti%d" % g])
                elif ph == 2:
                    if m >= 256:
                        P.act(lambda e, Lr=Lr, Rr=Rr: e.activation(out=Lr, in_=Rr, func=AF.Copy), r=hk + ["Tr%d" % g, "Ti%d" % g], w=["L%d" % g])
                        P.act(lambda e, Li=Li, Ri=Ri: e.activation(out=Li, in_=Ri, func=AF.Copy), r=hk + ["Tr%d" % g, "Ti%d" % g], w=["L%d" % g])
                    else:
                        P.dve(lambda e, Lr=Lr, Rr=Rr: e.tensor_copy(out=Lr, in_=Rr), r=hk + ["Tr%d" % g, "Ti%d" % g], w=["L%d" % g])
                        P.dve(lambda e, Li=Li, Ri=Ri: e.tensor_copy(out=Li, in_=Ri), r=hk + ["Tr%d" % g, "Ti%d" % g], w=["L%d" % g])
                else:
                    P.dve(lambda e, Rr=Rr, m=m, TR=TR: e.tensor_copy(out=Rr, in_=TR[:, 0:m]), r=["Tr%d" % g, "L%d" % g], w=hk)
                    P.dve(lambda e, Ri=Ri, m=m, TI=TI: e.tensor_copy(out=Ri, in_=TI[:, 0:m]), r=["Ti%d" % g, "L%d" % g], w=hk)
    for g in range(GC):
        hk = ["H%d" % g, "Hr%d" % g, "Hi%d" % g, "L%d" % g]
        P.act(lambda e, g=g: e.activation(out=Hbre[:, g, :], in_=Hre[:, g, :], func=AF.Copy), r=hk, w=["Hb%d" % g])
        P.act(lambda e, g=g: e.activation(out=Hbim[:, g, :], in_=Him[:, g, :], func=AF.Copy), r=hk, w=["Hb%d" % g])
    for g in range(G):
        hf, gc = g // 2, g % 2
        hs = slice(hf * 64, (hf + 1) * 64)
        o = yo[g % 2]; ok = "yo%d" % (g % 2)
        for nt in range(NT):
            sl = slice(nt * 512, (nt + 1) * 512)
            ps = pss[pi % 4]; pk = "ps%d" % (pi % 4); pi += 1
            rr = ["Cw", "Dm", "ubb%d" % g, "Hb%d" % gc]
            P.pe(lambda e, ps=ps, gc=gc, hs=hs, sl=sl: e.matmul(ps[:], lhsT=Cre[hs, gc, :], rhs=Hbre[hs, gc, sl], start=True, stop=False), r=rr, w=[pk])
            P.pe(lambda e, ps=ps, gc=gc, hs=hs, sl=sl: e.matmul(ps[:], lhsT=Cimn[hs, gc, :], rhs=Hbim[hs, gc, sl], start=False, stop=False), r=rr, w=[pk])
            P.pe(lambda e, ps=ps, g=g, sl=sl: e.matmul(ps[:], lhsT=Dm[:, g, :], rhs=ubb[:, g, sl], start=False, stop=True), r=rr, w=[pk])
            P.act(lambda e, ps=ps, o=o, sl=sl: e.activation(out=o[:, sl], in_=ps[:], func=AF.Copy), r=[pk], w=[ok])
        P.dma(yb[g], o[:], r=[ok], w=["out"], is_out=True)
    return P.finish()


def s5_mixer_dev(uT, lam_re, lam_im, log_dt, b_re, b_im, c_re, c_im, d_skip):
    T = uT.shape[1]
    NB = T // 8
    key = ("s5", NB)
    if key not in _CACHE:
        _CACHE[key] = build_s5(NB)
    ub = uT.reshape(32, 16, NB, 8).transpose(0, 3, 1, 2).reshape(32, 128, NB)
    ii, jj = np.arange(128) // 16, np.arange(128) // 16
    cmask = (jj[None, :] >= ii[:, None]).astype(np.float32)
    ident = np.eye(128, dtype=np.float32)
    maps = []

    def pl(a):
        sh = a.shape[2:]
        a = a.reshape((2, 2, 64) + sh)
        a = np.moveaxis(a, 1, 2)
        return np.ascontiguousarray(a.reshape((128, 2) + sh))

    for c in range(NCORE):
        gs = slice(4 * c, 4 * c + 4)
        maps.append({
            "ub": np.ascontiguousarray(ub[gs]),
            "lre": pl(lam_re[gs]), "lim": pl(lam_im[gs]),
            "ldt": pl(np.ascontiguousarray(np.broadcast_to(log_dt[gs][:, None], (4, 64)))),
            "bre": pl(b_re[gs]), "bim": pl(b_im[gs]),
            "cre": pl(np.ascontiguousarray(c_re[gs].transpose(0, 2, 1))), "cim": pl(np.ascontiguousarray(c_im[gs].transpose(0, 2, 1))),
            "dcol": np.ascontiguousarray(np.tile(d_skip.reshape(32, 16)[gs].T, (8, 1))),
            "cmask": cmask, "ident": ident,
        })
    res = run(_CACHE[key], maps)
    yb = np.concatenate([r["yb"] for r in res], axis=0)
    return np.ascontiguousarray(yb.reshape(32, 8, 16, NB).transpose(0, 2, 3, 1).reshape(512, T))


def build_resln(N, D):
    P = Prog()
    x = P.dram_in("x", [N, D]); m = P.dram_in("m", [N, D]); gb = P.dram_in("gb", [128, 2, D])
    y = P.dram_out("y", [N, D])
    gbt = P.sb("gbt", [128, 2, D], F32)
    P.dma(gbt[:], gb, w=["gb"], q="gpsimd")
    NB_ = 6
    xt = [P.sb("xt%d" % i, [128, D], F32) for i in range(NB_)]
    mt = [P.sb("mt%d" % i, [128, D], F32) for i in range(NB_)]
    yt = [P.sb("yt%d" % i, [128, D], F32) for i in range(NB_)]
    jk = [P.sb("jk%d" % i, [128, D], F32) for i in range(2)]
    st = [P.sb("st%d" % i, [128, 8], F32) for i in range(NB_)]
    NTL = N // 128
    PF = 4

    def names(i):
        b = i % NB_
        return xt[b], mt[b], yt[b], st[b], "xt%d" % b, "mt%d" % b, "yt%d" % b, "st%d" % b

    def loads(i):
        X, M, Y, S, xk, mk, yk, sk = names(i)
        rs = slice(i * 128, (i + 1) * 128)
        P.dma(X[:], x[rs, :], w=[xk], q="sync")
        P.dma(M[:], m[rs, :], w=[mk], q="act")

    def stage_a(i):
        X, M, Y, S, xk, mk, yk, sk = names(i)
        P.dve(lambda e: e.memset(S[:], 0.0), w=[sk])
        P.dve(lambda e: e.scalar_tensor_tensor(out=X[:], in0=X[:], scalar=float(ALPHA), in1=M[:], op0=ALU.mult, op1=ALU.add), r=[xk, mk], w=[xk])
        P.act(lambda e: e.activation(out=jk[0][:], in_=X[:], func=AF.Copy, accum_out=S[:, 0:1]), r=[xk, sk], w=["jk0", sk])
        P.act(lambda e: e.activation(out=jk[1][:], in_=X[:], func=AF.Square, accum_out=S[:, 1:2]), r=[xk, sk], w=["jk1", sk])

    def stage_b(i):
        X, M, Y, S, xk, mk, yk, sk = names(i)
        P.dve(lambda e: e.tensor_scalar(out=S[:, 2:4], in0=S[:, 0:2], scalar1=1.0 / D, scalar2=None, op0=ALU.mult), r=[sk], w=[sk])
        P.dve(lambda e: e.tensor_tensor(out=S[:, 4:5], in0=S[:, 2:3], in1=S[:, 2:3], op=ALU.mult), r=[sk], w=[sk])
        P.dve(lambda e: e.tensor_tensor(out=S[:, 4:5], in0=S[:, 3:4], in1=S[:, 4:5], op=ALU.subtract), r=[sk], w=[sk])
        P.dve(lambda e: e.tensor_scalar(out=S[:, 4:5], in0=S[:, 4:5], scalar1=LN_EPS, scalar2=None, op0=ALU.add), r=[sk], w=[sk])
        P.act(lambda e: e.activation(out=S[:, 4:5], in_=S[:, 4:5], func=AF.Sqrt), r=[sk], w=[sk])

    def stage_c(i):
        X, M, Y, S, xk, mk, yk, sk = names(i)
        P.dve(lambda e: e.reciprocal(out=S[:, 5:6], in_=S[:, 4:5]), r=[sk], w=[sk])
        P.dve(lambda e: e.scalar_tensor_tensor(out=S[:, 6:7], in0=S[:, 2:3], scalar=-1.0, in1=S[:, 5:6], op0=ALU.mult, op1=ALU.mult), r=[sk], w=[sk])
        P.act(lambda e: e.activation(out=Y[:], in_=X[:], func=AF.Identity, scale=S[:, 5:6], bias=S[:, 6:7]), r=[xk, sk], w=[yk])

    def stage_d(i):
        X, M, Y, S, xk, mk, yk, sk = names(i)
        rs = slice(i * 128, (i + 1) * 128)
        P.dve(lambda e: e.tensor_tensor(out=Y[:], in0=Y[:], in1=gbt[:, 0, :], op=ALU.mult), r=[yk, "gb"], w=[yk])
        P.dve(lambda e: e.tensor_tensor(out=Y[:], in0=Y[:], in1=gbt[:, 1, :], op=ALU.add), r=[yk, "gb"], w=[yk])
        P.dma(y[rs, :], Y[:], r=[yk], w=["out"], is_out=True, q="gpsimd")

    for i in range(min(PF, NTL)):
        loads(i)
    for s_ in range(NTL + 3):
        if 0 <= s_ - 3 < NTL:
            stage_d(s_ - 3)
        if 0 <= s_ - 2 < NTL:
            stage_c(s_ - 2)
        if 0 <= s_ - 1 < NTL:
            stage_b(s_ - 1)
        if s_ < NTL:
            stage_a(s_)
            if s_ + PF < NTL:
                loads(s_ + PF)
    return P.finish()


def resln(x_tm, m_tm, g, b):
    T, D = x_tm.shape
    N = T // NCORE
    key = ("resln", N, D)
    if key not in _CACHE:
        _CACHE[key] = build_resln(N, D)
    gb = np.ascontiguousarray(np.broadcast_to(np.stack([g, b])[None], (128, 2, D))).astype(np.float32)
    maps = [{"x": np.ascontiguousarray(x_tm[c * N:(c + 1) * N]), "m": np.ascontiguousarray(m_tm[c * N:(c + 1) * N]), "gb": gb} for c in range(NCORE)]
    res = run(_CACHE[key], maps)
    return np.concatenate([r["y"] for r in res], axis=0)


def build_conv(N):
    P = Prog()
    bT = P.dram_in("bT", [512, N]); cT = P.dram_in("cT", [512, N + 2]); xT = P.dram_in("xT", [512, N + 2])
    w = P.dram_in("w", [128, 4, 3])
    yT = P.dram_out("yT", [512, N])
    wt = P.sb("wt", [128, 4, 3], F32)
    P.dma(wt[:], w, w=["w"])
    for a in range(4):
        bt = P.sb("bt%d" % a, [128, N], F32); ct = P.sb("ct%d" % a, [128, N + 2], F32); xt = P.sb("xt%d" % a, [128, N + 2], F32)
        acc = P.sb("acc%d" % a, [128, N], F32)
        rs = slice(a * 128, (a + 1) * 128)
        k = "c%d" % a
        P.dma(bt[:], bT[rs, :], w=[k + "b"]); P.dma(ct[:], cT[rs, :], w=[k]); P.dma(xt[:], xT[rs, :], w=[k + "x"])
        P.dve(lambda e, ct=ct, xt=xt: e.tensor_tensor(out=ct[:], in0=ct[:], in1=xt[:], op=ALU.mult), r=[k, k + "x"], w=[k])
        P.dve(lambda e, ct=ct, acc=acc, a=a: e.tensor_scalar(out=acc[:], in0=ct[:, 0:N], scalar1=wt[:, a, 0:1], scalar2=None, op0=ALU.mult), r=[k, "w"], w=[k + "a"])
        for j in (1, 2):
            P.dve(lambda e, ct=ct, acc=acc, a=a, j=j: e.scalar_tensor_tensor(out=acc[:], in0=ct[:, j:j + N], scalar=wt[:, a, j:j + 1], in1=acc[:], op0=ALU.mult, op1=ALU.add), r=[k, "w", k + "a"], w=[k + "a"])
        P.dve(lambda e, acc=acc, bt=bt: e.tensor_tensor(out=acc[:], in0=acc[:], in1=bt[:], op=ALU.mult), r=[k + "a", k + "b"], w=[k + "a"])
        P.dma(yT[rs, :], acc[:], r=[k + "a"], w=["out"], is_out=True)
    return P.finish()


def conv_dev(bT, cT, xT, cw):
    T = bT.shape[1]
    N = T // NCORE
    key = ("conv", N)
    if key not in _CACHE:
        _CACHE[key] = build_conv(N)
    cp = np.concatenate([np.zeros((512, 2), np.float32), cT], axis=1)
    xp = np.concatenate([np.zeros((512, 2), np.float32), xT], axis=1)
    w = np.ascontiguousarray(cw.reshape(3, 4, 128).transpose(2, 1, 0))
    maps = [{"bT": np.ascontiguousarray(bT[:, c * N:(c + 1) * N]), "cT": np.ascontiguousarray(cp[:, c * N:(c + 1) * N + 2]),
             "xT": np.ascontiguousarray(xp[:, c * N:(c + 1) * N + 2]), "w": w} for c in range(NCORE)]
    res = run(_CACHE[key], maps)
    return np.concatenate([r["yT"] for r in res], axis=1)


def build_pool(N):
    P = Prog()
    zT = P.dram_in("zT", [512, N + 16]); invc = P.dram_in("invc", [128, 4, N])
    oT = P.dram_out("oT", [512, N])
    for gi in range(4):
        z = P.sb("z%d" % gi, [128, N + 16], F32)
        sa = P.sb("sa%d" % gi, [128, N + 16], F32); sb_ = P.sb("sb%d" % gi, [128, N + 16], F32)
        ic = P.sb("ic%d" % gi, [128, N], F32)
        rs = slice(gi * 128, (gi + 1) * 128)
        k = "p%d" % gi
        P.dma(z[:], zT[rs, :], w=[k + "z"]); P.dma(ic[:], invc[:, gi, :], w=[k + "i"])
        cur, curk = z, k + "z"
        bufs = [(sa, k + "a"), (sb_, k + "b")]
        for step in range(gi + 1):
            sh = 1 << step
            nxt, nk = bufs[step % 2]
            P.dve(lambda e, cur=cur, nxt=nxt, sh=sh: e.tensor_tensor(out=nxt[:, sh:], in0=cur[:, sh:], in1=cur[:, 0:N + 16 - sh], op=ALU.add), r=[curk], w=[nk])
            cur, curk = nxt, nk
        o, okey = bufs[(gi + 1) % 2]
        P.dve(lambda e, cur=cur, o=o, ic=ic: e.tensor_tensor(out=o[:, 16:], in0=cur[:, 16:], in1=ic[:], op=ALU.mult), r=[curk, k + "i"], w=[okey])
        P.dve(lambda e, o=o, z=z: e.tensor_tensor(out=o[:, 16:], in0=o[:, 16:], in1=z[:, 16:], op=ALU.subtract), r=[okey, k + "z"], w=[okey])
        P.dma(oT[rs, :], o[:, 16:], r=[okey], w=["out"], is_out=True)
    return P.finish()


def pool_dev(zT):
    T = zT.shape[1]
    N = T // NCORE
    key = ("pool", N)
    if key not in _CACHE:
        _CACHE[key] = build_pool(N)
    zp = np.concatenate([np.zeros((512, 16), np.float32), zT], axis=1)
    t = np.arange(T)
    inv = np.stack([1.0 / np.minimum(t + 1, w) for w in (2, 4, 8, 16)]).astype(np.float32)
    maps = []
    for c in range(NCORE):
        ic = np.ascontiguousarray(np.broadcast_to(inv[None, :, c * N:(c + 1) * N], (128, 4, N)))
        maps.append({"zT": np.ascontiguousarray(zp[:, c * N:(c + 1) * N + 16]), "invc": ic})
    res = run(_CACHE[key], maps)
    return np.concatenate([r["oT"] for r in res], axis=1)


def build_attn(T):
    P = Prog()
    NBK = T // 128
    qT = P.dram_in("qT", [64, T]); kT = P.dram_in("kT", [64, T]); v = P.dram_in("v", [T, 64])
    bias = P.dram_in("bias", [128, 5, 128])
    o_tm = P.dram_out("o", [T, 64])
    qb = P.sb("qb", [64, T], BF16); kb = P.sb("kb", [64, T], BF16); vb = P.sb("vb", [128, NBK, 65], BF16)
    bf = P.sb("bf", [128, 5, 128], F32); eb = P.sb("eb", [128, 5, 128], BF16)
    P.dve(lambda e: e.memset(vb[:], 1.0), w=["vb"])
    P.dma(bf[:], bias, w=["bf"])
    make_stage(P, 2048, n=4)
    for c0 in range(0, T, 2048):
        c1 = min(T, c0 + 2048)
        stage_cast(P, kb[:, c0:c1], kT[:, c0:c1], 64, c1 - c0, "kb")
        stage_cast(P, qb[:, c0:c1], qT[:, c0:c1], 64, c1 - c0, "qb", scale=0.125)
    vv = v.rearrange("(n p) d -> p n d", p=128)
    for j0 in range(0, NBK, 16):
        j1 = min(NBK, j0 + 16)
        P.dma(vb[:, j0:j1, 0:64], vv[:, j0:j1, :], w=["vb"], cast=True)
    P.act(lambda e: e.activation(out=eb[:], in_=bf[:], func=AF.Exp), r=["bf"], w=["eb"])
    NBUF = 3
    psS = [P.ps("psS%d" % i, [128, 5, 128]) for i in range(2)]
    psO = [P.ps("psO%d" % i, [128, 512]) for i in range(2)]
    pf = [P.sb("pf%d" % i, [128, 5, 128], BF16) for i in range(NBUF)]
    pt = [P.sb("pt%d" % i, [128, 5, 128], BF16) for i in range(NBUF)]
    rec = [P.sb("rec%d" % i, [128, 1], F32) for i in range(NBUF)]
    ob = [P.sb("ob%d" % i, [128, 16, 64], F32) for i in range(2)]
    o_v = o_tm.rearrange("(n p) d -> p n d", p=128)
    def names(m):
        b = m % 2
        return (psS[b], psO[b], pf[m % NBUF], pt[m % NBUF], rec[m % NBUF],
                "S%d" % b, "O%d" % b, "pf%d" % (m % NBUF), "pt%d" % (m % NBUF), "rc%d" % (m % NBUF))

    def front(m):
        S, O, PF, PT, RC, sk, okk, fk, pk, rk = names(m)
        qs = slice(m * 128, (m + 1) * 128)
        i0_ = max(0, 4 - m)
        for i in range(i0_, 5):
            kt = m - 4 + i
            P.pe(lambda e, i=i, kt=kt: e.matmul(S[:, i, :], lhsT=kb[:, kt * 128:(kt + 1) * 128], rhs=qb[:, qs], start=True, stop=True), r=["kb", "qb"], w=[sk])
        if i0_ < 4:
            P.act(lambda e: e.activation(out=PF[:, i0_:4, :], in_=S[:, i0_:4, :], func=AF.Exp), r=[sk], w=[fk])
        P.act(lambda e: e.activation(out=PF[:, 4, :], in_=S[:, 4, :], func=AF.Exp), r=[sk], w=[fk])
        P.dve(lambda e: e.tensor_tensor(out=PT[:, i0_:5, :], in0=PF[:, i0_:5, :], in1=eb[:, i0_:5, :], op=ALU.mult), r=[fk, "eb"], w=[pk])

    def back(m):
        S, O, PF, PT, RC, sk, okk, fk, pk, rk = names(m)
        OB = ob[(m // 16) % 2]; obk = "ob%d" % ((m // 16) % 2)
        val = list(range(max(0, 4 - m), 5))
        for n, i in enumerate(val):
            kt = m - 4 + i
            P.pe(lambda e, i=i, kt=kt, n=n: e.matmul(O[:, 0:65], lhsT=PT[:, i, :], rhs=vb[:, kt, :], start=(n == 0), stop=(n == len(val) - 1)), r=["vb", pk], w=[okk])
        P.dve(lambda e: e.reciprocal(out=RC[:], in_=O[:, 64:65]), r=[okk], w=[rk])
        c = m % 16
        P.dve(lambda e: e.tensor_scalar(out=OB[:, c, :], in0=O[:, 0:64], scalar1=RC[:, 0:1], scalar2=None, op0=ALU.mult), r=[okk, rk], w=[obk])
        if m % 16 == 15:
            g0 = (m // 16) * 16
            P.dma(o_v[:, g0:g0 + 16, :], OB[:], r=[obk], w=["out"], is_out=True)

    for s_ in range(NBK + 1):
        if s_ < NBK:
            front(s_)
        if s_ >= 1:
            back(s_ - 1)
    return P.finish()


def attn_dev(qT, kT, vT, rel_bias):
    T = qT.shape[1]
    key = ("attn", T)
    if key not in _CACHE:
        _CACHE[key] = build_attn(T)
    kk = np.arange(640)[:, None]; qq = np.arange(128)[None, :]
    qc = qq // 64; kc = kk // 64
    dist = (qq - (kk - 512))
    rel = np.clip(dist, -128, 128) + 128
    band = kc - qc
    valid = (band >= 0) & (band <= 8)
    maps = []
    for h in range(NCORE):
        b2 = np.where(valid, rel_bias[h][rel], np.float32(-30000.0)).astype(np.float32)
        b2 = np.ascontiguousarray(b2.reshape(5, 128, 128).transpose(1, 0, 2))
        hs = slice(h * 64, (h + 1) * 64)
        maps.append({"qT": np.ascontiguousarray(qT[hs]), "kT": np.ascontiguousarray(kT[hs]),
                     "v": np.ascontiguousarray(vT[hs].T), "bias": b2})
    res = run(_CACHE[key], maps)
    return np.ascontiguousarray(np.concatenate([r["o"].T for r in res], axis=0))


def _outproj(P, mixin, Wout, outT, N, pss, pi, mixkeys):
    NT = N // 512
    Wv = Wout.rearrange("(kt p) m -> p kt m", p=128)
    wo = [P.sb("wo%d" % i, [128, 8, 128], BF16) for i in range(3)]
    ot = [P.sb("oto%d" % i, [128, N], F32) for i in range(2)]
    for m in range(8):
        s_ = m % 3
        o = ot[m % 2]; ok = "oto%d" % (m % 2)
        P.dma(wo[s_][:], Wv[:, :, m * 128:(m + 1) * 128], w=["wo%d" % s_], cast=True)
        for nt in range(NT):
            sl = slice(nt * 512, (nt + 1) * 512)
            ps = pss[pi % len(pss)]; pk = "ps%d" % (pi % len(pss)); pi += 1
            for kt in range(8):
                P.pe(lambda e, ps=ps, s_=s_, kt=kt, sl=sl: e.matmul(ps[:], lhsT=wo[s_][:, kt, :], rhs=mixin[:, kt, sl], start=(kt == 0), stop=(kt == 7)), r=["wo%d" % s_, mixkeys[kt]], w=[pk])
            P.act(lambda e, o=o, ps=ps, sl=sl: e.activation(out=o[:, sl], in_=ps[:], func=AF.Identity), r=[pk], w=[ok])
        P.dma(outT[m * 128:(m + 1) * 128, :], o[:], r=[ok], w=["out"], is_out=True)
    return pi


def build_even_tail(N):
    P = Prog()
    NT = N // 512
    yS = P.dram_in("yS", [512, N]); bT = P.dram_in("bT", [512, N]); cT = P.dram_in("cT", [512, N + 2]); xT = P.dram_in("xT", [512, N + 2])
    cw = P.dram_in("cw", [128, 4, 3]); Wg = P.dram_in("Wg", [512, 512]); bg = P.dram_in("bg", [128, 4]); Wout = P.dram_in("Wout", [1024, 1024])
    outT = P.dram_out("outT", [1024, N])
    mixin = P.sb("mixin", [128, 8, N], BF16)
    mixkeys = ["mix%d" % k for k in range(8)]
    gf = P.sb("gf", [128, 4, N], F32); inb = P.sb("inb", [128, 4, N], BF16)
    xs = [P.sb("xs%d" % i, [128, N], F32) for i in range(2)]
    tq = P.sb("tq", [128, N], F32)
    bt_ = P.sb("bgt", [128, 4], F32); wt = P.sb("cwt", [128, 4, 3], F32)
    P.dma(bt_[:], bg, w=["bg"]); P.dma(wt[:], cw, w=["cw"])
    pss = [P.ps("ps%d" % i, [128, 512]) for i in range(4)]
    for kt in range(4):
        x = xs[kt % 2]; xk = "xs%d" % (kt % 2)
        P.dma(x[:], yS[kt * 128:(kt + 1) * 128, :], w=[xk])
        P.dve(lambda e, x=x: e.tensor_tensor(out=tq[:], in0=x[:], in1=x[:], op=ALU.mult), r=[xk], w=["tq"])
        P.dve(lambda e: e.tensor_scalar(out=tq[:], in0=tq[:], scalar1=0.044715, scalar2=1.0, op0=ALU.mult, op1=ALU.add), r=["tq"], w=["tq"])
        P.dve(lambda e, x=x: e.tensor_tensor(out=tq[:], in0=tq[:], in1=x[:], op=ALU.mult), r=["tq", xk], w=["tq"])
        P.act(lambda e: e.activation(out=tq[:], in_=tq[:], func=AF.Sigmoid, scale=1.5957691216057308), r=["tq"], w=["tq"])
        P.dve(lambda e, x=x, kt=kt: e.tensor_tensor(out=gf[:, kt, :], in0=tq[:], in1=x[:], op=ALU.mult), r=["tq", xk], w=["gf%d" % kt])
        P.act(lambda e, kt=kt: e.activation(out=inb[:, kt, :], in_=gf[:, kt, :], func=AF.Copy), r=["gf%d" % kt], w=["inb%d" % kt])
    cb_ = [P.sb("cvb%d" % i, [128, N], F32) for i in range(2)]
    cc_ = [P.sb("cvc%d" % i, [128, N + 2], F32) for i in range(2)]
    cx_ = [P.sb("cvx%d" % i, [128, N + 2], F32) for i in range(2)]
    ca_ = [P.sb("cva%d" % i, [128, N], F32) for i in range(2)]
    for a in range(4):
        b = a % 2
        bt, ct, xt, acc = cb_[b], cc_[b], cx_[b], ca_[b]
        rs = slice(a * 128, (a + 1) * 128)
        k = "cv%d" % b
        P.dma(bt[:], bT[rs, :], w=[k + "b"], q="act"); P.dma(ct[:], cT[rs, :], w=[k], q="sync"); P.dma(xt[:], xT[rs, :], w=[k + "x"], q="act")
        P.dve(lambda e, ct=ct, xt=xt: e.tensor_tensor(out=ct[:], in0=ct[:], in1=xt[:], op=ALU.mult), r=[k, k + "x"], w=[k])
        P.dve(lambda e, ct=ct, acc=acc, a=a: e.tensor_scalar(out=acc[:], in0=ct[:, 0:N], scalar1=wt[:, a, 0:1], scalar2=None, op0=ALU.mult), r=[k, "cw"], w=[k + "a"])
        for j in (1, 2):
            P.dve(lambda e, ct=ct, acc=acc, a=a, j=j: e.scalar_tensor_tensor(out=acc[:], in0=ct[:, j:j + N], scalar=wt[:, a, j:j + 1], in1=acc[:], op0=ALU.mult, op1=ALU.add), r=[k, "cw", k + "a"], w=[k + "a"])
        P.dve(lambda e, acc=acc, bt=bt, a=a: e.tensor_tensor(out=mixin[:, 4 + a, :], in0=acc[:], in1=bt[:], op=ALU.mult), r=[k + "a", k + "b"], w=[mixkeys[4 + a]])
    Wgv = Wg.rearrange("(kt p) m -> p kt m", p=128)
    wg = [P.sb("wg%d" % i, [128, 4, 128], BF16) for i in range(2)]
    sg = [P.sb("sg%d" % i, [128, 512], F32) for i in range(2)]
    pi = 0
    for m in range(4):
        s_ = m % 2
        P.dma(wg[s_][:], Wgv[:, :, m * 128:(m + 1) * 128], w=["wg%d" % s_], cast=True)
        for nt in range(NT):
            sl = slice(nt * 512, (nt + 1) * 512)
            ps = pss[pi % 4]; pk = "ps%d" % (pi % 4); pi += 1
            for kt in range(4):
                P.pe(lambda e, ps=ps, s_=s_, kt=kt, sl=sl: e.matmul(ps[:], lhsT=wg[s_][:, kt, :], rhs=inb[:, kt, sl], start=(kt == 0), stop=(kt == 3)), r=["wg%d" % s_, "inb%d" % kt], w=[pk])
            t = sg[nt % 2]; tk = "sg%d" % (nt % 2)
            P.act(lambda e, t=t, ps=ps, m=m: e.activation(out=t[:], in_=ps[:], func=AF.Sigmoid, bias=bt_[:, m:m + 1]), r=[pk, "bg"], w=[tk])
            P.dve(lambda e, t=t, m=m, sl=sl: e.tensor_tensor(out=mixin[:, m, sl], in0=t[:], in1=gf[:, m, sl], op=ALU.mult), r=[tk, "gf%d" % m], w=[mixkeys[m]])
    _outproj(P, mixin, Wout, outT, N, pss, pi, mixkeys)
    return P.finish()


def even_tail_dev(yS, hT, cw, Wg, bg, Wout):
    T = yS.shape[1]
    N = T // NCORE
    key = ("even_tail", N)
    if key not in _CACHE:
        _CACHE[key] = build_even_tail(N)
    bT, cT, xT = hT[512:1024], hT[1024:1536], hT[1536:2048]
    cp = np.concatenate([np.zeros((512, 2), np.float32), cT], axis=1)
    xp = np.concatenate([np.zeros((512, 2), np.float32), xT], axis=1)
    w = np.ascontiguousarray(cw.reshape(3, 4, 128).transpose(2, 1, 0))
    b = np.ascontiguousarray(bg.reshape(4, 128).T)
    maps = [{"yS": np.ascontiguousarray(yS[:, c * N:(c + 1) * N]), "bT": np.ascontiguousarray(bT[:, c * N:(c + 1) * N]),
             "cT": np.ascontiguousarray(cp[:, c * N:(c + 1) * N + 2]), "xT": np.ascontiguousarray(xp[:, c * N:(c + 1) * N + 2]),
             "cw": w, "Wg": np.ascontiguousarray(Wg), "bg": b, "Wout": np.ascontiguousarray(Wout)} for c in range(NCORE)]
    res = run(_CACHE[key], maps)
    return np.concatenate([r["outT"] for r in res], axis=1)


def build_odd_tail(N):
    P = Prog()
    NT = N // 512
    yc = P.dram_in("yc", [512, N]); zT = P.dram_in("zT", [512, N + 16]); invc = P.dram_in("invc", [128, 4, N])
    pw = P.dram_in("pw", [128, 4, 128]); psc = P.dram_in("psc", [128, 4]); Wout = P.dram_in("Wout", [1024, 1024])
    outT = P.dram_out("outT", [1024, N])
    mixin = P.sb("mixin", [128, 8, N], BF16)
    mixkeys = ["mix%d" % k for k in range(8)]
    make_stage(P, N)
    for kt in range(4):
        stage_cast(P, mixin[:, kt, :], yc[kt * 128:(kt + 1) * 128, :], 128, N, mixkeys[kt])
    pwb = P.sb("pwb", [128, 4, 128], BF16); sct = P.sb("sct", [128, 4], F32)
    P.dma(pwb[:], pw, w=["pw"], cast=True); P.dma(sct[:], psc, w=["psc"])
    pooled = P.sb("pooled", [128, 4, N], BF16)
    pss = [P.ps("ps%d" % i, [128, 512]) for i in range(4)]
    zb = [P.sb("pz%d" % i, [128, N + 16], F32) for i in range(2)]
    sab = [P.sb("psa%d" % i, [128, N + 16], F32) for i in range(2)]
    sbb = [P.sb("psb%d" % i, [128, N + 16], F32) for i in range(2)]
    icb = [P.sb("pic%d" % i, [128, N], F32) for i in range(2)]
    pi = 0
    for gi in range(4):
        b = gi % 2
        z, sa, sb_, ic = zb[b], sab[b], sbb[b], icb[b]
        rs = slice(gi * 128, (gi + 1) * 128)
        k = "pl%d" % b
        P.dma(z[:], zT[rs, :], w=[k + "z"], q="sync"); P.dma(ic[:], invc[:, gi, :], w=[k + "i"], q="act")
        cur, curk = z, k + "z"
        bufs = [(sa, k + "a"), (sb_, k + "b")]
        for step in range(gi + 1):
            sh = 1 << step
            nxt, nk = bufs[step % 2]
            P.dve(lambda e, cur=cur, nxt=nxt, sh=sh: e.tensor_tensor(out=nxt[:, sh:], in0=cur[:, sh:], in1=cur[:, 0:N + 16 - sh], op=ALU.add), r=[curk], w=[nk])
            cur, curk = nxt, nk
        o, okey = bufs[(gi + 1) % 2]
        P.dve(lambda e, cur=cur, o=o, ic=ic: e.tensor_tensor(out=o[:, 16:], in0=cur[:, 16:], in1=ic[:], op=ALU.mult), r=[curk, k + "i"], w=[okey])
        P.dve(lambda e, o=o, z=z, gi=gi: e.tensor_tensor(out=pooled[:, gi, :], in0=o[:, 16:], in1=z[:, 16:], op=ALU.subtract), r=[okey, k + "z"], w=["pooled%d" % gi])
        for nt in range(NT):
            sl = slice(nt * 512, (nt + 1) * 512)
            ps = pss[pi % 4]; pk = "ps%d" % (pi % 4); pi += 1
            P.pe(lambda e, ps=ps, gi=gi, sl=sl: e.matmul(ps[:], lhsT=pwb[:, gi, :], rhs=pooled[:, gi, sl], start=True, stop=True), r=["pw", "pooled%d" % gi], w=[pk])
            P.act(lambda e, ps=ps, gi=gi, sl=sl: e.activation(out=mixin[:, 4 + gi, sl], in_=ps[:], func=AF.Identity, scale=sct[:, gi:gi + 1]), r=[pk, "psc"], w=[mixkeys[4 + gi]])
    _outproj(P, mixin, Wout, outT, N, pss, pi, mixkeys)
    return P.finish()


def odd_tail_dev(ycT, zT, pool_w, pool_scale, Wout):
    T = zT.shape[1]
    N = T // NCORE
    key = ("odd_tail", N)
    if key not in _CACHE:
        _CACHE[key] = build_odd_tail(N)
    zp = np.concatenate([np.zeros((512, 16), np.float32), zT], axis=1)
    t = np.arange(T)
    inv = np.stack([1.0 / np.minimum(t + 1, w) for w in (2, 4, 8, 16)]).astype(np.float32)
    pw = np.ascontiguousarray(pool_w.transpose(1, 0, 2))
    psc = np.ascontiguousarray(pool_scale.reshape(4, 128).T)
    maps = []
    for c in range(NCORE):
        ic = np.ascontiguousarray(np.broadcast_to(inv[None, :, c * N:(c + 1) * N], (128, 4, N)))
        maps.append({"yc": np.ascontiguousarray(ycT[:, c * N:(c + 1) * N]), "zT": np.ascontiguousarray(zp[:, c * N:(c + 1) * N + 16]),
                     "invc": ic, "pw": pw, "psc": psc, "Wout": np.ascontiguousarray(Wout)})
    res = run(_CACHE[key], maps)
    return np.concatenate([r["outT"] for r in res], axis=1)


def kernel(x, p, ev_w_in, ev_lambda_re, ev_lambda_im, ev_log_dt, ev_b_re, ev_b_im,
           ev_c_re, ev_c_im, ev_d, ev_w_glu, ev_b_glu, ev_conv_w, ev_w_out,
           od_w_in, od_rel_bias, od_pool_w, od_pool_scale, od_w_out,
           ln_mix_g, ln_mix_b, ln_ffn_g, ln_ffn_b, ffn_w_up, ffn_w_down,
           ple_w_proj, ple_w_gate, ple_b_gate):
    f = lambda a: np.asarray(a, dtype=np.float32)
    x_tm = f(x)[0]
    xT = np.ascontiguousarray(x_tm.T)
    hT = lin(xT, f(ev_w_in[0]))
    for i in range(DEPTH):
        if i % 2 == 0:
            e = i // 2
            yS = s5_mixer_dev(np.ascontiguousarray(hT[0:512]), f(ev_lambda_re[e]), f(ev_lambda_im[e]), f(ev_log_dt[e]),
                              f(ev_b_re[e]), f(ev_b_im[e]), f(ev_c_re[e]), f(ev_c_im[e]), f(ev_d[e]))
            mixT = even_tail_dev(yS, hT, f(ev_conv_w[e]), f(ev_w_glu[e]), f(ev_b_glu[e]), f(ev_w_out[e]))
        else:
            o = i // 2
            ycT = attn_dev(hT[0:512], hT[512:1024], hT[1024:1536], f(od_rel_bias[o]))
            mixT = odd_tail_dev(ycT, hT[1536:2048], f(od_pool_w[o]), f(od_pool_scale[o]), f(od_w_out[o]))
        x1 = resln(x_tm, np.ascontiguousarray(mixT.T), f(ln_mix_g[i]), f(ln_mix_b[i]))
        x1T = np.ascontiguousarray(x1.T)
        ffnT = ffn_dev(x1T, f(ffn_w_up[i]), f(ffn_w_down[i]))
        x2 = resln(x1, np.ascontiguousarray(ffnT.T), f(ln_ffn_g[i]), f(ln_ffn_b[i]))
        x2T = np.ascontiguousarray(x2.T)
        pT = np.ascontiguousarray(f(p[i])[0].T)
        if i + 1 < DEPTH:
            wn = f(od_w_in[(i + 1) // 2]) if (i + 1) % 2 == 1 else f(ev_w_in[(i + 1) // 2])
        else:
            wn = None
        xT, hT = ple_dev(x2T, pT, f(ple_w_gate[i]), f(ple_b_gate[i]), f(ple_w_proj[i]), wn)
        x_tm = np.ascontiguousarray(xT.T)
    return x_tm[None].astype(np.float32)
```

```python
import math
import numpy as np
import concourse.bass as bass
import concourse.mybir as mybir
from concourse.bass_utils import run_bass_kernel_spmd

F32 = mybir.dt.float32
BF16 = mybir.dt.bfloat16
AF = mybir.ActivationFunctionType
ALU = mybir.AluOpType
AX = mybir.AxisListType
NCORE = 8
MAGIC = 12582912.0
TWO_PI = 2.0 * math.pi

D_MODEL = 1024
SEQ = 16384
DEPTH = 4
D_FF = 2816
ALPHA = (2 * DEPTH) ** 0.25
LN_EPS = 1e-5


class Prog:
    ENGS = ("sync", "gpsimd", "act", "dve", "pe")
    NDS = 8

    def __init__(self):
        self.nc = bass.Bass("TRN2", target_bir_lowering=False)
        self.ops = {e: [] for e in self.ENGS}
        self.cnt = {}
        self.lastw = {}
        self.reads = {}
        self.seen = {e: {} for e in self.ENGS}
        self.ndma = {e: 0 for e in self.ENGS}
        self.ctx = []
        self.out_waits = []

    def enter(self, cm):
        v = cm.__enter__()
        self.ctx.append(cm)
        return v

    def sb(self, name, shape, dt):
        return self.enter(self.nc.sbuf_tensor(name, list(shape), dt))

    def ps(self, name, shape, dt=F32):
        return self.enter(self.nc.psum_tensor(name, list(shape), dt))

    def dram_in(self, name, shape, dt=F32):
        return self.nc.dram_tensor(name, list(shape), dt, kind="ExternalInput").ap()

    def dram_out(self, name, shape, dt=F32):
        return self.nc.dram_tensor(name, list(shape), dt, kind="ExternalOutput").ap()

    def _op(self, eng, fn, r, w, dma=False, is_out=False):
        waits = {}

        def need(sv):
            s, v = sv
            waits[s] = max(waits.get(s, 0), v)

        for k in r:
            if k in self.lastw:
                need(self.lastw[k])
        for k in w:
            if k in self.lastw:
                need(self.lastw[k])
            for sv in self.reads.get(k, ()):
                need(sv)
        if dma:
            i = self.ndma[eng]
            self.ndma[eng] += 1
            sem = "%s_d%d" % (eng, i % self.NDS)
            inc = 16
        else:
            sem = eng
            inc = 1
        prev = self.cnt.get(sem, 0)
        self.cnt[sem] = prev + inc
        me = (sem, self.cnt[sem])
        wl = []
        if dma and prev > 0:
            waits[sem] = max(waits.get(sem, 0), prev)
        for s, v in waits.items():
            if eng == "pe" and s == "pe":
                continue
            if self.seen[eng].get(s, 0) >= v:
                continue
            self.seen[eng][s] = v
            wl.append((s, v))
        self.ops[eng].append((wl, fn, sem, inc))
        for k in w:
            self.lastw[k] = me
            self.reads[k] = []
        for k in r:
            self.reads.setdefault(k, []).append(me)
        if is_out:
            self.out_waits.append(me)

    def dma(self, out, in_, r=(), w=(), cast=False, is_out=False, q=None):
        eng = "gpsimd" if cast else (q or "sync")
        self._op(eng, lambda e: e.dma_start(out=out, in_=in_), r, w, dma=True, is_out=is_out)

    def act(self, fn, r=(), w=()):
        self._op("act", fn, r, w)

    def dve(self, fn, r=(), w=()):
        self._op("dve", fn, r, w)

    def pe(self, fn, r=(), w=()):
        self._op("pe", fn, r, w)

    def finish(self):
        nc = self.nc
        fin = {}
        for s, v in self.out_waits:
            fin[s] = max(fin.get(s, 0), v)
        names = sorted(self.cnt.keys())
        sems = {}
        for n in names:
            sems[n] = self.enter(nc.semaphore(n))
        block = self.enter(nc.Block())
        engmap = {"sync": block.sync, "gpsimd": block.gpsimd, "act": block.scalar,
                  "dve": block.vector, "pe": block.tensor}

        def make(ename):
            def body(e):
                for wl, fn, sem, inc in self.ops[ename]:
                    for s, v in wl:
                        e.wait_ge(sems[s], v)
                    fn(e).then_inc(sems[sem], inc)
                if ename == "sync":
                    for s, v in fin.items():
                        e.wait_ge(sems[s], v)
            return body

        for ename in self.ENGS:
            if self.ops[ename] or ename == "sync":
                engmap[ename](make(ename))
        for cm in reversed(self.ctx):
            cm.__exit__(None, None, None)
        self.ctx = []
        return nc


def make_stage(P, ncol, n=3):
    P._stg = [P.sb("stg%d" % i, [128, ncol], F32) for i in range(n)]
    P._stgi = 0


def stage_cast(P, dst, src, npart, ncol, wkey, scale=None):
    idx = P._stgi
    P._stgi += 1
    b = idx % len(P._stg)
    st = P._stg[b][0:npart, 0:ncol]
    sk = "stg%d" % b
    P.dma(st, src, w=[sk], q=("sync" if idx % 2 == 0 else "act"))
    if idx % 2 == 0:
        if scale is None:
            P.act(lambda e: e.activation(out=dst, in_=st, func=AF.Copy), r=[sk], w=[wkey])
        else:
            P.act(lambda e: e.activation(out=dst, in_=st, func=AF.Copy, scale=float(scale)), r=[sk], w=[wkey])
    else:
        if scale is None:
            P.dve(lambda e: e.tensor_copy(out=dst, in_=st), r=[sk], w=[wkey])
        else:
            P.dve(lambda e: e.tensor_scalar(out=dst, in0=st, scalar1=float(scale), scalar2=None, op0=ALU.mult), r=[sk], w=[wkey])


def run(nc, in_maps):
    res = run_bass_kernel_spmd(nc, in_maps, core_ids=list(range(NCORE)))
    return res.results


_CACHE = {}


def build_lin(K, M, N, mode="plain", func=None, has_bias=False, has_scale=False):
    P = Prog()
    nc = P.nc
    KT = K // 128
    MO = M // 2 if mode == "swiglu" else M
    MT = MO // 128
    NT = N // 512
    inT = P.dram_in("inT", [K, N])
    W = P.dram_in("W", [K, M])
    outT = P.dram_out("outT", [MO, N])
    bias = P.dram_in("bias", [128, MT]) if has_bias else None
    scale = P.dram_in("scale", [128, MT]) if has_scale else None
    A = P.dram_in("A", [MO, N]) if mode == "fma" else None
    B = P.dram_in("B", [MO, N]) if mode == "fma" else None
    inb = P.sb("inb", [128, KT, N], BF16)
    NW = 3
    wb = [P.sb("wb%d" % i, [128, KT, 128], BF16) for i in range(NW * (2 if mode == "swiglu" else 1))]
    ot = [P.sb("ot%d" % i, [128, N], F32) for i in range(2)]
    pss = [P.ps("ps%d" % i, [128, 512]) for i in range(4)]
    if has_bias:
        bt = P.sb("bt", [128, MT], F32)
        P.dma(bt[:], bias, w=["bt"])
    if has_scale:
        st = P.sb("st", [128, MT], F32)
        P.dma(st[:], scale, w=["st"])
    if mode == "mulin":
        gf = P.sb("gf", [128, KT, N], F32)
        xs = [P.sb("xs%d" % i, [128, N], F32) for i in range(2)]
        tq = P.sb("tq", [128, N], F32)
        for kt in range(KT):
            x = xs[kt % 2]
            xk = "xs%d" % (kt % 2)
            P.dma(x[:], inT[kt * 128:(kt + 1) * 128, :], w=[xk])
            P.dve(lambda e, x=x: e.tensor_tensor(out=tq[:], in0=x[:], in1=x[:], op=ALU.mult), r=[xk], w=["tq"])
            P.dve(lambda e: e.tensor_scalar(out=tq[:], in0=tq[:], scalar1=0.044715, scalar2=1.0, op0=ALU.mult, op1=ALU.add), r=["tq"], w=["tq"])
            P.dve(lambda e, x=x: e.tensor_tensor(out=tq[:], in0=tq[:], in1=x[:], op=ALU.mult), r=["tq", xk], w=["tq"])
            P.act(lambda e: e.activation(out=tq[:], in_=tq[:], func=AF.Sigmoid, scale=1.5957691216057308), r=["tq"], w=["tq"])
            P.dve(lambda e, x=x, kt=kt: e.tensor_tensor(out=gf[:, kt, :], in0=tq[:], in1=x[:], op=ALU.mult), r=["tq", xk], w=["gf%d" % kt])
            P.act(lambda e, kt=kt: e.activation(out=inb[:, kt, :], in_=gf[:, kt, :], func=AF.Copy), r=["gf%d" % kt], w=["inb%d" % kt])
    else:
        make_stage(P, N)
        for kt in range(KT):
            stage_cast(P, inb[:, kt, :], inT[kt * 128:(kt + 1) * 128, :], 128, N, "inb%d" % kt)
    if mode == "swiglu":
        tmp = [P.sb("tmp%d" % i, [128, 512], F32) for i in range(2)]
    if mode == "fma":
        At = [P.sb("At%d" % i, [128, N], F32) for i in range(2)]
        Bt = [P.sb("Bt%d" % i, [128, N], F32) for i in range(2)]
    Wv = W.rearrange("(kt p) m -> p kt m", p=128)
    pi = 0
    for m in range(MT):
        s = m % NW
        o = ot[m % 2]
        ok = "ot%d" % (m % 2)
        P.dma(wb[s][:], Wv[:, :, m * 128:(m + 1) * 128], w=["wb%d" % s], cast=True)
        if mode == "swiglu":
            P.dma(wb[NW + s][:], Wv[:, :, MO + m * 128:MO + (m + 1) * 128], w=["wb%d" % (NW + s)], cast=True)
        if mode == "fma":
            P.dma(At[m % 2][:], A[m * 128:(m + 1) * 128, :], w=["At%d" % (m % 2)])
            P.dma(Bt[m % 2][:], B[m * 128:(m + 1) * 128, :], w=["Bt%d" % (m % 2)])
        for nt in range(NT):
            sl = slice(nt * 512, (nt + 1) * 512)
            ps = pss[pi % 4]
            pk = "ps%d" % (pi % 4)
            pi += 1
            for kt in range(KT):
                P.pe(lambda e, ps=ps, s=s, kt=kt, sl=sl: e.matmul(ps[:], lhsT=wb[s][:, kt, :], rhs=inb[:, kt, sl], start=(kt == 0), stop=(kt == KT - 1)),
                     r=["wb%d" % s, "inb%d" % kt], w=[pk])
            if mode == "swiglu":
                ps2 = pss[pi % 4]
                pk2 = "ps%d" % (pi % 4)
                pi += 1
                for kt in range(KT):
                    P.pe(lambda e, ps2=ps2, s=s, kt=kt, sl=sl: e.matmul(ps2[:], lhsT=wb[NW + s][:, kt, :], rhs=inb[:, kt, sl], start=(kt == 0), stop=(kt == KT - 1)),
                         r=["wb%d" % (NW + s), "inb%d" % kt], w=[pk2])
                t = tmp[nt % 2]
                tk = "tmp%d" % (nt % 2)
                P.act(lambda e, t=t, ps=ps: e.activation(out=t[:], in_=ps[:], func=AF.Silu), r=[pk], w=[tk])
                P.dve(lambda e, o=o, t=t, ps2=ps2, sl=sl: e.tensor_tensor(out=o[:, sl], in0=t[:], in1=ps2[:], op=ALU.mult), r=[tk, pk2], w=[ok])
            elif mode == "fma":
                P.dve(lambda e, o=o, ps=ps, sl=sl, m=m: e.tensor_tensor(out=o[:, sl], in0=Bt[m % 2][:, sl], in1=ps[:], op=ALU.mult), r=[pk, "Bt%d" % (m % 2)], w=[ok])
                P.dve(lambda e, o=o, sl=sl, m=m: e.tensor_tensor(out=o[:, sl], in0=o[:, sl], in1=At[m % 2][:, sl], op=ALU.add), r=[ok, "At%d" % (m % 2)], w=[ok])
            elif mode == "mulin":
                P.act(lambda e, o=o, ps=ps, sl=sl, m=m: e.activation(out=o[:, sl], in_=ps[:], func=AF.Sigmoid, bias=bt[:, m:m + 1]), r=[pk, "bt"], w=[ok])
                P.dve(lambda e, o=o, sl=sl, m=m: e.tensor_tensor(out=o[:, sl], in0=o[:, sl], in1=gf[:, m, sl], op=ALU.mult), r=[ok, "gf%d" % m], w=[ok])
            else:
                kw = {}
                rr = [pk]
                if has_bias:
                    kw["bias"] = bt[:, m:m + 1]
                    rr.append("bt")
                if has_scale:
                    kw["scale"] = st[:, m:m + 1]
                    rr.append("st")
                f = func if func is not None else AF.Identity
                P.act(lambda e, o=o, ps=ps, sl=sl, kw=kw, f=f: e.activation(out=o[:, sl], in_=ps[:], func=f, **kw), r=rr, w=[ok])
        P.dma(outT[m * 128:(m + 1) * 128, :], o[:], r=[ok], w=["out"], is_out=True)
    return P.finish()


def build_ple(N, with_next=False):
    P = Prog()
    KT, K2, MT, NT = 8, 2, 8, N // 512
    x2T = P.dram_in("x2T", [1024, N]); pT = P.dram_in("pT", [256, N])
    Wg = P.dram_in("Wg", [1024, 1024]); Wp = P.dram_in("Wp", [256, 1024]); bias = P.dram_in("bias", [128, MT])
    outT = P.dram_out("outT", [1024, N])
    if with_next:
        Win = P.dram_in("Win", [1024, 2048]); hT = P.dram_out("hT", [2048, N])
        x3b = P.sb("x3b", [128, 8, N], BF16)
    inb = P.sb("inb", [128, KT, N], BF16); pb = P.sb("pb", [128, K2, N], BF16)
    bt = P.sb("bt", [128, MT], F32)
    P.dma(bt[:], bias, w=["bt"])
    make_stage(P, N)
    for kt in range(KT):
        stage_cast(P, inb[:, kt, :], x2T[kt * 128:(kt + 1) * 128, :], 128, N, "inb%d" % kt)
    for kt in range(K2):
        stage_cast(P, pb[:, kt, :], pT[kt * 128:(kt + 1) * 128, :], 128, N, "pb")
    NW = 3
    wg = [P.sb("wg%d" % i, [128, KT, 128], BF16) for i in range(NW)]
    wp = [P.sb("wp%d" % i, [128, K2, 128], BF16) for i in range(NW)]
    At = [P.sb("At%d" % i, [128, N], F32) for i in range(2)]
    ot = [P.sb("ot%d" % i, [128, N], F32) for i in range(2)]
    tmp = [P.sb("tmp%d" % i, [128, 512], F32) for i in range(2)]
    pss = [P.ps("ps%d" % i, [128, 512]) for i in range(4)]
    Wgv = Wg.rearrange("(kt p) m -> p kt m", p=128); Wpv = Wp.rearrange("(kt p) m -> p kt m", p=128)
    pi = 0
    for m in range(MT):
        s_ = m % NW
        o = ot[m % 2]; ok = "ot%d" % (m % 2); A = At[m % 2]; ak = "At%d" % (m % 2)
        ms = slice(m * 128, (m + 1) * 128)
        P.dma(wg[s_][:], Wgv[:, :, ms], w=["wg%d" % s_], cast=True)
        P.dma(wp[s_][:], Wpv[:, :, ms], w=["wp%d" % s_], cast=True)
        P.dma(A[:], x2T[ms, :], w=[ak], q="act")
        for nt in range(NT):
            sl = slice(nt * 512, (nt + 1) * 512)
            ps = pss[pi % 4]; pk = "ps%d" % (pi % 4); pi += 1
            ps2 = pss[pi % 4]; pk2 = "ps%d" % (pi % 4); pi += 1
            for kt in range(KT):
                P.pe(lambda e, ps=ps, s_=s_, kt=kt, sl=sl: e.matmul(ps[:], lhsT=wg[s_][:, kt, :], rhs=inb[:, kt, sl], start=(kt == 0), stop=(kt == KT - 1)), r=["wg%d" % s_, "inb%d" % kt], w=[pk])
            for kt in range(K2):
                P.pe(lambda e, ps2=ps2, s_=s_, kt=kt, sl=sl: e.matmul(ps2[:], lhsT=wp[s_][:, kt, :], rhs=pb[:, kt, sl], start=(kt == 0), stop=(kt == K2 - 1)), r=["wp%d" % s_, "pb"], w=[pk2])
            t = tmp[nt % 2]; tk = "tmp%d" % (nt % 2)
            P.act(lambda e, t=t, ps=ps, m=m: e.activation(out=t[:], in_=ps[:], func=AF.Sigmoid, bias=bt[:, m:m + 1]), r=[pk, "bt"], w=[tk])
            P.dve(lambda e, o=o, t=t, ps2=ps2, sl=sl: e.tensor_tensor(out=o[:, sl], in0=t[:], in1=ps2[:], op=ALU.mult), r=[tk, pk2], w=[ok])
            P.dve(lambda e, o=o, A=A, sl=sl: e.tensor_tensor(out=o[:, sl], in0=o[:, sl], in1=A[:, sl], op=ALU.add), r=[ok, ak], w=[ok])
            if with_next:
                P.dve(lambda e, o=o, sl=sl, m=m: e.tensor_copy(out=x3b[:, m, sl], in_=o[:, sl]), r=[ok], w=["x3b%d" % m])
        P.dma(outT[ms, :], o[:], r=[ok], w=["out"], is_out=True)
    if with_next:
        Wiv = Win.rearrange("(kt p) m -> p kt m", p=128)
        wi = [P.sb("wi%d" % i, [128, 8, 128], BF16) for i in range(3)]
        ht = [P.sb("ht%d" % i, [128, N], F32) for i in range(2)]
        for m in range(16):
            s_ = m % 3
            o = ht[m % 2]; ok = "ht%d" % (m % 2)
            P.dma(wi[s_][:], Wiv[:, :, m * 128:(m + 1) * 128], w=["wi%d" % s_], cast=True)
            for nt in range(NT):
                sl = slice(nt * 512, (nt + 1) * 512)
                ps = pss[pi % 4]; pk = "ps%d" % (pi % 4); pi += 1
                for kt in range(8):
                    P.pe(lambda e, ps=ps, s_=s_, kt=kt, sl=sl: e.matmul(ps[:], lhsT=wi[s_][:, kt, :], rhs=x3b[:, kt, sl], start=(kt == 0), stop=(kt == 7)), r=["wi%d" % s_, "x3b%d" % kt], w=[pk])
                P.act(lambda e, o=o, ps=ps, sl=sl: e.activation(out=o[:, sl], in_=ps[:], func=AF.Identity), r=[pk], w=[ok])
            P.dma(hT[m * 128:(m + 1) * 128, :], o[:], r=[ok], w=["out"], is_out=True)
    return P.finish()


def ple_dev(x2T, pT, Wg, bg, Wp, Win=None):
    T = x2T.shape[1]
    N = T // NCORE
    key = ("ple", N, Win is not None)
    if key not in _CACHE:
        _CACHE[key] = build_ple(N, Win is not None)
    b = np.ascontiguousarray(bg.reshape(8, 128).T)
    maps = [{"x2T": np.ascontiguousarray(x2T[:, c * N:(c + 1) * N]), "pT": np.ascontiguousarray(pT[:, c * N:(c + 1) * N]),
             "Wg": np.ascontiguousarray(Wg), "Wp": np.ascontiguousarray(Wp), "bias": b} for c in range(NCORE)]
    if Win is not None:
        for d in maps:
            d["Win"] = np.ascontiguousarray(Win)
    res = run(_CACHE[key], maps)
    xT = np.concatenate([r["outT"] for r in res], axis=1)
    if Win is None:
        return xT, None
    return xT, np.concatenate([r["hT"] for r in res], axis=1)


def build_ffn(N):
    P = Prog()
    KT, JT, MT, NT = 8, D_FF // 128, 8, N // 512
    inT = P.dram_in("inT", [1024, N]); Wu = P.dram_in("Wu", [1024, 2 * D_FF]); Wd = P.dram_in("Wd", [D_FF, 1024])
    outT = P.dram_out("outT", [1024, N])
    inb = P.sb("inb", [128, KT, N], BF16)
    actb = P.sb("actb", [128, JT, N], BF16)
    make_stage(P, N)
    for kt in range(KT):
        stage_cast(P, inb[:, kt, :], inT[kt * 128:(kt + 1) * 128, :], 128, N, "inb%d" % kt)
    NW = 3
    wb = [P.sb("wb%d" % i, [128, KT, 128], BF16) for i in range(2 * NW)]
    wd = [P.sb("wd%d" % i, [128, JT, 128], BF16) for i in range(2)]
    ot = [P.sb("ot%d" % i, [128, N], F32) for i in range(2)]
    tmp = [P.sb("tmp%d" % i, [128, 512], F32) for i in range(2)]
    pss = [P.ps("ps%d" % i, [128, 512]) for i in range(6)]
    Wuv = Wu.rearrange("(kt p) m -> p kt m", p=128); Wdv = Wd.rearrange("(jt p) m -> p jt m", p=128)
    pi = 0
    for j in range(JT):
        s_ = j % NW
        P.dma(wb[s_][:], Wuv[:, :, j * 128:(j + 1) * 128], w=["wb%d" % s_], cast=True)
        P.dma(wb[NW + s_][:], Wuv[:, :, D_FF + j * 128:D_FF + (j + 1) * 128], w=["wb%d" % (NW + s_)], cast=True)
        for nt in range(NT):
            sl = slice(nt * 512, (nt + 1) * 512)
            ps = pss[pi % 6]; pk = "ps%d" % (pi % 6); pi += 1
            ps2 = pss[pi % 6]; pk2 = "ps%d" % (pi % 6); pi += 1
            for kt in range(KT):
                P.pe(lambda e, ps=ps, s_=s_, kt=kt, sl=sl: e.matmul(ps[:], lhsT=wb[s_][:, kt, :], rhs=inb[:, kt, sl], start=(kt == 0), stop=(kt == KT - 1)), r=["wb%d" % s_, "inb%d" % kt], w=[pk])
            for kt in range(KT):
                P.pe(lambda e, ps2=ps2, s_=s_, kt=kt, sl=sl: e.matmul(ps2[:], lhsT=wb[NW + s_][:, kt, :], rhs=inb[:, kt, sl], start=(kt == 0), stop=(kt == KT - 1)), r=["wb%d" % (NW + s_), "inb%d" % kt], w=[pk2])
            t = tmp[nt % 2]; tk = "tmp%d" % (nt % 2)
            P.act(lambda e, t=t, ps=ps: e.activation(out=t[:], in_=ps[:], func=AF.Silu), r=[pk], w=[tk])
            P.dve(lambda e, t=t, ps2=ps2, j=j, sl=sl: e.tensor_tensor(out=actb[:, j, sl], in0=t[:], in1=ps2[:], op=ALU.mult), r=[tk, pk2], w=["actb%d" % j])
    allact = ["actb%d" % j for j in range(JT)]
    for m in range(MT):
        s_ = m % 2
        o = ot[m % 2]; ok = "ot%d" % (m % 2)
        P.dma(wd[s_][:], Wdv[:, :, m * 128:(m + 1) * 128], w=["wd%d" % s_], cast=True)
        for nt in range(NT):
            sl = slice(nt * 512, (nt + 1) * 512)
            ps = pss[pi % 6]; pk = "ps%d" % (pi % 6); pi += 1
            for j in range(JT):
                P.pe(lambda e, ps=ps, s_=s_, j=j, sl=sl: e.matmul(ps[:], lhsT=wd[s_][:, j, :], rhs=actb[:, j, sl], start=(j == 0), stop=(j == JT - 1)), r=["wd%d" % s_] + allact, w=[pk])
            P.act(lambda e, o=o, ps=ps, sl=sl: e.activation(out=o[:, sl], in_=ps[:], func=AF.Identity), r=[pk], w=[ok])
        P.dma(outT[m * 128:(m + 1) * 128, :], o[:], r=[ok], w=["out"], is_out=True)
    return P.finish()


def ffn_dev(x1T, Wu, Wd):
    T = x1T.shape[1]
    N = T // NCORE
    key = ("ffn", N)
    if key not in _CACHE:
        _CACHE[key] = build_ffn(N)
    maps = [{"inT": np.ascontiguousarray(x1T[:, c * N:(c + 1) * N]), "Wu": np.ascontiguousarray(Wu), "Wd": np.ascontiguousarray(Wd)} for c in range(NCORE)]
    res = run(_CACHE[key], maps)
    return np.concatenate([r["outT"] for r in res], axis=1)


def lin(inT_full, W, mode="plain", func=None, bias=None, scale=None, A=None, B=None):
    K, T = inT_full.shape
    M = W.shape[1]
    N = T // NCORE
    key = ("lin", K, M, N, mode, str(func), bias is not None, scale is not None)
    if key not in _CACHE:
        _CACHE[key] = build_lin(K, M, N, mode, func, bias is not None, scale is not None)
    nc = _CACHE[key]
    MO = M // 2 if mode == "swiglu" else M
    maps = []
    for c in range(NCORE):
        sl = slice(c * N, (c + 1) * N)
        d = {"inT": np.ascontiguousarray(inT_full[:, sl]), "W": np.ascontiguousarray(W)}
        if bias is not None:
            d["bias"] = np.ascontiguousarray(bias.reshape(MO // 128, 128).T)
        if scale is not None:
            d["scale"] = np.ascontiguousarray(scale.reshape(MO // 128, 128).T)
        if A is not None:
            d["A"] = np.ascontiguousarray(A[:, sl])
            d["B"] = np.ascontiguousarray(B[:, sl])
        maps.append(d)
    res = run(nc, maps)
    return np.concatenate([r["outT"] for r in res], axis=1)


def build_s5(NB):
    P = Prog()
    G, GC = 4, 2
    ub = P.dram_in("ub", [G, 128, NB])
    yb = P.dram_out("yb", [G, 128, NB])
    prm = {n: P.dram_in(n, [128, GC]) for n in ("lre", "lim", "ldt")}
    bin_ = {n: P.dram_in(n, [128, GC, 16]) for n in ("bre", "bim", "cre", "cim")}
    dcol_d = P.dram_in("dcol", [128, G])
    cmask_d = P.dram_in("cmask", [128, 128])
    ident_d = P.dram_in("ident", [128, 128])
    NL = int(math.log2(NB))
    t = {}
    for n in ("lre", "lim", "ldt", "dt", "a", "ang", "nr", "den", "rre", "rim", "nrim", "t1", "t2", "m2", "ire", "iim", "niim"):
        t[n] = P.sb("t_" + n, [128, GC], F32)
    for n in ("ak", "arg", "tr", "mag", "cs", "sn"):
        t[n] = P.sb("t_" + n, [128, 8, GC], F32)
    for n in ("bre", "bim", "cre", "cim", "bbre", "bbim", "t16a", "t16b"):
        t[n] = P.sb("t_" + n, [128, GC, 16], F32)
    PRE = P.sb("PRE", [128, 9, GC], F32); PIM = P.sb("PIM", [128, 9, GC], F32); NPIM = P.sb("NPIM", [128, 9, GC], F32)
    PWRE = P.sb("PWRE", [128, NL, GC], F32); PWIM = P.sb("PWIM", [128, NL, GC], F32); NPWIM = P.sb("NPWIM", [128, NL, GC], F32)
    BTre = P.sb("BTre", [128, GC, 128], F32); BTim = P.sb("BTim", [128, GC, 128], F32)
    CTre = P.sb("CTre", [128, GC, 128], F32); CTim = P.sb("CTim", [128, GC, 128], F32)
    BPre = P.sb("BPre", [128, GC, 128], F32); BPimn = P.sb("BPimn", [128, GC, 128], F32)
    X1 = P.sb("X1", [128, GC, 128], F32); X2 = P.sb("X2", [128, GC, 128], F32)
    X3 = P.sb("X3", [128, GC, 128], F32); X4 = P.sb("X4", [128, GC, 128], F32)
    dcol = P.sb("dcolt", [128, G], F32); cmask = P.sb("cmaskt", [128, 128], F32); ident = P.sb("identt", [128, 128], F32)
    dtmp = P.sb("dtmp", [128, 128], F32)
    Bre = P.sb("Bre", [128, G, 128], BF16); Bim = P.sb("Bim", [128, G, 128], BF16)
    Cre = P.sb("Cre", [128, GC, 128], BF16); Cimn = P.sb("Cimn", [128, GC, 128], BF16)
    Dm = P.sb("Dm", [128, G, 128], BF16)
    ubb = P.sb("ubb", [128, G, NB], BF16)
    Hre = P.sb("Hre", [128, GC, NB], F32); Him = P.sb("Him", [128, GC, NB], F32)
    Hbre = P.sb("Hbre", [128, GC, NB], BF16); Hbim = P.sb("Hbim", [128, GC, NB], BF16)
    Tres = [P.sb("Tre%d" % g, [128, NB // 2], F32) for g in range(GC)]
    Tims = [P.sb("Tim%d" % g, [128, NB // 2], F32) for g in range(GC)]
    yo = [P.sb("yo%d" % i, [128, NB], F32) for i in range(2)]
    pss = [P.ps("ps%d" % i, [128, 512]) for i in range(4)]
    K = ["prep"]

    for n in ("lre", "lim", "ldt"):
        P.dma(t[n][:], prm[n], w=K)
    for n in ("bre", "bim", "cre", "cim"):
        P.dma(t[n][:], bin_[n], w=K)
    P.dma(dcol[:], dcol_d, w=K); P.dma(cmask[:], cmask_d, w=K); P.dma(ident[:], ident_d, w=K)
    make_stage(P, NB, n=2)
    for g in range(G):
        stage_cast(P, ubb[:, g, :], ub[g], 128, NB, "ubb%d" % g)
    P.dve(lambda e: e.memset(Bre[:], 0.0), w=["Bpad"])
    P.dve(lambda e: e.memset(Bim[:], 0.0), w=["Bpad"])

    def tt(o, a, b, op, r=K, w=K):
        P.dve(lambda e: e.tensor_tensor(out=o, in0=a, in1=b, op=op), r=r, w=w)

    def ts(o, a, s1, op0, s2=None, op1=None, r=K, w=K):
        if op1 is None:
            P.dve(lambda e: e.tensor_scalar(out=o, in0=a, scalar1=s1, scalar2=None, op0=op0), r=r, w=w)
        else:
            P.dve(lambda e: e.tensor_scalar(out=o, in0=a, scalar1=s1, scalar2=s2, op0=op0, op1=op1), r=r, w=w)

    def stt(o, a, s_, b, op0, op1, r=K, w=K):
        P.dve(lambda e: e.scalar_tensor_tensor(out=o, in0=a, scalar=s_, in1=b, op0=op0, op1=op1), r=r, w=w)

    def act(o, a, f, **kw):
        P.act(lambda e: e.activation(out=o, in_=a, func=f, **kw), r=K, w=K)

    def sin_of(o, arg):
        ts(t["tr"][:], arg, 1.0 / TWO_PI, ALU.mult, MAGIC, ALU.add)
        ts(t["tr"][:], t["tr"][:], MAGIC, ALU.subtract, -TWO_PI, ALU.mult)
        tt(t["tr"][:], t["tr"][:], arg, ALU.add)
        ts(t["tr"][:], t["tr"][:], math.pi, ALU.min, -math.pi, ALU.max)
        act(o, t["tr"][:], AF.Sin)

    act(t["dt"][:], t["ldt"][:], AF.Exp)
    tt(t["a"][:], t["lre"][:], t["dt"][:], ALU.mult)
    tt(t["ang"][:], t["lim"][:], t["dt"][:], ALU.mult)
    P.dve(lambda e: e.memset(PRE[:, 0, :], 1.0), r=K, w=K)
    P.dve(lambda e: e.memset(PIM[:, 0, :], 0.0), r=K, w=K)
    for k in range(1, 9):
        ts(t["ak"][:, k - 1, :], t["a"][:], float(k), ALU.mult)
        ts(t["arg"][:, k - 1, :], t["ang"][:], float(k), ALU.mult)
    act(t["mag"][:], t["ak"][:], AF.Exp)
    sin_of(t["sn"][:], t["arg"][:])
    ts(t["arg"][:], t["arg"][:], math.pi / 2, ALU.add)
    sin_of(t["cs"][:], t["arg"][:])
    tt(PRE[:, 1:9, :], t["mag"][:], t["cs"][:], ALU.mult)
    tt(PIM[:, 1:9, :], t["mag"][:], t["sn"][:], ALU.mult)
    ts(NPIM[:], PIM[:], -1.0, ALU.mult)
    ts(t["nr"][:], PRE[:, 1, :], -1.0, ALU.add)
    tt(t["t1"][:], t["lre"][:], t["lre"][:], ALU.mult)
    tt(t["t2"][:], t["lim"][:], t["lim"][:], ALU.mult)
    tt(t["den"][:], t["t1"][:], t["t2"][:], ALU.add)
    P.dve(lambda e: e.reciprocal(out=t["den"][:], in_=t["den"][:]), r=K, w=K)
    tt(t["t1"][:], t["nr"][:], t["lre"][:], ALU.mult)
    tt(t["t2"][:], PIM[:, 1, :], t["lim"][:], ALU.mult)
    tt(t["t1"][:], t["t1"][:], t["t2"][:], ALU.add)
    tt(t["rre"][:], t["t1"][:], t["den"][:], ALU.mult)
    tt(t["t1"][:], PIM[:, 1, :], t["lre"][:], ALU.mult)
    tt(t["t2"][:], t["nr"][:], t["lim"][:], ALU.mult)
    tt(t["t1"][:], t["t1"][:], t["t2"][:], ALU.subtract)
    tt(t["rim"][:], t["t1"][:], t["den"][:], ALU.mult)
    ts(t["nrim"][:], t["rim"][:], -1.0, ALU.mult)
    tt(t["t1"][:], PRE[:, 8, :], PRE[:, 8, :], ALU.mult)
    tt(t["t2"][:], PIM[:, 8, :], PIM[:, 8, :], ALU.mult)
    tt(t["m2"][:], t["t1"][:], t["t2"][:], ALU.add)
    P.dve(lambda e: e.reciprocal(out=t["m2"][:], in_=t["m2"][:]), r=K, w=K)
    tt(t["ire"][:], PRE[:, 8, :], t["m2"][:], ALU.mult)
    tt(t["niim"][:], PIM[:, 8, :], t["m2"][:], ALU.mult)
    ts(t["iim"][:], t["niim"][:], -1.0, ALU.mult)
    for gc in range(GC):
        gs = slice(gc, gc + 1)
        ts(t["t16a"][:, gc, :], t["bre"][:, gc, :], t["rre"][:, gs], ALU.mult)
        ts(t["t16b"][:, gc, :], t["bim"][:, gc, :], t["rre"][:, gs], ALU.mult)
    for gc in range(GC):
        gs = slice(gc, gc + 1)
        stt(t["bbre"][:, gc, :], t["bim"][:, gc, :], t["nrim"][:, gs], t["t16a"][:, gc, :], ALU.mult, ALU.add)
        stt(t["bbim"][:, gc, :], t["bre"][:, gc, :], t["rim"][:, gs], t["t16b"][:, gc, :], ALU.mult, ALU.add)
    KB = ["prepB"]
    for gc in range(GC):
        gs = slice(gc, gc + 1)
        for i in range(8):
            isl = slice(i * 16, (i + 1) * 16)
            ts(X1[:, gc, isl], t["bbre"][:, gc, :], PRE[:, 7 - i, gs], ALU.mult, r=K, w=KB)
            ts(X2[:, gc, isl], t["bbim"][:, gc, :], PRE[:, 7 - i, gs], ALU.mult, r=K, w=KB)
            ts(X3[:, gc, isl], t["cre"][:, gc, :], PRE[:, i + 1, gs], ALU.mult, r=K, w=KB)
            ts(X4[:, gc, isl], t["cim"][:, gc, :], PRE[:, i + 1, gs], ALU.mult, r=K, w=KB)
    KC = ["prepC"]
    for gc in range(GC):
        gs = slice(gc, gc + 1)
        for i in range(8):
            isl = slice(i * 16, (i + 1) * 16)
            stt(BTre[:, gc, isl], t["bbim"][:, gc, :], NPIM[:, 7 - i, gs], X1[:, gc, isl], ALU.mult, ALU.add, r=K + KB, w=KC)
            stt(BTim[:, gc, isl], t["bbre"][:, gc, :], PIM[:, 7 - i, gs], X2[:, gc, isl], ALU.mult, ALU.add, r=K + KB, w=KC)
            stt(CTre[:, gc, isl], t["cim"][:, gc, :], NPIM[:, i + 1, gs], X3[:, gc, isl], ALU.mult, ALU.add, r=K + KB, w=KC)
            stt(CTim[:, gc, isl], t["cre"][:, gc, :], PIM[:, i + 1, gs], X4[:, gc, isl], ALU.mult, ALU.add, r=K + KB, w=KC)
    KD = ["prepD"]
    for gc in range(GC):
        gs = slice(gc, gc + 1)
        ts(X1[:, gc, :], BTre[:, gc, :], t["ire"][:, gs], ALU.mult, r=K + KC, w=KD)
        ts(X2[:, gc, :], BTim[:, gc, :], t["ire"][:, gs], ALU.mult, r=K + KC, w=KD)
    KE = ["prepE"]
    for gc in range(GC):
        gs = slice(gc, gc + 1)
        stt(BPre[:, gc, :], BTim[:, gc, :], t["niim"][:, gs], X1[:, gc, :], ALU.mult, ALU.add, r=K + KC + KD, w=KE)
        stt(X3[:, gc, :], BTre[:, gc, :], t["iim"][:, gs], X2[:, gc, :], ALU.mult, ALU.add, r=K + KC + KD, w=KE)
    ts(BPimn[:], X3[:], -1.0, ALU.mult, r=KE, w=KE)
    P.act(lambda e: e.activation(out=Cre[:], in_=CTre[:], func=AF.Copy), r=KC, w=["Cw"])
    P.act(lambda e: e.activation(out=Cimn[:], in_=CTim[:], func=AF.Copy, scale=-1.0), r=KC, w=["Cw"])
    for g in range(G):
        hf, gc = g // 2, g % 2
        hs = slice(hf * 64, (hf + 1) * 64)
        for n_, (src, dst) in enumerate(((BTre, Bre), (BTim, Bim))):
            ps = pss[n_]; pk = "ps%d" % n_
            P.pe(lambda e, src=src, gc=gc, hs=hs, ps=ps: e.matmul(ps[:, 0:64], lhsT=src[hs, gc, :], rhs=ident[hs, hs], start=True, stop=True), r=KC + K, w=[pk])
            P.act(lambda e, dst=dst, g=g, hs=hs, ps=ps: e.activation(out=dst[:, g, hs], in_=ps[:, 0:64], func=AF.Copy), r=[pk, "Bpad"], w=["Bpad"])
        ps = pss[2]
        P.pe(lambda e, gc=gc, hs=hs, ps=ps: e.matmul(ps[:, 0:128], lhsT=BPre[hs, gc, :], rhs=CTre[hs, gc, :], start=True, stop=False), r=KE + KC, w=["ps2"])
        P.pe(lambda e, gc=gc, hs=hs, ps=ps: e.matmul(ps[:, 0:128], lhsT=BPimn[hs, gc, :], rhs=CTim[hs, gc, :], start=False, stop=True), r=KE + KC, w=["ps2"])
        P.dve(lambda e, ps=ps: e.tensor_tensor(out=dtmp[:], in0=ps[:, 0:128], in1=cmask[:], op=ALU.mult), r=K + ["ps2"], w=["dtmp"])
        stt(Dm[:, g, :], ident[:], dcol[:, g:g + 1], dtmp[:], ALU.mult, ALU.add, r=K + ["dtmp"], w=["Dm"])
    tt(PWRE[:, 0, :], PRE[:, 8, :], PRE[:, 8, :], ALU.max)
    tt(PWIM[:, 0, :], PIM[:, 8, :], PIM[:, 8, :], ALU.max)
    for k in range(1, NL):
        tt(t["t1"][:], PWRE[:, k - 1, :], PWRE[:, k - 1, :], ALU.mult)
        tt(t["t2"][:], PWIM[:, k - 1, :], PWIM[:, k - 1, :], ALU.mult)
        tt(PWRE[:, k, :], t["t1"][:], t["t2"][:], ALU.subtract)
        tt(t["t1"][:], PWRE[:, k - 1, :], PWIM[:, k - 1, :], ALU.mult)
        ts(PWIM[:, k, :], t["t1"][:], 2.0, ALU.mult)
    ts(NPWIM[:], PWIM[:], -1.0, ALU.mult)

    pi = 0
    NT = NB // 512
    for gc in range(GC):
        for nt in range(NT):
            sl = slice(nt * 512, (nt + 1) * 512)
            for wsrc, H in ((Bre, Hre), (Bim, Him)):
                ps = pss[pi % 4]; pk = "ps%d" % (pi % 4); pi += 1
                P.pe(lambda e, ps=ps, wsrc=wsrc, gc=gc, sl=sl: e.matmul(ps[:], lhsT=wsrc[:, gc, :], rhs=ubb[:, gc, sl], start=True, stop=False), r=["Bpad", "ubb%d" % gc], w=[pk])
                P.pe(lambda e, ps=ps, wsrc=wsrc, gc=gc, sl=sl: e.matmul(ps[:], lhsT=wsrc[:, 2 + gc, :], rhs=ubb[:, 2 + gc, sl], start=False, stop=True), r=["Bpad", "ubb%d" % (2 + gc)], w=[pk])
                P.act(lambda e, ps=ps, H=H, gc=gc, sl=sl: e.activation(out=H[:, gc, sl], in_=ps[:], func=AF.Copy), r=[pk], w=["H%d" % gc])
    def views(g, s):
        hr = Hre[:, g, :]; hi = Him[:, g, :]
        vr = hr.rearrange("p (m t) -> p m t", t=2 * s); vi = hi.rearrange("p (m t) -> p m t", t=2 * s)
        return vr[:, :, 2 * s - 1], vr[:, :, s - 1], vi[:, :, 2 * s - 1], vi[:, :, s - 1]

    for k in range(NL):
        s = 1 << k
        for ph in range(2):
            for g in range(GC):
                hk = ["H%d" % g]
                tr_, sr_, ti_, si_ = views(g, s)
                a_r, a_i, na_i = PWRE[:, k, g:g + 1], PWIM[:, k, g:g + 1], NPWIM[:, k, g:g + 1]
                if ph == 0:
                    stt(tr_, sr_, a_r, tr_, ALU.mult, ALU.add, r=K + hk, w=["Hr%d" % g])
                    stt(ti_, si_, a_r, ti_, ALU.mult, ALU.add, r=K + hk, w=["Hi%d" % g])
                else:
                    stt(tr_, si_, na_i, tr_, ALU.mult, ALU.add, r=K + ["Hr%d" % g, "Hi%d" % g], w=hk + ["Hr%d" % g])
                    stt(ti_, sr_, a_i, ti_, ALU.mult, ALU.add, r=K + ["Hr%d" % g, "Hi%d" % g], w=hk + ["Hi%d" % g])
    for g in range(GC):
        hk = ["H%d" % g, "Hr%d" % g, "Hi%d" % g]
        P.dve(lambda e, g=g: e.memset(Hre[:, g, NB - 1:NB], 0.0), r=hk, w=hk)
        P.dve(lambda e, g=g: e.memset(Him[:, g, NB - 1:NB], 0.0), r=hk, w=hk)
    for k in range(NL - 1, -1, -1):
        s = 1 << k
        m = NB // (2 * s)
        for ph in range(4):
            for g in range(GC):
                hk = ["H%d" % g, "Hr%d" % g, "Hi%d" % g]
                TR, TI = Tres[g], Tims[g]
                Rr, Lr, Ri, Li = views(g, s)
                a_r, a_i, na_i = PWRE[:, k, g:g + 1], PWIM[:, k, g:g + 1], NPWIM[:, k, g:g + 1]
                if ph == 0:
                    stt(TR[:, 0:m], Rr, a_r, Lr, ALU.mult, ALU.add, r=K + hk, w=["Tr%d" % g])
                    stt(TI[:, 0:m], Ri, a_r, Li, ALU.mult, ALU.add, r=K + hk, w=["Ti%d" % g])
                elif ph == 1:
                    stt(TR[:, 0:m], Ri, na_i, TR[:, 0:m], ALU.mult, ALU.add, r=K + hk + ["Tr%d" % g], w=["Tr%d" % g])
                    stt(TI[:, 0:m], Rr, a_i, TI[:, 0:m], ALU.mult, ALU.add, r=K + hk + ["Ti%d" % g], w=["Ti%d" % g])
                elif ph == 2:
                    if m >= 256:
                        P.act(lambda e, Lr=Lr, Rr=Rr: e.activation(out=Lr, in_=Rr, func=AF.Copy), r=hk + ["Tr%d" % g, "Ti%d" % g], w=["L%d" % g])
                        P.act(lambda e, Li=Li, Ri=Ri: e.activation(out=Li, in_=Ri, func=AF.Copy), r=hk + ["Tr%d" % g, "Ti%d" % g], w=["L%d" % g])
                    else:
                        P.dve(lambda e, Lr=Lr, Rr=Rr: e.tensor_copy(out=Lr, in_=Rr), r=hk + ["Tr%d" % g, "Ti%d" % g], w=["L%d" % g])
                        P.dve(lambda e, Li=Li, Ri=Ri: e.tensor_copy(out=Li, in_=Ri), r=hk + ["Tr%d" % g, "Ti%d" % g], w=["L%d" % g])
                else:
                    P.dve(lambda e, Rr=Rr, m=m, TR=TR: e.tensor_copy(out=Rr, in_=TR[:, 0:m]), r=["Tr%d" % g, "L%d" % g], w=hk)
                    P.dve(lambda e, Ri=Ri, m=m, TI=TI: e.tensor_copy(out=Ri, in_=TI[:, 0:m]), r=["Ti%d" % g, "L%d" % g], w=hk)
    for g in range(GC):
        hk = ["H%d" % g, "Hr%d" % g, "Hi%d" % g, "L%d" % g]
        P.act(lambda e, g=g: e.activation(out=Hbre[:, g, :], in_=Hre[:, g, :], func=AF.Copy), r=hk, w=["Hb%d" % g])
        P.act(lambda e, g=g: e.activation(out=Hbim[:, g, :], in_=Him[:, g, :], func=AF.Copy), r=hk, w=["Hb%d" % g])
    for g in range(G):
        hf, gc = g // 2, g % 2
        hs = slice(hf * 64, (hf + 1) * 64)
        o = yo[g % 2]; ok = "yo%d" % (g % 2)
        for nt in range(NT):
            sl = slice(nt * 512, (nt + 1) * 512)
            ps = pss[pi % 4]; pk = "ps%d" % (pi % 4); pi += 1
            rr = ["Cw", "Dm", "ubb%d" % g, "Hb%d" % gc]
            P.pe(lambda e, ps=ps, gc=gc, hs=hs, sl=sl: e.matmul(ps[:], lhsT=Cre[hs, gc, :], rhs=Hbre[hs, gc, sl], start=True, stop=False), r=rr, w=[pk])
            P.pe(lambda e, ps=ps, gc=gc, hs=hs, sl=sl: e.matmul(ps[:], lhsT=Cimn[hs, gc, :], rhs=Hbim[hs, gc, sl], start=False, stop=False), r=rr, w=[pk])
            P.pe(lambda e, ps=ps, g=g, sl=sl: e.matmul(ps[:], lhsT=Dm[:, g, :], rhs=ubb[:, g, sl], start=False, stop=True), r=rr, w=[pk])
            P.act(lambda e, ps=ps, o=o, sl=sl: e.activation(out=o[:, sl], in_=ps[:], func=AF.Copy), r=[pk], w=[ok])
        P.dma(yb[g], o[:], r=[ok], w=["out"], is_out=True)
    return P.finish()


def s5_mixer_dev(uT, lam_re, lam_im, log_dt, b_re, b_im, c_re, c_im, d_skip):
    T = uT.shape[1]
    NB = T // 8
    key = ("s5", NB)
    if key not in _CACHE:
        _CACHE[key] = build_s5(NB)
    ub = uT.reshape(32, 16, NB, 8).transpose(0, 3, 1, 2).reshape(32, 128, NB)
    ii, jj = np.arange(128) // 16, np.arange(128) // 16
    cmask = (jj[None, :] >= ii[:, None]).astype(np.float32)
    ident = np.eye(128, dtype=np.float32)
    maps = []

    def pl(a):
        sh = a.shape[2:]
        a = a.reshape((2, 2, 64) + sh)
        a = np.moveaxis(a, 1, 2)
        return np.ascontiguousarray(a.reshape((128, 2) + sh))

    for c in range(NCORE):
        gs = slice(4 * c, 4 * c + 4)
        maps.append({
            "ub": np.ascontiguousarray(ub[gs]),
            "lre": pl(lam_re[gs]), "lim": pl(lam_im[gs]),
            "ldt": pl(np.ascontiguousarray(np.broadcast_to(log_dt[gs][:, None], (4, 64)))),
            "bre": pl(b_re[gs]), "bim": pl(b_im[gs]),
            "cre": pl(np.ascontiguousarray(c_re[gs].transpose(0, 2, 1))), "cim": pl(np.ascontiguousarray(c_im[gs].transpose(0, 2, 1))),
            "dcol": np.ascontiguousarray(np.tile(d_skip.reshape(32, 16)[gs].T, (8, 1))),
            "cmask": cmask, "ident": ident,
        })
    res = run(_CACHE[key], maps)
    yb = np.concatenate([r["yb"] for r in res], axis=0)
    return np.ascontiguousarray(yb.reshape(32, 8, 16, NB).transpose(0, 2, 3, 1).reshape(512, T))


def build_resln(N, D):
    P = Prog()
    x = P.dram_in("x", [N, D]); m = P.dram_in("m", [N, D]); gb = P.dram_in("gb", [128, 2, D])
    y = P.dram_out("y", [N, D])
    gbt = P.sb("gbt", [128, 2, D], F32)
    P.dma(gbt[:], gb, w=["gb"], q="gpsimd")
    NB_ = 6
    xt = [P.sb("xt%d" % i, [128, D], F32) for i in range(NB_)]
    mt = [P.sb("mt%d" % i, [128, D], F32) for i in range(NB_)]
    yt = [P.sb("yt%d" % i, [128, D], F32) for i in range(NB_)]
    jk = [P.sb("jk%d" % i, [128, D], F32) for i in range(2)]
    st = [P.sb("st%d" % i, [128, 8], F32) for i in range(NB_)]
    NTL = N // 128
    PF = 4

    def names(i):
        b = i % NB_
        return xt[b], mt[b], yt[b], st[b], "xt%d" % b, "mt%d" % b, "yt%d" % b, "st%d" % b

    def loads(i):
        X, M, Y, S, xk, mk, yk, sk = names(i)
        rs = slice(i * 128, (i + 1) * 128)
        P.dma(X[:], x[rs, :], w=[xk], q="sync")
        P.dma(M[:], m[rs, :], w=[mk], q="act")

    def stage_a(i):
        X, M, Y, S, xk, mk, yk, sk = names(i)
        P.dve(lambda e: e.memset(S[:], 0.0), w=[sk])
        P.dve(lambda e: e.scalar_tensor_tensor(out=X[:], in0=X[:], scalar=float(ALPHA), in1=M[:], op0=ALU.mult, op1=ALU.add), r=[xk, mk], w=[xk])
        P.act(lambda e: e.activation(out=jk[0][:], in_=X[:], func=AF.Copy, accum_out=S[:, 0:1]), r=[xk, sk], w=["jk0", sk])
        P.act(lambda e: e.activation(out=jk[1][:], in_=X[:], func=AF.Square, accum_out=S[:, 1:2]), r=[xk, sk], w=["jk1", sk])

    def stage_b(i):
        X, M, Y, S, xk, mk, yk, sk = names(i)
        P.dve(lambda e: e.tensor_scalar(out=S[:, 2:4], in0=S[:, 0:2], scalar1=1.0 / D, scalar2=None, op0=ALU.mult), r=[sk], w=[sk])
        P.dve(lambda e: e.tensor_tensor(out=S[:, 4:5], in0=S[:, 2:3], in1=S[:, 2:3], op=ALU.mult), r=[sk], w=[sk])
        P.dve(lambda e: e.tensor_tensor(out=S[:, 4:5], in0=S[:, 3:4], in1=S[:, 4:5], op=ALU.subtract), r=[sk], w=[sk])
        P.dve(lambda e: e.tensor_scalar(out=S[:, 4:5], in0=S[:, 4:5], scalar1=LN_EPS, scalar2=None, op0=ALU.add), r=[sk], w=[sk])
        P.act(lambda e: e.activation(out=S[:, 4:5], in_=S[:, 4:5], func=AF.Sqrt), r=[sk], w=[sk])

    def stage_c(i):
        X, M, Y, S, xk, mk, yk, sk = names(i)
        P.dve(lambda e: e.reciprocal(out=S[:, 5:6], in_=S[:, 4:5]), r=[sk], w=[sk])
        P.dve(lambda e: e.scalar_tensor_tensor(out=S[:, 6:7], in0=S[:, 2:3], scalar=-1.0, in1=S[:, 5:6], op0=ALU.mult, op1=ALU.mult), r=[sk], w=[sk])
        P.act(lambda e: e.activation(out=Y[:], in_=X[:], func=AF.Identity, scale=S[:, 5:6], bias=S[:, 6:7]), r=[xk, sk], w=[yk])

    def stage_d(i):
        X, M, Y, S, xk, mk, yk, sk = names(i)
        rs = slice(i * 128, (i + 1) * 128)
        P.dve(lambda e: e.tensor_tensor(out=Y[:], in0=Y[:], in1=gbt[:, 0, :], op=ALU.mult), r=[yk, "gb"], w=[yk])
        P.dve(lambda e: e.tensor_tensor(out=Y[:], in0=Y[:], in1=gbt[:, 1, :], op=ALU.add), r=[yk, "gb"], w=[yk])
        P.dma(y[rs, :], Y[:], r=[yk], w=["out"], is_out=True, q="gpsimd")

    for i in range(min(PF, NTL)):
        loads(i)
    for s_ in range(NTL + 3):
        if 0 <= s_ - 3 < NTL:
            stage_d(s_ - 3)
        if 0 <= s_ - 2 < NTL:
            stage_c(s_ - 2)
        if 0 <= s_ - 1 < NTL:
            stage_b(s_ - 1)
        if s_ < NTL:
            stage_a(s_)
            if s_ + PF < NTL:
                loads(s_ + PF)
    return P.finish()


def resln(x_tm, m_tm, g, b):
    T, D = x_tm.shape
    N = T // NCORE
    key = ("resln", N, D)
    if key not in _CACHE:
        _CACHE[key] = build_resln(N, D)
    gb = np.ascontiguousarray(np.broadcast_to(np.stack([g, b])[None], (128, 2, D))).astype(np.float32)
    maps = [{"x": np.ascontiguousarray(x_tm[c * N:(c + 1) * N]), "m": np.ascontiguousarray(m_tm[c * N:(c + 1) * N]), "gb": gb} for c in range(NCORE)]
    res = run(_CACHE[key], maps)
    return np.concatenate([r["y"] for r in res], axis=0)


def build_conv(N):
    P = Prog()
    bT = P.dram_in("bT", [512, N]); cT = P.dram_in("cT", [512, N + 2]); xT = P.dram_in("xT", [512, N + 2])
    w = P.dram_in("w", [128, 4, 3])
    yT = P.dram_out("yT", [512, N])
    wt = P.sb("wt", [128, 4, 3], F32)
    P.dma(wt[:], w, w=["w"])
    for a in range(4):
        bt = P.sb("bt%d" % a, [128, N], F32); ct = P.sb("ct%d" % a, [128, N + 2], F32); xt = P.sb("xt%d" % a, [128, N + 2], F32)
        acc = P.sb("acc%d" % a, [128, N], F32)
        rs = slice(a * 128, (a + 1) * 128)
        k = "c%d" % a
        P.dma(bt[:], bT[rs, :], w=[k + "b"]); P.dma(ct[:], cT[rs, :], w=[k]); P.dma(xt[:], xT[rs, :], w=[k + "x"])
        P.dve(lambda e, ct=ct, xt=xt: e.tensor_tensor(out=ct[:], in0=ct[:], in1=xt[:], op=ALU.mult), r=[k, k + "x"], w=[k])
        P.dve(lambda e, ct=ct, acc=acc, a=a: e.tensor_scalar(out=acc[:], in0=ct[:, 0:N], scalar1=wt[:, a, 0:1], scalar2=None, op0=ALU.mult), r=[k, "w"], w=[k + "a"])
        for j in (1, 2):
            P.dve(lambda e, ct=ct, acc=acc, a=a, j=j: e.scalar_tensor_tensor(out=acc[:], in0=ct[:, j:j + N], scalar=wt[:, a, j:j + 1], in1=acc[:], op0=ALU.mult, op1=ALU.add), r=[k, "w", k + "a"], w=[k + "a"])
        P.dve(lambda e, acc=acc, bt=bt: e.tensor_tensor(out=acc[:], in0=acc[:], in1=bt[:], op=ALU.mult), r=[k + "a", k + "b"], w=[k + "a"])
        P.dma(yT[rs, :], acc[:], r=[k + "a"], w=["out"], is_out=True)
    return P.finish()


def conv_dev(bT, cT, xT, cw):
    T = bT.shape[1]
    N = T // NCORE
    key = ("conv", N)
    if key not in _CACHE:
        _CACHE[key] = build_conv(N)
    cp = np.concatenate([np.zeros((512, 2), np.float32), cT], axis=1)
    xp = np.concatenate([np.zeros((512, 2), np.float32), xT], axis=1)
    w = np.ascontiguousarray(cw.reshape(3, 4, 128).transpose(2, 1, 0))
    maps = [{"bT": np.ascontiguousarray(bT[:, c * N:(c + 1) * N]), "cT": np.ascontiguousarray(cp[:, c * N:(c + 1) * N + 2]),
             "xT": np.ascontiguousarray(xp[:, c * N:(c + 1) * N + 2]), "w": w} for c in range(NCORE)]
    res = run(_CACHE[key], maps)
    return np.concatenate([r["yT"] for r in res], axis=1)


def build_pool(N):
    P = Prog()
    zT = P.dram_in("zT", [512, N + 16]); invc = P.dram_in("invc", [128, 4, N])
    oT = P.dram_out("oT", [512, N])
    for gi in range(4):
        z = P.sb("z%d" % gi, [128, N + 16], F32)
        sa = P.sb("sa%d" % gi, [128, N + 16], F32); sb_ = P.sb("sb%d" % gi, [128, N + 16], F32)
        ic = P.sb("ic%d" % gi, [128, N], F32)
        rs = slice(gi * 128, (gi + 1) * 128)
        k = "p%d" % gi
        P.dma(z[:], zT[rs, :], w=[k + "z"]); P.dma(ic[:], invc[:, gi, :], w=[k + "i"])
        P.dve(lambda e, sa=sa: e.memset(sa[:, 0:16], 0.0), w=[k + "a"])
        P.dve(lambda e, sb_=sb_: e.memset(sb_[:, 0:16], 0.0), w=[k + "b"])
        cur, curk = z, k + "z"
        bufs = [(sa, k + "a"), (sb_, k + "b")]
        for step in range(gi + 1):
            sh = 1 << step
            nxt, nk = bufs[step % 2]
            P.dve(lambda e, cur=cur, nxt=nxt, sh=sh: e.tensor_tensor(out=nxt[:, sh:], in0=cur[:, sh:], in1=cur[:, 0:N + 16 - sh], op=ALU.add), r=[curk], w=[nk])
            cur, curk = nxt, nk
        o, okey = bufs[(gi + 1) % 2]
        P.dve(lambda e, cur=cur, o=o, ic=ic: e.tensor_tensor(out=o[:, 16:], in0=cur[:, 16:], in1=ic[:], op=ALU.mult), r=[curk, k + "i"], w=[okey])
        P.dve(lambda e, o=o, z=z: e.tensor_tensor(out=o[:, 16:], in0=o[:, 16:], in1=z[:, 16:], op=ALU.subtract), r=[okey, k + "z"], w=[okey])
        P.dma(oT[rs, :], o[:, 16:], r=[okey], w=["out"], is_out=True)
    return P.finish()


def pool_dev(zT):
    T = zT.shape[1]
    N = T // NCORE
    key = ("pool", N)
    if key not in _CACHE:
        _CACHE[key] = build_pool(N)
    zp = np.concatenate([np.zeros((512, 16), np.float32), zT], axis=1)
    t = np.arange(T)
    inv = np.stack([1.0 / np.minimum(t + 1, w) for w in (2, 4, 8, 16)]).astype(np.float32)
    maps = []
    for c in range(NCORE):
        ic = np.ascontiguousarray(np.broadcast_to(inv[None, :, c * N:(c + 1) * N], (128, 4, N)))
        maps.append({"zT": np.ascontiguousarray(zp[:, c * N:(c + 1) * N + 16]), "invc": ic})
    res = run(_CACHE[key], maps)
    return np.concatenate([r["oT"] for r in res], axis=1)


def build_attn(T):
    P = Prog()
    NBK = T // 128
    qT = P.dram_in("qT", [64, T]); kT = P.dram_in("kT", [64, T]); v = P.dram_in("v", [T, 64])
    bias = P.dram_in("bias", [128, 5, 128])
    o_tm = P.dram_out("o", [T, 64])
    qb = P.sb("qb", [64, T], BF16); kb = P.sb("kb", [64, T], BF16); vb = P.sb("vb", [128, NBK, 65], BF16)
    bf = P.sb("bf", [128, 5, 128], F32); eb = P.sb("eb", [128, 5, 128], BF16)
    P.dve(lambda e: e.memset(vb[:], 1.0), w=["vb"])
    P.dma(bf[:], bias, w=["bf"])
    make_stage(P, 2048, n=4)
    for c0 in range(0, T, 2048):
        c1 = min(T, c0 + 2048)
        stage_cast(P, kb[:, c0:c1], kT[:, c0:c1], 64, c1 - c0, "kb")
        stage_cast(P, qb[:, c0:c1], qT[:, c0:c1], 64, c1 - c0, "qb", scale=0.125)
    vv = v.rearrange("(n p) d -> p n d", p=128)
    for j0 in range(0, NBK, 16):
        j1 = min(NBK, j0 + 16)
        P.dma(vb[:, j0:j1, 0:64], vv[:, j0:j1, :], w=["vb"], cast=True)
    P.act(lambda e: e.activation(out=eb[:], in_=bf[:], func=AF.Exp), r=["bf"], w=["eb"])
    NBUF = 3
    psS = [P.ps("psS%d" % i, [128, 5, 128]) for i in range(2)]
    psO = [P.ps("psO%d" % i, [128, 512]) for i in range(2)]
    pf = [P.sb("pf%d" % i, [128, 5, 128], BF16) for i in range(NBUF)]
    pt = [P.sb("pt%d" % i, [128, 5, 128], BF16) for i in range(NBUF)]
    rec = [P.sb("rec%d" % i, [128, 1], F32) for i in range(NBUF)]
    ob = [P.sb("ob%d" % i, [128, 16, 64], F32) for i in range(2)]
    o_v = o_tm.rearrange("(n p) d -> p n d", p=128)
    def names(m):
        b = m % 2
        return (psS[b], psO[b], pf[m % NBUF], pt[m % NBUF], rec[m % NBUF],
                "S%d" % b, "O%d" % b, "pf%d" % (m % NBUF), "pt%d" % (m % NBUF), "rc%d" % (m % NBUF))

    def front(m):
        S, O, PF, PT, RC, sk, okk, fk, pk, rk = names(m)
        qs = slice(m * 128, (m + 1) * 128)
        i0_ = max(0, 4 - m)
        for i in range(i0_, 5):
            kt = m - 4 + i
            P.pe(lambda e, i=i, kt=kt: e.matmul(S[:, i, :], lhsT=kb[:, kt * 128:(kt + 1) * 128], rhs=qb[:, qs], start=True, stop=True), r=["kb", "qb"], w=[sk])
        if i0_ < 4:
            P.act(lambda e: e.activation(out=PF[:, i0_:4, :], in_=S[:, i0_:4, :], func=AF.Exp), r=[sk], w=[fk])
        P.act(lambda e: e.activation(out=PF[:, 4, :], in_=S[:, 4, :], func=AF.Exp), r=[sk], w=[fk])
        P.dve(lambda e: e.tensor_tensor(out=PT[:, i0_:5, :], in0=PF[:, i0_:5, :], in1=eb[:, i0_:5, :], op=ALU.mult), r=[fk, "eb"], w=[pk])

    def back(m):
        S, O, PF, PT, RC, sk, okk, fk, pk, rk = names(m)
        OB = ob[(m // 16) % 2]; obk = "ob%d" % ((m // 16) % 2)
        val = list(range(max(0, 4 - m), 5))
        for n, i in enumerate(val):
            kt = m - 4 + i
            P.pe(lambda e, i=i, kt=kt, n=n: e.matmul(O[:, 0:65], lhsT=PT[:, i, :], rhs=vb[:, kt, :], start=(n == 0), stop=(n == len(val) - 1)), r=["vb", pk], w=[okk])
        P.dve(lambda e: e.reciprocal(out=RC[:], in_=O[:, 64:65]), r=[okk], w=[rk])
        c = m % 16
        P.dve(lambda e: e.tensor_scalar(out=OB[:, c, :], in0=O[:, 0:64], scalar1=RC[:, 0:1], scalar2=None, op0=ALU.mult), r=[okk, rk], w=[obk])
        if m % 16 == 15:
            g0 = (m // 16) * 16
            P.dma(o_v[:, g0:g0 + 16, :], OB[:], r=[obk], w=["out"], is_out=True)

    for s_ in range(NBK + 1):
        if s_ < NBK:
            front(s_)
        if s_ >= 1:
            back(s_ - 1)
    return P.finish()


def attn_dev(qT, kT, vT, rel_bias):
    T = qT.shape[1]
    key = ("attn", T)
    if key not in _CACHE:
        _CACHE[key] = build_attn(T)
    kk = np.arange(640)[:, None]; qq = np.arange(128)[None, :]
    qc = qq // 64; kc = kk // 64
    dist = (qq - (kk - 512))
    rel = np.clip(dist, -128, 128) + 128
    band = kc - qc
    valid = (band >= 0) & (band <= 8)
    maps = []
    for h in range(NCORE):
        b2 = np.where(valid, rel_bias[h][rel], np.float32(-30000.0)).astype(np.float32)
        b2 = np.ascontiguousarray(b2.reshape(5, 128, 128).transpose(1, 0, 2))
        hs = slice(h * 64, (h + 1) * 64)
        maps.append({"qT": np.ascontiguousarray(qT[hs]), "kT": np.ascontiguousarray(kT[hs]),
                     "v": np.ascontiguousarray(vT[hs].T), "bias": b2})
    res = run(_CACHE[key], maps)
    return np.ascontiguousarray(np.concatenate([r["o"].T for r in res], axis=0))


def _outproj(P, mixin, Wout, outT, N, pss, pi, mixkeys):
    NT = N // 512
    Wv = Wout.rearrange("(kt p) m -> p kt m", p=128)
    wo = [P.sb("wo%d" % i, [128, 8, 128], BF16) for i in range(3)]
    ot = [P.sb("oto%d" % i, [128, N], F32) for i in range(2)]
    for m in range(8):
        s_ = m % 3
        o = ot[m % 2]; ok = "oto%d" % (m % 2)
        P.dma(wo[s_][:], Wv[:, :, m * 128:(m + 1) * 128], w=["wo%d" % s_], cast=True)
        for nt in range(NT):
            sl = slice(nt * 512, (nt + 1) * 512)
            ps = pss[pi % len(pss)]; pk = "ps%d" % (pi % len(pss)); pi += 1
            for kt in range(8):
                P.pe(lambda e, ps=ps, s_=s_, kt=kt, sl=sl: e.matmul(ps[:], lhsT=wo[s_][:, kt, :], rhs=mixin[:, kt, sl], start=(kt == 0), stop=(kt == 7)), r=["wo%d" % s_, mixkeys[kt]], w=[pk])
            P.act(lambda e, o=o, ps=ps, sl=sl: e.activation(out=o[:, sl], in_=ps[:], func=AF.Identity), r=[pk], w=[ok])
        P.dma(outT[m * 128:(m + 1) * 128, :], o[:], r=[ok], w=["out"], is_out=True)
    return pi


def build_even_tail(N):
    P = Prog()
    NT = N // 512
    yS = P.dram_in("yS", [512, N]); bT = P.dram_in("bT", [512, N]); cT = P.dram_in("cT", [512, N + 2]); xT = P.dram_in("xT", [512, N + 2])
    cw = P.dram_in("cw", [128, 4, 3]); Wg = P.dram_in("Wg", [512, 512]); bg = P.dram_in("bg", [128, 4]); Wout = P.dram_in("Wout", [1024, 1024])
    outT = P.dram_out("outT", [1024, N])
    mixin = P.sb("mixin", [128, 8, N], BF16)
    mixkeys = ["mix%d" % k for k in range(8)]
    gf = P.sb("gf", [128, 4, N], F32); inb = P.sb("inb", [128, 4, N], BF16)
    xs = [P.sb("xs%d" % i, [128, N], F32) for i in range(2)]
    tq = P.sb("tq", [128, N], F32)
    bt_ = P.sb("bgt", [128, 4], F32); wt = P.sb("cwt", [128, 4, 3], F32)
    P.dma(bt_[:], bg, w=["bg"]); P.dma(wt[:], cw, w=["cw"])
    pss = [P.ps("ps%d" % i, [128, 512]) for i in range(4)]
    for kt in range(4):
        x = xs[kt % 2]; xk = "xs%d" % (kt % 2)
        P.dma(x[:], yS[kt * 128:(kt + 1) * 128, :], w=[xk])
        P.dve(lambda e, x=x: e.tensor_tensor(out=tq[:], in0=x[:], in1=x[:], op=ALU.mult), r=[xk], w=["tq"])
        P.dve(lambda e: e.tensor_scalar(out=tq[:], in0=tq[:], scalar1=0.044715, scalar2=1.0, op0=ALU.mult, op1=ALU.add), r=["tq"], w=["tq"])
        P.dve(lambda e, x=x: e.tensor_tensor(out=tq[:], in0=tq[:], in1=x[:], op=ALU.mult), r=["tq", xk], w=["tq"])
        P.act(lambda e: e.activation(out=tq[:], in_=tq[:], func=AF.Sigmoid, scale=1.5957691216057308), r=["tq"], w=["tq"])
        P.dve(lambda e, x=x, kt=kt: e.tensor_tensor(out=gf[:, kt, :], in0=tq[:], in1=x[:], op=ALU.mult), r=["tq", xk], w=["gf%d" % kt])
        P.act(lambda e, kt=kt: e.activation(out=inb[:, kt, :], in_=gf[:, kt, :], func=AF.Copy), r=["gf%d" % kt], w=["inb%d" % kt])
    cb_ = [P.sb("cvb%d" % i, [128, N], F32) for i in range(2)]
    cc_ = [P.sb("cvc%d" % i, [128, N + 2], F32) for i in range(2)]
    cx_ = [P.sb("cvx%d" % i, [128, N + 2], F32) for i in range(2)]
    ca_ = [P.sb("cva%d" % i, [128, N], F32) for i in range(2)]
    for a in range(4):
        b = a % 2
        bt, ct, xt, acc = cb_[b], cc_[b], cx_[b], ca_[b]
        rs = slice(a * 128, (a + 1) * 128)
        k = "cv%d" % b
        P.dma(bt[:], bT[rs, :], w=[k + "b"], q="act"); P.dma(ct[:], cT[rs, :], w=[k], q="sync"); P.dma(xt[:], xT[rs, :], w=[k + "x"], q="act")
        P.dve(lambda e, ct=ct, xt=xt: e.tensor_tensor(out=ct[:], in0=ct[:], in1=xt[:], op=ALU.mult), r=[k, k + "x"], w=[k])
        P.dve(lambda e, ct=ct, acc=acc, a=a: e.tensor_scalar(out=acc[:], in0=ct[:, 0:N], scalar1=wt[:, a, 0:1], scalar2=None, op0=ALU.mult), r=[k, "cw"], w=[k + "a"])
        for j in (1, 2):
            P.dve(lambda e, ct=ct, acc=acc, a=a, j=j: e.scalar_tensor_tensor(out=acc[:], in0=ct[:, j:j + N], scalar=wt[:, a, j:j + 1], in1=acc[:], op0=ALU.mult, op1=ALU.add), r=[k, "cw", k + "a"], w=[k + "a"])
        P.dve(lambda e, acc=acc, bt=bt, a=a: e.tensor_tensor(out=mixin[:, 4 + a, :], in0=acc[:], in1=bt[:], op=ALU.mult), r=[k + "a", k + "b"], w=[mixkeys[4 + a]])
    Wgv = Wg.rearrange("(kt p) m -> p kt m", p=128)
    wg = [P.sb("wg%d" % i, [128, 4, 128], BF16) for i in range(2)]
    sg = [P.sb("sg%d" % i, [128, 512], F32) for i in range(2)]
    pi = 0
    for m in range(4):
        s_ = m % 2
        P.dma(wg[s_][:], Wgv[:, :, m * 128:(m + 1) * 128], w=["wg%d" % s_], cast=True)
        for nt in range(NT):
            sl = slice(nt * 512, (nt + 1) * 512)
            ps = pss[pi % 4]; pk = "ps%d" % (pi % 4); pi += 1
            for kt in range(4):
                P.pe(lambda e, ps=ps, s_=s_, kt=kt, sl=sl: e.matmul(ps[:], lhsT=wg[s_][:, kt, :], rhs=inb[:, kt, sl], start=(kt == 0), stop=(kt == 3)), r=["wg%d" % s_, "inb%d" % kt], w=[pk])
            t = sg[nt % 2]; tk = "sg%d" % (nt % 2)
            P.act(lambda e, t=t, ps=ps, m=m: e.activation(out=t[:], in_=ps[:], func=AF.Sigmoid, bias=bt_[:, m:m + 1]), r=[pk, "bg"], w=[tk])
            P.dve(lambda e, t=t, m=m, sl=sl: e.tensor_tensor(out=mixin[:, m, sl], in0=t[:], in1=gf[:, m, sl], op=ALU.mult), r=[tk, "gf%d" % m], w=[mixkeys[m]])
    _outproj(P, mixin, Wout, outT, N, pss, pi, mixkeys)
    return P.finish()


def even_tail_dev(yS, hT, cw, Wg, bg, Wout):
    T = yS.shape[1]
    N = T // NCORE
    key = ("even_tail", N)
    if key not in _CACHE:
        _CACHE[key] = build_even_tail(N)
    bT, cT, xT = hT[512:1024], hT[1024:1536], hT[1536:2048]
    cp = np.concatenate([np.zeros((512, 2), np.float32), cT], axis=1)
    xp = np.concatenate([np.zeros((512, 2), np.float32), xT], axis=1)
    w = np.ascontiguousarray(cw.reshape(3, 4, 128).transpose(2, 1, 0))
    b = np.ascontiguousarray(bg.reshape(4, 128).T)
    maps = [{"yS": np.ascontiguousarray(yS[:, c * N:(c + 1) * N]), "bT": np.ascontiguousarray(bT[:, c * N:(c + 1) * N]),
             "cT": np.ascontiguousarray(cp[:, c * N:(c + 1) * N + 2]), "xT": np.ascontiguousarray(xp[:, c * N:(c + 1) * N + 2]),
             "cw": w, "Wg": np.ascontiguousarray(Wg), "bg": b, "Wout": np.ascontiguousarray(Wout)} for c in range(NCORE)]
    res = run(_CACHE[key], maps)
    return np.concatenate([r["outT"] for r in res], axis=1)


def build_odd_tail(N):
    P = Prog()
    NT = N // 512
    yc = P.dram_in("yc", [512, N]); zT = P.dram_in("zT", [512, N + 16]); invc = P.dram_in("invc", [128, 4, N])
    pw = P.dram_in("pw", [128, 4, 128]); psc = P.dram_in("psc", [128, 4]); Wout = P.dram_in("Wout", [1024, 1024])
    outT = P.dram_out("outT", [1024, N])
    mixin = P.sb("mixin", [128, 8, N], BF16)
    mixkeys = ["mix%d" % k for k in range(8)]
    make_stage(P, N)
    for kt in range(4):
        stage_cast(P, mixin[:, kt, :], yc[kt * 128:(kt + 1) * 128, :], 128, N, mixkeys[kt])
    pwb = P.sb("pwb", [128, 4, 128], BF16); sct = P.sb("sct", [128, 4], F32)
    P.dma(pwb[:], pw, w=["pw"], cast=True); P.dma(sct[:], psc, w=["psc"])
    pooled = P.sb("pooled", [128, 4, N], BF16)
    pss = [P.ps("ps%d" % i, [128, 512]) for i in range(4)]
    zb = [P.sb("pz%d" % i, [128, N + 16], F32) for i in range(2)]
    sab = [P.sb("psa%d" % i, [128, N + 16], F32) for i in range(2)]
    sbb = [P.sb("psb%d" % i, [128, N + 16], F32) for i in range(2)]
    icb = [P.sb("pic%d" % i, [128, N], F32) for i in range(2)]
    for b in range(2):
        P.dve(lambda e, b=b: e.memset(sab[b][:, 0:16], 0.0), w=["pl%da" % b])
        P.dve(lambda e, b=b: e.memset(sbb[b][:, 0:16], 0.0), w=["pl%db" % b])
    pi = 0
    for gi in range(4):
        b = gi % 2
        z, sa, sb_, ic = zb[b], sab[b], sbb[b], icb[b]
        rs = slice(gi * 128, (gi + 1) * 128)
        k = "pl%d" % b
        P.dma(z[:], zT[rs, :], w=[k + "z"], q="sync"); P.dma(ic[:], invc[:, gi, :], w=[k + "i"], q="act")
        cur, curk = z, k + "z"
        bufs = [(sa, k + "a"), (sb_, k + "b")]
        for step in range(gi + 1):
            sh = 1 << step
            nxt, nk = bufs[step % 2]
            P.dve(lambda e, cur=cur, nxt=nxt, sh=sh: e.tensor_tensor(out=nxt[:, sh:], in0=cur[:, sh:], in1=cur[:, 0:N + 16 - sh], op=ALU.add), r=[curk], w=[nk])
            cur, curk = nxt, nk
        o, okey = bufs[(gi + 1) % 2]
        P.dve(lambda e, cur=cur, o=o, ic=ic: e.tensor_tensor(out=o[:, 16:], in0=cur[:, 16:], in1=ic[:], op=ALU.mult), r=[curk, k + "i"], w=[okey])
        P.dve(lambda e, o=o, z=z, gi=gi: e.tensor_tensor(out=pooled[:, gi, :], in0=o[:, 16:], in1=z[:, 16:], op=ALU.subtract), r=[okey, k + "z"], w=["pooled%d" % gi])
        for nt in range(NT):
            sl = slice(nt * 512, (nt + 1) * 512)
            ps = pss[pi % 4]; pk = "ps%d" % (pi % 4); pi += 1
            P.pe(lambda e, ps=ps, gi=gi, sl=sl: e.matmul(ps[:], lhsT=pwb[:, gi, :], rhs=pooled[:, gi, sl], start=True, stop=True), r=["pw", "pooled%d" % gi], w=[pk])
            P.act(lambda e, ps=ps, gi=gi, sl=sl: e.activation(out=mixin[:, 4 + gi, sl], in_=ps[:], func=AF.Identity, scale=sct[:, gi:gi + 1]), r=[pk, "psc"], w=[mixkeys[4 + gi]])
    _outproj(P, mixin, Wout, outT, N, pss, pi, mixkeys)
    return P.finish()


def odd_tail_dev(ycT, zT, pool_w, pool_scale, Wout):
    T = zT.shape[1]
    N = T // NCORE
    key = ("odd_tail", N)
    if key not in _CACHE:
        _CACHE[key] = build_odd_tail(N)
    zp = np.concatenate([np.zeros((512, 16), np.float32), zT], axis=1)
    t = np.arange(T)
    inv = np.stack([1.0 / np.minimum(t + 1, w) for w in (2, 4, 8, 16)]).astype(np.float32)
    pw = np.ascontiguousarray(pool_w.transpose(1, 0, 2))
    psc = np.ascontiguousarray(pool_scale.reshape(4, 128).T)
    maps = []
    for c in range(NCORE):
        ic = np.ascontiguousarray(np.broadcast_to(inv[None, :, c * N:(c + 1) * N], (128, 4, N)))
        maps.append({"yc": np.ascontiguousarray(ycT[:, c * N:(c + 1) * N]), "zT": np.ascontiguousarray(zp[:, c * N:(c + 1) * N + 16]),
                     "invc": ic, "pw": pw, "psc": psc, "Wout": np.ascontiguousarray(Wout)})
    res = run(_CACHE[key], maps)
    return np.concatenate([r["outT"] for r in res], axis=1)


def kernel(x, p, ev_w_in, ev_lambda_re, ev_lambda_im, ev_log_dt, ev_b_re, ev_b_im,
           ev_c_re, ev_c_im, ev_d, ev_w_glu, ev_b_glu, ev_conv_w, ev_w_out,
           od_w_in, od_rel_bias, od_pool_w, od_pool_scale, od_w_out,
           ln_mix_g, ln_mix_b, ln_ffn_g, ln_ffn_b, ffn_w_up, ffn_w_down,
           ple_w_proj, ple_w_gate, ple_b_gate):
    f = lambda a: np.asarray(a, dtype=np.float32)
    x_tm = f(x)[0]
    xT = np.ascontiguousarray(x_tm.T)
    hT = lin(xT, f(ev_w_in[0]))
    for i in range(DEPTH):
        if i % 2 == 0:
            e = i // 2
            yS = s5_mixer_dev(np.ascontiguousarray(hT[0:512]), f(ev_lambda_re[e]), f(ev_lambda_im[e]), f(ev_log_dt[e]),
                              f(ev_b_re[e]), f(ev_b_im[e]), f(ev_c_re[e]), f(ev_c_im[e]), f(ev_d[e]))
            mixT = even_tail_dev(yS, hT, f(ev_conv_w[e]), f(ev_w_glu[e]), f(ev_b_glu[e]), f(ev_w_out[e]))
        else:
            o = i // 2
            ycT = attn_dev(hT[0:512], hT[512:1024], hT[1024:1536], f(od_rel_bias[o]))
            mixT = odd_tail_dev(ycT, hT[1536:2048], f(od_pool_w[o]), f(od_pool_scale[o]), f(od_w_out[o]))
        x1 = resln(x_tm, np.ascontiguousarray(mixT.T), f(ln_mix_g[i]), f(ln_mix_b[i]))
        x1T = np.ascontiguousarray(x1.T)
        ffnT = ffn_dev(x1T, f(ffn_w_up[i]), f(ffn_w_down[i]))
        x2 = resln(x1, np.ascontiguousarray(ffnT.T), f(ln_ffn_g[i]), f(ln_ffn_b[i]))
        x2T = np.ascontiguousarray(x2.T)
        pT = np.ascontiguousarray(f(p[i])[0].T)
        if i + 1 < DEPTH:
            wn = f(od_w_in[(i + 1) // 2]) if (i + 1) % 2 == 1 else f(ev_w_in[(i + 1) // 2])
        else:
            wn = None
        xT, hT = ple_dev(x2T, pT, f(ple_w_gate[i]), f(ple_b_gate[i]), f(ple_w_proj[i]), wn)
        x_tm = np.ascontiguousarray(xT.T)
    return x_tm[None].astype(np.float32)
```
